# Optimizing a Trainium2 kernel written in Bass

```python
import math
import jax, jax.numpy as jnp
from jax import lax
import numpy as np

D_MODEL = 2048
BATCH = 16
SEQ = 256
DEPTH = 1
DEC_BATCH = 2
DEC_SEQ = 2048
PAST_LEN = 512

GRID_W = 64
CHUNK = 128
N_DIR = 2
MIX_M = D_MODEL // 2
M_HEADS = 4
M_DV = MIX_M // M_HEADS
M_DQK = M_DV // 2
MIX_S = D_MODEL - MIX_M
S_HEADDIM = 64
S_HEADS = MIX_S // S_HEADDIM
S_STATE = 128
S_GROUPS = 4
CONV_K = 3
D_FF = 4 * D_MODEL
EPS = 1e-6
IN_SIZES = (M_HEADS * M_DQK, M_HEADS * M_DQK, MIX_M, MIX_M, N_DIR * M_HEADS, N_DIR * M_HEADS,
            MIX_S, MIX_S + 2 * S_GROUPS * S_STATE, N_DIR * S_HEADS)
D_IN_PROJ = sum(IN_SIZES)

kernel_name = "bidir_mlstm_ssd_hybrid_diffusion_step"


def _split_points(sizes):
    pts, acc = [], 0
    for s in sizes[:-1]:
        acc += s
        pts.append(acc)
    return pts


def rmsnorm(x, g):
    xf = x.astype(jnp.float32)
    y = xf * lax.rsqrt(jnp.mean(xf * xf, -1, keepdims=True) + EPS)
    return (y * g.astype(jnp.float32)).astype(x.dtype)


def modulation(cond, w_mod, b_mod):
    m = jax.nn.silu(cond) @ w_mod + b_mod
    return jnp.split(m[:, None, :], 6, axis=-1)


def grid_dwconv(u, w, b, rows):
    bn, L, cn = u.shape
    img = u.reshape(bn, rows, L // rows, cn)
    out = lax.conv_general_dilated(img, w[:, :, None, :].astype(u.dtype), (1, 1), 'SAME',
                                   dimension_numbers=('NHWC', 'HWIO', 'NHWC'), feature_group_count=cn)
    return out.reshape(bn, L, cn) + b


def mlstm_chunkwise(q, k, v, i_pre, logf, c0, n0, m0):
    f32 = jnp.float32
    bn, H, L, dqk = q.shape
    nc = L // CHUNK

    def to_chunks(t):
        return jnp.moveaxis(t.astype(f32).reshape(bn, H, nc, CHUNK, *t.shape[3:]), 2, 0)

    qc, kc, vc, ic, fc = map(to_chunks, (q * (dqk ** -0.5), k, v, i_pre, logf))
    causal = jnp.tril(jnp.ones((CHUNK, CHUNK), bool))

    def step(carry, inp):
        C, n, m = carry
        qb, kb, vb, ib, fb = inp
        b = jnp.cumsum(fb, -1)
        dmat = jnp.where(causal, b[..., :, None] - b[..., None, :] + ib[..., None, :], -jnp.inf)
        m_inter = b + m[..., None]
        m_t = jnp.maximum(m_inter, jnp.max(dmat, -1))
        w_inter = jnp.exp(m_inter - m_t)
        s = jnp.einsum('bhtd,bhsd->bhts', qb, kb) * jnp.exp(dmat - m_t[..., None])
        num = w_inter[..., None] * jnp.einsum('bhtd,bhde->bhte', qb, C) + jnp.einsum('bhts,bhse->bhte', s, vb)
        den = w_inter * jnp.einsum('bhtd,bhd->bht', qb, n) + jnp.sum(s, -1)
        h = num / jnp.maximum(jnp.abs(den), jnp.exp(-m_t))[..., None]
        b_last = b[..., -1]
        g = b_last[..., None] - b + ib
        m_new = jnp.maximum(b_last + m, jnp.max(g, -1))
        decay = jnp.exp(b_last + m - m_new)
        wk = jnp.exp(g - m_new[..., None])
        C_new = decay[..., None, None] * C + jnp.einsum('bhs,bhsd,bhse->bhde', wk, kb, vb)
        n_new = decay[..., None] * n + jnp.einsum('bhs,bhsd->bhd', wk, kb)
        return (C_new, n_new, m_new), h

    (C, n, m), hs = lax.scan(step, (c0.astype(f32), n0.astype(f32), m0.astype(f32)), (qc, kc, vc, ic, fc))
    h = jnp.moveaxis(hs, 0, 2).reshape(bn, H, L, -1)
    return h, C, n, m


def ssd_chunkwise(x, dt, a, Bm, Cm, s0):
    f32 = jnp.float32
    bn, L, H, P = x.shape
    rep = H // S_GROUPS
    nc = L // CHUNK

    def to_chunks(t):
        return jnp.moveaxis(t.astype(f32).reshape(bn, nc, CHUNK, *t.shape[2:]), 1, 0)

    xg = to_chunks(x.reshape(bn, L, S_GROUPS, rep, P))
    dtg = dt.astype(f32).reshape(bn, L, S_GROUPS, rep)
    lag = to_chunks(dtg * a.astype(f32).reshape(S_GROUPS, rep))
    dtc = to_chunks(dtg)
    Bc, Cc = to_chunks(Bm), to_chunks(Cm)
    causal = jnp.tril(jnp.ones((CHUNK, CHUNK), bool))[None, :, :, None, None]

    def step(S, inp):
        xb, lab, dtb, Bb, Cb = inp
        cs = jnp.cumsum(lab, axis=1)
        seg = jnp.exp(jnp.where(causal, cs[:, :, None] - cs[:, None, :], -jnp.inf))
        cb = jnp.einsum('btgn,bsgn->btsg', Cb, Bb)
        mix = cb[..., None] * seg * dtb[:, None]
        y = jnp.einsum('btsgr,bsgrp->btgrp', mix, xb) + jnp.einsum('btgn,bgrpn,btgr->btgrp', Cb, S, jnp.exp(cs))
        tot = cs[:, -1]
        wk = jnp.exp(tot[:, None] - cs) * dtb
        S_new = jnp.exp(tot)[..., None, None] * S + jnp.einsum('bsgr,bsgrp,bsgn->bgrpn', wk, xb, Bb)
        return S_new, y

    S, ys = lax.scan(step, s0.astype(f32).reshape(bn, S_GROUPS, rep, P, S_STATE), (xg, lag, dtc, Bc, Cc))
    y = jnp.moveaxis(ys, 0, 1).reshape(bn, L, H, P)
    return y, S.reshape(bn, H, P, S_STATE)


def mixer(u, rows, state, p):
    f32 = jnp.float32
    bn, L, _ = u.shape
    c0, n0, m0, s0 = state
    q, k, v, o, ig, fg, z, xbc, dt = jnp.split(u @ p['w_in'], _split_points(IN_SIZES), axis=-1)

    def heads(t, d):
        return t.reshape(bn, L, M_HEADS, d).transpose(0, 2, 1, 3)
    qh, kh, vh = heads(q, M_DQK), heads(k, M_DQK), heads(v, M_DV)
    ig = (ig.astype(f32).reshape(bn, L, N_DIR, M_HEADS) + p['b_igate']).transpose(2, 0, 3, 1)
    lf = jax.nn.log_sigmoid(fg.astype(f32).reshape(bn, L, N_DIR, M_HEADS) + p['b_fgate']).transpose(2, 0, 3, 1)
    fl = lambda t: jnp.flip(t, axis=2)
    h_f, cf, nf, mf = mlstm_chunkwise(qh, kh, vh, ig[0], lf[0], c0[:, 0], n0[:, 0], m0[:, 0])
    h_b, cbk, nbk, mbk = mlstm_chunkwise(fl(qh), fl(kh), fl(vh), fl(ig[1]), fl(lf[1]), c0[:, 1], n0[:, 1], m0[:, 1])
    h_m = (h_f + fl(h_b)).transpose(0, 2, 1, 3).astype(u.dtype)
    h_m = rmsnorm(h_m, p['g_mlstm_norm'].reshape(M_HEADS, M_DV)).reshape(bn, L, MIX_M) * jax.nn.sigmoid(o)

    xbc = jax.nn.silu(grid_dwconv(xbc, p['conv_w'], p['conv_b'], rows))
    xs, Bs, Cs = jnp.split(xbc, [MIX_S, MIX_S + S_GROUPS * S_STATE], axis=-1)
    xs = xs.reshape(bn, L, S_HEADS, S_HEADDIM)
    Bs = Bs.reshape(bn, L, S_GROUPS, S_STATE)
    Cs = Cs.reshape(bn, L, S_GROUPS, S_STATE)
    dt = jax.nn.softplus(dt.astype(f32).reshape(bn, L, N_DIR, S_HEADS) + p['dt_bias'])
    a = -jnp.exp(p['a_log'].astype(f32))
    f1 = lambda t: jnp.flip(t, axis=1)
    y_f, sf = ssd_chunkwise(xs, dt[:, :, 0], a[0], Bs, Cs, s0[:, 0])
    y_b, sbk = ssd_chunkwise(f1(xs), f1(dt[:, :, 1]), a[1], f1(Bs), f1(Cs), s0[:, 1])
    y_s = (y_f + f1(y_b) + p['d_skip'][:, None].astype(f32) * xs.astype(f32)).astype(u.dtype)
    y_s = rmsnorm(y_s.reshape(bn, L, MIX_S) * jax.nn.silu(z), p['g_ssd_norm'])

    out = jnp.concatenate([h_m, y_s], axis=-1) @ p['w_out']
    new_state = (jnp.stack([cf, cbk], 1), jnp.stack([nf, nbk], 1), jnp.stack([mf, mbk], 1), jnp.stack([sf, sbk], 1))
    return out, new_state


def block(x, cond, rows, state, p):
    sh1, sc1, g1, sh2, sc2, g2 = modulation(cond, p['w_mod'], p['b_mod'])
    u = rmsnorm(x, p['g_pre_mix']) * (1.0 + sc1) + sh1
    mix, new_state = mixer(u, rows, state, p)
    x = x + g1 * rmsnorm(mix, p['g_post_mix'])
    u = rmsnorm(x, p['g_pre_mlp']) * (1.0 + sc2) + sh2
    hdn = jnp.square(jax.nn.relu(u @ p['w_mlp_in']))
    x = x + g2 * rmsnorm(hdn @ p['w_mlp_out'], p['g_post_mlp'])
    return x, new_state


def setup_inputs(seed: int = 0) -> dict:
    key = jax.random.key(seed)
    ks = iter(jax.random.split(key, 40))
    f32 = jnp.float32
    D = D_MODEL

    def nrm(shape, s):
        return s * jax.random.normal(next(ks), shape, f32)

    def gain(shape):
        return 1.0 + nrm(shape, 0.02)

    dt0 = jnp.exp(jax.random.uniform(next(ks), (DEPTH, N_DIR, S_HEADS), f32, math.log(1e-3), math.log(1e-1)))
    dt_bias = dt0 + jnp.log(-jnp.expm1(-dt0))
    a_log = jnp.log(jax.random.uniform(next(ks), (DEPTH, N_DIR, S_HEADS), f32, 1.0, 16.0))
    b_fgate = jax.random.uniform(next(ks), (DEPTH, N_DIR, M_HEADS), f32, 3.0, 6.0)
    return {
        "x_prompt": nrm((BATCH, SEQ, D), 1.0),
        "x_sample": nrm((DEC_BATCH, DEC_SEQ, D), 1.0),
        "state_mlstm_c": nrm((DEC_BATCH, DEPTH, N_DIR, M_HEADS, M_DQK, M_DV), 0.1),
        "state_mlstm_n": nrm((DEC_BATCH, DEPTH, N_DIR, M_HEADS, M_DQK), 0.5),
        "state_mlstm_m": nrm((DEC_BATCH, DEPTH, N_DIR, M_HEADS), 1.0),
        "state_ssd": nrm((DEC_BATCH, DEPTH, N_DIR, S_HEADS, S_HEADDIM, S_STATE), 0.1),
        "c": nrm((DEC_BATCH, D), 1.0),
        "c_ctx": nrm((D,), 1.0),
        "w_mod": nrm((DEPTH, D, 6 * D), 0.2 * D ** -0.5),
        "b_mod": nrm((DEPTH, 6 * D), 0.01),
        "g_pre_mix": gain((DEPTH, D)),
        "g_post_mix": gain((DEPTH, D)),
        "w_in": nrm((DEPTH, D, D_IN_PROJ), D ** -0.5),
        "b_igate": nrm((DEPTH, N_DIR, M_HEADS), 0.1),
        "b_fgate": b_fgate,
        "conv_w": nrm((DEPTH, CONV_K, CONV_K, MIX_S + 2 * S_GROUPS * S_STATE), (CONV_K * CONV_K) ** -0.5),
        "conv_b": nrm((DEPTH, MIX_S + 2 * S_GROUPS * S_STATE), 0.01),
        "dt_bias": dt_bias,
        "a_log": a_log,
        "d_skip": gain((DEPTH, S_HEADS)),
        "g_mlstm_norm": gain((DEPTH, MIX_M)),
        "g_ssd_norm": gain((DEPTH, MIX_S)),
        "w_out": nrm((DEPTH, D, D), D ** -0.5),
        "g_pre_mlp": gain((DEPTH, D)),
        "g_post_mlp": gain((DEPTH, D)),
        "w_mlp_in": nrm((DEPTH, D, D_FF), D ** -0.5),
        "w_mlp_out": nrm((DEPTH, D_FF, D), D_FF ** -0.5),
    }


def reference(x_prompt, x_sample, state_mlstm_c, state_mlstm_n, state_mlstm_m, state_ssd, c, c_ctx,
              w_mod, b_mod, g_pre_mix, g_post_mix, w_in, b_igate, b_fgate, conv_w, conv_b, dt_bias, a_log,
              d_skip, g_mlstm_norm, g_ssd_norm, w_out, g_pre_mlp, g_post_mlp, w_mlp_in, w_mlp_out):
    f32 = jnp.float32
    bp = x_prompt.shape[0]
    rows = x_sample.shape[1] // GRID_W
    zero_state = (jnp.zeros((bp, N_DIR, M_HEADS, M_DQK, M_DV), f32),
                  jnp.zeros((bp, N_DIR, M_HEADS, M_DQK), f32),
                  jnp.zeros((bp, N_DIR, M_HEADS), f32),
                  jnp.zeros((bp, N_DIR, S_HEADS, S_HEADDIM, S_STATE), f32))
    hp, hs = x_prompt, x_sample
    nc_list, nn_list, nm_list, ns_list = [], [], [], []
    for l in range(DEPTH):
        p = dict(w_mod=w_mod[l], b_mod=b_mod[l], g_pre_mix=g_pre_mix[l], g_post_mix=g_post_mix[l],
                 w_in=w_in[l], b_igate=b_igate[l], b_fgate=b_fgate[l], conv_w=conv_w[l], conv_b=conv_b[l],
                 dt_bias=dt_bias[l], a_log=a_log[l], d_skip=d_skip[l], g_mlstm_norm=g_mlstm_norm[l],
                 g_ssd_norm=g_ssd_norm[l], w_out=w_out[l], g_pre_mlp=g_pre_mlp[l], g_post_mlp=g_post_mlp[l],
                 w_mlp_in=w_mlp_in[l], w_mlp_out=w_mlp_out[l])
        hp, st = block(hp, c_ctx[None, :], 1, zero_state, p)
        nc_list.append(st[0]); nn_list.append(st[1]); nm_list.append(st[2]); ns_list.append(st[3])
        cache_l = (state_mlstm_c[:, l], state_mlstm_n[:, l], state_mlstm_m[:, l], state_ssd[:, l])
        hs, _ = block(hs, c, rows, cache_l, p)
    new_c = jnp.stack(nc_list, 1).astype(x_prompt.dtype)
    new_n = jnp.stack(nn_list, 1).astype(x_prompt.dtype)
    new_m = jnp.stack(nm_list, 1).astype(x_prompt.dtype)
    new_s = jnp.stack(ns_list, 1).astype(x_prompt.dtype)
    return (hp, hs, new_c, new_n, new_m, new_s)
```

```python
import contextlib
import numpy as np
import concourse.bass as bass
import concourse.mybir as mybir
from concourse.bass_utils import run_bass_kernel_spmd

F32 = mybir.dt.float32
BF16 = mybir.dt.bfloat16
AF = mybir.ActivationFunctionType
ALU = mybir.AluOpType
AX = mybir.AxisListType

D = 2048
KT = 16
EPS = 1e-6
NEG = -30000.0
NFLAG = 80
ESZ = 1024 + 4 + 4 + 1024


class Buf:
    __slots__ = ("name", "last_w", "readers", "excl")

    def __init__(self, name="", excl=False):
        self.name = name
        self.last_w = None
        self.readers = []
        self.excl = excl


class Op:
    __slots__ = ("idx", "eng", "fn", "deps", "is_dma", "sig", "sem", "val", "pre")

    def __init__(self, idx, eng, fn, is_dma):
        self.idx = idx
        self.eng = eng
        self.fn = fn
        self.deps = set()
        self.is_dma = is_dma
        self.sig = False
        self.sem = None
        self.val = 0
        self.pre = None


ENGS = ("pe", "act", "dve", "pool", "sp")
N_DMA_SEMS = 48


class Prog:
    def __init__(self, nc):
        self.nc = nc
        self.ops = []
        self.last_on = {}
        self.dma_since_bar = []

    def add(self, eng, fn, reads=(), writes=(), dma=False):
        op = Op(len(self.ops), eng, fn, dma)
        for r in reads:
            if r.last_w is not None:
                op.deps.add(r.last_w)
            if r.excl:
                for rd in r.readers:
                    if rd.eng != eng:
                        op.deps.add(rd)
        for w in writes:
            lw = w.last_w
            if lw is not None and (dma or lw.is_dma or lw.eng != eng):
                op.deps.add(lw)
            last_rd = {}
            for rd in w.readers:
                if rd.is_dma:
                    op.deps.add(rd)
                elif dma or rd.eng != eng:
                    last_rd[rd.eng] = rd
            for rd in last_rd.values():
                op.deps.add(rd)
        for w in writes:
            w.last_w = op
            w.readers = []
        for r in reads:
            r.readers.append(op)
        op.deps.discard(op)
        if not dma:
            self.last_on[eng] = op
        else:
            self.dma_since_bar.append(op)
        self.ops.append(op)
        return op

    def pe(self, fn, reads=(), writes=()):
        return self.add("pe", fn, reads, writes)

    def act(self, fn, reads=(), writes=()):
        return self.add("act", fn, reads, writes)

    def dve(self, fn, reads=(), writes=()):
        return self.add("dve", fn, reads, writes)

    def pool(self, fn, reads=(), writes=()):
        return self.add("pool", fn, reads, writes)

    def dma(self, fn, reads=(), writes=(), eng="sp"):
        return self.add(eng, fn, reads, writes, dma=True)

    def barrier(self):
        prev = [o for o in self.last_on.values()] + list(self.dma_since_bar)
        self.dma_since_bar = []
        for eng in ENGS:
            op = Op(len(self.ops), eng, (lambda e: e.nop()), False)
            op.deps = set(prev)
            self.ops.append(op)
            self.last_on[eng] = op

    def emit(self):
        nc = self.nc
        ops = self.ops
        for op in ops:
            for d in op.deps:
                d.sig = True
        with contextlib.ExitStack() as es:
            esem = {e: es.enter_context(nc.semaphore("s_" + e)) for e in ENGS}
            dsems = [es.enter_context(nc.semaphore("d%d" % i)) for i in range(N_DMA_SEMS)]
            cnt = {e: 0 for e in esem}
            dcnt = [0] * N_DMA_SEMS
            dlast = [None] * N_DMA_SEMS
            rr = 0
            rr_sw = 0
            rr_hw = 0
            N_SW = 16
            for op in ops:
                if op.is_dma:
                    if op.eng == "pool":
                        s = rr_sw % N_SW
                        rr_sw += 1
                    else:
                        s = N_SW + rr_hw % (N_DMA_SEMS - N_SW)
                        rr_hw += 1
                    rr += 1
                    op.pre = dlast[s]
                    dcnt[s] += 16
                    op.sem = dsems[s]
                    op.val = dcnt[s]
                    dlast[s] = op
                elif op.sig:
                    cnt[op.eng] += 1
                    op.sem = esem[op.eng]
                    op.val = cnt[op.eng]
            self.stats = dict(cnt=dict(cnt), n_dma=rr, n_ops=len(ops))
            by_eng = {e: [o for o in ops if o.eng == e] for e in ENGS}
            last_dma = [d for d in dlast if d is not None]
            blk = es.enter_context(nc.Block())

            def run(engname, e):
                seen = {}

                def wait(d):
                    key = id(d.sem)
                    if seen.get(key, 0) >= d.val:
                        return
                    seen[key] = d.val
                    e.wait_ge(d.sem, d.val)

                for op in by_eng[engname]:
                    if op.pre is not None:
                        wait(op.pre)
                    for d in sorted(op.deps, key=lambda o: o.idx):
                        wait(d)
                    ins = op.fn(e)
                    if op.sem is not None:
                        ins.then_inc(op.sem, 16 if op.is_dma else 1)
                if engname == "sp":
                    for d in last_dma:
                        wait(d)

            @blk.tensor
            def _(e):
                run("pe", e)

            @blk.scalar
            def _(e):
                run("act", e)

            @blk.vector
            def _(e):
                run("dve", e)

            @blk.gpsimd
            def _(e):
                run("pool", e)

            @blk.sync
            def _(e):
                run("sp", e)


class NS:
    pass


NSC = 232
SC_A, SC_B, SC_BT, SC_AMX = 0, 8, 16, 24
SC_DT, SC_CS, SC_ECS, SC_WK, SC_ETOT, SC_NCS = 32, 64, 96, 128, 160, 200
R_BIG, R_BFG, R_DTB, R_ALOG, R_DSK = 0, 8, 16, 48, 80
C_GPRE, C_GMLP, C_CW, C_CB = 0, 16, 32, 176


class _Stop(Exception):
    pass


def build_program(stop_after=None):
    nc = bass.Bass("TRN2", target_bir_lowering=False)
    P = Prog(nc)

    def check(stage):
        if stop_after == stage:
            P.emit()
            raise _Stop()

    def din(name, shape):
        return nc.dram_tensor(name, list(shape), F32, kind="ExternalInput").ap()

    def dout(name, shape):
        return nc.dram_tensor(name, list(shape), F32, kind="ExternalOutput").ap()

    def dscr(name, shape, dt=F32):
        return nc.dram_tensor(name, list(shape), dt, kind="Internal").ap()

    xp = din("xp", [512, D])
    xo = din("xo", [512, D])
    xh = din("xh", [128, D])
    xf = din("xf", [2048, D])
    cv = din("cv", [32, 128])
    st_c = din("st_c", [2, 4, 128, 256])
    st_n = din("st_n", [2, 4, 128])
    st_m = din("st_m", [1, 8])
    st_s = din("st_s", [2, 1024, 128])
    flags_d = din("flags", [1, NFLAG])
    w_mod = din("w_mod", [D, 6 * D])
    b_mod = din("b_mod", [1, 6 * D])
    g_pre_mix = din("g_pre_mix", [16, 128])
    g_post_mix = din("g_post_mix", [1, D])
    w_in = din("w_in", [D, 6192])
    b_ig = din("b_igate", [1, 8])
    b_fg = din("b_fgate", [1, 8])
    conv_w = din("conv_w", [144, 128])
    conv_b = din("conv_b", [16, 128])
    dt_bias = din("dt_bias", [1, 32])
    a_log = din("a_log", [1, 32])
    d_skip = din("d_skip", [1, 16])
    g_ml = din("g_mlstm_norm", [1, 1024])
    g_sn = din("g_ssd_norm", [1, 1024])
    w_out = din("w_out", [D, D])
    g_pre_mlp = din("g_pre_mlp", [16, 128])
    g_post_mlp = din("g_post_mlp", [1, D])
    w_mlp_in = din("w_mlp_in", [D, 4 * D])
    w_mlp_out = din("w_mlp_out", [4 * D, D])
    yp = dout("yp", [512, D])
    ys = dout("ys", [512, D])
    o_c = dout("o_c", [2, 2, 4, 128, 256])
    o_n = dout("o_n", [2, 2, 4, 128])
    o_m = dout("o_m", [2, 2, 4])
    o_s = dout("o_s", [2, 2, 1024, 128])
    modscr = dscr("modscr", [2, 6 * D])
    stg = dscr("stg", [16, 128, 3072], BF16)
    x1s = dscr("x1s", [1024, D])
    catTs = dscr("catTs", [8, 128, KT * 128], BF16)
    Es = dscr("Es", [4, 128, ESZ])
    B_modscr = Buf("modscr")
    B_stg = [Buf("stg%d" % i) for i in range(16)]
    B_x1s = [Buf("x1s%d" % i) for i in range(8)]
    B_cats = [Buf("cats%d" % i) for i in range(8)]
    B_Es = [Buf("Es%d" % i) for i in range(4)]

    ES = contextlib.ExitStack()
    with ES:
        tcount = [0]

        def T(name, shape, dt=F32, es=ES):
            tcount[0] += 1
            return es.enter_context(nc.sbuf_tensor("%s_%d" % (name, tcount[0]), list(shape), dt))

        pbig = ES.enter_context(nc.psum_tensor("pbig", [128, 2048], F32))
        PB = [Buf("pb%d" % i, excl=True) for i in range(4)]
        ptr = [ES.enter_context(nc.psum_tensor("ptr%d" % i, [128, 512], F32)) for i in range(2)]
        PT = [Buf("pt%d" % i, excl=True) for i in range(2)]
        psm = [ES.enter_context(nc.psum_tensor("psm%d" % i, [128, 512], F32)) for i in range(2)]
        PS = [Buf("ps%d" % i, excl=True) for i in range(2)]
        rot = {"tr": 0, "sm": 0, "big": 0}

        def next_tr():
            i = rot["tr"] % 2
            rot["tr"] += 1
            return ptr[i], PT[i]

        def next_sm():
            i = rot["sm"] % 2
            rot["sm"] += 1
            return psm[i], PS[i]

        def next_big():
            i = rot["big"] % 4
            rot["big"] += 1
            return pbig[:, i * 512:(i + 1) * 512], PB[i]

        def next_big_m(M):
            i = 2 * M.pair + (M.brot[0] % 2)
            M.brot[0] += 1
            return pbig[:, i * 512:(i + 1) * 512], PB[i]

        B_c = Buf("consts")
        ident = T("ident", [128, 128])
        identb = T("identb", [128, 128], BF16)
        LT = T("LT", [128, 128])
        UT = T("UT", [128, 128])
        NM = [T("negf", [128, 128]), T("negb", [128, 128])]
        MK = [LT, UT]
        ones = T("ones", [128, 128])
        flags = T("flags_sb", [128, NFLAG])
        P.pool(lambda e: e.memset(ident[:], 0.0), writes=[B_c])
        P.pool(lambda e: e.affine_select(out=ident[:], in_=ident[:], pattern=[[-1, 128]], compare_op=ALU.not_equal,
                                         fill=1.0, base=0, channel_multiplier=1), reads=[B_c], writes=[B_c])
        P.pool(lambda e: e.memset(ones[:], 1.0), writes=[B_c])
        P.pool(lambda e: e.memset(LT[:], 1.0), writes=[B_c])
        P.pool(lambda e: e.affine_select(out=LT[:], in_=LT[:], pattern=[[1, 128]], compare_op=ALU.is_ge,
                                         fill=0.0, base=0, channel_multiplier=-1), reads=[B_c], writes=[B_c])
        P.pool(lambda e: e.memset(UT[:], 1.0), writes=[B_c])
        P.pool(lambda e: e.affine_select(out=UT[:], in_=UT[:], pattern=[[-1, 128]], compare_op=ALU.is_ge,
                                         fill=0.0, base=0, channel_multiplier=1), reads=[B_c], writes=[B_c])
        P.dve(lambda e: e.tensor_copy(out=identb[:], in_=ident[:]), reads=[B_c], writes=[B_c])
        for dd in range(2):
            P.dve(lambda e, dd=dd: e.tensor_scalar(out=NM[dd][:], in0=MK[dd][:], scalar1=-1.0, scalar2=-NEG,
                                                   op0=ALU.add, op1=ALU.mult), reads=[B_c], writes=[B_c])
        P.dma(lambda e: e.dma_start(out=flags[:], in_=flags_d[0:1, :].to_broadcast([128, NFLAG])), writes=[B_c])

        rowp = T("rowp", [128, 96])
        for (src, off, n) in ((b_ig, R_BIG, 8), (b_fg, R_BFG, 8), (dt_bias, R_DTB, 32), (a_log, R_ALOG, 32), (d_skip, R_DSK, 16)):
            P.dma(lambda e, src=src, off=off, n=n: e.dma_start(out=rowp[:, off:off + n], in_=src[0:1, :].to_broadcast([128, n])),
                  writes=[B_c])
        arow = T("arow", [128, 32])
        P.act(lambda e: e.activation(out=arow[:], in_=rowp[:, R_ALOG:R_ALOG + 32], func=AF.Exp), reads=[B_c], writes=[B_c])
        P.dve(lambda e: e.tensor_scalar(out=arow[:], in0=arow[:], scalar1=-1.0, scalar2=None, op0=ALU.mult), reads=[B_c], writes=[B_c])

        colp = T("colp", [128, 192])
        ld = T("ldtmp", [128, 128])
        B_ld = Buf("ld")

        def col_load(src, nrows, off, r0=0):
            P.dma(lambda e: e.dma_start(out=ld[0:nrows, :], in_=src[r0:r0 + nrows, :]), writes=[B_ld])
            pt, pb = next_sm()
            P.pe(lambda e: e.transpose(pt[:, 0:nrows], ld[0:nrows, :], ident[0:nrows, 0:nrows]), reads=[B_ld, B_c], writes=[pb])
            P.dve(lambda e: e.tensor_copy(out=colp[:, off:off + nrows], in_=pt[:, 0:nrows]), reads=[pb], writes=[B_c])

        col_load(g_pre_mix, 16, C_GPRE)
        col_load(g_pre_mlp, 16, C_GMLP)
        col_load(conv_w, 128, C_CW)
        col_load(conv_w, 16, C_CW + 128, r0=128)
        col_load(conv_b, 16, C_CB)

        def cw(ct, kk):
            c0 = C_CW + kk * 16 + ct
            return colp[:, c0:c0 + 1]

        def cbias(ct):
            return colp[:, C_CB + ct:C_CB + ct + 1]

        wg = T("wg", [128, KT, 48])
        P.dma(lambda e: e.dma_start(out=wg[:, :, 0:16], in_=w_in[:, 3072:3088].rearrange("(kt p) c -> p kt c", p=128)), writes=[B_c])
        P.dma(lambda e: e.dma_start(out=wg[:, :, 16:48], in_=w_in[:, 6160:6192].rearrange("(kt p) c -> p kt c", p=128)), writes=[B_c])

        wgm = T("wgm", [128, 2, KT, 48])
        gbB = T("gbB", [128, 2, 48])
        B_wgm = Buf("wgm")
        csT = T("csT", [128, 32], BF16)
        B_cv = Buf("cv")
        modT = T("modT", [128, 2, 96])
        B_modT = Buf("modT")
        modc = T("modc", [128, 2, 4, 16])
        B_modc = Buf("modc")
        stat = T("stat", [128, 64])
        B_stat = Buf("stat")
        sq_junk = T("sqjunk", [128, D], BF16)
        B_junk = Buf("junk")
        gtmp = T("gtmp", [128, 128])
        B_gt = Buf("gtmp")
        amx8 = T("amx8", [8, 128])
        B_amx = Buf("amx8")

        def alloc_inproj(es, nbuf=2):
            I = NS()
            I.xt = [T("xt%d" % i, [128, D], es=es) for i in range(2)]
            I.B_xt = [Buf("xt0"), Buf("xt1")]
            I.xrot = [0]
            I.uTf_flat = T("uTf", [128, KT * 128], es=es)
            I.uTf = I.uTf_flat[:].rearrange("p (k t) -> p k t", k=KT)
            I.B_uTf = Buf("uTf")
            I.wbuf = [T("wbuf%d" % i, [128, KT, 512], BF16, es=es) for i in range(nbuf)]
            I.B_w = [Buf("wbuf%d" % i) for i in range(nbuf)]
            I.wrot = [0]
            return I

        def load_wblock(I, src_ap):
            i = I.wrot[0] % len(I.wbuf)
            I.wrot[0] += 1
            ncols = src_ap.shape[2]
            nk = src_ap.shape[1]
            wt, wb = I.wbuf[i], I.B_w[i]
            P.dma(lambda e: e.dma_start(out=wt[:, 0:nk, 0:ncols], in_=src_ap), writes=[wb], eng="pool")
            return wt, wb

        def win_cols(c0, n):
            return w_in[:, c0:c0 + n].rearrange("(kt p) c -> p kt c", p=128)

        def rstd_of(ss_ap, out_ap, n):
            P.dve(lambda e: e.tensor_scalar(out=out_ap, in0=ss_ap, scalar1=1.0 / n, scalar2=EPS, op0=ALU.mult, op1=ALU.add),
                  reads=[B_stat], writes=[B_stat])
            P.act(lambda e: e.activation(out=out_ap, in_=out_ap, func=AF.Sqrt), reads=[B_stat], writes=[B_stat])
            P.dve(lambda e: e.reciprocal(out=out_ap, in_=out_ap), reads=[B_stat], writes=[B_stat])

        def uT_s1(I, src_rows_ap):
            i = I.xrot[0] % 2
            I.xrot[0] += 1
            xt, bx = I.xt[i], I.B_xt[i]
            c0 = 40 + 2 * i
            P.dma(lambda e: e.dma_start(out=xt[:], in_=src_rows_ap), writes=[bx])
            P.act(lambda e: e.activation(out=sq_junk[:], in_=xt[:], func=AF.Square, accum_out=stat[:, c0:c0 + 1]),
                  reads=[bx], writes=[B_junk, B_stat])
            rstd_of(stat[:, c0:c0 + 1], stat[:, c0 + 1:c0 + 2], D)
            P.act(lambda e: e.activation(out=xt[:], in_=xt[:], func=AF.Copy, scale=stat[:, c0 + 1:c0 + 2]), reads=[bx, B_stat], writes=[bx])
            return xt, bx

        def uT_s2(I, h, ci, which, uT, tcol, B_u, want_f32=False):
            xn, bxn = h
            uTf, B_uTf = I.uTf, I.B_uTf
            for g4 in range(4):
                pt, pb = next_tr()
                for j in range(4):
                    kt = g4 * 4 + j
                    P.pe(lambda e, kt=kt, j=j, pt=pt: e.transpose(pt[:, j * 128:(j + 1) * 128], xn[:, kt * 128:(kt + 1) * 128], ident[:]),
                         reads=[bxn, B_c], writes=[pb])
                for j in range(4):
                    kt = g4 * 4 + j
                    a_ap = modc[:, ci, which, kt:kt + 1]
                    s_ap = modc[:, ci, which + 1, kt:kt + 1]
                    P.dve(lambda e, kt=kt, j=j, pt=pt, a_ap=a_ap, s_ap=s_ap: e.tensor_scalar(
                        out=uT[:, kt, tcol:tcol + 128], in0=pt[:, j * 128:(j + 1) * 128], scalar1=a_ap, scalar2=s_ap,
                        op0=ALU.mult, op1=ALU.add), reads=[pb, B_modc], writes=[B_u])
                if want_f32:
                    P.act(lambda e, g4=g4, pt=pt: e.copy(out=uTf[:, g4 * 4:g4 * 4 + 4, :], in_=pt[:, 0:512].rearrange("p (k t) -> p k t", k=4)),
                          reads=[pb], writes=[B_uTf])

        def make_uT(I, src_rows_ap, ci, which, uT, tcol, B_u, want_f32=False, src_sb=None, B_src=None):
            if src_sb is None:
                i = I.xrot[0] % 2
                I.xrot[0] += 1
                xt, bx = I.xt[i], I.B_xt[i]
                P.dma(lambda e: e.dma_start(out=xt[:], in_=src_rows_ap), writes=[bx])
            else:
                xt, bx = src_sb, B_src
            P.act(lambda e: e.activation(out=sq_junk[:], in_=xt[:], func=AF.Square, accum_out=stat[:, 0:1]),
                  reads=[bx], writes=[B_junk, B_stat])
            rstd_of(stat[:, 0:1], stat[:, 1:2], D)
            if src_sb is None:
                xn, bxn = xt, bx
            else:
                i2 = I.xrot[0] % 2
                I.xrot[0] += 1
                xn, bxn = I.xt[i2], I.B_xt[i2]
            P.act(lambda e: e.activation(out=xn[:], in_=xt[:], func=AF.Copy, scale=stat[:, 1:2]), reads=[bx, B_stat], writes=[bxn])
            uTf, B_uTf = I.uTf, I.B_uTf
            for g4 in range(4):
                pt, pb = next_tr()
                for j in range(4):
                    kt = g4 * 4 + j
                    P.pe(lambda e, kt=kt, j=j, pt=pt: e.transpose(pt[:, j * 128:(j + 1) * 128], xn[:, kt * 128:(kt + 1) * 128], ident[:]),
                         reads=[bxn, B_c], writes=[pb])
                for j in range(4):
                    kt = g4 * 4 + j
                    a_ap = modc[:, ci, which, kt:kt + 1]
                    s_ap = modc[:, ci, which + 1, kt:kt + 1]
                    if True:
                        P.dve(lambda e, kt=kt, j=j, pt=pt, a_ap=a_ap, s_ap=s_ap: e.tensor_scalar(
                            out=uT[:, kt, tcol:tcol + 128], in0=pt[:, j * 128:(j + 1) * 128], scalar1=a_ap, scalar2=s_ap,
                            op0=ALU.mult, op1=ALU.add), reads=[pb, B_modc], writes=[B_u])
                    else:
                        P.act(lambda e, kt=kt, j=j, pt=pt, a_ap=a_ap, s_ap=s_ap: e.activation(
                            out=uT[:, kt, tcol:tcol + 128], in_=pt[:, j * 128:(j + 1) * 128], func=AF.Identity, bias=s_ap, scale=a_ap),
                            reads=[pb, B_modc], writes=[B_u])
                if want_f32:
                    P.act(lambda e, g4=g4, pt=pt: e.copy(out=uTf[:, g4 * 4:g4 * 4 + 4, :], in_=pt[:, 0:512].rearrange("p (k t) -> p k t", k=4)),
                          reads=[pb], writes=[B_uTf])

        def gate_preact(I, ci, pgall, c, B_pg):
            uTf, B_uTf = I.uTf, I.B_uTf
            pg, pgb = next_sm()
            for kt in range(KT):
                P.pe(lambda e, kt=kt: e.matmul(pg[:, 0:48], lhsT=uTf[:, kt, :], rhs=wgm[:, ci, kt, :], start=(kt == 0), stop=(kt == KT - 1)),
                     reads=[B_uTf, B_wgm], writes=[pgb])
            P.dve(lambda e: e.tensor_tensor(out=pgall[:, c, :], in0=pg[:, 0:48], in1=gbB[:, ci, :], op=ALU.add),
                  reads=[pgb, B_wgm], writes=[B_pg])

        def prep_scalars_batched(nch, pgall, B_pg, scall, B_sc_list, es, gt_flat, B_g):
            gt = gt_flat[:, 0:nch * 128].rearrange("p (c k) -> p c k", k=128)
            amxb = T("amxb", [128, 128], es=es)
            B_am = Buf("amxb")
            n8 = nch * 8

            def S(off, n):
                return scall[:, 0:nch, off:off + n]
            P.act(lambda e: e.activation(out=gt[:, :, 72:80], in_=pgall[:, 0:nch, 8:16], func=AF.Exp, scale=-1.0), reads=[B_pg], writes=[B_g])
            P.act(lambda e: e.activation(out=gt[:, :, 72:80], in_=gt[:, :, 72:80], func=AF.Ln, bias=1.0, scale=1.0), reads=[B_g], writes=[B_g])
            P.dve(lambda e: e.tensor_scalar(out=gt[:, :, 0:8], in0=gt[:, :, 72:80], scalar1=-1.0, scalar2=None, op0=ALU.mult), reads=[B_g], writes=[B_g])
            P.act(lambda e: e.activation(out=gt[:, :, 40:72], in_=pgall[:, 0:nch, 16:48], func=AF.Exp), reads=[B_pg], writes=[B_g])
            P.act(lambda e: e.activation(out=S(SC_DT, 32), in_=gt[:, :, 40:72], func=AF.Ln, bias=1.0, scale=1.0), reads=[B_g], writes=B_sc_list)
            P.dve(lambda e: e.tensor_tensor(out=gt[:, :, 8:40], in0=S(SC_DT, 32), in1=arow[:].unsqueeze(1).to_broadcast([128, nch, 32]), op=ALU.mult),
                  reads=B_sc_list + [B_c], writes=[B_g])
            p1, p1b = next_sm()
            p2, p2b = next_tr()
            p3, p3b = next_tr()
            p1v = p1[:, 0:nch * 16].rearrange("p (c k) -> p c k", k=16)
            p2v = p2[:, 0:nch * 32].rearrange("p (c k) -> p c k", k=32)
            p3v = p3[:, 0:nch * 32].rearrange("p (c k) -> p c k", k=32)
            P.pe(lambda e: e.matmul(p1v[:, :, 0:4], lhsT=LT[:], rhs=gt[:, :, 0:4], start=True, stop=True), reads=[B_g, B_c], writes=[p1b])
            P.pe(lambda e: e.matmul(p1v[:, :, 4:8], lhsT=UT[:], rhs=gt[:, :, 4:8], start=True, stop=True), reads=[B_g, B_c], writes=[p1b])
            P.pe(lambda e: e.matmul(p1v[:, :, 8:16], lhsT=ones[:], rhs=gt[:, :, 0:8], start=True, stop=True), reads=[B_g, B_c], writes=[p1b])
            P.pe(lambda e: e.matmul(p2v[:, :, 0:16], lhsT=LT[:], rhs=gt[:, :, 8:24], start=True, stop=True), reads=[B_g, B_c], writes=[p2b])
            P.pe(lambda e: e.matmul(p2v[:, :, 16:32], lhsT=UT[:], rhs=gt[:, :, 24:40], start=True, stop=True), reads=[B_g, B_c], writes=[p2b])
            P.pe(lambda e: e.matmul(p3v, lhsT=ones[:], rhs=gt[:, :, 8:40], start=True, stop=True), reads=[B_g, B_c], writes=[p3b])
            P.dve(lambda e: e.tensor_copy(out=S(SC_B, 16), in_=p1v), reads=[p1b], writes=B_sc_list)
            P.dve(lambda e: e.tensor_copy(out=S(SC_CS, 32), in_=p2v), reads=[p2b], writes=B_sc_list)
            P.dve(lambda e: e.tensor_scalar(out=S(SC_NCS, 32), in0=p2v, scalar1=-1.0, scalar2=None, op0=ALU.mult), reads=[p2b], writes=B_sc_list)
            P.dve(lambda e: e.tensor_tensor(out=S(SC_A, 8), in0=pgall[:, 0:nch, 0:8], in1=S(SC_B, 8), op=ALU.subtract),
                  reads=[B_pg] + B_sc_list, writes=B_sc_list)
            P.act(lambda e: e.activation(out=S(SC_ECS, 32), in_=S(SC_CS, 32), func=AF.Exp), reads=B_sc_list, writes=B_sc_list)
            P.act(lambda e: e.activation(out=S(SC_ETOT, 32), in_=p3v, func=AF.Exp), reads=[p3b], writes=B_sc_list)
            P.dve(lambda e: e.tensor_tensor(out=gt[:, :, 40:72], in0=p3v, in1=S(SC_CS, 32), op=ALU.subtract), reads=[p3b] + B_sc_list, writes=[B_g])
            P.act(lambda e: e.activation(out=gt[:, :, 40:72], in_=gt[:, :, 40:72], func=AF.Exp), reads=[B_g], writes=[B_g])
            P.dve(lambda e: e.tensor_tensor(out=S(SC_WK, 32), in0=gt[:, :, 40:72], in1=S(SC_DT, 32), op=ALU.mult), reads=[B_g] + B_sc_list, writes=B_sc_list)
            P.dve(lambda e: e.tensor_copy(out=gt[:, :, 80:88], in_=S(SC_A, 8)), reads=B_sc_list, writes=[B_g])
            P.dve(lambda e: e.tensor_copy(out=amxb[:, 0:n8].rearrange("p (c k) -> p c k", k=8), in_=gt[:, :, 80:88]), reads=[B_g], writes=[B_am])
            pa, pab = next_sm()
            P.pe(lambda e: e.transpose(pa[0:n8, 0:128], amxb[:, 0:n8], ident[:]), reads=[B_am, B_c], writes=[pab])
            P.dve(lambda e: e.tensor_reduce(out=gtmp[0:n8, 120:121], in_=pa[0:n8, 0:128], axis=AX.X, op=ALU.max), reads=[pab], writes=[B_gt])
            P.dve(lambda e: e.tensor_copy(out=amxb[0:n8, :], in_=gtmp[0:n8, 120:121].to_broadcast([n8, 128])), reads=[B_gt, B_am], writes=[B_am])
            pb2, pb2b = next_sm()
            P.pe(lambda e: e.matmul(pb2[:, 0:n8], lhsT=amxb[0:n8, :], rhs=ident[0:n8, 0:n8], start=True, stop=True), reads=[B_am, B_c], writes=[pb2b])
            P.dve(lambda e: e.tensor_copy(out=S(SC_AMX, 8), in_=pb2[:, 0:n8].rearrange("p (c k) -> p c k", k=8)), reads=[pb2b], writes=B_sc_list)

        def conv_apply(center_in, center_out, taps, ct, B_pre, B_acc, acc_flat, dst, B_dst, do_silu=True):
            P.act(lambda e: e.activation(out=center_out, in_=center_in, func=AF.Identity, scale=cw(ct, 4), bias=cbias(ct)),
                  reads=[B_pre, B_c], writes=[B_acc])
            for (ov, iv, kk) in taps:
                P.dve(lambda e, ov=ov, iv=iv, kk=kk: e.scalar_tensor_tensor(out=ov, in0=iv, scalar=cw(ct, kk), in1=ov, op0=ALU.mult, op1=ALU.add),
                      reads=[B_pre, B_acc, B_c], writes=[B_acc])
            if do_silu:
                P.act(lambda e: e.activation(out=dst, in_=acc_flat, func=AF.Silu), reads=[B_acc], writes=[B_dst])

        def grid_taps(pre_flat, acc_flat, nrows):
            pv = pre_flat.rearrange("p (r c) -> p r c", c=64)
            av = acc_flat.rearrange("p (r c) -> p r c", c=64)
            taps = []
            for di in range(3):
                for dj in range(3):
                    if di == 1 and dj == 1:
                        continue
                    if dj == 0:
                        oc, ic = slice(1, 64), slice(0, 63)
                    elif dj == 1:
                        oc, ic = slice(0, 64), slice(0, 64)
                    else:
                        oc, ic = slice(0, 63), slice(1, 64)
                    taps.append((av[:, :, oc], pv[:, di:di + nrows, ic], di * 3 + dj))
            return pv[:, 1:1 + nrows, :], av[:, :, :], taps

        def seq_taps(pre_flat, acc_flat):
            pv = pre_flat.rearrange("p (s t) -> p s t", s=2)
            av = acc_flat.rearrange("p (s t) -> p s t", s=2)
            taps = [(av[:, :, 1:256], pv[:, :, 0:255], 3), (av[:, :, 0:255], pv[:, :, 1:256], 5)]
            return pv, av, taps

        mix_count = [0]

        def alloc_mixer(es):
            M = NS()
            M.pair = mix_count[0] % 2
            mix_count[0] += 1
            M.brot = [0]
            M.chs = T("chs", [128, 32], es=es); M.B_chs = Buf("chs")
            M.wvb = T("wvb", [128, 4], BF16, es=es); M.B_wvb = Buf("wvb")
            M.vp = T("vp", [128, 4, 256], BF16, es=es); M.B_vp = Buf("vp")
            M.Sm = T("Sm", [128, 4, 128], BF16, es=es); M.B_Sm = Buf("Sm")
            M.htmp = T("htmp", [128, 1024], es=es); M.B_ht = Buf("htmp")
            M.argt2 = [T("argt%d" % i, [128, 512], es=es) for i in range(2)]; M.B_arg2 = [Buf("argt0"), Buf("argt1")]
            M.Wt = [T("Wt%d" % i, [128, 4, 128], BF16, es=es) for i in range(2)]
            M.B_Wt = [Buf("Wt0"), Buf("Wt1")]
            M.GTs = T("GTs", [128, 4, 128], es=es); M.B_GT = Buf("GTs")
            M.xw = T("xw", [128, 1024], BF16, es=es); M.B_xw = Buf("xw")
            M.qs = T("qs", [128, 4, 128], BF16, es=es); M.B_qs = Buf("qs")
            return M

        def new_state(name, es, bf=True):
            s = NS()
            s.Cn = T(name + "_Cn", [128, 4, 256], es=es)
            s.Nn = T(name + "_Nn", [128, 4], es=es)
            s.m = T(name + "_m", [128, 4], es=es)
            s.Sf = T(name + "_Sf", [128, 1024], es=es)
            if bf:
                s.Cb = T(name + "_Cb", [128, 4, 256], BF16, es=es)
                s.Nb = T(name + "_Nb", [128, 4], BF16, es=es)
                s.Sb = T(name + "_Sb", [128, 1024], BF16, es=es)
            s.B = Buf(name)
            s.Bm = s.B
            return s

        def state_zero(s, bf=True):
            P.pool(lambda e: e.memset(s.Cn[:], 0.0), writes=[s.B])
            P.pool(lambda e: e.memset(s.Nn[:], 0.0), writes=[s.B])
            P.pool(lambda e: e.memset(s.m[:], 0.0), writes=[s.B])
            P.pool(lambda e: e.memset(s.Sf[:], 0.0), writes=[s.B])
            if bf:
                P.pool(lambda e: e.memset(s.Cb[:], 0.0), writes=[s.B])
                P.pool(lambda e: e.memset(s.Nb[:], 0.0), writes=[s.B])
                P.pool(lambda e: e.memset(s.Sb[:], 0.0), writes=[s.B])

        def state_refresh_bf(s):
            P.act(lambda e: e.copy(out=s.Cb[:], in_=s.Cn[:]), reads=[s.B], writes=[s.B])
            P.act(lambda e: e.copy(out=s.Nb[:], in_=s.Nn[:]), reads=[s.B], writes=[s.B])
            P.act(lambda e: e.copy(out=s.Sb[:], in_=s.Sf[:]), reads=[s.B], writes=[s.B])

        def state_macc(dst, src, fcol):
            f_ap = flags[:, fcol:fcol + 1]
            for a, b_ in ((dst.Cn, src.Cn), (dst.Nn, src.Nn), (dst.m, src.m), (dst.Sf, src.Sf)):
                P.dve(lambda e, a=a, b_=b_: e.scalar_tensor_tensor(out=a[:], in0=b_[:], scalar=f_ap, in1=a[:], op0=ALU.mult, op1=ALU.add),
                      reads=[src.B, dst.B, B_c], writes=[dst.B])

        def state_store_scr(s, i):
            P.dma(lambda e: e.dma_start(out=Es[i, :, 0:1024], in_=s.Cn[:].rearrange("p h e -> p (h e)")), reads=[s.B], writes=[B_Es[i]])
            P.dma(lambda e: e.dma_start(out=Es[i, :, 1024:1028], in_=s.Nn[:]), reads=[s.B], writes=[B_Es[i]])
            P.dma(lambda e: e.dma_start(out=Es[i, :, 1028:1032], in_=s.m[:]), reads=[s.B], writes=[B_Es[i]])
            P.dma(lambda e: e.dma_start(out=Es[i, :, 1032:2056], in_=s.Sf[:]), reads=[s.B], writes=[B_Es[i]])

        def state_load_scr(s, i):
            P.dma(lambda e: e.dma_start(out=s.Cn[:].rearrange("p h e -> p (h e)"), in_=Es[i, :, 0:1024]), reads=[B_Es[i]], writes=[s.B])
            P.dma(lambda e: e.dma_start(out=s.Nn[:], in_=Es[i, :, 1024:1028]), reads=[B_Es[i]], writes=[s.B])
            P.dma(lambda e: e.dma_start(out=s.m[:], in_=Es[i, :, 1028:1032]), reads=[B_Es[i]], writes=[s.B])
            P.dma(lambda e: e.dma_start(out=s.Sf[:], in_=Es[i, :, 1032:2056]), reads=[B_Es[i]], writes=[s.B])
            state_refresh_bf(s)

        def mlstm_chain_scalars(M, sc, B_sc, d, st, qT, B_qk, tsl):
            chs, B_chs = M.chs, M.B_chs
            wvb, B_wvb = M.wvb, M.B_wvb
            o4 = 4 * d
            P.dve(lambda e: e.tensor_tensor(out=chs[:, 0:4], in0=st.m[:], in1=sc[:, SC_AMX + o4:SC_AMX + o4 + 4], op=ALU.max),
                  reads=[st.Bm, B_sc], writes=[B_chs])
            P.dve(lambda e: e.tensor_tensor(out=chs[:, 4:8], in0=sc[:, SC_A + o4:SC_A + o4 + 4], in1=chs[:, 0:4], op=ALU.subtract),
                  reads=[B_sc, B_chs], writes=[B_chs])
            P.dve(lambda e: e.scalar_tensor_tensor(out=chs[:, 8:12], in0=sc[:, SC_B + o4:SC_B + o4 + 4], scalar=-1.0, in1=chs[:, 0:4],
                                                   op0=ALU.mult, op1=ALU.subtract), reads=[B_sc, B_chs], writes=[B_chs])
            P.dve(lambda e: e.tensor_tensor(out=chs[:, 12:16], in0=st.m[:], in1=chs[:, 0:4], op=ALU.subtract),
                  reads=[st.Bm, B_chs], writes=[B_chs])
            P.act(lambda e: e.activation(out=chs[:, 4:16], in_=chs[:, 4:16], func=AF.Exp), reads=[B_chs], writes=[B_chs])
            P.dve(lambda e: e.tensor_copy(out=wvb[:], in_=chs[:, 4:8]), reads=[B_chs], writes=[B_wvb])
            P.dve(lambda e: e.tensor_tensor(out=st.m[:], in0=sc[:, SC_BT + o4:SC_BT + o4 + 4], in1=chs[:, 0:4], op=ALU.add),
                  reads=[B_sc, B_chs, st.Bm], writes=[st.Bm])
            for h in range(4):
                P.dve(lambda e, h=h: e.tensor_scalar(out=M.qs[:, h, :], in0=qT[:, h, tsl], scalar1=chs[:, 12 + h:13 + h], scalar2=None, op0=ALU.mult),
                      reads=[B_qk, B_chs], writes=[M.B_qs])

        def make_vp(M, v_ap, B_v):
            vp, chs = M.vp, M.chs
            for h in range(4):
                P.act(lambda e, h=h: e.activation(out=vp[:, h, :], in_=v_ap[:, h, :], func=AF.Copy, scale=chs[:, 4 + h:5 + h]),
                      reads=[B_v, M.B_chs], writes=[M.B_vp])

        def mlstm_update(M, st, k_tm, B_k):
            vp, wvb, chs = M.vp, M.wvb, M.chs
            pcs = [next_big_m(M), next_big_m(M)]
            for half in range(2):
                po0, pob0 = pcs[half]
                for hh in range(2):
                    h = half * 2 + hh
                    P.pe(lambda e, h=h, hh=hh, po0=po0: e.matmul(po0[:, hh * 256:(hh + 1) * 256], lhsT=k_tm[:, h, :], rhs=vp[:, h, :], start=True, stop=True),
                         reads=[B_k, M.B_vp], writes=[pob0])
            pn, pnb = next_sm()
            for h in range(4):
                P.pe(lambda e, h=h: e.matmul(pn[:, h:h + 1], lhsT=k_tm[:, h, :], rhs=wvb[:, h:h + 1], start=True, stop=True),
                     reads=[B_k, M.B_wvb], writes=[pnb])
            for h in range(4):
                po0, pob0 = pcs[h // 2]
                P.dve(lambda e, h=h, po0=po0: e.scalar_tensor_tensor(out=st.Cn[:, h, :], in0=st.Cn[:, h, :], scalar=chs[:, 12 + h:13 + h],
                                                                    in1=po0[:, (h % 2) * 256:(h % 2 + 1) * 256], op0=ALU.mult, op1=ALU.add),
                      reads=[st.B, M.B_chs, pob0], writes=[st.B])
            P.dve(lambda e: e.tensor_tensor(out=st.Nn[:], in0=st.Nn[:], in1=chs[:, 12:16], op=ALU.mult), reads=[st.B, M.B_chs], writes=[st.B])
            P.dve(lambda e: e.tensor_tensor(out=st.Nn[:], in0=pn[:, 0:4], in1=st.Nn[:], op=ALU.add), reads=[pnb, st.B], writes=[st.B])
            P.act(lambda e: e.copy(out=st.Cb[:], in_=st.Cn[:]), reads=[st.B], writes=[st.B])
            P.act(lambda e: e.copy(out=st.Nb[:], in_=st.Nn[:]), reads=[st.B], writes=[st.B])

        def mlstm_out(M, d, st, qT, kT, B_qk, tsl, hs_ap, B_hs, first):
            chs, B_chs, Sm, vp, wvb, htmp = M.chs, M.B_chs, M.Sm, M.vp, M.wvb, M.htmp
            pS, pSb = next_tr()
            for h in range(4):
                P.pe(lambda e, h=h: e.matmul(pS[:, h * 128:(h + 1) * 128], lhsT=kT[:, h, tsl], rhs=qT[:, h, tsl], start=True, stop=True),
                     reads=[B_qk], writes=[pSb])
            P.dve(lambda e: e.tensor_tensor(out=Sm[:], in0=pS[:, 0:512].rearrange("p (h t) -> p h t", h=4),
                                            in1=MK[d][:].unsqueeze(1).to_broadcast([128, 4, 128]), op=ALU.mult),
                  reads=[pSb, B_c], writes=[M.B_Sm])
            pbs = []
            for half in range(2):
                po0, pob0 = next_big_m(M)
                pbs.append((po0, pob0))
                for hh in range(2):
                    h = half * 2 + hh
                    P.pe(lambda e, h=h, hh=hh, po0=po0: e.matmul(po0[:, hh * 256:(hh + 1) * 256], lhsT=M.qs[:, h, :], rhs=st.Cb[:, h, :], start=True, stop=False),
                         reads=[M.B_qs, st.B], writes=[pob0])
                    P.pe(lambda e, h=h, hh=hh, po0=po0: e.matmul(po0[:, hh * 256:(hh + 1) * 256], lhsT=Sm[:, h, :], rhs=vp[:, h, :], start=False, stop=True),
                         reads=[M.B_Sm, M.B_vp], writes=[pob0])
            pn, pnb = next_sm()
            for h in range(4):
                P.pe(lambda e, h=h: e.matmul(pn[:, h:h + 1], lhsT=M.qs[:, h, :], rhs=st.Nb[:, h:h + 1], start=True, stop=False),
                     reads=[M.B_qs, st.B], writes=[pnb])
                P.pe(lambda e, h=h: e.matmul(pn[:, h:h + 1], lhsT=Sm[:, h, :], rhs=wvb[:, h:h + 1], start=False, stop=True),
                     reads=[M.B_Sm, M.B_wvb], writes=[pnb])
            P.dve(lambda e: e.tensor_copy(out=chs[:, 24:28], in_=pn[:, 0:4]), reads=[pnb], writes=[B_chs])
            P.dve(lambda e: e.scalar_tensor_tensor(out=chs[:, 16:20], in0=chs[:, 24:28], scalar=-1.0, in1=chs[:, 24:28], op0=ALU.mult, op1=ALU.max),
                  reads=[B_chs], writes=[B_chs])
            P.dve(lambda e: e.tensor_tensor(out=chs[:, 16:20], in0=chs[:, 16:20], in1=chs[:, 8:12], op=ALU.max), reads=[B_chs], writes=[B_chs])
            P.dve(lambda e: e.reciprocal(out=chs[:, 20:24], in_=chs[:, 16:20]), reads=[B_chs], writes=[B_chs])
            for half in range(2):
                po0, pob0 = pbs[half]
                dst = hs_ap[:, half * 512:(half + 1) * 512].rearrange("p (h e) -> p h e", h=2)
                rdb = chs[:, 20 + half * 2:22 + half * 2].unsqueeze(2).to_broadcast([128, 2, 256])
                if first:
                    P.dve(lambda e, po0=po0, dst=dst, rdb=rdb: e.tensor_tensor(out=dst, in0=po0.rearrange("p (h e) -> p h e", h=2), in1=rdb, op=ALU.mult),
                          reads=[pob0, B_chs], writes=[B_hs])
                else:
                    tmp = htmp[:, half * 512:(half + 1) * 512].rearrange("p (h e) -> p h e", h=2)
                    P.dve(lambda e, po0=po0, tmp=tmp, rdb=rdb: e.tensor_tensor(out=tmp, in0=po0.rearrange("p (h e) -> p h e", h=2), in1=rdb, op=ALU.mult),
                          reads=[pob0, B_chs], writes=[M.B_ht])
                    P.pool(lambda e, dst=dst, tmp=tmp: e.tensor_tensor(out=dst, in0=dst, in1=tmp, op=ALU.add), reads=[M.B_ht, B_hs], writes=[B_hs])

        def ssd_update(M, sc, B_sc, d, st, x_tm_ap, B_x, Btm, B_B):
            xw = M.xw
            o16 = 16 * d
            P.pool(lambda e: e.tensor_tensor(out=xw[:].rearrange("p (j q) -> p j q", j=16), in0=x_tm_ap.rearrange("p (j q) -> p j q", j=16),
                                             in1=sc[:, SC_WK + o16:SC_WK + o16 + 16].unsqueeze(2).to_broadcast([128, 16, 64]), op=ALU.mult),
                   reads=[B_x, B_sc], writes=[M.B_xw])
            P.pool(lambda e: e.tensor_tensor(out=st.Sf[:].rearrange("p (j q) -> p j q", j=16), in0=st.Sf[:].rearrange("p (j q) -> p j q", j=16),
                                             in1=sc[:, SC_ETOT + o16:SC_ETOT + o16 + 16].unsqueeze(2).to_broadcast([128, 16, 64]), op=ALU.mult),
                   reads=[st.B, B_sc], writes=[st.B])
            for half in range(2):
                po0, pob0 = next_big_m(M)
                for gg in range(2):
                    g = half * 2 + gg
                    P.pe(lambda e, g=g, gg=gg, po0=po0: e.matmul(po0[:, gg * 256:(gg + 1) * 256], lhsT=Btm[:, g, :], rhs=xw[:, g * 256:(g + 1) * 256], start=True, stop=True),
                         reads=[B_B, M.B_xw], writes=[pob0])
                P.dve(lambda e, half=half, po0=po0: e.tensor_tensor(out=st.Sf[:, half * 512:(half + 1) * 512], in0=po0,
                                                                    in1=st.Sf[:, half * 512:(half + 1) * 512], op=ALU.add),
                      reads=[pob0, st.B], writes=[st.B])
            P.act(lambda e: e.copy(out=st.Sb[:], in_=st.Sf[:]), reads=[st.B], writes=[st.B])

        def ssd_gt(M, BT, CT, B_bc, tsl):
            GTs = M.GTs
            pG, pGb = next_tr()
            for g in range(4):
                P.pe(lambda e, g=g: e.matmul(pG[:, g * 128:(g + 1) * 128], lhsT=BT[:, g, tsl], rhs=CT[:, g, tsl], start=True, stop=True),
                     reads=[B_bc], writes=[pGb])
            P.act(lambda e: e.copy(out=GTs[:], in_=pG[:, 0:512].rearrange("p (g t) -> p g t", g=4)), reads=[pGb], writes=[M.B_GT])

        def ssd_out(M, sc, B_sc, d, st, CT, B_bc, tsl, x_tm_ap, B_x, ys_ap, B_ys, first):
            Wt, B_Wt, GTs, htmp = M.Wt, M.B_Wt, M.GTs, M.htmp
            o16 = 16 * d

            def stage0(g):
                argt, B_arg = M.argt2[g % 2], M.B_arg2[g % 2]
                wi = g % 2
                pC, pCb = next_tr()
                for jj in range(4):
                    col = SC_CS + o16 + g * 4 + jj
                    P.pe(lambda e, jj=jj, col=col: e.matmul(pC[:, jj * 128:(jj + 1) * 128], lhsT=sc[:, col:col + 1].to_broadcast([128, 128]),
                                                           rhs=ident[:], start=True, stop=True), reads=[B_sc, B_c], writes=[pCb])
                P.dve(lambda e: e.tensor_tensor(out=argt[:].rearrange("p (j t) -> p j t", j=4), in0=pC[:, 0:512].rearrange("p (j t) -> p j t", j=4),
                                                in1=NM[d][:].unsqueeze(1).to_broadcast([128, 4, 128]), op=ALU.add),
                      reads=[pCb, B_c], writes=[B_arg])
                for jj in range(4):
                    col = SC_NCS + o16 + g * 4 + jj
                    P.act(lambda e, jj=jj, col=col: e.activation(out=argt[:, jj * 128:(jj + 1) * 128], in_=argt[:, jj * 128:(jj + 1) * 128], func=AF.Exp,
                                                                bias=sc[:, col:col + 1], scale=1.0), reads=[B_arg, B_sc], writes=[B_arg])

            def stage0b(g):
                argt, B_arg = M.argt2[g % 2], M.B_arg2[g % 2]
                wi = g % 2
                for jj in range(4):
                    col = SC_DT + o16 + g * 4 + jj
                    P.dve(lambda e, jj=jj, col=col: e.scalar_tensor_tensor(out=Wt[wi][:, jj, :], in0=argt[:, jj * 128:(jj + 1) * 128],
                                                                          scalar=sc[:, col:col + 1], in1=GTs[:, g, :], op0=ALU.mult, op1=ALU.mult),
                          reads=[B_arg, B_sc, M.B_GT], writes=[B_Wt[wi]])

            def stage1(g, po0, pob0):
                wi = g % 2
                for jj in range(4):
                    j = g * 4 + jj
                    c0 = (j % 8) * 64
                    P.pe(lambda e, jj=jj, j=j, c0=c0: e.matmul(po0[:, c0:c0 + 64], lhsT=Wt[wi][:, jj, :], rhs=x_tm_ap[:, j * 64:(j + 1) * 64],
                                                             start=True, stop=True), reads=[B_Wt[wi], B_x], writes=[pob0])

            def finish(half, po0, pob0):
                po2, pob2 = next_big_m(M)
                for gg in range(2):
                    g = half * 2 + gg
                    P.pe(lambda e, g=g, gg=gg: e.matmul(po2[:, gg * 256:(gg + 1) * 256], lhsT=CT[:, g, tsl], rhs=st.Sb[:, g * 256:(g + 1) * 256],
                                                        start=True, stop=True), reads=[B_bc, st.B], writes=[pob2])
                tmp = htmp[:, half * 512:(half + 1) * 512]
                ecs = sc[:, SC_ECS + o16 + half * 8:SC_ECS + o16 + half * 8 + 8]
                P.dve(lambda e: e.tensor_tensor(
                    out=tmp.rearrange("p (j q) -> p j q", j=8), in0=po2.rearrange("p (j q) -> p j q", j=8),
                    in1=ecs.unsqueeze(2).to_broadcast([128, 8, 64]), op=ALU.mult), reads=[pob2, B_sc], writes=[M.B_ht])
                dst = ys_ap[:, half * 512:(half + 1) * 512]
                if first:
                    P.dve(lambda e: e.tensor_tensor(out=dst, in0=po0, in1=tmp, op=ALU.add), reads=[pob0, M.B_ht], writes=[B_ys])
                else:
                    P.dve(lambda e: e.tensor_tensor(out=tmp, in0=po0, in1=tmp, op=ALU.add), reads=[pob0, M.B_ht], writes=[M.B_ht])
                    P.pool(lambda e: e.tensor_tensor(out=dst, in0=dst, in1=tmp, op=ALU.add), reads=[M.B_ht, B_ys], writes=[B_ys])

            pa, pab = next_big_m(M)
            stage0(0)
            yield
            stage0(1)
            stage0b(0)
            yield
            stage1(0, pa, pab)
            stage0(2)
            stage0b(1)
            yield
            stage1(1, pa, pab)
            finish(0, pa, pab)
            yield
            pb_, pbb = next_big_m(M)
            stage0(3)
            stage0b(2)
            yield
            stage1(2, pb_, pbb)
            stage0b(3)
            stage1(3, pb_, pbb)
            finish(1, pb_, pbb)
            yield

        cvt = T("cvt", [32, 128])
        bmr = [T("bmr%d" % i, [2, 512]) for i in range(3)]
        B_bmr = [Buf("bmr%d" % i) for i in range(3)]
        mrow = [T("mrow%d" % i, [2, 512]) for i in range(3)]
        B_mrow = [Buf("mrow%d" % i) for i in range(3)]
        csT3 = csT[:].rearrange("p (c k) -> p c k", c=2)

        def mod_prologue():
            P.dma(lambda e: e.dma_start(out=cvt[:], in_=cv), writes=[B_cv])
            P.act(lambda e: e.activation(out=cvt[:], in_=cvt[:], func=AF.Silu), reads=[B_cv], writes=[B_cv])
            pt, pb = next_sm()
            P.pe(lambda e: e.transpose(pt[:, 0:32], cvt[:], ident[0:32, 0:32]), reads=[B_cv, B_c], writes=[pb])
            P.dve(lambda e: e.tensor_copy(out=csT[:], in_=pt[:, 0:32]), reads=[pb], writes=[B_cv])

        def mod_load(c0, ncols, wt, wb):
            P.dma(lambda e: e.dma_start(out=wt[:, :, 0:ncols], in_=w_mod[:, c0:c0 + ncols].rearrange("(kt p) c -> p kt c", p=128)), writes=[wb], eng="pool")

        mod_cnt = [0]

        def mod_compute(c0, ncols, wt, wb, defer=False, small_psum=False):
            mi = mod_cnt[0] % 3
            mod_cnt[0] += 1
            P.dma(lambda e: e.dma_start(out=bmr[mi][:, 0:ncols], in_=b_mod[0:1, c0:c0 + ncols].to_broadcast([2, ncols])), writes=[B_bmr[mi]])
            po, pob = next_sm() if small_psum else next_big()
            for kt in range(KT):
                P.pe(lambda e, kt=kt: e.matmul(po[0:2, 0:ncols], lhsT=csT3[:, :, kt], rhs=wt[:, kt, 0:ncols], start=(kt == 0), stop=(kt == KT - 1)),
                     reads=[B_cv, wb], writes=[pob])
            P.dve(lambda e: e.tensor_tensor(out=mrow[mi][:, 0:ncols], in0=po[0:2, 0:ncols], in1=bmr[mi][:, 0:ncols], op=ALU.add),
                  reads=[pob, B_bmr[mi]], writes=[B_mrow[mi]])
            def store():
                P.dma(lambda e: e.dma_start(out=modscr[:, c0:c0 + ncols], in_=mrow[mi][:, 0:ncols]),
                      reads=[B_mrow[mi]], writes=[B_modscr])
            if defer:
                return store
            store()
            return None

        def mod_cols(j0, nj):
            for ci in range(2):
                P.dma(lambda e, ci=ci: e.dma_start(out=ld[0:nj, :], in_=modscr[ci:ci + 1, j0 * 128:(j0 + nj) * 128].rearrange("o (j p) -> (o j) p", p=128)),
                      reads=[B_modscr], writes=[B_ld])
                pt, pb = next_sm()
                P.pe(lambda e, pt=pt: e.transpose(pt[:, 0:nj], ld[0:nj, :], ident[0:nj, 0:nj]), reads=[B_ld, B_c], writes=[pb])
                P.dve(lambda e, pt=pt, ci=ci: e.tensor_copy(out=modT[:, ci, j0:j0 + nj], in_=pt[:, 0:nj]), reads=[pb], writes=[B_modT])

        def mod_consts(dst, gcol, sc_off, sh_off):
            for ci in range(2):
                P.dve(lambda e, ci=ci: e.scalar_tensor_tensor(
                    out=modc[:, ci, dst, :], in0=modT[:, ci, sc_off:sc_off + 16], scalar=1.0, in1=colp[:, gcol:gcol + 16],
                    op0=ALU.add, op1=ALU.mult), reads=[B_modT, B_c], writes=[B_modc])
                P.dve(lambda e, ci=ci: e.tensor_copy(out=modc[:, ci, dst + 1, :], in_=modT[:, ci, sh_off:sh_off + 16]),
                      reads=[B_modT], writes=[B_modc])

        def gate_consts():
            for ci in range(2):
                P.dve(lambda e, ci=ci: e.tensor_tensor(out=wgm[:, ci, :, :], in0=wg[:], in1=modc[:, ci, 0, :].unsqueeze(2).to_broadcast([128, KT, 48]), op=ALU.mult),
                      reads=[B_c, B_modc], writes=[B_wgm])
                pg, pgb = next_sm()
                for kt in range(KT):
                    P.pe(lambda e, kt=kt, ci=ci, pg=pg: e.matmul(pg[:, 0:48], lhsT=modc[:, ci, 1, kt:kt + 1].to_broadcast([128, 128]), rhs=wg[:, kt, :],
                                                                 start=(kt == 0), stop=(kt == KT - 1)), reads=[B_modc, B_c], writes=[pgb])
                P.dve(lambda e, ci=ci, pg=pg: e.tensor_tensor(out=gbB[:, ci, :], in0=pg[:, 0:48], in1=rowp[:, 0:48], op=ALU.add),
                      reads=[pgb, B_c], writes=[B_wgm])

        with contextlib.ExitStack() as es1:
            scall = T("scall", [128, 16, NSC], es=es1)
            B_scall = [Buf("scall%d" % i) for i in range(16)]
            with contextlib.ExitStack() as es:
                I = alloc_inproj(es)
                uTs = T("uTs", [128, KT, 2048], BF16, es=es)
                B_uTs = [Buf("uTs%d" % i) for i in range(16)]
                pre2 = [T("pre_s%d" % i, [128, 34 * 64], es=es) for i in range(2)]
                B_pre2 = [Buf("pre_s0"), Buf("pre_s1")]
                cacc = T("cacc_s", [128, 2048], es=es)
                B_cacc = Buf("cacc_s")
                cvo = T("cvo_s", [128, 2048], BF16, es=es)
                B_cvo = Buf("cvo_s")
                ost = [T("ost%d" % i, [128, 512], BF16, es=es) for i in range(2)]
                B_ost = [Buf("ost0"), Buf("ost1")]
                for i_ in range(2):
                    P.pool(lambda e, i_=i_: e.memset(pre2[i_][:], 0.0), writes=[B_pre2[i_]])
                pgall_flat = T("pgall", [128, 2048], es=es)
                pgall = pgall_flat[:, 0:768].rearrange("p (c k) -> p c k", k=48)
                B_pg = Buf("pgall")
                mod_prologue()
                for blk_i in range(8):
                    wi_ = I.wrot[0] % 2
                    I.wrot[0] += 1
                    mod_load(blk_i * 512, 512, I.wbuf[wi_], I.B_w[wi_])
                    mod_compute(blk_i * 512, 512, I.wbuf[wi_], I.B_w[wi_])
                mod_cols(0, 32)
                mod_consts(0, C_GPRE, 16, 0)
                gate_consts()
                check("A0")
                hnd = uT_s1(I, xf[0:128, :])
                for c in range(16):
                    nxt = uT_s1(I, xf[(c + 1) * 128:(c + 2) * 128, :]) if c < 15 else None
                    uT_s2(I, hnd, 1, 0, uTs, c * 128, B_uTs[c], want_f32=True)
                    gate_preact(I, 1, pgall, c, B_pg)
                    hnd = nxt
                check("A1")
                prep_scalars_batched(16, pgall, B_pg, scall, B_scall, es, cacc, B_cacc)
                check("B1a1")
                orot = [0]
                for (c0, dst0) in ((512, 0), (1024, 512), (1536, 1024)):
                    wt, wb = load_wblock(I, win_cols(c0, 512))
                    for c in range(16):
                        po, pob = next_big()
                        for kt in range(KT):
                            P.pe(lambda e, kt=kt, c=c, wt=wt, po=po: e.matmul(po, lhsT=uTs[:, kt, c * 128:(c + 1) * 128], rhs=wt[:, kt, :],
                                                                          start=(kt == 0), stop=(kt == KT - 1)), reads=[B_uTs[c], wb], writes=[pob])
                        oi = orot[0] % 2
                        orot[0] += 1
                        P.act(lambda e, oi=oi, po=po: e.copy(out=ost[oi][:], in_=po), reads=[pob], writes=[B_ost[oi]])
                        P.dma(lambda e, oi=oi, c=c, dst0=dst0: e.dma_start(out=stg[c, :, dst0:dst0 + 512], in_=ost[oi][:]), reads=[B_ost[oi]], writes=[B_stg[c]])
                check("B1a2")
                cacc2 = [cacc, pgall_flat]
                B_cacc2 = [B_cacc, B_pg]
                tap_sets = [grid_taps(pre2[i_][:], cacc2[i_][:, 0:2048], 32) for i_ in range(2)]
                cvo2 = [cvo, I.uTf_flat[:].bitcast(BF16)[:, 0:2048]]
                B_cvo2 = [B_cvo, I.B_uTf]

                def fm_front(ct, cc, wt, wb):
                    pre, B_pre = pre2[ct % 2], B_pre2[ct % 2]
                    cen_in, cen_out, taps = tap_sets[ct % 2]
                    for g in range(4):
                        po, pob = next_big()
                        for kt in range(KT):
                            P.pe(lambda e, kt=kt, g=g, po=po: e.matmul(po, lhsT=wt[:, kt, cc * 128:(cc + 1) * 128],
                                                                      rhs=uTs[:, kt, g * 512:(g + 1) * 512], start=(kt == 0), stop=(kt == KT - 1)),
                                 reads=B_uTs[4 * g:4 * g + 4] + [wb], writes=[pob])
                        P.act(lambda e, g=g, po=po: e.copy(out=pre[:, 64 + 512 * g:64 + 512 * (g + 1)], in_=po), reads=[pob], writes=[B_pre])
                    conv_apply(cen_in, cen_out, taps, ct, B_pre, B_cacc2[ct % 2], cacc2[ct % 2][:, 0:2048], None, None, do_silu=False)

                def fm_back(ct):
                    cv_, bcv = cvo2[ct % 2], B_cvo2[ct % 2]
                    P.act(lambda e: e.activation(out=cv_[:], in_=cacc2[ct % 2][:, 0:2048], func=AF.Silu), reads=[B_cacc2[ct % 2]], writes=[bcv])
                    for c4 in range(4):
                        pt, pb = next_tr()
                        ptb = pt[:].bitcast(BF16)
                        for j in range(4):
                            c = c4 * 4 + j
                            P.pe(lambda e, c=c, j=j, ptb=ptb: e.transpose(ptb[:, j * 128:(j + 1) * 128], cv_[:, c * 128:(c + 1) * 128], identb[:]),
                                 reads=[bcv, B_c], writes=[pb])
                        oi = orot[0] % 2
                        orot[0] += 1
                        P.act(lambda e, oi=oi, ptb=ptb: e.copy(out=ost[oi][:], in_=ptb[:, 0:512]), reads=[pb], writes=[B_ost[oi]])
                        dcol = 1536 + ct * 128
                        P.dma(lambda e, oi=oi, c4=c4, dcol=dcol: e.dma_start(out=stg[c4 * 4:c4 * 4 + 4, :, dcol:dcol + 128].rearrange("c p n -> p c n"),
                                                                            in_=ost[oi][:].rearrange("p (c n) -> p c n", c=4)),
                              reads=[B_ost[oi]], writes=B_stg[c4 * 4:c4 * 4 + 4])

                pending = None
                for blk_i in range(3):
                    wt, wb = load_wblock(I, win_cols(4112 + blk_i * 512, 512))
                    for cc in range(4):
                        ct = blk_i * 4 + cc
                        fm_front(ct, cc, wt, wb)
                        if pending is not None:
                            fm_back(pending)
                        pending = ct
                fm_back(pending)
            P.barrier()
            check("B1a")
            with contextlib.ExitStack() as es:
                Ms = [alloc_mixer(es), alloc_mixer(es)]
                sgt = [[T("sgt%d_%d" % (d, i), [128, 3072], BF16, es=es) for i in range(2)] for d in range(2)]
                B_sgt = [[Buf("sgt"), Buf("sgt")] for d in range(2)]
                stcs = [new_state("stc%d" % d, es, bf=False) for d in range(2)]
                E = [new_state("E%d" % i, es, bf=False) for i in range(4)]
                for s_ in E:
                    state_zero(s_, bf=False)
                ld2 = [T("ld2_%d" % i, [128, 128], es=es) for i in range(2)]
                B_ld2 = [Buf("ld2a"), Buf("ld2b")]

                refall = T("refall", [128, 16, 8], es=es)
                cdall = T("cdall", [128, 16, 8], es=es)
                wvall = T("wvall", [128, 16, 8], es=es)
                wvball = T("wvball", [128, 16, 8], BF16, es=es)
                B_rf = [Buf("rf0"), Buf("rf1")]

                def sp_chain(d):
                    stc = stcs[d]
                    M = Ms[d]
                    d4, d16 = 4 * d, 16 * d
                    order = list(range(16)) if d == 0 else list(range(15, -1, -1))
                    P.dma(lambda e: e.dma_start(out=stc.m[:], in_=st_m[0:1, d4:d4 + 4].to_broadcast([128, 4])), writes=[stc.B])
                    for n_, c in enumerate(order):
                        if (d == 0 and c % 4 == 0) or (d == 1 and c % 4 == 1):
                            fcol = (0 if d == 0 else 32) + c
                            P.dve(lambda e, fcol=fcol: e.scalar_tensor_tensor(out=E[2 * d].m[:], in0=stc.m[:], scalar=flags[:, fcol:fcol + 1], in1=E[2 * d].m[:],
                                                                              op0=ALU.mult, op1=ALU.add), reads=[stc.B, E[2 * d].B, B_c], writes=[E[2 * d].B])
                        if (d == 0 and c % 4 == 2) or (d == 1 and c % 4 == 3):
                            fcol = (16 if d == 0 else 48) + c
                            P.dve(lambda e, fcol=fcol: e.scalar_tensor_tensor(out=E[2 * d + 1].m[:], in0=stc.m[:], scalar=flags[:, fcol:fcol + 1], in1=E[2 * d + 1].m[:],
                                                                              op0=ALU.mult, op1=ALU.add), reads=[stc.B, E[2 * d + 1].B, B_c], writes=[E[2 * d + 1].B])
                        if n_ == 15:
                            break
                        P.dve(lambda e, c=c: e.tensor_tensor(out=refall[:, c, d4:d4 + 4], in0=stc.m[:], in1=scall[:, c, SC_AMX + d4:SC_AMX + d4 + 4], op=ALU.max),
                              reads=[stc.B, B_scall[c]], writes=[B_rf[d]])
                        P.dve(lambda e, c=c: e.tensor_tensor(out=cdall[:, c, d4:d4 + 4], in0=stc.m[:], in1=refall[:, c, d4:d4 + 4], op=ALU.subtract),
                              reads=[stc.B, B_rf[d]], writes=[B_rf[d]])
                        P.dve(lambda e, c=c: e.tensor_tensor(out=stc.m[:], in0=scall[:, c, SC_BT + d4:SC_BT + d4 + 4], in1=refall[:, c, d4:d4 + 4], op=ALU.add),
                              reads=[B_scall[c], B_rf[d]], writes=[stc.B])
                        if n_ % 4 == 3:
                            yield
                    cs_ = slice(0, 15) if d == 0 else slice(1, 16)
                    P.dve(lambda e: e.tensor_tensor(out=wvall[:, cs_, d4:d4 + 4], in0=scall[:, cs_, SC_A + d4:SC_A + d4 + 4], in1=refall[:, cs_, d4:d4 + 4], op=ALU.subtract),
                          reads=B_scall + [B_rf[d]], writes=[B_rf[d]])
                    P.act(lambda e: e.activation(out=wvall[:, cs_, d4:d4 + 4], in_=wvall[:, cs_, d4:d4 + 4], func=AF.Exp), reads=[B_rf[d]], writes=[B_rf[d]])
                    P.act(lambda e: e.activation(out=cdall[:, cs_, d4:d4 + 4], in_=cdall[:, cs_, d4:d4 + 4], func=AF.Exp), reads=[B_rf[d]], writes=[B_rf[d]])
                    P.dve(lambda e: e.tensor_copy(out=wvball[:, cs_, d4:d4 + 4], in_=wvall[:, cs_, d4:d4 + 4]), reads=[B_rf[d]], writes=[B_rf[d]])
                    yield
                    P.dma(lambda e: e.dma_start(out=stc.Cn[:], in_=st_c[d].rearrange("h d e -> d h e")), writes=[stc.B])
                    P.dma(lambda e: e.dma_start(out=ld2[d][0:4, :], in_=st_n[d]), writes=[B_ld2[d]])
                    pt, pb = next_sm()
                    P.pe(lambda e, pt=pt: e.transpose(pt[:, 0:4], ld2[d][0:4, :], ident[0:4, 0:4]), reads=[B_ld2[d], B_c], writes=[pb])
                    P.dve(lambda e, pt=pt: e.tensor_copy(out=stc.Nn[:], in_=pt[:, 0:4]), reads=[pb], writes=[stc.B])
                    for a in range(8):
                        P.dma(lambda e, a=a: e.dma_start(out=ld2[d][:], in_=st_s[d, a * 128:(a + 1) * 128, :]), writes=[B_ld2[d]])
                        pt, pb = next_sm()
                        P.pe(lambda e, pt=pt: e.transpose(pt[:, 0:128], ld2[d][:], ident[:]), reads=[B_ld2[d], B_c], writes=[pb])
                        P.dve(lambda e, pt=pt, a=a: e.tensor_copy(out=stc.Sf[:, a * 128:(a + 1) * 128], in_=pt[:, 0:128]), reads=[pb], writes=[stc.B])
                        if a % 4 == 3:
                            yield
                    for n_, c in enumerate(order):
                        si = n_ % 2
                        sg, bsg = sgt[d][si], B_sgt[d][si]

                        def macc3(Et, fcol):
                            f_ap = flags[:, fcol:fcol + 1]
                            for a_, b_ in ((Et.Cn, stc.Cn), (Et.Nn, stc.Nn), (Et.Sf, stc.Sf)):
                                P.dve(lambda e, a_=a_, b_=b_: e.scalar_tensor_tensor(out=a_[:], in0=b_[:], scalar=f_ap, in1=a_[:], op0=ALU.mult, op1=ALU.add),
                                      reads=[stc.B, Et.B, B_c], writes=[Et.B])
                        if d == 0 and c % 4 == 0:
                            macc3(E[0], c)
                        if d == 0 and c % 4 == 2:
                            macc3(E[1], 16 + c)
                        if d == 1 and c % 4 == 1:
                            macc3(E[2], 32 + c)
                        if d == 1 and c % 4 == 3:
                            macc3(E[3], 48 + c)
                        if n_ == 15:
                            break
                        P.dma(lambda e, sg=sg, c=c: e.dma_start(out=sg[:], in_=stg[c]), reads=[B_stg[c]], writes=[bsg])
                        vp, xw = M.vp, M.xw
                        k_tm = sg[:, 0:512].rearrange("p (h d) -> p h d", h=4)
                        Btm = sg[:, 2560:3072].rearrange("p (g n) -> p g n", g=4)
                        for h in range(4):
                            P.act(lambda e, sg=sg, c=c, h=h: e.activation(out=vp[:, h, :], in_=sg[:, 512 + h * 256:512 + (h + 1) * 256], func=AF.Copy,
                                                                         scale=wvall[:, c, d4 + h:d4 + h + 1]),
                                  reads=[bsg, B_rf[d]], writes=[M.B_vp])
                        P.pool(lambda e, sg=sg, c=c: e.tensor_tensor(out=xw[:].rearrange("p (j q) -> p j q", j=16), in0=sg[:, 1536:2560].rearrange("p (j q) -> p j q", j=16),
                                                                     in1=scall[:, c, SC_WK + d16:SC_WK + d16 + 16].unsqueeze(2).to_broadcast([128, 16, 64]), op=ALU.mult),
                               reads=[bsg, B_scall[c]], writes=[M.B_xw])
                        pcs = [next_big_m(M), next_big_m(M)]
                        for half in range(2):
                            po0, pob0 = pcs[half]
                            for hh in range(2):
                                h = half * 2 + hh
                                P.pe(lambda e, h=h, hh=hh, po0=po0, k_tm=k_tm: e.matmul(po0[:, hh * 256:(hh + 1) * 256], lhsT=k_tm[:, h, :], rhs=vp[:, h, :], start=True, stop=True),
                                     reads=[bsg, M.B_vp], writes=[pob0])
                        pn, pnb = next_sm()
                        for h in range(4):
                            P.pe(lambda e, h=h, c=c, k_tm=k_tm, pn=pn: e.matmul(pn[:, h:h + 1], lhsT=k_tm[:, h, :], rhs=wvball[:, c, d4 + h:d4 + h + 1], start=True, stop=True),
                                 reads=[bsg, B_rf[d]], writes=[pnb])
                        for h in range(4):
                            po0, pob0 = pcs[h // 2]
                            P.dve(lambda e, h=h, c=c, po0=po0: e.scalar_tensor_tensor(out=stc.Cn[:, h, :], in0=stc.Cn[:, h, :], scalar=cdall[:, c, d4 + h:d4 + h + 1],
                                                                                   in1=po0[:, (h % 2) * 256:(h % 2 + 1) * 256], op0=ALU.mult, op1=ALU.add),
                                  reads=[stc.B, B_rf[d], pob0], writes=[stc.B])
                        P.dve(lambda e, c=c: e.tensor_tensor(out=stc.Nn[:], in0=stc.Nn[:], in1=cdall[:, c, d4:d4 + 4], op=ALU.mult), reads=[stc.B, B_rf[d]], writes=[stc.B])
                        P.dve(lambda e, pn=pn: e.tensor_tensor(out=stc.Nn[:], in0=pn[:, 0:4], in1=stc.Nn[:], op=ALU.add), reads=[pnb, stc.B], writes=[stc.B])
                        yield
                        pss = [next_big_m(M), next_big_m(M)]
                        for half in range(2):
                            po0, pob0 = pss[half]
                            for gg in range(2):
                                g = half * 2 + gg
                                P.pe(lambda e, g=g, gg=gg, po0=po0, Btm=Btm: e.matmul(po0[:, gg * 256:(gg + 1) * 256], lhsT=Btm[:, g, :], rhs=xw[:, g * 256:(g + 1) * 256], start=True, stop=True),
                                     reads=[bsg, M.B_xw], writes=[pob0])
                        P.pool(lambda e, c=c: e.tensor_tensor(out=stc.Sf[:].rearrange("p (j q) -> p j q", j=16), in0=stc.Sf[:].rearrange("p (j q) -> p j q", j=16),
                                                              in1=scall[:, c, SC_ETOT + d16:SC_ETOT + d16 + 16].unsqueeze(2).to_broadcast([128, 16, 64]), op=ALU.mult),
                               reads=[stc.B, B_scall[c]], writes=[stc.B])
                        for half in range(2):
                            po0, pob0 = pss[half]
                            P.dve(lambda e, half=half, po0=po0: e.tensor_tensor(out=stc.Sf[:, half * 512:(half + 1) * 512], in0=po0,
                                                                              in1=stc.Sf[:, half * 512:(half + 1) * 512], op=ALU.add),
                                  reads=[pob0, stc.B], writes=[stc.B])
                        yield

                gens = [sp_chain(0), sp_chain(1)]
                nstep = 0
                while gens:
                    for g_ in list(gens):
                        try:
                            next(g_)
                        except StopIteration:
                            gens.remove(g_)
                for i in range(4):
                    state_store_scr(E[i], i)
        P.barrier()
        check("B1b")

        mod_k = [16]
        mod_pend = [None]

        def super_unit(su):
            ci = su
            ntok = 512 if su == 0 else 640
            src = xp if su == 0 else xo
            with contextlib.ExitStack() as esu:
                sc4 = T("sc4", [128, 4, NSC], es=esu)
                B_sc4 = [Buf("sc4_%d" % i) for i in range(4)]
                qT = T("qT", [128, 4, 512], BF16, es=esu)
                kTt = T("kT", [128, 4, 512], BF16, es=esu)
                B_qk = Buf("qk")
                k_tm = T("k_tm", [128, 4, 4, 128], BF16, es=esu)
                B_ktm = [Buf("ktm%d" % i) for i in range(4)]
                v_tm = T("v_tm", [128, 4, 1024], BF16, es=esu)
                B_v = [Buf("v%d" % i) for i in range(4)]
                og = T("og", [128, 4, 1024], BF16, es=esu)
                B_og = [Buf("og%d" % i) for i in range(4)]
                sz = T("sz", [128, 4, 1024], BF16, es=esu)
                B_sz = [Buf("sz%d" % i) for i in range(4)]
                BCT = T("BCT", [128, 8, 512], BF16, es=esu)
                B_bct = Buf("BCT")
                x_tm = T("x_tm", [128, 4, 1024], BF16, es=esu)
                B_xtm = [Buf("xtm%d" % i) for i in range(4)]
                B_tm = T("B_tm", [128, 4, 4, 128], BF16, es=esu)
                B_Btm = [Buf("Btm%d" % i) for i in range(4)]
                gml = T("gml", [128, 1024], es=esu)
                gsn = T("gsn", [128, 1024], es=esu)
                B_gg = Buf("gg")
                P.dma(lambda e: e.dma_start(out=gml[:], in_=g_ml[0:1, :].to_broadcast([128, 1024])), writes=[B_gg])
                P.dma(lambda e: e.dma_start(out=gsn[:], in_=g_sn[0:1, :].to_broadcast([128, 1024])), writes=[B_gg])
                with contextlib.ExitStack() as es:
                    I = alloc_inproj(es, nbuf=3)
                    uT = T("uT", [128, KT, ntok], BF16, es=es)
                    B_uT = [Buf("uT%d" % i) for i in range(ntok // 128)]
                    pre = T("pre_u", [128, 640], es=es)
                    B_pre = Buf("pre_u")
                    cacc = T("cacc_u", [128, 512], es=es)
                    B_cacc = Buf("cacc_u")
                    xTt = [T("xTt%d" % i, [128, 512], BF16, es=es) for i in range(2)]
                    B_xTt = [Buf("xTt0"), Buf("xTt1")]
                    sig = T("sigt", [128, 512], es=es)
                    B_sig = Buf("sigt")
                    pg4 = T("pg4", [128, 4, 48], es=es)
                    B_pg4 = Buf("pg4")
                    hnd = uT_s1(I, src[0:128, :])
                    for tt in range(4):
                        if tt < 3:
                            nxt = uT_s1(I, src[(tt + 1) * 128:(tt + 2) * 128, :])
                        else:
                            nxt = uT_s1(I, xh[:, :]) if su == 1 else None
                        uT_s2(I, hnd, ci, 0, uT, tt * 128, B_uT[tt], want_f32=True)
                        gate_preact(I, ci, pg4, tt, B_pg4)
                        hnd = nxt
                    if su == 1:
                        uT_s2(I, hnd, ci, 0, uT, 512, B_uT[4])
                    prep_scalars_batched(4, pg4, B_pg4, sc4, B_sc4, es, cacc, B_cacc)
                    B_own = B_uT[0:4]
                    for (c0, dstT, scale) in ((0, qT, 128.0 ** -0.5), (512, kTt, 1.0)):
                        wt, wb = load_wblock(I, win_cols(c0, 512))
                        for h in range(4):
                            po, pob = next_big()
                            for kt in range(KT):
                                P.pe(lambda e, kt=kt, h=h, wt=wt, po=po: e.matmul(po, lhsT=wt[:, kt, h * 128:(h + 1) * 128], rhs=uT[:, kt, 0:512],
                                                                              start=(kt == 0), stop=(kt == KT - 1)), reads=B_own + [wb], writes=[pob])
                            P.act(lambda e, h=h, po=po, dstT=dstT, scale=scale: e.activation(out=dstT[:, h, :], in_=po, func=AF.Copy, scale=scale),
                                  reads=[pob], writes=[B_qk])
                    for c in range(4):
                        pt, pb = next_tr()
                        ptb = pt[:].bitcast(BF16)
                        for h in range(4):
                            P.pe(lambda e, c=c, h=h, ptb=ptb: e.transpose(ptb[:, h * 128:(h + 1) * 128], kTt[:, h, c * 128:(c + 1) * 128], identb[:]),
                                 reads=[B_qk, B_c], writes=[pb])
                        P.dve(lambda e, c=c, ptb=ptb: e.tensor_copy(out=k_tm[:, c, :, :], in_=ptb[:, 0:512].rearrange("p (h d) -> p h d", h=4)),
                              reads=[pb], writes=[B_ktm[c]])

                    def tm_block(c0, evac):
                        wt, wb = load_wblock(I, win_cols(c0, 512))
                        for c in range(4):
                            po, pob = next_big()
                            for kt in range(KT):
                                P.pe(lambda e, kt=kt, c=c, wt=wt, po=po: e.matmul(po, lhsT=uT[:, kt, c * 128:(c + 1) * 128], rhs=wt[:, kt, :],
                                                                              start=(kt == 0), stop=(kt == KT - 1)), reads=[B_uT[c], wb], writes=[pob])
                            evac(c, po, pob)

                    for half in range(2):
                        def ev_v(c, po, pob, half=half):
                            P.act(lambda e: e.copy(out=v_tm[:, c, half * 512:(half + 1) * 512], in_=po), reads=[pob], writes=[B_v[c]])
                        tm_block(1024 + half * 512, ev_v)
                    for half in range(2):
                        def ev_o(c, po, pob, half=half):
                            P.act(lambda e: e.activation(out=sig[:], in_=po, func=AF.Sigmoid), reads=[pob], writes=[B_sig])
                            P.dve(lambda e: e.tensor_tensor(out=og[:, c, half * 512:(half + 1) * 512], in0=sig[:], in1=gml[:, half * 512:(half + 1) * 512], op=ALU.mult),
                                  reads=[B_sig, B_gg], writes=[B_og[c]])
                        tm_block(2048 + half * 512, ev_o)
                    for half in range(2):
                        def ev_z(c, po, pob, half=half):
                            P.act(lambda e: e.activation(out=sz[:, c, half * 512:(half + 1) * 512], in_=po, func=AF.Silu), reads=[pob], writes=[B_sz[c]])
                        tm_block(3088 + half * 512, ev_z)
                    if su == 0:
                        cen_in, cen_out, taps = seq_taps(pre[:, 0:512], cacc[:])
                    else:
                        cen_in, cen_out, taps = grid_taps(pre[:], cacc[:], 8)
                    cacc_b = [cacc, T("cacc_u2", [128, 512], es=es)]
                    B_cacc_b = [B_cacc, Buf("cacc_u2")]
                    pre_b = [pre, T("pre_u2", [128, 640], es=es)]
                    B_pre_b = [B_pre, Buf("pre_u2")]
                    if su == 0:
                        tapsets = [seq_taps(pre_b[i_][:, 0:512], cacc_b[i_][:]) for i_ in range(2)]
                    else:
                        tapsets = [grid_taps(pre_b[i_][:], cacc_b[i_][:], 8) for i_ in range(2)]

                    def x_front(ct, cc, wt, wb):
                        pr, bpr = pre_b[ct % 2], B_pre_b[ct % 2]
                        cen_in, cen_out, taps = tapsets[ct % 2]
                        po, pob = next_big()
                        for kt in range(KT):
                            P.pe(lambda e, kt=kt, po=po: e.matmul(po, lhsT=wt[:, kt, cc * 128:(cc + 1) * 128], rhs=uT[:, kt, 0:512],
                                                                  start=(kt == 0), stop=(kt == KT - 1)), reads=B_own + [wb], writes=[pob])
                        if su == 0:
                            P.act(lambda e, po=po: e.copy(out=pr[:, 0:512], in_=po), reads=[pob], writes=[bpr])
                        else:
                            P.act(lambda e, po=po: e.copy(out=pr[:, 64:576], in_=po), reads=[pob], writes=[bpr])
                            po2, pob2 = next_big()
                            for kt in range(KT):
                                P.pe(lambda e, kt=kt, po2=po2: e.matmul(po2[:, 0:128], lhsT=wt[:, kt, cc * 128:(cc + 1) * 128], rhs=uT[:, kt, 512:640],
                                                                        start=(kt == 0), stop=(kt == KT - 1)), reads=[B_uT[4], wb], writes=[pob2])
                            P.dve(lambda e, po2=po2: e.tensor_scalar(out=pr[:, 0:64], in0=po2[:, 0:64], scalar1=flags[:, 64:65], scalar2=None, op0=ALU.mult),
                                  reads=[pob2, B_c], writes=[bpr])
                            P.dve(lambda e, po2=po2: e.tensor_scalar(out=pr[:, 576:640], in0=po2[:, 64:128], scalar1=flags[:, 65:66], scalar2=None, op0=ALU.mult),
                                  reads=[pob2, B_c], writes=[bpr])
                        conv_apply(cen_in, cen_out, taps, ct, bpr, B_cacc_b[ct % 2], cacc_b[ct % 2][:], None, None, do_silu=False)

                    def x_back(ct):
                        if ct < 8:
                            P.act(lambda e: e.activation(out=xTt[ct % 2][:], in_=cacc_b[ct % 2][:], func=AF.Silu), reads=[B_cacc_b[ct % 2]], writes=[B_xTt[ct % 2]])
                        else:
                            P.act(lambda e: e.activation(out=BCT[:, ct - 8, :], in_=cacc_b[ct % 2][:], func=AF.Silu), reads=[B_cacc_b[ct % 2]], writes=[B_bct])
                        if ct < 8:
                            xi = ct % 2
                            pt, pb = next_tr()
                            ptb = pt[:].bitcast(BF16)
                            for c in range(4):
                                P.pe(lambda e, c=c, ptb=ptb: e.transpose(ptb[:, c * 128:(c + 1) * 128], xTt[xi][:, c * 128:(c + 1) * 128], identb[:]),
                                     reads=[B_xTt[xi], B_c], writes=[pb])
                            P.act(lambda e, ptb=ptb: e.copy(out=x_tm[:, :, ct * 128:(ct + 1) * 128], in_=ptb[:, 0:512].rearrange("p (c n) -> p c n", c=4)),
                                  reads=[pb], writes=B_xtm)
                        elif ct < 12:
                            g = ct - 8
                            pt, pb = next_tr()
                            ptb = pt[:].bitcast(BF16)
                            for c in range(4):
                                P.pe(lambda e, c=c, ptb=ptb: e.transpose(ptb[:, c * 128:(c + 1) * 128], BCT[:, g, c * 128:(c + 1) * 128], identb[:]),
                                     reads=[B_bct, B_c], writes=[pb])
                            P.act(lambda e, ptb=ptb: e.copy(out=B_tm[:, :, g, :], in_=ptb[:, 0:512].rearrange("p (c n) -> p c n", c=4)),
                                  reads=[pb], writes=B_Btm)

                    pend = None
                    for blk_i in range(4):
                        wt, wb = load_wblock(I, win_cols(4112 + blk_i * 512, 512))
                        for cc in range(4):
                            ct = blk_i * 4 + cc
                            x_front(ct, cc, wt, wb)
                            if pend is not None:
                                x_back(pend)
                            pend = ct
                    x_back(pend)
                P.barrier()
                with contextlib.ExitStack() as es:
                    M = alloc_mixer(es)
                    M2 = alloc_mixer(es)
                    hsum = T("hsum", [128, 2, 1024], es=es)
                    B_hs = [Buf("hs0"), Buf("hs1")]
                    ysum = T("ysum", [128, 2, 1024], es=es)
                    B_ys = [Buf("ys0"), Buf("ys1")]
                    cat0 = T("cat0", [128, D], es=es)
                    cat = [cat0, cat0]
                    B_cat0 = Buf("cat0")
                    B_cat = [B_cat0, B_cat0]
                    cTo0 = T("cTout0", [128, KT * 128], BF16, es=es)
                    cTout = [cTo0, cTo0]
                    B_cTo0 = Buf("cTout0")
                    B_cTout = [B_cTo0, B_cTo0]
                    wmh = [T("wmh%d" % i, [128, KT, 256], BF16, es=es) for i in range(2)]
                    B_wmh = [Buf("wmh0"), Buf("wmh1")]
                    k_end = 32 if su == 1 else 48
                    if mod_k[0] < k_end:
                        mod_load(mod_k[0] * 256, 256, wmh[mod_k[0] % 2], B_wmh[mod_k[0] % 2])

                    def mod_bg():
                        k = mod_k[0]
                        if k >= k_end:
                            return
                        if k + 1 < k_end:
                            mod_load((k + 1) * 256, 256, wmh[(k + 1) % 2], B_wmh[(k + 1) % 2])
                        if mod_pend[0] is not None:
                            mod_pend[0]()
                        mod_pend[0] = mod_compute(k * 256, 256, wmh[k % 2], B_wmh[k % 2], defer=True, small_psum=True)
                        mod_k[0] += 1
                    sf = new_state("sf", es)
                    sb = new_state("sb", es)
                    nt4 = [T("nt4_%d" % i, [4, 128], es=es) for i in range(2)]
                    B_nt4 = [Buf("nt4a"), Buf("nt4b")]
                    sout = [M.htmp, M2.htmp]
                    B_sout = [M.B_ht, M2.B_ht]
                    htmp, B_ht = M.htmp, M.B_ht
                    def chain(Mx, un, d, st):
                        chunks = (2 * un, 2 * un + 1)
                        order = chunks if d == 0 else chunks[::-1]
                        for n_, c in enumerate(order):
                            lc = c - 2 * un
                            tsl = slice(c * 128, (c + 1) * 128)
                            sc = sc4[:, c, :]
                            first = (n_ == 0) if d == 0 else (n_ == 0)
                            first = (d == 0 and lc == 0) or (d == 1 and lc == 1)
                            mlstm_chain_scalars(Mx, sc, B_sc4[c], d, st, qT, B_qk, tsl)
                            yield
                            make_vp(Mx, v_tm[:, c, :].rearrange("p (h e) -> p h e", h=4), B_v[c])
                            mlstm_out(Mx, d, st, qT, kTt, B_qk, tsl, hsum[:, lc, :], B_hs[lc], first)
                            yield
                            ssd_gt(Mx, BCT[:, 0:4, :], BCT[:, 4:8, :], B_bct, tsl)
                            yield from ssd_out(Mx, sc, B_sc4[c], d, st, BCT[:, 4:8, :], B_bct, tsl, x_tm[:, c, :], B_xtm[c], ysum[:, lc, :], B_ys[lc], first)
                            yield
                            if su == 0 or n_ == 0:
                                mlstm_update(Mx, st, k_tm[:, c, :, :], B_ktm[c])
                                yield
                                ssd_update(Mx, sc, B_sc4[c], d, st, x_tm[:, c, :], B_xtm[c], B_tm[:, c, :, :], B_Btm[c])
                                yield
                        if su == 0:
                            seq = un
                            P.dma(lambda e, seq=seq, d=d, st=st: e.dma_start(out=o_c[seq, d].rearrange("h d e -> d h e"), in_=st.Cn[:]), reads=[st.B])
                            pt, pb = next_sm()
                            P.pe(lambda e, pt=pt, st=st: e.transpose(pt[0:4, 0:128], st.Nn[:], ident[:]), reads=[st.B, B_c], writes=[pb])
                            P.dve(lambda e, pt=pt: e.tensor_copy(out=nt4[d][:], in_=pt[0:4, 0:128]), reads=[pb], writes=[B_nt4[d]])
                            P.dma(lambda e, seq=seq, d=d: e.dma_start(out=o_n[seq, d], in_=nt4[d][:]), reads=[B_nt4[d]])
                            P.dma(lambda e, seq=seq, d=d, st=st: e.dma_start(out=o_m[seq, d:d + 1, :], in_=st.m[0:1, :]), reads=[st.B])
                            for a in range(8):
                                pt, pb = next_sm()
                                P.pe(lambda e, a=a, pt=pt, st=st: e.transpose(pt[:, 0:128], st.Sf[:, a * 128:(a + 1) * 128], ident[:]), reads=[st.B, B_c], writes=[pb])
                                P.act(lambda e, a=a, pt=pt: e.copy(out=sout[d][:, a * 128:(a + 1) * 128], in_=pt[:, 0:128]), reads=[pb], writes=[B_sout[d]])
                            P.dma(lambda e, seq=seq, d=d: e.dma_start(out=o_s[seq, d].rearrange("(a p) n -> p a n", p=128),
                                                                     in_=sout[d][:].rearrange("p (a n) -> p a n", a=8)), reads=[B_sout[d]])

                    for un in range(2):
                        chunks = (2 * un, 2 * un + 1)
                        if su == 0:
                            state_zero(sf)
                            state_zero(sb)
                        else:
                            state_load_scr(sf, 0 + un)
                            state_load_scr(sb, 2 + un)
                        gens = [chain(M, un, 0, sf), chain(M2, un, 1, sb)]
                        nstep = 0
                        while gens:
                            for g_ in list(gens):
                                try:
                                    next(g_)
                                except StopIteration:
                                    gens.remove(g_)
                                nstep += 1
                                if nstep % 5 == 0:
                                    mod_bg()
                        for c in chunks:
                            lc = c - 2 * un
                            tt = su * 4 + c
                            hs = hsum[:, lc, :]
                            ct_, B_ct = cat[lc], B_cat[lc]
                            for h in range(4):
                                P.act(lambda e, h=h, hs=hs: e.activation(out=sq_junk[:, 0:256], in_=hs[:, h * 256:(h + 1) * 256], func=AF.Square, accum_out=stat[:, 8 + h:9 + h]),
                                      reads=[B_hs[lc]], writes=[B_junk, B_stat])
                            rstd_of(stat[:, 8:12], stat[:, 12:16], 256)
                            P.dve(lambda e, hs=hs, ct_=ct_: e.tensor_tensor(out=ct_[:, 0:1024].rearrange("p (h e) -> p h e", h=4), in0=hs.rearrange("p (h e) -> p h e", h=4),
                                                                           in1=stat[:, 12:16].unsqueeze(2).to_broadcast([128, 4, 256]), op=ALU.mult),
                                  reads=[B_hs[lc], B_stat], writes=[B_ct])
                            P.dve(lambda e, c=c, ct_=ct_: e.tensor_tensor(out=ct_[:, 0:1024], in0=ct_[:, 0:1024], in1=og[:, c, :], op=ALU.mult), reads=[B_ct, B_og[c]], writes=[B_ct])
                            ysl = ysum[:, lc, :]
                            P.dve(lambda e, c=c: e.tensor_tensor(out=htmp[:].rearrange("p (j q) -> p j q", j=16), in0=x_tm[:, c, :].rearrange("p (j q) -> p j q", j=16),
                                                                 in1=rowp[:, R_DSK:R_DSK + 16].unsqueeze(2).to_broadcast([128, 16, 64]), op=ALU.mult),
                                  reads=[B_xtm[c], B_c], writes=[B_ht])
                            P.dve(lambda e, ysl=ysl: e.tensor_tensor(out=ysl, in0=ysl, in1=htmp[:], op=ALU.add), reads=[B_ys[lc], B_ht], writes=[B_ys[lc]])
                            P.dve(lambda e, ysl=ysl, c=c: e.tensor_tensor(out=ysl, in0=ysl, in1=sz[:, c, :], op=ALU.mult), reads=[B_ys[lc], B_sz[c]], writes=[B_ys[lc]])
                            P.act(lambda e, ysl=ysl: e.activation(out=sq_junk[:, 0:1024], in_=ysl, func=AF.Square, accum_out=stat[:, 16:17]),
                                  reads=[B_ys[lc]], writes=[B_junk, B_stat])
                            rstd_of(stat[:, 16:17], stat[:, 17:18], 1024)
                            P.dve(lambda e, ysl=ysl, ct_=ct_: e.scalar_tensor_tensor(out=ct_[:, 1024:2048], in0=ysl, scalar=stat[:, 17:18], in1=gsn[:], op0=ALU.mult, op1=ALU.mult),
                                  reads=[B_ys[lc], B_stat, B_gg], writes=[B_ct])
                            cTo, B_cTo = cTout[lc], B_cTout[lc]
                            for g4 in range(4):
                                pt, pb = next_tr()
                                for j in range(4):
                                    kt = g4 * 4 + j
                                    P.pe(lambda e, kt=kt, j=j, pt=pt, ct_=ct_: e.transpose(pt[:, j * 128:(j + 1) * 128], ct_[:, kt * 128:(kt + 1) * 128], ident[:]),
                                         reads=[B_ct, B_c], writes=[pb])
                                P.act(lambda e, g4=g4, pt=pt, cTo=cTo: e.copy(out=cTo[:, g4 * 512:(g4 + 1) * 512], in_=pt[:, 0:512]), reads=[pb], writes=[B_cTo])
                            P.dma(lambda e, tt=tt, cTo=cTo: e.dma_start(out=catTs[tt], in_=cTo[:]), reads=[B_cTo], writes=[B_cats[tt]])
                    while mod_k[0] < k_end:
                        mod_bg()
                    if mod_pend[0] is not None:
                        mod_pend[0]()
                        mod_pend[0] = None
            P.barrier()

        super_unit(1)
        check("SU1")
        super_unit(0)
        check("SU0")
        mod_cols(32, 64)
        mod_consts(2, C_GMLP, 64, 48)

        with contextlib.ExitStack() as escd:
            u2T = T("u2T", [128, KT, 1024], BF16, es=escd)
            B_u2T = [Buf("u2T%d" % i) for i in range(8)]
            with contextlib.ExitStack() as es:
                I = NS()
                I.xt = [T("xt%d" % i, [128, D], es=es) for i in range(2)]
                I.B_xt = [Buf("xt0"), Buf("xt1")]
                I.xrot = [0]
                I.uTf, I.B_uTf = None, None
                wo = T("wo", [128, KT, D], BF16, es=es)
                B_wo = Buf("wo")
                G1 = T("G1", [128, 2, D], es=es)
                B_G1 = Buf("G1")
                x1 = [T("x1_%d" % i, [128, D], es=es) for i in range(2)]
                B_x1 = [Buf("x1a"), Buf("x1b")]
                cTt = [T("cTt%d" % i, [128, KT, 128], BF16, es=es) for i in range(2)]
                B_cTt = [Buf("cTt0"), Buf("cTt1")]
                B_wo4 = [Buf("wo%d" % i) for i in range(4)]
                for cbk in range(4):
                    P.dma(lambda e, cbk=cbk: e.dma_start(out=wo[:, :, cbk * 512:(cbk + 1) * 512],
                                                         in_=w_out[:, cbk * 512:(cbk + 1) * 512].rearrange("(kt p) c -> p kt c", p=128)), writes=[B_wo4[cbk]], eng="pool")
                P.dma(lambda e: e.dma_start(out=x1[0][:], in_=g_post_mix[0:1, :].to_broadcast([128, D])), writes=[B_x1[0]])
                for ci in range(2):
                    P.dma(lambda e, ci=ci: e.dma_start(out=G1[:, ci, :], in_=modscr[ci:ci + 1, 2 * D:3 * D].to_broadcast([128, D])), reads=[B_modscr], writes=[B_G1])
                    P.dve(lambda e, ci=ci: e.tensor_tensor(out=G1[:, ci, :], in0=G1[:, ci, :], in1=x1[0][:], op=ALU.mult), reads=[B_G1, B_x1[0]], writes=[B_G1])
                def c_front(tt):
                    ci = 0 if tt < 4 else 1
                    li = tt % 2
                    P.dma(lambda e, li=li, tt=tt: e.dma_start(out=cTt[li][:].rearrange("p k t -> p (k t)"), in_=catTs[tt]), reads=[B_cats[tt]], writes=[B_cTt[li]])
                    for cbk in range(4):
                        for kt in range(KT):
                            P.pe(lambda e, kt=kt, cbk=cbk, li=li: e.matmul(pbig[:, cbk * 512:(cbk + 1) * 512], lhsT=cTt[li][:, kt, :],
                                                                           rhs=wo[:, kt, cbk * 512:(cbk + 1) * 512], start=(kt == 0), stop=(kt == KT - 1)),
                                 reads=[B_cTt[li], B_wo4[cbk]], writes=[PB[cbk]])

                def c_front_b(tt):
                    for cbk in range(4):
                        P.dve(lambda e, cbk=cbk, tt=tt: e.tensor_copy(out=x1[tt % 2][:, cbk * 512:(cbk + 1) * 512], in_=pbig[:, cbk * 512:(cbk + 1) * 512]),
                              reads=[PB[cbk]], writes=[B_x1[tt % 2]])

                def c_back(tt):
                    ci = 0 if tt < 4 else 1
                    xi = tt % 2
                    P.act(lambda e, xi=xi: e.activation(out=sq_junk[:], in_=x1[xi][:], func=AF.Square, accum_out=stat[:, 20:21]), reads=[B_x1[xi]], writes=[B_junk, B_stat])
                    rstd_of(stat[:, 20:21], stat[:, 21:22], D)
                    src_rows = xp[tt * 128:(tt + 1) * 128, :] if tt < 4 else xo[(tt - 4) * 128:(tt - 3) * 128, :]
                    xr, bxr = I.xt[I.xrot[0] % 2], I.B_xt[I.xrot[0] % 2]
                    I.xrot[0] += 1
                    P.dma(lambda e, xr=xr, src_rows=src_rows: e.dma_start(out=xr[:], in_=src_rows), writes=[bxr])
                    P.dve(lambda e, xi=xi, ci=ci: e.scalar_tensor_tensor(out=x1[xi][:], in0=x1[xi][:], scalar=stat[:, 21:22], in1=G1[:, ci, :], op0=ALU.mult, op1=ALU.mult),
                          reads=[B_x1[xi], B_stat, B_G1], writes=[B_x1[xi]])
                    P.dve(lambda e, xi=xi, xr=xr: e.tensor_tensor(out=x1[xi][:], in0=x1[xi][:], in1=xr[:], op=ALU.add), reads=[B_x1[xi], bxr], writes=[B_x1[xi]])
                    P.dma(lambda e, xi=xi, tt=tt: e.dma_start(out=x1s[tt * 128:(tt + 1) * 128, :], in_=x1[xi][:]), reads=[B_x1[xi]], writes=[B_x1s[tt]])
                    make_uT(I, None, ci, 2, u2T, tt * 128, B_u2T[tt], src_sb=x1[xi], B_src=B_x1[xi])

                for tt in range(9):
                    if tt < 8:
                        c_front(tt)
                    if tt >= 1:
                        c_back(tt - 1)
                    if tt < 8:
                        c_front_b(tt)
            P.barrier()
            check("C")

            with contextlib.ExitStack() as es:
                acc = T("acc", [128, 8, D], es=es)
                B_acc = [Buf("acc%d" % i) for i in range(8)]
                with contextlib.ExitStack() as es2:
                    hdn = T("hdn", [128, 4, 1024], BF16, es=es2)
                    B_hdn = Buf("hdn")
                    rl = [T("rl%d" % i, [128, 512], es=es2) for i in range(2)]
                    B_rl = [Buf("rl0"), Buf("rl1")]
                    w1 = [T("w1_%d" % i, [128, KT, 512], BF16, es=es2) for i in range(2)]
                    B_w1 = [Buf("w1a"), Buf("w1b")]
                    w2 = [T("w2_%d" % i, [128, 4, D], BF16, es=es2) for i in range(2)]
                    B_w2 = [Buf("w2a"), Buf("w2b")]
                    rli = [0]
                    for fb in range(16):
                        wi = fb % 2
                        P.dma(lambda e, wi=wi, fb=fb: e.dma_start(out=w1[wi][:], in_=w_mlp_in[:, fb * 512:(fb + 1) * 512].rearrange("(kt p) c -> p kt c", p=128)),
                              writes=[B_w1[wi]], eng="pool")
                        P.dma(lambda e, wi=wi, fb=fb: e.dma_start(out=w2[wi][:], in_=w_mlp_out[fb * 512:(fb + 1) * 512, :].rearrange("(ft p) c -> p ft c", p=128)),
                              writes=[B_w2[wi]], eng="pool")
                        for ft in range(4):
                            for half in range(2):
                                pt, pb = (next_tr() if (ft * 2 + half) % 2 == 0 else next_sm())
                                for kt in range(KT):
                                    P.pe(lambda e, kt=kt, ft=ft, half=half, wi=wi, pt=pt: e.matmul(pt[:], lhsT=w1[wi][:, kt, ft * 128:(ft + 1) * 128],
                                                                                                 rhs=u2T[:, kt, half * 512:(half + 1) * 512], start=(kt == 0), stop=(kt == KT - 1)),
                                         reads=B_u2T[4 * half:4 * half + 4] + [B_w1[wi]], writes=[pb])
                                ri = rli[0] % 2
                                rli[0] += 1
                                P.act(lambda e, ri=ri, pt=pt: e.activation(out=rl[ri][:], in_=pt[:], func=AF.Relu), reads=[pb], writes=[B_rl[ri]])
                                P.pool(lambda e, ri=ri, ft=ft, half=half: e.tensor_tensor(out=hdn[:, ft, half * 512:(half + 1) * 512], in0=rl[ri][:], in1=rl[ri][:], op=ALU.mult),
                                       reads=[B_rl[ri]], writes=[B_hdn])
                        for tt in range(8):
                            for cbk in range(4):
                                for ft in range(4):
                                    P.pe(lambda e, ft=ft, cbk=cbk, tt=tt, wi=wi: e.matmul(pbig[:, cbk * 512:(cbk + 1) * 512], lhsT=hdn[:, ft, tt * 128:(tt + 1) * 128],
                                                                                      rhs=w2[wi][:, ft, cbk * 512:(cbk + 1) * 512], start=(ft == 0), stop=(ft == 3)),
                                         reads=[B_hdn, B_w2[wi]], writes=[PB[cbk]])
                                if fb == 0:
                                    P.dve(lambda e, tt=tt, cbk=cbk: e.tensor_copy(out=acc[:, tt, cbk * 512:(cbk + 1) * 512], in_=pbig[:, cbk * 512:(cbk + 1) * 512]),
                                          reads=[PB[cbk]], writes=[B_acc[tt]])
                                else:
                                    P.dve(lambda e, tt=tt, cbk=cbk: e.tensor_tensor(out=acc[:, tt, cbk * 512:(cbk + 1) * 512], in0=pbig[:, cbk * 512:(cbk + 1) * 512],
                                                                                   in1=acc[:, tt, cbk * 512:(cbk + 1) * 512], op=ALU.add),
                                          reads=[PB[cbk], B_acc[tt]], writes=[B_acc[tt]])
                P.barrier()
                with contextlib.ExitStack() as es2:
                    G2 = T("G2", [128, 2, D], es=es2)
                    B_G2 = Buf("G2")
                    xr2 = [T("xr2_%d" % i, [128, D], es=es2) for i in range(2)]
                    B_xr2 = [Buf("xr2a"), Buf("xr2b")]
                    P.dma(lambda e: e.dma_start(out=xr2[0][:], in_=g_post_mlp[0:1, :].to_broadcast([128, D])), writes=[B_xr2[0]])
                    for ci in range(2):
                        P.dma(lambda e, ci=ci: e.dma_start(out=G2[:, ci, :], in_=modscr[ci:ci + 1, 5 * D:6 * D].to_broadcast([128, D])), reads=[B_modscr], writes=[B_G2])
                        P.dve(lambda e, ci=ci: e.tensor_tensor(out=G2[:, ci, :], in0=G2[:, ci, :], in1=xr2[0][:], op=ALU.mult), reads=[B_G2, B_xr2[0]], writes=[B_G2])
                    for tt in range(8):
                        ci = 0 if tt < 4 else 1
                        P.act(lambda e, tt=tt: e.activation(out=sq_junk[:], in_=acc[:, tt, :], func=AF.Square, accum_out=stat[:, 24:25]), reads=[B_acc[tt]], writes=[B_junk, B_stat])
                        rstd_of(stat[:, 24:25], stat[:, 25:26], D)
                        xi = tt % 2
                        P.dma(lambda e, xi=xi, tt=tt: e.dma_start(out=xr2[xi][:], in_=x1s[tt * 128:(tt + 1) * 128, :]), reads=[B_x1s[tt]], writes=[B_xr2[xi]])
                        P.dve(lambda e, tt=tt, ci=ci: e.scalar_tensor_tensor(out=acc[:, tt, :], in0=acc[:, tt, :], scalar=stat[:, 25:26], in1=G2[:, ci, :], op0=ALU.mult, op1=ALU.mult),
                              reads=[B_acc[tt], B_stat, B_G2], writes=[B_acc[tt]])
                        P.dve(lambda e, tt=tt, xi=xi: e.tensor_tensor(out=acc[:, tt, :], in0=acc[:, tt, :], in1=xr2[xi][:], op=ALU.add), reads=[B_acc[tt], B_xr2[xi]], writes=[B_acc[tt]])
                        dst = yp[tt * 128:(tt + 1) * 128, :] if tt < 4 else ys[(tt - 4) * 128:(tt - 3) * 128, :]
                        P.dma(lambda e, tt=tt, dst=dst: e.dma_start(out=dst, in_=acc[:, tt, :]), reads=[B_acc[tt]])
        P.emit()
    return nc, P


def build_debug(stop_after):
    try:
        return build_program(stop_after)
    except _Stop:
        pass


_CACHE = {}


def _prep_inputs(inp):
    f = lambda a: np.ascontiguousarray(np.asarray(a, dtype=np.float32))
    x_prompt = f(inp["x_prompt"])
    x_sample = f(inp["x_sample"])
    shared = {
        "w_mod": f(inp["w_mod"][0]), "b_mod": f(inp["b_mod"][0]).reshape(1, 6 * D),
        "g_pre_mix": f(inp["g_pre_mix"][0]).reshape(16, 128), "g_post_mix": f(inp["g_post_mix"][0]).reshape(1, D),
        "w_in": f(inp["w_in"][0]), "b_igate": f(inp["b_igate"][0]).reshape(1, 8), "b_fgate": f(inp["b_fgate"][0]).reshape(1, 8),
        "conv_w": f(inp["conv_w"][0]).reshape(144, 128), "conv_b": f(inp["conv_b"][0]).reshape(16, 128),
        "dt_bias": f(inp["dt_bias"][0]).reshape(1, 32), "a_log": f(inp["a_log"][0]).reshape(1, 32),
        "d_skip": f(inp["d_skip"][0]).reshape(1, 16), "g_mlstm_norm": f(inp["g_mlstm_norm"][0]).reshape(1, 1024),
        "g_ssd_norm": f(inp["g_ssd_norm"][0]).reshape(1, 1024), "w_out": f(inp["w_out"][0]),
        "g_pre_mlp": f(inp["g_pre_mlp"][0]).reshape(16, 128), "g_post_mlp": f(inp["g_post_mlp"][0]).reshape(1, D),
        "w_mlp_in": f(inp["w_mlp_in"][0]), "w_mlp_out": f(inp["w_mlp_out"][0]),
    }
    c = f(inp["c"])
    c_ctx = f(inp["c_ctx"])
    maps = []
    for r in range(8):
        b, q = r // 4, r % 4
        xs_b = x_sample[b]
        xh = np.zeros((128, D), np.float32)
        if q > 0:
            xh[0:64] = xs_b[512 * q - 64:512 * q]
        if q < 3:
            xh[64:128] = xs_b[512 * q + 512:512 * q + 576]
        fl = np.zeros((1, NFLAG), np.float32)
        fl[0, 0 + 4 * q] = 1.0
        fl[0, 16 + 4 * q + 2] = 1.0
        fl[0, 32 + 4 * q + 1] = 1.0
        fl[0, 48 + 4 * q + 3] = 1.0
        fl[0, 64] = 1.0 if q > 0 else 0.0
        fl[0, 65] = 1.0 if q < 3 else 0.0
        m = dict(shared)
        m.update({
            "xp": np.ascontiguousarray(x_prompt[2 * r:2 * r + 2].reshape(512, D)),
            "xo": np.ascontiguousarray(xs_b[512 * q:512 * q + 512]),
            "xh": xh,
            "xf": np.ascontiguousarray(xs_b),
            "cv": np.ascontiguousarray(np.stack([c_ctx, c[b]], 0).reshape(32, 128)),
            "st_c": f(inp["state_mlstm_c"][b, 0]),
            "st_n": f(inp["state_mlstm_n"][b, 0]),
            "st_m": f(inp["state_mlstm_m"][b, 0]).reshape(1, 8),
            "st_s": f(inp["state_ssd"][b, 0]).reshape(2, 1024, 128),
            "flags": fl,
        })
        maps.append(m)
    return maps


def kernel(**inp):
    if "nc" not in _CACHE:
        _CACHE["nc"] = build_program()[0]
    nc = _CACHE["nc"]
    maps = _prep_inputs(inp)
    res = run_bass_kernel_spmd(nc, maps, core_ids=list(range(8)))
    y_prompt = np.zeros((16, 256, D), np.float32)
    y_sample = np.zeros((2, 2048, D), np.float32)
    new_c = np.zeros((16, 1, 2, 4, 128, 256), np.float32)
    new_n = np.zeros((16, 1, 2, 4, 128), np.float32)
    new_m = np.zeros((16, 1, 2, 4), np.float32)
    new_s = np.zeros((16, 1, 2, 16, 64, 128), np.float32)
    for r in range(8):
        o = res.results[r]
        b, q = r // 4, r % 4
        y_prompt[2 * r:2 * r + 2] = np.asarray(o["yp"]).reshape(2, 256, D)
        y_sample[b, 512 * q:512 * q + 512] = np.asarray(o["ys"])
        new_c[2 * r:2 * r + 2, 0] = np.asarray(o["o_c"])
        new_n[2 * r:2 * r + 2, 0] = np.asarray(o["o_n"])
        new_m[2 * r:2 * r + 2, 0] = np.asarray(o["o_m"])
        new_s[2 * r:2 * r + 2, 0] = np.asarray(o["o_s"]).reshape(2, 2, 16, 64, 128)
    return (y_prompt, y_sample, new_c, new_n, new_m, new_s)
```

```python
import contextlib
import numpy as np
import concourse.bass as bass
import concourse.mybir as mybir
from concourse.bass_utils import run_bass_kernel_spmd

F32 = mybir.dt.float32
BF16 = mybir.dt.bfloat16
AF = mybir.ActivationFunctionType
ALU = mybir.AluOpType
AX = mybir.AxisListType

D = 2048
KT = 16
EPS = 1e-6
NEG = -30000.0
NFLAG = 80
ESZ = 1024 + 4 + 4 + 1024


class Buf:
    __slots__ = ("name", "last_w", "readers", "excl")

    def __init__(self, name="", excl=False):
        self.name = name
        self.last_w = None
        self.readers = []
        self.excl = excl


class Op:
    __slots__ = ("idx", "eng", "fn", "deps", "is_dma", "sig", "sem", "val", "pre")

    def __init__(self, idx, eng, fn, is_dma):
        self.idx = idx
        self.eng = eng
        self.fn = fn
        self.deps = set()
        self.is_dma = is_dma
        self.sig = False
        self.sem = None
        self.val = 0
        self.pre = None


ENGS = ("pe", "act", "dve", "pool", "sp")
N_DMA_SEMS = 48


class Prog:
    def __init__(self, nc):
        self.nc = nc
        self.ops = []
        self.last_on = {}
        self.dma_since_bar = []

    def add(self, eng, fn, reads=(), writes=(), dma=False):
        op = Op(len(self.ops), eng, fn, dma)
        for r in reads:
            if r.last_w is not None:
                op.deps.add(r.last_w)
            if r.excl:
                for rd in r.readers:
                    if rd.eng != eng:
                        op.deps.add(rd)
        for w in writes:
            lw = w.last_w
            if lw is not None and (dma or lw.is_dma or lw.eng != eng):
                op.deps.add(lw)
            last_rd = {}
            for rd in w.readers:
                if rd.is_dma:
                    op.deps.add(rd)
                elif dma or rd.eng != eng:
                    last_rd[rd.eng] = rd
            for rd in last_rd.values():
                op.deps.add(rd)
        for w in writes:
            w.last_w = op
            w.readers = []
        for r in reads:
            r.readers.append(op)
        op.deps.discard(op)
        if not dma:
            self.last_on[eng] = op
        else:
            self.dma_since_bar.append(op)
        self.ops.append(op)
        return op

    def pe(self, fn, reads=(), writes=()):
        return self.add("pe", fn, reads, writes)

    def act(self, fn, reads=(), writes=()):
        return self.add("act", fn, reads, writes)

    def dve(self, fn, reads=(), writes=()):
        return self.add("dve", fn, reads, writes)

    def pool(self, fn, reads=(), writes=()):
        return self.add("pool", fn, reads, writes)

    def dma(self, fn, reads=(), writes=(), eng="sp"):
        return self.add(eng, fn, reads, writes, dma=True)

    def barrier(self):
        prev = [o for o in self.last_on.values()] + list(self.dma_since_bar)
        self.dma_since_bar = []
        for eng in ENGS:
            op = Op(len(self.ops), eng, (lambda e: e.nop()), False)
            op.deps = set(prev)
            self.ops.append(op)
            self.last_on[eng] = op

    def emit(self):
        nc = self.nc
        ops = self.ops
        for op in ops:
            for d in op.deps:
                d.sig = True
        with contextlib.ExitStack() as es:
            esem = {e: es.enter_context(nc.semaphore("s_" + e)) for e in ENGS}
            dsems = [es.enter_context(nc.semaphore("d%d" % i)) for i in range(N_DMA_SEMS)]
            cnt = {e: 0 for e in esem}
            dcnt = [0] * N_DMA_SEMS
            dlast = [None] * N_DMA_SEMS
            rr = 0
            rr_sw = 0
            rr_hw = 0
            N_SW = 16
            for op in ops:
                if op.is_dma:
                    if op.eng == "pool":
                        s = rr_sw % N_SW
                        rr_sw += 1
                    else:
                        s = N_SW + rr_hw % (N_DMA_SEMS - N_SW)
                        rr_hw += 1
                    rr += 1
                    op.pre = dlast[s]
                    dcnt[s] += 16
                    op.sem = dsems[s]
                    op.val = dcnt[s]
                    dlast[s] = op
                elif op.sig:
                    cnt[op.eng] += 1
                    op.sem = esem[op.eng]
                    op.val = cnt[op.eng]
            self.stats = dict(cnt=dict(cnt), n_dma=rr, n_ops=len(ops))
            by_eng = {e: [o for o in ops if o.eng == e] for e in ENGS}
            last_dma = [d for d in dlast if d is not None]
            blk = es.enter_context(nc.Block())

            def run(engname, e):
                seen = {}

                def wait(d):
                    key = id(d.sem)
                    if seen.get(key, 0) >= d.val:
                        return
                    seen[key] = d.val
                    e.wait_ge(d.sem, d.val)

                for op in by_eng[engname]:
                    if op.pre is not None:
                        wait(op.pre)
                    for d in sorted(op.deps, key=lambda o: o.idx):
                        wait(d)
                    ins = op.fn(e)
                    if op.sem is not None:
                        ins.then_inc(op.sem, 16 if op.is_dma else 1)
                if engname == "sp":
                    for d in last_dma:
                        wait(d)

            @blk.tensor
            def _(e):
                run("pe", e)

            @blk.scalar
            def _(e):
                run("act", e)

            @blk.vector
            def _(e):
                run("dve", e)

            @blk.gpsimd
            def _(e):
                run("pool", e)

            @blk.sync
            def _(e):
                run("sp", e)


class NS:
    pass


NSC = 232
SC_A, SC_B, SC_BT, SC_AMX = 0, 8, 16, 24
SC_DT, SC_CS, SC_ECS, SC_WK, SC_ETOT, SC_NCS = 32, 64, 96, 128, 160, 200
R_BIG, R_BFG, R_DTB, R_ALOG, R_DSK = 0, 8, 16, 48, 80
C_GPRE, C_GMLP, C_CW, C_CB = 0, 16, 32, 176


class _Stop(Exception):
    pass


def build_program(stop_after=None):
    nc = bass.Bass("TRN2", target_bir_lowering=False)
    P = Prog(nc)

    def check(stage):
        if stop_after == stage:
            P.emit()
            raise _Stop()

    def din(name, shape):
        return nc.dram_tensor(name, list(shape), F32, kind="ExternalInput").ap()

    def dout(name, shape):
        return nc.dram_tensor(name, list(shape), F32, kind="ExternalOutput").ap()

    def dscr(name, shape, dt=F32):
        return nc.dram_tensor(name, list(shape), dt, kind="Internal").ap()

    xp = din("xp", [512, D])
    xo = din("xo", [512, D])
    xh = din("xh", [128, D])
    xf = din("xf", [2048, D])
    cv = din("cv", [32, 128])
    st_c = din("st_c", [2, 4, 128, 256])
    st_n = din("st_n", [2, 4, 128])
    st_m = din("st_m", [1, 8])
    st_s = din("st_s", [2, 1024, 128])
    flags_d = din("flags", [1, NFLAG])
    w_mod = din("w_mod", [D, 6 * D])
    b_mod = din("b_mod", [1, 6 * D])
    g_pre_mix = din("g_pre_mix", [16, 128])
    g_post_mix = din("g_post_mix", [1, D])
    w_in = din("w_in", [D, 6192])
    b_ig = din("b_igate", [1, 8])
    b_fg = din("b_fgate", [1, 8])
    conv_w = din("conv_w", [144, 128])
    conv_b = din("conv_b", [16, 128])
    dt_bias = din("dt_bias", [1, 32])
    a_log = din("a_log", [1, 32])
    d_skip = din("d_skip", [1, 16])
    g_ml = din("g_mlstm_norm", [1, 1024])
    g_sn = din("g_ssd_norm", [1, 1024])
    w_out = din("w_out", [D, D])
    g_pre_mlp = din("g_pre_mlp", [16, 128])
    g_post_mlp = din("g_post_mlp", [1, D])
    w_mlp_in = din("w_mlp_in", [D, 4 * D])
    w_mlp_out = din("w_mlp_out", [4 * D, D])
    yp = dout("yp", [512, D])
    ys = dout("ys", [512, D])
    o_c = dout("o_c", [2, 2, 4, 128, 256])
    o_n = dout("o_n", [2, 2, 4, 128])
    o_m = dout("o_m", [2, 2, 4])
    o_s = dout("o_s", [2, 2, 1024, 128])
    modscr = dscr("modscr", [2, 6 * D])
    stg = dscr("stg", [16, 128, 3072], BF16)
    x1s = dscr("x1s", [1024, D])
    catTs = dscr("catTs", [8, 128, KT * 128], BF16)
    Es = dscr("Es", [4, 128, ESZ])
    B_modscr = Buf("modscr")
    B_stg = [Buf("stg%d" % i) for i in range(16)]
    B_x1s = [Buf("x1s%d" % i) for i in range(8)]
    B_cats = [Buf("cats%d" % i) for i in range(8)]
    B_Es = [Buf("Es%d" % i) for i in range(4)]

    ES = contextlib.ExitStack()
    with ES:
        tcount = [0]

        def T(name, shape, dt=F32, es=ES):
            tcount[0] += 1
            return es.enter_context(nc.sbuf_tensor("%s_%d" % (name, tcount[0]), list(shape), dt))

        pbig = ES.enter_context(nc.psum_tensor("pbig", [128, 2048], F32))
        PB = [Buf("pb%d" % i, excl=True) for i in range(4)]
        ptr = [ES.enter_context(nc.psum_tensor("ptr%d" % i, [128, 512], F32)) for i in range(2)]
        PT = [Buf("pt%d" % i, excl=True) for i in range(2)]
        psm = [ES.enter_context(nc.psum_tensor("psm%d" % i, [128, 512], F32)) for i in range(2)]
        PS = [Buf("ps%d" % i, excl=True) for i in range(2)]
        rot = {"tr": 0, "sm": 0, "big": 0}

        def next_tr():
            i = rot["tr"] % 2
            rot["tr"] += 1
            return ptr[i], PT[i]

        def next_sm():
            i = rot["sm"] % 2
            rot["sm"] += 1
            return psm[i], PS[i]

        def next_big():
            i = rot["big"] % 4
            rot["big"] += 1
            return pbig[:, i * 512:(i + 1) * 512], PB[i]

        def next_big_m(M):
            i = 2 * M.pair + (M.brot[0] % 2)
            M.brot[0] += 1
            return pbig[:, i * 512:(i + 1) * 512], PB[i]

        B_c = Buf("consts")
        ident = T("ident", [128, 128])
        identb = T("identb", [128, 128], BF16)
        LT = T("LT", [128, 128])
        UT = T("UT", [128, 128])
        NM = [T("negf", [128, 128]), T("negb", [128, 128])]
        MK = [LT, UT]
        ones = T("ones", [128, 128])
        flags = T("flags_sb", [128, NFLAG])
        P.pool(lambda e: e.memset(ident[:], 0.0), writes=[B_c])
        P.pool(lambda e: e.affine_select(out=ident[:], in_=ident[:], pattern=[[-1, 128]], compare_op=ALU.not_equal,
                                         fill=1.0, base=0, channel_multiplier=1), reads=[B_c], writes=[B_c])
        P.pool(lambda e: e.memset(ones[:], 1.0), writes=[B_c])
        P.pool(lambda e: e.memset(LT[:], 1.0), writes=[B_c])
        P.pool(lambda e: e.affine_select(out=LT[:], in_=LT[:], pattern=[[1, 128]], compare_op=ALU.is_ge,
                                         fill=0.0, base=0, channel_multiplier=-1), reads=[B_c], writes=[B_c])
        P.pool(lambda e: e.memset(UT[:], 1.0), writes=[B_c])
        P.pool(lambda e: e.affine_select(out=UT[:], in_=UT[:], pattern=[[-1, 128]], compare_op=ALU.is_ge,
                                         fill=0.0, base=0, channel_multiplier=1), reads=[B_c], writes=[B_c])
        P.dve(lambda e: e.tensor_copy(out=identb[:], in_=ident[:]), reads=[B_c], writes=[B_c])
        for dd in range(2):
            P.dve(lambda e, dd=dd: e.tensor_scalar(out=NM[dd][:], in0=MK[dd][:], scalar1=-1.0, scalar2=-NEG,
                                                   op0=ALU.add, op1=ALU.mult), reads=[B_c], writes=[B_c])
        P.dma(lambda e: e.dma_start(out=flags[:], in_=flags_d[0:1, :].to_broadcast([128, NFLAG])), writes=[B_c])

        rowp = T("rowp", [128, 96])
        for (src, off, n) in ((b_ig, R_BIG, 8), (b_fg, R_BFG, 8), (dt_bias, R_DTB, 32), (a_log, R_ALOG, 32), (d_skip, R_DSK, 16)):
            P.dma(lambda e, src=src, off=off, n=n: e.dma_start(out=rowp[:, off:off + n], in_=src[0:1, :].to_broadcast([128, n])),
                  writes=[B_c])
        arow = T("arow", [128, 32])
        P.act(lambda e: e.activation(out=arow[:], in_=rowp[:, R_ALOG:R_ALOG + 32], func=AF.Exp), reads=[B_c], writes=[B_c])
        P.dve(lambda e: e.tensor_scalar(out=arow[:], in0=arow[:], scalar1=-1.0, scalar2=None, op0=ALU.mult), reads=[B_c], writes=[B_c])

        colp = T("colp", [128, 192])
        ld = T("ldtmp", [128, 128])
        B_ld = Buf("ld")

        def col_load(src, nrows, off, r0=0):
            P.dma(lambda e: e.dma_start(out=ld[0:nrows, :], in_=src[r0:r0 + nrows, :]), writes=[B_ld])
            pt, pb = next_sm()
            P.pe(lambda e: e.transpose(pt[:, 0:nrows], ld[0:nrows, :], ident[0:nrows, 0:nrows]), reads=[B_ld, B_c], writes=[pb])
            P.dve(lambda e: e.tensor_copy(out=colp[:, off:off + nrows], in_=pt[:, 0:nrows]), reads=[pb], writes=[B_c])

        col_load(g_pre_mix, 16, C_GPRE)
        col_load(g_pre_mlp, 16, C_GMLP)
        col_load(conv_w, 128, C_CW)
        col_load(conv_w, 16, C_CW + 128, r0=128)
        col_load(conv_b, 16, C_CB)

        def cw(ct, kk):
            c0 = C_CW + kk * 16 + ct
            return colp[:, c0:c0 + 1]

        def cbias(ct):
            return colp[:, C_CB + ct:C_CB + ct + 1]

        wg = T("wg", [128, KT, 48])
        P.dma(lambda e: e.dma_start(out=wg[:, :, 0:16], in_=w_in[:, 3072:3088].rearrange("(kt p) c -> p kt c", p=128)), writes=[B_c])
        P.dma(lambda e: e.dma_start(out=wg[:, :, 16:48], in_=w_in[:, 6160:6192].rearrange("(kt p) c -> p kt c", p=128)), writes=[B_c])

        wgm = T("wgm", [128, 2, KT, 48])
        gbB = T("gbB", [128, 2, 48])
        B_wgm = Buf("wgm")
        csT = T("csT", [128, 32], BF16)
        B_cv = Buf("cv")
        modT = T("modT", [128, 2, 96])
        B_modT = Buf("modT")
        modc = T("modc", [128, 2, 4, 16])
        B_modc = Buf("modc")
        stat = T("stat", [128, 64])
        B_stat = Buf("stat")
        sq_junk = T("sqjunk", [128, D], BF16)
        B_junk = Buf("junk")
        gtmp = T("gtmp", [128, 128])
        B_gt = Buf("gtmp")
        amx8 = T("amx8", [8, 128])
        B_amx = Buf("amx8")

        def alloc_inproj(es, nbuf=2):
            I = NS()
            I.xt = [T("xt%d" % i, [128, D], es=es) for i in range(2)]
            I.B_xt = [Buf("xt0"), Buf("xt1")]
            I.xrot = [0]
            I.uTf_flat = T("uTf", [128, KT * 128], es=es)
            I.uTf = I.uTf_flat[:].rearrange("p (k t) -> p k t", k=KT)
            I.B_uTf = Buf("uTf")
            I.wbuf = [T("wbuf%d" % i, [128, KT, 512], BF16, es=es) for i in range(nbuf)]
            I.B_w = [Buf("wbuf%d" % i) for i in range(nbuf)]
            I.wrot = [0]
            return I

        def load_wblock(I, src_ap):
            i = I.wrot[0] % len(I.wbuf)
            I.wrot[0] += 1
            ncols = src_ap.shape[2]
            nk = src_ap.shape[1]
            wt, wb = I.wbuf[i], I.B_w[i]
            P.dma(lambda e: e.dma_start(out=wt[:, 0:nk, 0:ncols], in_=src_ap), writes=[wb], eng="pool")
            return wt, wb

        def win_cols(c0, n):
            return w_in[:, c0:c0 + n].rearrange("(kt p) c -> p kt c", p=128)

        def rstd_of(ss_ap, out_ap, n):
            P.dve(lambda e: e.tensor_scalar(out=out_ap, in0=ss_ap, scalar1=1.0 / n, scalar2=EPS, op0=ALU.mult, op1=ALU.add),
                  reads=[B_stat], writes=[B_stat])
            P.act(lambda e: e.activation(out=out_ap, in_=out_ap, func=AF.Sqrt), reads=[B_stat], writes=[B_stat])
            P.dve(lambda e: e.reciprocal(out=out_ap, in_=out_ap), reads=[B_stat], writes=[B_stat])

        def uT_s1(I, src_rows_ap):
            i = I.xrot[0] % 2
            I.xrot[0] += 1
            xt, bx = I.xt[i], I.B_xt[i]
            c0 = 40 + 2 * i
            P.dma(lambda e: e.dma_start(out=xt[:], in_=src_rows_ap), writes=[bx])
            P.act(lambda e: e.activation(out=sq_junk[:], in_=xt[:], func=AF.Square, accum_out=stat[:, c0:c0 + 1]),
                  reads=[bx], writes=[B_junk, B_stat])
            rstd_of(stat[:, c0:c0 + 1], stat[:, c0 + 1:c0 + 2], D)
            P.act(lambda e: e.activation(out=xt[:], in_=xt[:], func=AF.Copy, scale=stat[:, c0 + 1:c0 + 2]), reads=[bx, B_stat], writes=[bx])
            return xt, bx

        def uT_s2(I, h, ci, which, uT, tcol, B_u, want_f32=False):
            xn, bxn = h
            uTf, B_uTf = I.uTf, I.B_uTf
            for g4 in range(4):
                pt, pb = next_tr()
                for j in range(4):
                    kt = g4 * 4 + j
                    P.pe(lambda e, kt=kt, j=j, pt=pt: e.transpose(pt[:, j * 128:(j + 1) * 128], xn[:, kt * 128:(kt + 1) * 128], ident[:]),
                         reads=[bxn, B_c], writes=[pb])
                for j in range(4):
                    kt = g4 * 4 + j
                    a_ap = modc[:, ci, which, kt:kt + 1]
                    s_ap = modc[:, ci, which + 1, kt:kt + 1]
                    P.dve(lambda e, kt=kt, j=j, pt=pt, a_ap=a_ap, s_ap=s_ap: e.tensor_scalar(
                        out=uT[:, kt, tcol:tcol + 128], in0=pt[:, j * 128:(j + 1) * 128], scalar1=a_ap, scalar2=s_ap,
                        op0=ALU.mult, op1=ALU.add), reads=[pb, B_modc], writes=[B_u])
                if want_f32:
                    P.act(lambda e, g4=g4, pt=pt: e.copy(out=uTf[:, g4 * 4:g4 * 4 + 4, :], in_=pt[:, 0:512].rearrange("p (k t) -> p k t", k=4)),
                          reads=[pb], writes=[B_uTf])

        def make_uT(I, src_rows_ap, ci, which, uT, tcol, B_u, want_f32=False, src_sb=None, B_src=None):
            if src_sb is None:
                i = I.xrot[0] % 2
                I.xrot[0] += 1
                xt, bx = I.xt[i], I.B_xt[i]
                P.dma(lambda e: e.dma_start(out=xt[:], in_=src_rows_ap), writes=[bx])
            else:
                xt, bx = src_sb, B_src
            P.act(lambda e: e.activation(out=sq_junk[:], in_=xt[:], func=AF.Square, accum_out=stat[:, 0:1]),
                  reads=[bx], writes=[B_junk, B_stat])
            rstd_of(stat[:, 0:1], stat[:, 1:2], D)
            if src_sb is None:
                xn, bxn = xt, bx
            else:
                i2 = I.xrot[0] % 2
                I.xrot[0] += 1
                xn, bxn = I.xt[i2], I.B_xt[i2]
            P.act(lambda e: e.activation(out=xn[:], in_=xt[:], func=AF.Copy, scale=stat[:, 1:2]), reads=[bx, B_stat], writes=[bxn])
            uTf, B_uTf = I.uTf, I.B_uTf
            for g4 in range(4):
                pt, pb = next_tr()
                for j in range(4):
                    kt = g4 * 4 + j
                    P.pe(lambda e, kt=kt, j=j, pt=pt: e.transpose(pt[:, j * 128:(j + 1) * 128], xn[:, kt * 128:(kt + 1) * 128], ident[:]),
                         reads=[bxn, B_c], writes=[pb])
                for j in range(4):
                    kt = g4 * 4 + j
                    a_ap = modc[:, ci, which, kt:kt + 1]
                    s_ap = modc[:, ci, which + 1, kt:kt + 1]
                    if True:
                        P.dve(lambda e, kt=kt, j=j, pt=pt, a_ap=a_ap, s_ap=s_ap: e.tensor_scalar(
                            out=uT[:, kt, tcol:tcol + 128], in0=pt[:, j * 128:(j + 1) * 128], scalar1=a_ap, scalar2=s_ap,
                            op0=ALU.mult, op1=ALU.add), reads=[pb, B_modc], writes=[B_u])
                    else:
                        P.act(lambda e, kt=kt, j=j, pt=pt, a_ap=a_ap, s_ap=s_ap: e.activation(
                            out=uT[:, kt, tcol:tcol + 128], in_=pt[:, j * 128:(j + 1) * 128], func=AF.Identity, bias=s_ap, scale=a_ap),
                            reads=[pb, B_modc], writes=[B_u])
                if want_f32:
                    P.act(lambda e, g4=g4, pt=pt: e.copy(out=uTf[:, g4 * 4:g4 * 4 + 4, :], in_=pt[:, 0:512].rearrange("p (k t) -> p k t", k=4)),
                          reads=[pb], writes=[B_uTf])

        def gate_preact(I, ci, pgall, c, B_pg):
            uTf, B_uTf = I.uTf, I.B_uTf
            pg, pgb = next_sm()
            for kt in range(KT):
                P.pe(lambda e, kt=kt: e.matmul(pg[:, 0:48], lhsT=uTf[:, kt, :], rhs=wgm[:, ci, kt, :], start=(kt == 0), stop=(kt == KT - 1)),
                     reads=[B_uTf, B_wgm], writes=[pgb])
            P.dve(lambda e: e.tensor_tensor(out=pgall[:, c, :], in0=pg[:, 0:48], in1=gbB[:, ci, :], op=ALU.add),
                  reads=[pgb, B_wgm], writes=[B_pg])

        def prep_scalars_batched(nch, pgall, B_pg, scall, B_sc_list, es, gt_flat, B_g):
            gt = gt_flat[:, 0:nch * 128].rearrange("p (c k) -> p c k", k=128)
            amxb = T("amxb", [128, 128], es=es)
            B_am = Buf("amxb")
            n8 = nch * 8

            def S(off, n):
                return scall[:, 0:nch, off:off + n]
            P.act(lambda e: e.activation(out=gt[:, :, 72:80], in_=pgall[:, 0:nch, 8:16], func=AF.Exp, scale=-1.0), reads=[B_pg], writes=[B_g])
            P.act(lambda e: e.activation(out=gt[:, :, 72:80], in_=gt[:, :, 72:80], func=AF.Ln, bias=1.0, scale=1.0), reads=[B_g], writes=[B_g])
            P.dve(lambda e: e.tensor_scalar(out=gt[:, :, 0:8], in0=gt[:, :, 72:80], scalar1=-1.0, scalar2=None, op0=ALU.mult), reads=[B_g], writes=[B_g])
            P.act(lambda e: e.activation(out=gt[:, :, 40:72], in_=pgall[:, 0:nch, 16:48], func=AF.Exp), reads=[B_pg], writes=[B_g])
            P.act(lambda e: e.activation(out=S(SC_DT, 32), in_=gt[:, :, 40:72], func=AF.Ln, bias=1.0, scale=1.0), reads=[B_g], writes=B_sc_list)
            P.dve(lambda e: e.tensor_tensor(out=gt[:, :, 8:40], in0=S(SC_DT, 32), in1=arow[:].unsqueeze(1).to_broadcast([128, nch, 32]), op=ALU.mult),
                  reads=B_sc_list + [B_c], writes=[B_g])
            p1, p1b = next_sm()
            p2, p2b = next_tr()
            p3, p3b = next_tr()
            p1v = p1[:, 0:nch * 16].rearrange("p (c k) -> p c k", k=16)
            p2v = p2[:, 0:nch * 32].rearrange("p (c k) -> p c k", k=32)
            p3v = p3[:, 0:nch * 32].rearrange("p (c k) -> p c k", k=32)
            P.pe(lambda e: e.matmul(p1v[:, :, 0:4], lhsT=LT[:], rhs=gt[:, :, 0:4], start=True, stop=True), reads=[B_g, B_c], writes=[p1b])
            P.pe(lambda e: e.matmul(p1v[:, :, 4:8], lhsT=UT[:], rhs=gt[:, :, 4:8], start=True, stop=True), reads=[B_g, B_c], writes=[p1b])
            P.pe(lambda e: e.matmul(p1v[:, :, 8:16], lhsT=ones[:], rhs=gt[:, :, 0:8], start=True, stop=True), reads=[B_g, B_c], writes=[p1b])
            P.pe(lambda e: e.matmul(p2v[:, :, 0:16], lhsT=LT[:], rhs=gt[:, :, 8:24], start=True, stop=True), reads=[B_g, B_c], writes=[p2b])
            P.pe(lambda e: e.matmul(p2v[:, :, 16:32], lhsT=UT[:], rhs=gt[:, :, 24:40], start=True, stop=True), reads=[B_g, B_c], writes=[p2b])
            P.pe(lambda e: e.matmul(p3v, lhsT=ones[:], rhs=gt[:, :, 8:40], start=True, stop=True), reads=[B_g, B_c], writes=[p3b])
            P.dve(lambda e: e.tensor_copy(out=S(SC_B, 16), in_=p1v), reads=[p1b], writes=B_sc_list)
            P.dve(lambda e: e.tensor_copy(out=S(SC_CS, 32), in_=p2v), reads=[p2b], writes=B_sc_list)
            P.dve(lambda e: e.tensor_scalar(out=S(SC_NCS, 32), in0=p2v, scalar1=-1.0, scalar2=None, op0=ALU.mult), reads=[p2b], writes=B_sc_list)
            P.dve(lambda e: e.tensor_tensor(out=S(SC_A, 8), in0=pgall[:, 0:nch, 0:8], in1=S(SC_B, 8), op=ALU.subtract),
                  reads=[B_pg] + B_sc_list, writes=B_sc_list)
            P.act(lambda e: e.activation(out=S(SC_ECS, 32), in_=S(SC_CS, 32), func=AF.Exp), reads=B_sc_list, writes=B_sc_list)
            P.act(lambda e: e.activation(out=S(SC_ETOT, 32), in_=p3v, func=AF.Exp), reads=[p3b], writes=B_sc_list)
            P.dve(lambda e: e.tensor_tensor(out=gt[:, :, 40:72], in0=p3v, in1=S(SC_CS, 32), op=ALU.subtract), reads=[p3b] + B_sc_list, writes=[B_g])
            P.act(lambda e: e.activation(out=gt[:, :, 40:72], in_=gt[:, :, 40:72], func=AF.Exp), reads=[B_g], writes=[B_g])
            P.dve(lambda e: e.tensor_tensor(out=S(SC_WK, 32), in0=gt[:, :, 40:72], in1=S(SC_DT, 32), op=ALU.mult), reads=[B_g] + B_sc_list, writes=B_sc_list)
            P.dve(lambda e: e.tensor_copy(out=gt[:, :, 80:88], in_=S(SC_A, 8)), reads=B_sc_list, writes=[B_g])
            P.dve(lambda e: e.tensor_copy(out=amxb[:, 0:n8].rearrange("p (c k) -> p c k", k=8), in_=gt[:, :, 80:88]), reads=[B_g], writes=[B_am])
            pa, pab = next_sm()
            P.pe(lambda e: e.transpose(pa[0:n8, 0:128], amxb[:, 0:n8], ident[:]), reads=[B_am, B_c], writes=[pab])
            P.dve(lambda e: e.tensor_reduce(out=gtmp[0:n8, 120:121], in_=pa[0:n8, 0:128], axis=AX.X, op=ALU.max), reads=[pab], writes=[B_gt])
            P.dve(lambda e: e.tensor_copy(out=amxb[0:n8, :], in_=gtmp[0:n8, 120:121].to_broadcast([n8, 128])), reads=[B_gt, B_am], writes=[B_am])
            pb2, pb2b = next_sm()
            P.pe(lambda e: e.matmul(pb2[:, 0:n8], lhsT=amxb[0:n8, :], rhs=ident[0:n8, 0:n8], start=True, stop=True), reads=[B_am, B_c], writes=[pb2b])
            P.dve(lambda e: e.tensor_copy(out=S(SC_AMX, 8), in_=pb2[:, 0:n8].rearrange("p (c k) -> p c k", k=8)), reads=[pb2b], writes=B_sc_list)

        def conv_apply(center_in, center_out, taps, ct, B_pre, B_acc, acc_flat, dst, B_dst, do_silu=True):
            P.act(lambda e: e.activation(out=center_out, in_=center_in, func=AF.Identity, scale=cw(ct, 4), bias=cbias(ct)),
                  reads=[B_pre, B_c], writes=[B_acc])
            for (ov, iv, kk) in taps:
                P.dve(lambda e, ov=ov, iv=iv, kk=kk: e.scalar_tensor_tensor(out=ov, in0=iv, scalar=cw(ct, kk), in1=ov, op0=ALU.mult, op1=ALU.add),
                      reads=[B_pre, B_acc, B_c], writes=[B_acc])
            if do_silu:
                P.act(lambda e: e.activation(out=dst, in_=acc_flat, func=AF.Silu), reads=[B_acc], writes=[B_dst])

        def grid_taps(pre_flat, acc_flat, nrows):
            pv = pre_flat.rearrange("p (r c) -> p r c", c=64)
            av = acc_flat.rearrange("p (r c) -> p r c", c=64)
            taps = []
            for di in range(3):
                for dj in range(3):
                    if di == 1 and dj == 1:
                        continue
                    if dj == 0:
                        oc, ic = slice(1, 64), slice(0, 63)
                    elif dj == 1:
                        oc, ic = slice(0, 64), slice(0, 64)
                    else:
                        oc, ic = slice(0, 63), slice(1, 64)
                    taps.append((av[:, :, oc], pv[:, di:di + nrows, ic], di * 3 + dj))
            return pv[:, 1:1 + nrows, :], av[:, :, :], taps

        def seq_taps(pre_flat, acc_flat):
            pv = pre_flat.rearrange("p (s t) -> p s t", s=2)
            av = acc_flat.rearrange("p (s t) -> p s t", s=2)
            taps = [(av[:, :, 1:256], pv[:, :, 0:255], 3), (av[:, :, 0:255], pv[:, :, 1:256], 5)]
            return pv, av, taps

        mix_count = [0]

        def alloc_mixer(es):
            M = NS()
            M.pair = mix_count[0] % 2
            mix_count[0] += 1
            M.brot = [0]
            M.chs = T("chs", [128, 32], es=es); M.B_chs = Buf("chs")
            M.wvb = T("wvb", [128, 4], BF16, es=es); M.B_wvb = Buf("wvb")
            M.vp = T("vp", [128, 4, 256], BF16, es=es); M.B_vp = Buf("vp")
            M.Sm = T("Sm", [128, 4, 128], BF16, es=es); M.B_Sm = Buf("Sm")
            M.htmp = T("htmp", [128, 1024], es=es); M.B_ht = Buf("htmp")
            M.argt2 = [T("argt%d" % i, [128, 512], es=es) for i in range(2)]; M.B_arg2 = [Buf("argt0"), Buf("argt1")]
            M.Wt = [T("Wt%d" % i, [128, 4, 128], BF16, es=es) for i in range(2)]
            M.B_Wt = [Buf("Wt0"), Buf("Wt1")]
            M.GTs = T("GTs", [128, 4, 128], es=es); M.B_GT = Buf("GTs")
            M.xw = T("xw", [128, 1024], BF16, es=es); M.B_xw = Buf("xw")
            M.qs = T("qs", [128, 4, 128], BF16, es=es); M.B_qs = Buf("qs")
            return M

        def new_state(name, es, bf=True):
            s = NS()
            s.Cn = T(name + "_Cn", [128, 4, 256], es=es)
            s.Nn = T(name + "_Nn", [128, 4], es=es)
            s.m = T(name + "_m", [128, 4], es=es)
            s.Sf = T(name + "_Sf", [128, 1024], es=es)
            if bf:
                s.Cb = T(name + "_Cb", [128, 4, 256], BF16, es=es)
                s.Nb = T(name + "_Nb", [128, 4], BF16, es=es)
                s.Sb = T(name + "_Sb", [128, 1024], BF16, es=es)
            s.B = Buf(name)
            s.Bm = s.B
            return s

        def state_zero(s, bf=True):
            P.pool(lambda e: e.memset(s.Cn[:], 0.0), writes=[s.B])
            P.pool(lambda e: e.memset(s.Nn[:], 0.0), writes=[s.B])
            P.pool(lambda e: e.memset(s.m[:], 0.0), writes=[s.B])
            P.pool(lambda e: e.memset(s.Sf[:], 0.0), writes=[s.B])
            if bf:
                P.pool(lambda e: e.memset(s.Cb[:], 0.0), writes=[s.B])
                P.pool(lambda e: e.memset(s.Nb[:], 0.0), writes=[s.B])
                P.pool(lambda e: e.memset(s.Sb[:], 0.0), writes=[s.B])

        def state_refresh_bf(s):
            P.act(lambda e: e.copy(out=s.Cb[:], in_=s.Cn[:]), reads=[s.B], writes=[s.B])
            P.act(lambda e: e.copy(out=s.Nb[:], in_=s.Nn[:]), reads=[s.B], writes=[s.B])
            P.act(lambda e: e.copy(out=s.Sb[:], in_=s.Sf[:]), reads=[s.B], writes=[s.B])

        def state_macc(dst, src, fcol):
            f_ap = flags[:, fcol:fcol + 1]
            for a, b_ in ((dst.Cn, src.Cn), (dst.Nn, src.Nn), (dst.m, src.m), (dst.Sf, src.Sf)):
                P.dve(lambda e, a=a, b_=b_: e.scalar_tensor_tensor(out=a[:], in0=b_[:], scalar=f_ap, in1=a[:], op0=ALU.mult, op1=ALU.add),
                      reads=[src.B, dst.B, B_c], writes=[dst.B])

        def state_store_scr(s, i):
            P.dma(lambda e: e.dma_start(out=Es[i, :, 0:1024], in_=s.Cn[:].rearrange("p h e -> p (h e)")), reads=[s.B], writes=[B_Es[i]])
            P.dma(lambda e: e.dma_start(out=Es[i, :, 1024:1028], in_=s.Nn[:]), reads=[s.B], writes=[B_Es[i]])
            P.dma(lambda e: e.dma_start(out=Es[i, :, 1028:1032], in_=s.m[:]), reads=[s.B], writes=[B_Es[i]])
            P.dma(lambda e: e.dma_start(out=Es[i, :, 1032:2056], in_=s.Sf[:]), reads=[s.B], writes=[B_Es[i]])

        def state_load_scr(s, i):
            P.dma(lambda e: e.dma_start(out=s.Cn[:].rearrange("p h e -> p (h e)"), in_=Es[i, :, 0:1024]), reads=[B_Es[i]], writes=[s.B])
            P.dma(lambda e: e.dma_start(out=s.Nn[:], in_=Es[i, :, 1024:1028]), reads=[B_Es[i]], writes=[s.B])
            P.dma(lambda e: e.dma_start(out=s.m[:], in_=Es[i, :, 1028:1032]), reads=[B_Es[i]], writes=[s.B])
            P.dma(lambda e: e.dma_start(out=s.Sf[:], in_=Es[i, :, 1032:2056]), reads=[B_Es[i]], writes=[s.B])
            state_refresh_bf(s)

        def mlstm_chain_scalars(M, sc, B_sc, d, st, qT, B_qk, tsl):
            chs, B_chs = M.chs, M.B_chs
            wvb, B_wvb = M.wvb, M.B_wvb
            o4 = 4 * d
            P.dve(lambda e: e.tensor_tensor(out=chs[:, 0:4], in0=st.m[:], in1=sc[:, SC_AMX + o4:SC_AMX + o4 + 4], op=ALU.max),
                  reads=[st.Bm, B_sc], writes=[B_chs])
            P.dve(lambda e: e.tensor_tensor(out=chs[:, 4:8], in0=sc[:, SC_A + o4:SC_A + o4 + 4], in1=chs[:, 0:4], op=ALU.subtract),
                  reads=[B_sc, B_chs], writes=[B_chs])
            P.dve(lambda e: e.scalar_tensor_tensor(out=chs[:, 8:12], in0=sc[:, SC_B + o4:SC_B + o4 + 4], scalar=-1.0, in1=chs[:, 0:4],
                                                   op0=ALU.mult, op1=ALU.subtract), reads=[B_sc, B_chs], writes=[B_chs])
            P.dve(lambda e: e.tensor_tensor(out=chs[:, 12:16], in0=st.m[:], in1=chs[:, 0:4], op=ALU.subtract),
                  reads=[st.Bm, B_chs], writes=[B_chs])
            P.act(lambda e: e.activation(out=chs[:, 4:16], in_=chs[:, 4:16], func=AF.Exp), reads=[B_chs], writes=[B_chs])
            P.dve(lambda e: e.tensor_copy(out=wvb[:], in_=chs[:, 4:8]), reads=[B_chs], writes=[B_wvb])
            P.dve(lambda e: e.tensor_tensor(out=st.m[:], in0=sc[:, SC_BT + o4:SC_BT + o4 + 4], in1=chs[:, 0:4], op=ALU.add),
                  reads=[B_sc, B_chs, st.Bm], writes=[st.Bm])
            for h in range(4):
                P.dve(lambda e, h=h: e.tensor_scalar(out=M.qs[:, h, :], in0=qT[:, h, tsl], scalar1=chs[:, 12 + h:13 + h], scalar2=None, op0=ALU.mult),
                      reads=[B_qk, B_chs], writes=[M.B_qs])

        def make_vp(M, v_ap, B_v):
            vp, chs = M.vp, M.chs
            for h in range(4):
                P.act(lambda e, h=h: e.activation(out=vp[:, h, :], in_=v_ap[:, h, :], func=AF.Copy, scale=chs[:, 4 + h:5 + h]),
                      reads=[B_v, M.B_chs], writes=[M.B_vp])

        def mlstm_update(M, st, k_tm, B_k):
            vp, wvb, chs = M.vp, M.wvb, M.chs
            pcs = [next_big_m(M), next_big_m(M)]
            for half in range(2):
                po0, pob0 = pcs[half]
                for hh in range(2):
                    h = half * 2 + hh
                    P.pe(lambda e, h=h, hh=hh, po0=po0: e.matmul(po0[:, hh * 256:(hh + 1) * 256], lhsT=k_tm[:, h, :], rhs=vp[:, h, :], start=True, stop=True),
                         reads=[B_k, M.B_vp], writes=[pob0])
            pn, pnb = next_sm()
            for h in range(4):
                P.pe(lambda e, h=h: e.matmul(pn[:, h:h + 1], lhsT=k_tm[:, h, :], rhs=wvb[:, h:h + 1], start=True, stop=True),
                     reads=[B_k, M.B_wvb], writes=[pnb])
            for h in range(4):
                po0, pob0 = pcs[h // 2]
                P.dve(lambda e, h=h, po0=po0: e.scalar_tensor_tensor(out=st.Cn[:, h, :], in0=st.Cn[:, h, :], scalar=chs[:, 12 + h:13 + h],
                                                                    in1=po0[:, (h % 2) * 256:(h % 2 + 1) * 256], op0=ALU.mult, op1=ALU.add),
                      reads=[st.B, M.B_chs, pob0], writes=[st.B])
            P.dve(lambda e: e.tensor_tensor(out=st.Nn[:], in0=st.Nn[:], in1=chs[:, 12:16], op=ALU.mult), reads=[st.B, M.B_chs], writes=[st.B])
            P.dve(lambda e: e.tensor_tensor(out=st.Nn[:], in0=pn[:, 0:4], in1=st.Nn[:], op=ALU.add), reads=[pnb, st.B], writes=[st.B])
            P.act(lambda e: e.copy(out=st.Cb[:], in_=st.Cn[:]), reads=[st.B], writes=[st.B])
            P.act(lambda e: e.copy(out=st.Nb[:], in_=st.Nn[:]), reads=[st.B], writes=[st.B])

        def mlstm_out(M, d, st, qT, kT, B_qk, tsl, hs_ap, B_hs, first):
            chs, B_chs, Sm, vp, wvb, htmp = M.chs, M.B_chs, M.Sm, M.vp, M.wvb, M.htmp
            pS, pSb = next_tr()
            for h in range(4):
                P.pe(lambda e, h=h: e.matmul(pS[:, h * 128:(h + 1) * 128], lhsT=kT[:, h, tsl], rhs=qT[:, h, tsl], start=True, stop=True),
                     reads=[B_qk], writes=[pSb])
            P.dve(lambda e: e.tensor_tensor(out=Sm[:], in0=pS[:, 0:512].rearrange("p (h t) -> p h t", h=4),
                                            in1=MK[d][:].unsqueeze(1).to_broadcast([128, 4, 128]), op=ALU.mult),
                  reads=[pSb, B_c], writes=[M.B_Sm])
            pbs = []
            for half in range(2):
                po0, pob0 = next_big_m(M)
                pbs.append((po0, pob0))
                for hh in range(2):
                    h = half * 2 + hh
                    P.pe(lambda e, h=h, hh=hh, po0=po0: e.matmul(po0[:, hh * 256:(hh + 1) * 256], lhsT=M.qs[:, h, :], rhs=st.Cb[:, h, :], start=True, stop=False),
                         reads=[M.B_qs, st.B], writes=[pob0])
                    P.pe(lambda e, h=h, hh=hh, po0=po0: e.matmul(po0[:, hh * 256:(hh + 1) * 256], lhsT=Sm[:, h, :], rhs=vp[:, h, :], start=False, stop=True),
                         reads=[M.B_Sm, M.B_vp], writes=[pob0])
            pn, pnb = next_sm()
            for h in range(4):
                P.pe(lambda e, h=h: e.matmul(pn[:, h:h + 1], lhsT=M.qs[:, h, :], rhs=st.Nb[:, h:h + 1], start=True, stop=False),
                     reads=[M.B_qs, st.B], writes=[pnb])
                P.pe(lambda e, h=h: e.matmul(pn[:, h:h + 1], lhsT=Sm[:, h, :], rhs=wvb[:, h:h + 1], start=False, stop=True),
                     reads=[M.B_Sm, M.B_wvb], writes=[pnb])
            P.dve(lambda e: e.tensor_copy(out=chs[:, 24:28], in_=pn[:, 0:4]), reads=[pnb], writes=[B_chs])
            P.dve(lambda e: e.scalar_tensor_tensor(out=chs[:, 16:20], in0=chs[:, 24:28], scalar=-1.0, in1=chs[:, 24:28], op0=ALU.mult, op1=ALU.max),
                  reads=[B_chs], writes=[B_chs])
            P.dve(lambda e: e.tensor_tensor(out=chs[:, 16:20], in0=chs[:, 16:20], in1=chs[:, 8:12], op=ALU.max), reads=[B_chs], writes=[B_chs])
            P.dve(lambda e: e.reciprocal(out=chs[:, 20:24], in_=chs[:, 16:20]), reads=[B_chs], writes=[B_chs])
            for half in range(2):
                po0, pob0 = pbs[half]
                dst = hs_ap[:, half * 512:(half + 1) * 512].rearrange("p (h e) -> p h e", h=2)
                rdb = chs[:, 20 + half * 2:22 + half * 2].unsqueeze(2).to_broadcast([128, 2, 256])
                if first:
                    P.dve(lambda e, po0=po0, dst=dst, rdb=rdb: e.tensor_tensor(out=dst, in0=po0.rearrange("p (h e) -> p h e", h=2), in1=rdb, op=ALU.mult),
                          reads=[pob0, B_chs], writes=[B_hs])
                else:
                    tmp = htmp[:, half * 512:(half + 1) * 512].rearrange("p (h e) -> p h e", h=2)
                    P.dve(lambda e, po0=po0, tmp=tmp, rdb=rdb: e.tensor_tensor(out=tmp, in0=po0.rearrange("p (h e) -> p h e", h=2), in1=rdb, op=ALU.mult),
                          reads=[pob0, B_chs], writes=[M.B_ht])
                    P.pool(lambda e, dst=dst, tmp=tmp: e.tensor_tensor(out=dst, in0=dst, in1=tmp, op=ALU.add), reads=[M.B_ht, B_hs], writes=[B_hs])

        def ssd_update(M, sc, B_sc, d, st, x_tm_ap, B_x, Btm, B_B):
            xw = M.xw
            o16 = 16 * d
            P.pool(lambda e: e.tensor_tensor(out=xw[:].rearrange("p (j q) -> p j q", j=16), in0=x_tm_ap.rearrange("p (j q) -> p j q", j=16),
                                             in1=sc[:, SC_WK + o16:SC_WK + o16 + 16].unsqueeze(2).to_broadcast([128, 16, 64]), op=ALU.mult),
                   reads=[B_x, B_sc], writes=[M.B_xw])
            P.pool(lambda e: e.tensor_tensor(out=st.Sf[:].rearrange("p (j q) -> p j q", j=16), in0=st.Sf[:].rearrange("p (j q) -> p j q", j=16),
                                             in1=sc[:, SC_ETOT + o16:SC_ETOT + o16 + 16].unsqueeze(2).to_broadcast([128, 16, 64]), op=ALU.mult),
                   reads=[st.B, B_sc], writes=[st.B])
            for half in range(2):
                po0, pob0 = next_big_m(M)
                for gg in range(2):
                    g = half * 2 + gg
                    P.pe(lambda e, g=g, gg=gg, po0=po0: e.matmul(po0[:, gg * 256:(gg + 1) * 256], lhsT=Btm[:, g, :], rhs=xw[:, g * 256:(g + 1) * 256], start=True, stop=True),
                         reads=[B_B, M.B_xw], writes=[pob0])
                P.dve(lambda e, half=half, po0=po0: e.tensor_tensor(out=st.Sf[:, half * 512:(half + 1) * 512], in0=po0,
                                                                    in1=st.Sf[:, half * 512:(half + 1) * 512], op=ALU.add),
                      reads=[pob0, st.B], writes=[st.B])
            P.act(lambda e: e.copy(out=st.Sb[:], in_=st.Sf[:]), reads=[st.B], writes=[st.B])

        def ssd_gt(M, BT, CT, B_bc, tsl):
            GTs = M.GTs
            pG, pGb = next_tr()
            for g in range(4):
                P.pe(lambda e, g=g: e.matmul(pG[:, g * 128:(g + 1) * 128], lhsT=BT[:, g, tsl], rhs=CT[:, g, tsl], start=True, stop=True),
                     reads=[B_bc], writes=[pGb])
            P.act(lambda e: e.copy(out=GTs[:], in_=pG[:, 0:512].rearrange("p (g t) -> p g t", g=4)), reads=[pGb], writes=[M.B_GT])

        def ssd_out(M, sc, B_sc, d, st, CT, B_bc, tsl, x_tm_ap, B_x, ys_ap, B_ys, first):
            Wt, B_Wt, GTs, htmp = M.Wt, M.B_Wt, M.GTs, M.htmp
            o16 = 16 * d

            def stage0(g):
                argt, B_arg = M.argt2[g % 2], M.B_arg2[g % 2]
                wi = g % 2
                pC, pCb = next_tr()
                for jj in range(4):
                    col = SC_CS + o16 + g * 4 + jj
                    P.pe(lambda e, jj=jj, col=col: e.matmul(pC[:, jj * 128:(jj + 1) * 128], lhsT=sc[:, col:col + 1].to_broadcast([128, 128]),
                                                           rhs=ident[:], start=True, stop=True), reads=[B_sc, B_c], writes=[pCb])
                P.dve(lambda e: e.tensor_tensor(out=argt[:].rearrange("p (j t) -> p j t", j=4), in0=pC[:, 0:512].rearrange("p (j t) -> p j t", j=4),
                                                in1=NM[d][:].unsqueeze(1).to_broadcast([128, 4, 128]), op=ALU.add),
                      reads=[pCb, B_c], writes=[B_arg])
                for jj in range(4):
                    col = SC_NCS + o16 + g * 4 + jj
                    P.act(lambda e, jj=jj, col=col: e.activation(out=argt[:, jj * 128:(jj + 1) * 128], in_=argt[:, jj * 128:(jj + 1) * 128], func=AF.Exp,
                                                                bias=sc[:, col:col + 1], scale=1.0), reads=[B_arg, B_sc], writes=[B_arg])

            def stage0b(g):
                argt, B_arg = M.argt2[g % 2], M.B_arg2[g % 2]
                wi = g % 2
                for jj in range(4):
                    col = SC_DT + o16 + g * 4 + jj
                    P.dve(lambda e, jj=jj, col=col: e.scalar_tensor_tensor(out=Wt[wi][:, jj, :], in0=argt[:, jj * 128:(jj + 1) * 128],
                                                                          scalar=sc[:, col:col + 1], in1=GTs[:, g, :], op0=ALU.mult, op1=ALU.mult),
                          reads=[B_arg, B_sc, M.B_GT], writes=[B_Wt[wi]])

            def stage1(g, po0, pob0):
                wi = g % 2
                for jj in range(4):
                    j = g * 4 + jj
                    c0 = (j % 8) * 64
                    P.pe(lambda e, jj=jj, j=j, c0=c0: e.matmul(po0[:, c0:c0 + 64], lhsT=Wt[wi][:, jj, :], rhs=x_tm_ap[:, j * 64:(j + 1) * 64],
                                                             start=True, stop=True), reads=[B_Wt[wi], B_x], writes=[pob0])

            def finish(half, po0, pob0):
                po2, pob2 = next_big_m(M)
                for gg in range(2):
                    g = half * 2 + gg
                    P.pe(lambda e, g=g, gg=gg: e.matmul(po2[:, gg * 256:(gg + 1) * 256], lhsT=CT[:, g, tsl], rhs=st.Sb[:, g * 256:(g + 1) * 256],
                                                        start=True, stop=True), reads=[B_bc, st.B], writes=[pob2])
                tmp = htmp[:, half * 512:(half + 1) * 512]
                ecs = sc[:, SC_ECS + o16 + half * 8:SC_ECS + o16 + half * 8 + 8]
                P.dve(lambda e: e.tensor_tensor(
                    out=tmp.rearrange("p (j q) -> p j q", j=8), in0=po2.rearrange("p (j q) -> p j q", j=8),
                    in1=ecs.unsqueeze(2).to_broadcast([128, 8, 64]), op=ALU.mult), reads=[pob2, B_sc], writes=[M.B_ht])
                dst = ys_ap[:, half * 512:(half + 1) * 512]
                if first:
                    P.dve(lambda e: e.tensor_tensor(out=dst, in0=po0, in1=tmp, op=ALU.add), reads=[pob0, M.B_ht], writes=[B_ys])
                else:
                    P.dve(lambda e: e.tensor_tensor(out=tmp, in0=po0, in1=tmp, op=ALU.add), reads=[pob0, M.B_ht], writes=[M.B_ht])
                    P.pool(lambda e: e.tensor_tensor(out=dst, in0=dst, in1=tmp, op=ALU.add), reads=[M.B_ht, B_ys], writes=[B_ys])

            pa, pab = next_big_m(M)
            stage0(0)
            yield
            stage0(1)
            stage0b(0)
            yield
            stage1(0, pa, pab)
            stage0(2)
            stage0b(1)
            yield
            stage1(1, pa, pab)
            finish(0, pa, pab)
            yield
            pb_, pbb = next_big_m(M)
            stage0(3)
            stage0b(2)
            yield
            stage1(2, pb_, pbb)
            stage0b(3)
            stage1(3, pb_, pbb)
            finish(1, pb_, pbb)
            yield

        cvt = T("cvt", [32, 128])
        bmr = [T("bmr%d" % i, [2, 512]) for i in range(3)]
        B_bmr = [Buf("bmr%d" % i) for i in range(3)]
        mrow = [T("mrow%d" % i, [2, 512]) for i in range(3)]
        B_mrow = [Buf("mrow%d" % i) for i in range(3)]
        csT3 = csT[:].rearrange("p (c k) -> p c k", c=2)

        def mod_prologue():
            P.dma(lambda e: e.dma_start(out=cvt[:], in_=cv), writes=[B_cv])
            P.act(lambda e: e.activation(out=cvt[:], in_=cvt[:], func=AF.Silu), reads=[B_cv], writes=[B_cv])
            pt, pb = next_sm()
            P.pe(lambda e: e.transpose(pt[:, 0:32], cvt[:], ident[0:32, 0:32]), reads=[B_cv, B_c], writes=[pb])
            P.dve(lambda e: e.tensor_copy(out=csT[:], in_=pt[:, 0:32]), reads=[pb], writes=[B_cv])

        def mod_load(c0, ncols, wt, wb):
            P.dma(lambda e: e.dma_start(out=wt[:, :, 0:ncols], in_=w_mod[:, c0:c0 + ncols].rearrange("(kt p) c -> p kt c", p=128)), writes=[wb], eng="pool")

        mod_cnt = [0]

        def mod_compute(c0, ncols, wt, wb, defer=False, small_psum=False):
            mi = mod_cnt[0] % 3
            mod_cnt[0] += 1
            P.dma(lambda e: e.dma_start(out=bmr[mi][:, 0:ncols], in_=b_mod[0:1, c0:c0 + ncols].to_broadcast([2, ncols])), writes=[B_bmr[mi]])
            po, pob = next_sm() if small_psum else next_big()
            for kt in range(KT):
                P.pe(lambda e, kt=kt: e.matmul(po[0:2, 0:ncols], lhsT=csT3[:, :, kt], rhs=wt[:, kt, 0:ncols], start=(kt == 0), stop=(kt == KT - 1)),
                     reads=[B_cv, wb], writes=[pob])
            P.dve(lambda e: e.tensor_tensor(out=mrow[mi][:, 0:ncols], in0=po[0:2, 0:ncols], in1=bmr[mi][:, 0:ncols], op=ALU.add),
                  reads=[pob, B_bmr[mi]], writes=[B_mrow[mi]])
            def store():
                P.dma(lambda e: e.dma_start(out=modscr[:, c0:c0 + ncols], in_=mrow[mi][:, 0:ncols]),
                      reads=[B_mrow[mi]], writes=[B_modscr])
            if defer:
                return store
            store()
            return None

        def mod_cols(j0, nj):
            for ci in range(2):
                P.dma(lambda e, ci=ci: e.dma_start(out=ld[0:nj, :], in_=modscr[ci:ci + 1, j0 * 128:(j0 + nj) * 128].rearrange("o (j p) -> (o j) p", p=128)),
                      reads=[B_modscr], writes=[B_ld])
                pt, pb = next_sm()
                P.pe(lambda e, pt=pt: e.transpose(pt[:, 0:nj], ld[0:nj, :], ident[0:nj, 0:nj]), reads=[B_ld, B_c], writes=[pb])
                P.dve(lambda e, pt=pt, ci=ci: e.tensor_copy(out=modT[:, ci, j0:j0 + nj], in_=pt[:, 0:nj]), reads=[pb], writes=[B_modT])

        def mod_consts(dst, gcol, sc_off, sh_off):
            for ci in range(2):
                P.dve(lambda e, ci=ci: e.scalar_tensor_tensor(
                    out=modc[:, ci, dst, :], in0=modT[:, ci, sc_off:sc_off + 16], scalar=1.0, in1=colp[:, gcol:gcol + 16],
                    op0=ALU.add, op1=ALU.mult), reads=[B_modT, B_c], writes=[B_modc])
                P.dve(lambda e, ci=ci: e.tensor_copy(out=modc[:, ci, dst + 1, :], in_=modT[:, ci, sh_off:sh_off + 16]),
                      reads=[B_modT], writes=[B_modc])

        def gate_consts():
            for ci in range(2):
                P.dve(lambda e, ci=ci: e.tensor_tensor(out=wgm[:, ci, :, :], in0=wg[:], in1=modc[:, ci, 0, :].unsqueeze(2).to_broadcast([128, KT, 48]), op=ALU.mult),
                      reads=[B_c, B_modc], writes=[B_wgm])
                pg, pgb = next_sm()
                for kt in range(KT):
                    P.pe(lambda e, kt=kt, ci=ci, pg=pg: e.matmul(pg[:, 0:48], lhsT=modc[:, ci, 1, kt:kt + 1].to_broadcast([128, 128]), rhs=wg[:, kt, :],
                                                                 start=(kt == 0), stop=(kt == KT - 1)), reads=[B_modc, B_c], writes=[pgb])
                P.dve(lambda e, ci=ci, pg=pg: e.tensor_tensor(out=gbB[:, ci, :], in0=pg[:, 0:48], in1=rowp[:, 0:48], op=ALU.add),
                      reads=[pgb, B_c], writes=[B_wgm])

        with contextlib.ExitStack() as es1:
            scall = T("scall", [128, 16, NSC], es=es1)
            B_scall = [Buf("scall%d" % i) for i in range(16)]
            with contextlib.ExitStack() as es:
                I = alloc_inproj(es)
                uTs = T("uTs", [128, KT, 2048], BF16, es=es)
                B_uTs = [Buf("uTs%d" % i) for i in range(16)]
                pre2 = [T("pre_s%d" % i, [128, 34 * 64], es=es) for i in range(2)]
                B_pre2 = [Buf("pre_s0"), Buf("pre_s1")]
                cacc = T("cacc_s", [128, 2048], es=es)
                B_cacc = Buf("cacc_s")
                cvo = T("cvo_s", [128, 2048], BF16, es=es)
                B_cvo = Buf("cvo_s")
                ost = [T("ost%d" % i, [128, 512], BF16, es=es) for i in range(2)]
                B_ost = [Buf("ost0"), Buf("ost1")]
                for i_ in range(2):
                    P.pool(lambda e, i_=i_: e.memset(pre2[i_][:], 0.0), writes=[B_pre2[i_]])
                pgall_flat = T("pgall", [128, 2048], es=es)
                pgall = pgall_flat[:, 0:768].rearrange("p (c k) -> p c k", k=48)
                B_pg = Buf("pgall")
                mod_prologue()
                for blk_i in range(8):
                    wi_ = I.wrot[0] % 2
                    I.wrot[0] += 1
                    mod_load(blk_i * 512, 512, I.wbuf[wi_], I.B_w[wi_])
                    mod_compute(blk_i * 512, 512, I.wbuf[wi_], I.B_w[wi_])
                mod_cols(0, 32)
                mod_consts(0, C_GPRE, 16, 0)
                gate_consts()
                check("A0")
                hnd = uT_s1(I, xf[0:128, :])
                for c in range(16):
                    nxt = uT_s1(I, xf[(c + 1) * 128:(c + 2) * 128, :]) if c < 15 else None
                    uT_s2(I, hnd, 1, 0, uTs, c * 128, B_uTs[c], want_f32=True)
                    gate_preact(I, 1, pgall, c, B_pg)
                    hnd = nxt
                check("A1")
                prep_scalars_batched(16, pgall, B_pg, scall, B_scall, es, cacc, B_cacc)
                check("B1a1")
                orot = [0]
                for (c0, dst0) in ((512, 0), (1024, 512), (1536, 1024)):
                    wt, wb = load_wblock(I, win_cols(c0, 512))
                    for c in range(16):
                        po, pob = next_big()
                        for kt in range(KT):
                            P.pe(lambda e, kt=kt, c=c, wt=wt, po=po: e.matmul(po, lhsT=uTs[:, kt, c * 128:(c + 1) * 128], rhs=wt[:, kt, :],
                                                                          start=(kt == 0), stop=(kt == KT - 1)), reads=[B_uTs[c], wb], writes=[pob])
                        oi = orot[0] % 2
                        orot[0] += 1
                        P.act(lambda e, oi=oi, po=po: e.copy(out=ost[oi][:], in_=po), reads=[pob], writes=[B_ost[oi]])
                        P.dma(lambda e, oi=oi, c=c, dst0=dst0: e.dma_start(out=stg[c, :, dst0:dst0 + 512], in_=ost[oi][:]), reads=[B_ost[oi]], writes=[B_stg[c]])
                check("B1a2")
                cacc2 = [cacc, pgall_flat]
                B_cacc2 = [B_cacc, B_pg]
                tap_sets = [grid_taps(pre2[i_][:], cacc2[i_][:, 0:2048], 32) for i_ in range(2)]
                cvo2 = [cvo, I.uTf_flat[:].bitcast(BF16)[:, 0:2048]]
                B_cvo2 = [B_cvo, I.B_uTf]

                def fm_front(ct, cc, wt, wb):
                    pre, B_pre = pre2[ct % 2], B_pre2[ct % 2]
                    cen_in, cen_out, taps = tap_sets[ct % 2]
                    for g in range(4):
                        po, pob = next_big()
                        for kt in range(KT):
                            P.pe(lambda e, kt=kt, g=g, po=po: e.matmul(po, lhsT=wt[:, kt, cc * 128:(cc + 1) * 128],
                                                                      rhs=uTs[:, kt, g * 512:(g + 1) * 512], start=(kt == 0), stop=(kt == KT - 1)),
                                 reads=B_uTs[4 * g:4 * g + 4] + [wb], writes=[pob])
                        P.act(lambda e, g=g, po=po: e.copy(out=pre[:, 64 + 512 * g:64 + 512 * (g + 1)], in_=po), reads=[pob], writes=[B_pre])
                    conv_apply(cen_in, cen_out, taps, ct, B_pre, B_cacc2[ct % 2], cacc2[ct % 2][:, 0:2048], None, None, do_silu=False)

                def fm_back(ct):
                    cv_, bcv = cvo2[ct % 2], B_cvo2[ct % 2]
                    P.act(lambda e: e.activation(out=cv_[:], in_=cacc2[ct % 2][:, 0:2048], func=AF.Silu), reads=[B_cacc2[ct % 2]], writes=[bcv])
                    for c4 in range(4):
                        pt, pb = next_tr()
                        ptb = pt[:].bitcast(BF16)
                        for j in range(4):
                            c = c4 * 4 + j
                            P.pe(lambda e, c=c, j=j, ptb=ptb: e.transpose(ptb[:, j * 128:(j + 1) * 128], cv_[:, c * 128:(c + 1) * 128], identb[:]),
                                 reads=[bcv, B_c], writes=[pb])
                        oi = orot[0] % 2
                        orot[0] += 1
                        P.act(lambda e, oi=oi, ptb=ptb: e.copy(out=ost[oi][:], in_=ptb[:, 0:512]), reads=[pb], writes=[B_ost[oi]])
                        dcol = 1536 + ct * 128
                        P.dma(lambda e, oi=oi, c4=c4, dcol=dcol: e.dma_start(out=stg[c4 * 4:c4 * 4 + 4, :, dcol:dcol + 128].rearrange("c p n -> p c n"),
                                                                            in_=ost[oi][:].rearrange("p (c n) -> p c n", c=4)),
                              reads=[B_ost[oi]], writes=B_stg[c4 * 4:c4 * 4 + 4])

                pending = None
                for blk_i in range(3):
                    wt, wb = load_wblock(I, win_cols(4112 + blk_i * 512, 512))
                    for cc in range(4):
                        ct = blk_i * 4 + cc
                        fm_front(ct, cc, wt, wb)
                        if pending is not None:
                            fm_back(pending)
                        pending = ct
                fm_back(pending)
            P.barrier()
            check("B1a")
            with contextlib.ExitStack() as es:
                Ms = [alloc_mixer(es), alloc_mixer(es)]
                sgt = [[T("sgt%d_%d" % (d, i), [128, 3072], BF16, es=es) for i in range(2)] for d in range(2)]
                B_sgt = [[Buf("sgt"), Buf("sgt")] for d in range(2)]
                stcs = [new_state("stc%d" % d, es, bf=False) for d in range(2)]
                E = [new_state("E%d" % i, es, bf=False) for i in range(4)]
                for s_ in E:
                    state_zero(s_, bf=False)
                ld2 = [T("ld2_%d" % i, [128, 128], es=es) for i in range(2)]
                B_ld2 = [Buf("ld2a"), Buf("ld2b")]

                refall = T("refall", [128, 16, 8], es=es)
                cdall = T("cdall", [128, 16, 8], es=es)
                wvall = T("wvall", [128, 16, 8], es=es)
                wvball = T("wvball", [128, 16, 8], BF16, es=es)
                B_rf = [Buf("rf0"), Buf("rf1")]

                def sp_chain(d):
                    stc = stcs[d]
                    M = Ms[d]
                    d4, d16 = 4 * d, 16 * d
                    order = list(range(16)) if d == 0 else list(range(15, -1, -1))
                    P.dma(lambda e: e.dma_start(out=stc.m[:], in_=st_m[0:1, d4:d4 + 4].to_broadcast([128, 4])), writes=[stc.B])
                    for n_, c in enumerate(order):
                        if (d == 0 and c % 4 == 0) or (d == 1 and c % 4 == 1):
                            fcol = (0 if d == 0 else 32) + c
                            P.dve(lambda e, fcol=fcol: e.scalar_tensor_tensor(out=E[2 * d].m[:], in0=stc.m[:], scalar=flags[:, fcol:fcol + 1], in1=E[2 * d].m[:],
                                                                              op0=ALU.mult, op1=ALU.add), reads=[stc.B, E[2 * d].B, B_c], writes=[E[2 * d].B])
                        if (d == 0 and c % 4 == 2) or (d == 1 and c % 4 == 3):
                            fcol = (16 if d == 0 else 48) + c
                            P.dve(lambda e, fcol=fcol: e.scalar_tensor_tensor(out=E[2 * d + 1].m[:], in0=stc.m[:], scalar=flags[:, fcol:fcol + 1], in1=E[2 * d + 1].m[:],
                                                                              op0=ALU.mult, op1=ALU.add), reads=[stc.B, E[2 * d + 1].B, B_c], writes=[E[2 * d + 1].B])
                        if n_ == 15:
                            break
                        P.dve(lambda e, c=c: e.tensor_tensor(out=refall[:, c, d4:d4 + 4], in0=stc.m[:], in1=scall[:, c, SC_AMX + d4:SC_AMX + d4 + 4], op=ALU.max),
                              reads=[stc.B, B_scall[c]], writes=[B_rf[d]])
                        P.dve(lambda e, c=c: e.tensor_tensor(out=cdall[:, c, d4:d4 + 4], in0=stc.m[:], in1=refall[:, c, d4:d4 + 4], op=ALU.subtract),
                              reads=[stc.B, B_rf[d]], writes=[B_rf[d]])
                        P.dve(lambda e, c=c: e.tensor_tensor(out=stc.m[:], in0=scall[:, c, SC_BT + d4:SC_BT + d4 + 4], in1=refall[:, c, d4:d4 + 4], op=ALU.add),
                              reads=[B_scall[c], B_rf[d]], writes=[stc.B])
                        if n_ % 4 == 3:
                            yield
                    cs_ = slice(0, 15) if d == 0 else slice(1, 16)
                    P.dve(lambda e: e.tensor_tensor(out=wvall[:, cs_, d4:d4 + 4], in0=scall[:, cs_, SC_A + d4:SC_A + d4 + 4], in1=refall[:, cs_, d4:d4 + 4], op=ALU.subtract),
                          reads=B_scall + [B_rf[d]], writes=[B_rf[d]])
                    P.act(lambda e: e.activation(out=wvall[:, cs_, d4:d4 + 4], in_=wvall[:, cs_, d4:d4 + 4], func=AF.Exp), reads=[B_rf[d]], writes=[B_rf[d]])
                    P.act(lambda e: e.activation(out=cdall[:, cs_, d4:d4 + 4], in_=cdall[:, cs_, d4:d4 + 4], func=AF.Exp), reads=[B_rf[d]], writes=[B_rf[d]])
                    P.dve(lambda e: e.tensor_copy(out=wvball[:, cs_, d4:d4 + 4], in_=wvall[:, cs_, d4:d4 + 4]), reads=[B_rf[d]], writes=[B_rf[d]])
                    yield
                    P.dma(lambda e: e.dma_start(out=stc.Cn[:], in_=st_c[d].rearrange("h d e -> d h e")), writes=[stc.B])
                    P.dma(lambda e: e.dma_start(out=ld2[d][0:4, :], in_=st_n[d]), writes=[B_ld2[d]])
                    pt, pb = next_sm()
                    P.pe(lambda e, pt=pt: e.transpose(pt[:, 0:4], ld2[d][0:4, :], ident[0:4, 0:4]), reads=[B_ld2[d], B_c], writes=[pb])
                    P.dve(lambda e, pt=pt: e.tensor_copy(out=stc.Nn[:], in_=pt[:, 0:4]), reads=[pb], writes=[stc.B])
                    for a in range(8):
                        P.dma(lambda e, a=a: e.dma_start(out=ld2[d][:], in_=st_s[d, a * 128:(a + 1) * 128, :]), writes=[B_ld2[d]])
                        pt, pb = next_sm()
                        P.pe(lambda e, pt=pt: e.transpose(pt[:, 0:128], ld2[d][:], ident[:]), reads=[B_ld2[d], B_c], writes=[pb])
                        P.dve(lambda e, pt=pt, a=a: e.tensor_copy(out=stc.Sf[:, a * 128:(a + 1) * 128], in_=pt[:, 0:128]), reads=[pb], writes=[stc.B])
                        if a % 4 == 3:
                            yield
                    for n_, c in enumerate(order):
                        si = n_ % 2
                        sg, bsg = sgt[d][si], B_sgt[d][si]

                        def macc3(Et, fcol):
                            f_ap = flags[:, fcol:fcol + 1]
                            for a_, b_ in ((Et.Cn, stc.Cn), (Et.Nn, stc.Nn), (Et.Sf, stc.Sf)):
                                P.dve(lambda e, a_=a_, b_=b_: e.scalar_tensor_tensor(out=a_[:], in0=b_[:], scalar=f_ap, in1=a_[:], op0=ALU.mult, op1=ALU.add),
                                      reads=[stc.B, Et.B, B_c], writes=[Et.B])
                        if d == 0 and c % 4 == 0:
                            macc3(E[0], c)
                        if d == 0 and c % 4 == 2:
                            macc3(E[1], 16 + c)
                        if d == 1 and c % 4 == 1:
                            macc3(E[2], 32 + c)
                        if d == 1 and c % 4 == 3:
                            macc3(E[3], 48 + c)
                        if n_ == 15:
                            break
                        P.dma(lambda e, sg=sg, c=c: e.dma_start(out=sg[:], in_=stg[c]), reads=[B_stg[c]], writes=[bsg])
                        vp, xw = M.vp, M.xw
                        k_tm = sg[:, 0:512].rearrange("p (h d) -> p h d", h=4)
                        Btm = sg[:, 2560:3072].rearrange("p (g n) -> p g n", g=4)
                        for h in range(4):
                            P.act(lambda e, sg=sg, c=c, h=h: e.activation(out=vp[:, h, :], in_=sg[:, 512 + h * 256:512 + (h + 1) * 256], func=AF.Copy,
                                                                         scale=wvall[:, c, d4 + h:d4 + h + 1]),
                                  reads=[bsg, B_rf[d]], writes=[M.B_vp])
                        P.pool(lambda e, sg=sg, c=c: e.tensor_tensor(out=xw[:].rearrange("p (j q) -> p j q", j=16), in0=sg[:, 1536:2560].rearrange("p (j q) -> p j q", j=16),
                                                                     in1=scall[:, c, SC_WK + d16:SC_WK + d16 + 16].unsqueeze(2).to_broadcast([128, 16, 64]), op=ALU.mult),
                               reads=[bsg, B_scall[c]], writes=[M.B_xw])
                        pcs = [next_big_m(M), next_big_m(M)]
                        for half in range(2):
                            po0, pob0 = pcs[half]
                            for hh in range(2):
                                h = half * 2 + hh
                                P.pe(lambda e, h=h, hh=hh, po0=po0, k_tm=k_tm: e.matmul(po0[:, hh * 256:(hh + 1) * 256], lhsT=k_tm[:, h, :], rhs=vp[:, h, :], start=True, stop=True),
                                     reads=[bsg, M.B_vp], writes=[pob0])
                        pn, pnb = next_sm()
                        for h in range(4):
                            P.pe(lambda e, h=h, c=c, k_tm=k_tm, pn=pn: e.matmul(pn[:, h:h + 1], lhsT=k_tm[:, h, :], rhs=wvball[:, c, d4 + h:d4 + h + 1], start=True, stop=True),
                                 reads=[bsg, B_rf[d]], writes=[pnb])
                        for h in range(4):
                            po0, pob0 = pcs[h // 2]
                            P.dve(lambda e, h=h, c=c, po0=po0: e.scalar_tensor_tensor(out=stc.Cn[:, h, :], in0=stc.Cn[:, h, :], scalar=cdall[:, c, d4 + h:d4 + h + 1],
                                                                                   in1=po0[:, (h % 2) * 256:(h % 2 + 1) * 256], op0=ALU.mult, op1=ALU.add),
                                  reads=[stc.B, B_rf[d], pob0], writes=[stc.B])
                        P.dve(lambda e, c=c: e.tensor_tensor(out=stc.Nn[:], in0=stc.Nn[:], in1=cdall[:, c, d4:d4 + 4], op=ALU.mult), reads=[stc.B, B_rf[d]], writes=[stc.B])
                        P.dve(lambda e, pn=pn: e.tensor_tensor(out=stc.Nn[:], in0=pn[:, 0:4], in1=stc.Nn[:], op=ALU.add), reads=[pnb, stc.B], writes=[stc.B])
                        yield
                        pss = [next_big_m(M), next_big_m(M)]
                        for half in range(2):
                            po0, pob0 = pss[half]
                            for gg in range(2):
                                g = half * 2 + gg
                                P.pe(lambda e, g=g, gg=gg, po0=po0, Btm=Btm: e.matmul(po0[:, gg * 256:(gg + 1) * 256], lhsT=Btm[:, g, :], rhs=xw[:, g * 256:(g + 1) * 256], start=True, stop=True),
                                     reads=[bsg, M.B_xw], writes=[pob0])
                        P.pool(lambda e, c=c: e.tensor_tensor(out=stc.Sf[:].rearrange("p (j q) -> p j q", j=16), in0=stc.Sf[:].rearrange("p (j q) -> p j q", j=16),
                                                              in1=scall[:, c, SC_ETOT + d16:SC_ETOT + d16 + 16].unsqueeze(2).to_broadcast([128, 16, 64]), op=ALU.mult),
                               reads=[stc.B, B_scall[c]], writes=[stc.B])
                        for half in range(2):
                            po0, pob0 = pss[half]
                            P.dve(lambda e, half=half, po0=po0: e.tensor_tensor(out=stc.Sf[:, half * 512:(half + 1) * 512], in0=po0,
                                                                              in1=stc.Sf[:, half * 512:(half + 1) * 512], op=ALU.add),
                                  reads=[pob0, stc.B], writes=[stc.B])
                        yield

                gens = [sp_chain(0), sp_chain(1)]
                nstep = 0
                while gens:
                    for g_ in list(gens):
                        try:
                            next(g_)
                        except StopIteration:
                            gens.remove(g_)
                for i in range(4):
                    state_store_scr(E[i], i)
        P.barrier()
        check("B1b")

        mod_k = [16]
        mod_pend = [None]

        def super_unit(su):
            ci = su
            ntok = 512 if su == 0 else 640
            src = xp if su == 0 else xo
            with contextlib.ExitStack() as esu:
                sc4 = T("sc4", [128, 4, NSC], es=esu)
                B_sc4 = [Buf("sc4_%d" % i) for i in range(4)]
                qT = T("qT", [128, 4, 512], BF16, es=esu)
                kTt = T("kT", [128, 4, 512], BF16, es=esu)
                B_qk = Buf("qk")
                k_tm = T("k_tm", [128, 4, 4, 128], BF16, es=esu)
                B_ktm = [Buf("ktm%d" % i) for i in range(4)]
                v_tm = T("v_tm", [128, 4, 1024], BF16, es=esu)
                B_v = [Buf("v%d" % i) for i in range(4)]
                og = T("og", [128, 4, 1024], BF16, es=esu)
                B_og = [Buf("og%d" % i) for i in range(4)]
                sz = T("sz", [128, 4, 1024], BF16, es=esu)
                B_sz = [Buf("sz%d" % i) for i in range(4)]
                BCT = T("BCT", [128, 8, 512], BF16, es=esu)
                B_bct = Buf("BCT")
                x_tm = T("x_tm", [128, 4, 1024], BF16, es=esu)
                B_xtm = [Buf("xtm%d" % i) for i in range(4)]
                B_tm = T("B_tm", [128, 4, 4, 128], BF16, es=esu)
                B_Btm = [Buf("Btm%d" % i) for i in range(4)]
                gml = T("gml", [128, 1024], es=esu)
                gsn = T("gsn", [128, 1024], es=esu)
                B_gg = Buf("gg")
                P.dma(lambda e: e.dma_start(out=gml[:], in_=g_ml[0:1, :].to_broadcast([128, 1024])), writes=[B_gg])
                P.dma(lambda e: e.dma_start(out=gsn[:], in_=g_sn[0:1, :].to_broadcast([128, 1024])), writes=[B_gg])
                with contextlib.ExitStack() as es:
                    I = alloc_inproj(es, nbuf=3)
                    uT = T("uT", [128, KT, ntok], BF16, es=es)
                    B_uT = [Buf("uT%d" % i) for i in range(ntok // 128)]
                    pre = T("pre_u", [128, 640], es=es)
                    B_pre = Buf("pre_u")
                    cacc = T("cacc_u", [128, 512], es=es)
                    B_cacc = Buf("cacc_u")
                    xTt = [T("xTt%d" % i, [128, 512], BF16, es=es) for i in range(2)]
                    B_xTt = [Buf("xTt0"), Buf("xTt1")]
                    sig = T("sigt", [128, 512], es=es)
                    B_sig = Buf("sigt")
                    pg4 = T("pg4", [128, 4, 48], es=es)
                    B_pg4 = Buf("pg4")
                    hnd = uT_s1(I, src[0:128, :])
                    for tt in range(4):
                        if tt < 3:
                            nxt = uT_s1(I, src[(tt + 1) * 128:(tt + 2) * 128, :])
                        else:
                            nxt = uT_s1(I, xh[:, :]) if su == 1 else None
                        uT_s2(I, hnd, ci, 0, uT, tt * 128, B_uT[tt], want_f32=True)
                        gate_preact(I, ci, pg4, tt, B_pg4)
                        hnd = nxt
                    if su == 1:
                        uT_s2(I, hnd, ci, 0, uT, 512, B_uT[4])
                    prep_scalars_batched(4, pg4, B_pg4, sc4, B_sc4, es, cacc, B_cacc)
                    B_own = B_uT[0:4]
                    for (c0, dstT, scale) in ((0, qT, 128.0 ** -0.5), (512, kTt, 1.0)):
                        wt, wb = load_wblock(I, win_cols(c0, 512))
                        for h in range(4):
                            po, pob = next_big()
                            for kt in range(KT):
                                P.pe(lambda e, kt=kt, h=h, wt=wt, po=po: e.matmul(po, lhsT=wt[:, kt, h * 128:(h + 1) * 128], rhs=uT[:, kt, 0:512],
                                                                              start=(kt == 0), stop=(kt == KT - 1)), reads=B_own + [wb], writes=[pob])
                            P.act(lambda e, h=h, po=po, dstT=dstT, scale=scale: e.activation(out=dstT[:, h, :], in_=po, func=AF.Copy, scale=scale),
                                  reads=[pob], writes=[B_qk])
                    for c in range(4):
                        pt, pb = next_tr()
                        ptb = pt[:].bitcast(BF16)
                        for h in range(4):
                            P.pe(lambda e, c=c, h=h, ptb=ptb: e.transpose(ptb[:, h * 128:(h + 1) * 128], kTt[:, h, c * 128:(c + 1) * 128], identb[:]),
                                 reads=[B_qk, B_c], writes=[pb])
                        P.dve(lambda e, c=c, ptb=ptb: e.tensor_copy(out=k_tm[:, c, :, :], in_=ptb[:, 0:512].rearrange("p (h d) -> p h d", h=4)),
                              reads=[pb], writes=[B_ktm[c]])

                    def tm_block(c0, evac):
                        wt, wb = load_wblock(I, win_cols(c0, 512))
                        for c in range(4):
                            po, pob = next_big()
                            for kt in range(KT):
                                P.pe(lambda e, kt=kt, c=c, wt=wt, po=po: e.matmul(po, lhsT=uT[:, kt, c * 128:(c + 1) * 128], rhs=wt[:, kt, :],
                                                                              start=(kt == 0), stop=(kt == KT - 1)), reads=[B_uT[c], wb], writes=[pob])
                            evac(c, po, pob)

                    for half in range(2):
                        def ev_v(c, po, pob, half=half):
                            P.act(lambda e: e.copy(out=v_tm[:, c, half * 512:(half + 1) * 512], in_=po), reads=[pob], writes=[B_v[c]])
                        tm_block(1024 + half * 512, ev_v)
                    for half in range(2):
                        def ev_o(c, po, pob, half=half):
                            P.act(lambda e: e.activation(out=sig[:], in_=po, func=AF.Sigmoid), reads=[pob], writes=[B_sig])
                            P.dve(lambda e: e.tensor_tensor(out=og[:, c, half * 512:(half + 1) * 512], in0=sig[:], in1=gml[:, half * 512:(half + 1) * 512], op=ALU.mult),
                                  reads=[B_sig, B_gg], writes=[B_og[c]])
                        tm_block(2048 + half * 512, ev_o)
                    for half in range(2):
                        def ev_z(c, po, pob, half=half):
                            P.act(lambda e: e.activation(out=sz[:, c, half * 512:(half + 1) * 512], in_=po, func=AF.Silu), reads=[pob], writes=[B_sz[c]])
                        tm_block(3088 + half * 512, ev_z)
                    if su == 0:
                        cen_in, cen_out, taps = seq_taps(pre[:, 0:512], cacc[:])
                    else:
                        cen_in, cen_out, taps = grid_taps(pre[:], cacc[:], 8)
                    cacc_b = [cacc, T("cacc_u2", [128, 512], es=es)]
                    B_cacc_b = [B_cacc, Buf("cacc_u2")]
                    pre_b = [pre, T("pre_u2", [128, 640], es=es)]
                    B_pre_b = [B_pre, Buf("pre_u2")]
                    if su == 0:
                        tapsets = [seq_taps(pre_b[i_][:, 0:512], cacc_b[i_][:]) for i_ in range(2)]
                    else:
                        tapsets = [grid_taps(pre_b[i_][:], cacc_b[i_][:], 8) for i_ in range(2)]

                    def x_front(ct, cc, wt, wb):
                        pr, bpr = pre_b[ct % 2], B_pre_b[ct % 2]
                        cen_in, cen_out, taps = tapsets[ct % 2]
                        po, pob = next_big()
                        for kt in range(KT):
                            P.pe(lambda e, kt=kt, po=po: e.matmul(po, lhsT=wt[:, kt, cc * 128:(cc + 1) * 128], rhs=uT[:, kt, 0:512],
                                                                  start=(kt == 0), stop=(kt == KT - 1)), reads=B_own + [wb], writes=[pob])
                        if su == 0:
                            P.act(lambda e, po=po: e.copy(out=pr[:, 0:512], in_=po), reads=[pob], writes=[bpr])
                        else:
                            P.act(lambda e, po=po: e.copy(out=pr[:, 64:576], in_=po), reads=[pob], writes=[bpr])
                            po2, pob2 = next_big()
                            for kt in range(KT):
                                P.pe(lambda e, kt=kt, po2=po2: e.matmul(po2[:, 0:128], lhsT=wt[:, kt, cc * 128:(cc + 1) * 128], rhs=uT[:, kt, 512:640],
                                                                        start=(kt == 0), stop=(kt == KT - 1)), reads=[B_uT[4], wb], writes=[pob2])
                            P.dve(lambda e, po2=po2: e.tensor_scalar(out=pr[:, 0:64], in0=po2[:, 0:64], scalar1=flags[:, 64:65], scalar2=None, op0=ALU.mult),
                                  reads=[pob2, B_c], writes=[bpr])
                            P.dve(lambda e, po2=po2: e.tensor_scalar(out=pr[:, 576:640], in0=po2[:, 64:128], scalar1=flags[:, 65:66], scalar2=None, op0=ALU.mult),
                                  reads=[pob2, B_c], writes=[bpr])
                        conv_apply(cen_in, cen_out, taps, ct, bpr, B_cacc_b[ct % 2], cacc_b[ct % 2][:], None, None, do_silu=False)

                    def x_back(ct):
                        if ct < 8:
                            P.act(lambda e: e.activation(out=xTt[ct % 2][:], in_=cacc_b[ct % 2][:], func=AF.Silu), reads=[B_cacc_b[ct % 2]], writes=[B_xTt[ct % 2]])
                        else:
                            P.act(lambda e: e.activation(out=BCT[:, ct - 8, :], in_=cacc_b[ct % 2][:], func=AF.Silu), reads=[B_cacc_b[ct % 2]], writes=[B_bct])
                        if ct < 8:
                            xi = ct % 2
                            pt, pb = next_tr()
                            ptb = pt[:].bitcast(BF16)
                            for c in range(4):
                                P.pe(lambda e, c=c, ptb=ptb: e.transpose(ptb[:, c * 128:(c + 1) * 128], xTt[xi][:, c * 128:(c + 1) * 128], identb[:]),
                                     reads=[B_xTt[xi], B_c], writes=[pb])
                            P.act(lambda e, ptb=ptb: e.copy(out=x_tm[:, :, ct * 128:(ct + 1) * 128], in_=ptb[:, 0:512].rearrange("p (c n) -> p c n", c=4)),
                                  reads=[pb], writes=B_xtm)
                        elif ct < 12:
                            g = ct - 8
                            pt, pb = next_tr()
                            ptb = pt[:].bitcast(BF16)
                            for c in range(4):
                                P.pe(lambda e, c=c, ptb=ptb: e.transpose(ptb[:, c * 128:(c + 1) * 128], BCT[:, g, c * 128:(c + 1) * 128], identb[:]),
                                     reads=[B_bct, B_c], writes=[pb])
                            P.act(lambda e, ptb=ptb: e.copy(out=B_tm[:, :, g, :], in_=ptb[:, 0:512].rearrange("p (c n) -> p c n", c=4)),
                                  reads=[pb], writes=B_Btm)

                    pend = None
                    for blk_i in range(4):
                        wt, wb = load_wblock(I, win_cols(4112 + blk_i * 512, 512))
                        for cc in range(4):
                            ct = blk_i * 4 + cc
                            x_front(ct, cc, wt, wb)
                            if pend is not None:
                                x_back(pend)
                            pend = ct
                    x_back(pend)
                P.barrier()
                with contextlib.ExitStack() as es:
                    M = alloc_mixer(es)
                    M2 = alloc_mixer(es)
                    hsum = T("hsum", [128, 2, 1024], es=es)
                    B_hs = [Buf("hs0"), Buf("hs1")]
                    ysum = T("ysum", [128, 2, 1024], es=es)
                    B_ys = [Buf("ys0"), Buf("ys1")]
                    cat0 = T("cat0", [128, D], es=es)
                    cat = [cat0, cat0]
                    B_cat0 = Buf("cat0")
                    B_cat = [B_cat0, B_cat0]
                    cTo0 = T("cTout0", [128, KT * 128], BF16, es=es)
                    cTout = [cTo0, cTo0]
                    B_cTo0 = Buf("cTout0")
                    B_cTout = [B_cTo0, B_cTo0]
                    wmh = [T("wmh%d" % i, [128, KT, 256], BF16, es=es) for i in range(2)]
                    B_wmh = [Buf("wmh0"), Buf("wmh1")]
                    k_end = 32 if su == 1 else 48
                    if mod_k[0] < k_end:
                        mod_load(mod_k[0] * 256, 256, wmh[mod_k[0] % 2], B_wmh[mod_k[0] % 2])

                    def mod_bg():
                        k = mod_k[0]
                        if k >= k_end:
                            return
                        if k + 1 < k_end:
                            mod_load((k + 1) * 256, 256, wmh[(k + 1) % 2], B_wmh[(k + 1) % 2])
                        if mod_pend[0] is not None:
                            mod_pend[0]()
                        mod_pend[0] = mod_compute(k * 256, 256, wmh[k % 2], B_wmh[k % 2], defer=True, small_psum=True)
                        mod_k[0] += 1
                    sf = new_state("sf", es)
                    sb = new_state("sb", es)
                    nt4 = [T("nt4_%d" % i, [4, 128], es=es) for i in range(2)]
                    B_nt4 = [Buf("nt4a"), Buf("nt4b")]
                    sout = [M.htmp, M2.htmp]
                    B_sout = [M.B_ht, M2.B_ht]
                    htmp, B_ht = M.htmp, M.B_ht
                    def chain(Mx, un, d, st):
                        chunks = (2 * un, 2 * un + 1)
                        order = chunks if d == 0 else chunks[::-1]
                        for n_, c in enumerate(order):
                            lc = c - 2 * un
                            tsl = slice(c * 128, (c + 1) * 128)
                            sc = sc4[:, c, :]
                            first = (n_ == 0) if d == 0 else (n_ == 0)
                            first = (d == 0 and lc == 0) or (d == 1 and lc == 1)
                            mlstm_chain_scalars(Mx, sc, B_sc4[c], d, st, qT, B_qk, tsl)
                            yield
                            make_vp(Mx, v_tm[:, c, :].rearrange("p (h e) -> p h e", h=4), B_v[c])
                            mlstm_out(Mx, d, st, qT, kTt, B_qk, tsl, hsum[:, lc, :], B_hs[lc], first)
                            yield
                            ssd_gt(Mx, BCT[:, 0:4, :], BCT[:, 4:8, :], B_bct, tsl)
                            yield from ssd_out(Mx, sc, B_sc4[c], d, st, BCT[:, 4:8, :], B_bct, tsl, x_tm[:, c, :], B_xtm[c], ysum[:, lc, :], B_ys[lc], first)
                            yield
                            if su == 0 or n_ == 0:
                                mlstm_update(Mx, st, k_tm[:, c, :, :], B_ktm[c])
                                yield
                                ssd_update(Mx, sc, B_sc4[c], d, st, x_tm[:, c, :], B_xtm[c], B_tm[:, c, :, :], B_Btm[c])
                                yield
                        if su == 0:
                            seq = un
                            P.dma(lambda e, seq=seq, d=d, st=st: e.dma_start(out=o_c[seq, d].rearrange("h d e -> d h e"), in_=st.Cn[:]), reads=[st.B])
                            pt, pb = next_sm()
                            P.pe(lambda e, pt=pt, st=st: e.transpose(pt[0:4, 0:128], st.Nn[:], ident[:]), reads=[st.B, B_c], writes=[pb])
                            P.dve(lambda e, pt=pt: e.tensor_copy(out=nt4[d][:], in_=pt[0:4, 0:128]), reads=[pb], writes=[B_nt4[d]])
                            P.dma(lambda e, seq=seq, d=d: e.dma_start(out=o_n[seq, d], in_=nt4[d][:]), reads=[B_nt4[d]])
                            P.dma(lambda e, seq=seq, d=d, st=st: e.dma_start(out=o_m[seq, d:d + 1, :], in_=st.m[0:1, :]), reads=[st.B])
                            for a in range(8):
                                pt, pb = next_sm()
                                P.pe(lambda e, a=a, pt=pt, st=st: e.transpose(pt[:, 0:128], st.Sf[:, a * 128:(a + 1) * 128], ident[:]), reads=[st.B, B_c], writes=[pb])
                                P.act(lambda e, a=a, pt=pt: e.copy(out=sout[d][:, a * 128:(a + 1) * 128], in_=pt[:, 0:128]), reads=[pb], writes=[B_sout[d]])
                            P.dma(lambda e, seq=seq, d=d: e.dma_start(out=o_s[seq, d].rearrange("(a p) n -> p a n", p=128),
                                                                     in_=sout[d][:].rearrange("p (a n) -> p a n", a=8)), reads=[B_sout[d]])

                    for un in range(2):
                        chunks = (2 * un, 2 * un + 1)
                        if su == 0:
                            state_zero(sf)
                            state_zero(sb)
                        else:
                            state_load_scr(sf, 0 + un)
                            state_load_scr(sb, 2 + un)
                        gens = [chain(M, un, 0, sf), chain(M2, un, 1, sb)]
                        nstep = 0
                        while gens:
                            for g_ in list(gens):
                                try:
                                    next(g_)
                                except StopIteration:
                                    gens.remove(g_)
                                nstep += 1
                                if nstep % 6 == 0:
                                    mod_bg()
                        for c in chunks:
                            lc = c - 2 * un
                            tt = su * 4 + c
                            hs = hsum[:, lc, :]
                            ct_, B_ct = cat[lc], B_cat[lc]
                            for h in range(4):
                                P.act(lambda e, h=h, hs=hs: e.activation(out=sq_junk[:, 0:256], in_=hs[:, h * 256:(h + 1) * 256], func=AF.Square, accum_out=stat[:, 8 + h:9 + h]),
                                      reads=[B_hs[lc]], writes=[B_junk, B_stat])
                            rstd_of(stat[:, 8:12], stat[:, 12:16], 256)
                            P.dve(lambda e, hs=hs, ct_=ct_: e.tensor_tensor(out=ct_[:, 0:1024].rearrange("p (h e) -> p h e", h=4), in0=hs.rearrange("p (h e) -> p h e", h=4),
                                                                           in1=stat[:, 12:16].unsqueeze(2).to_broadcast([128, 4, 256]), op=ALU.mult),
                                  reads=[B_hs[lc], B_stat], writes=[B_ct])
                            P.dve(lambda e, c=c, ct_=ct_: e.tensor_tensor(out=ct_[:, 0:1024], in0=ct_[:, 0:1024], in1=og[:, c, :], op=ALU.mult), reads=[B_ct, B_og[c]], writes=[B_ct])
                            ysl = ysum[:, lc, :]
                            P.dve(lambda e, c=c: e.tensor_tensor(out=htmp[:].rearrange("p (j q) -> p j q", j=16), in0=x_tm[:, c, :].rearrange("p (j q) -> p j q", j=16),
                                                                 in1=rowp[:, R_DSK:R_DSK + 16].unsqueeze(2).to_broadcast([128, 16, 64]), op=ALU.mult),
                                  reads=[B_xtm[c], B_c], writes=[B_ht])
                            P.dve(lambda e, ysl=ysl: e.tensor_tensor(out=ysl, in0=ysl, in1=htmp[:], op=ALU.add), reads=[B_ys[lc], B_ht], writes=[B_ys[lc]])
                            P.dve(lambda e, ysl=ysl, c=c: e.tensor_tensor(out=ysl, in0=ysl, in1=sz[:, c, :], op=ALU.mult), reads=[B_ys[lc], B_sz[c]], writes=[B_ys[lc]])
                            P.act(lambda e, ysl=ysl: e.activation(out=sq_junk[:, 0:1024], in_=ysl, func=AF.Square, accum_out=stat[:, 16:17]),
                                  reads=[B_ys[lc]], writes=[B_junk, B_stat])
                            rstd_of(stat[:, 16:17], stat[:, 17:18], 1024)
                            P.dve(lambda e, ysl=ysl, ct_=ct_: e.scalar_tensor_tensor(out=ct_[:, 1024:2048], in0=ysl, scalar=stat[:, 17:18], in1=gsn[:], op0=ALU.mult, op1=ALU.mult),
                                  reads=[B_ys[lc], B_stat, B_gg], writes=[B_ct])
                            cTo, B_cTo = cTout[lc], B_cTout[lc]
                            for g4 in range(4):
                                pt, pb = next_tr()
                                for j in range(4):
                                    kt = g4 * 4 + j
                                    P.pe(lambda e, kt=kt, j=j, pt=pt, ct_=ct_: e.transpose(pt[:, j * 128:(j + 1) * 128], ct_[:, kt * 128:(kt + 1) * 128], ident[:]),
                                         reads=[B_ct, B_c], writes=[pb])
                                P.act(lambda e, g4=g4, pt=pt, cTo=cTo: e.copy(out=cTo[:, g4 * 512:(g4 + 1) * 512], in_=pt[:, 0:512]), reads=[pb], writes=[B_cTo])
                            P.dma(lambda e, tt=tt, cTo=cTo: e.dma_start(out=catTs[tt], in_=cTo[:]), reads=[B_cTo], writes=[B_cats[tt]])
                    while mod_k[0] < k_end:
                        mod_bg()
                    if mod_pend[0] is not None:
                        mod_pend[0]()
                        mod_pend[0] = None
            P.barrier()

        super_unit(1)
        check("SU1")
        super_unit(0)
        check("SU0")
        mod_cols(32, 64)
        mod_consts(2, C_GMLP, 64, 48)

        with contextlib.ExitStack() as escd:
            u2T = T("u2T", [128, KT, 1024], BF16, es=escd)
            B_u2T = [Buf("u2T%d" % i) for i in range(8)]
            with contextlib.ExitStack() as es:
                I = NS()
                I.xt = [T("xt%d" % i, [128, D], es=es) for i in range(2)]
                I.B_xt = [Buf("xt0"), Buf("xt1")]
                I.xrot = [0]
                I.uTf, I.B_uTf = None, None
                wo = T("wo", [128, KT, D], BF16, es=es)
                B_wo = Buf("wo")
                G1 = T("G1", [128, 2, D], es=es)
                B_G1 = Buf("G1")
                x1 = [T("x1_%d" % i, [128, D], es=es) for i in range(2)]
                B_x1 = [Buf("x1a"), Buf("x1b")]
                cTt = [T("cTt%d" % i, [128, KT, 128], BF16, es=es) for i in range(2)]
                B_cTt = [Buf("cTt0"), Buf("cTt1")]
                B_wo4 = [Buf("wo%d" % i) for i in range(4)]
                for cbk in range(4):
                    P.dma(lambda e, cbk=cbk: e.dma_start(out=wo[:, :, cbk * 512:(cbk + 1) * 512],
                                                         in_=w_out[:, cbk * 512:(cbk + 1) * 512].rearrange("(kt p) c -> p kt c", p=128)), writes=[B_wo4[cbk]], eng="pool")
                P.dma(lambda e: e.dma_start(out=x1[0][:], in_=g_post_mix[0:1, :].to_broadcast([128, D])), writes=[B_x1[0]])
                for ci in range(2):
                    P.dma(lambda e, ci=ci: e.dma_start(out=G1[:, ci, :], in_=modscr[ci:ci + 1, 2 * D:3 * D].to_broadcast([128, D])), reads=[B_modscr], writes=[B_G1])
                    P.dve(lambda e, ci=ci: e.tensor_tensor(out=G1[:, ci, :], in0=G1[:, ci, :], in1=x1[0][:], op=ALU.mult), reads=[B_G1, B_x1[0]], writes=[B_G1])
                def c_front(tt):
                    ci = 0 if tt < 4 else 1
                    li = tt % 2
                    P.dma(lambda e, li=li, tt=tt: e.dma_start(out=cTt[li][:].rearrange("p k t -> p (k t)"), in_=catTs[tt]), reads=[B_cats[tt]], writes=[B_cTt[li]])
                    for cbk in range(4):
                        for kt in range(KT):
                            P.pe(lambda e, kt=kt, cbk=cbk, li=li: e.matmul(pbig[:, cbk * 512:(cbk + 1) * 512], lhsT=cTt[li][:, kt, :],
                                                                           rhs=wo[:, kt, cbk * 512:(cbk + 1) * 512], start=(kt == 0), stop=(kt == KT - 1)),
                                 reads=[B_cTt[li], B_wo4[cbk]], writes=[PB[cbk]])

                def c_front_b(tt):
                    for cbk in range(4):
                        P.dve(lambda e, cbk=cbk, tt=tt: e.tensor_copy(out=x1[tt % 2][:, cbk * 512:(cbk + 1) * 512], in_=pbig[:, cbk * 512:(cbk + 1) * 512]),
                              reads=[PB[cbk]], writes=[B_x1[tt % 2]])

                def c_back(tt):
                    ci = 0 if tt < 4 else 1
                    xi = tt % 2
                    P.act(lambda e, xi=xi: e.activation(out=sq_junk[:], in_=x1[xi][:], func=AF.Square, accum_out=stat[:, 20:21]), reads=[B_x1[xi]], writes=[B_junk, B_stat])
                    rstd_of(stat[:, 20:21], stat[:, 21:22], D)
                    src_rows = xp[tt * 128:(tt + 1) * 128, :] if tt < 4 else xo[(tt - 4) * 128:(tt - 3) * 128, :]
                    xr, bxr = I.xt[I.xrot[0] % 2], I.B_xt[I.xrot[0] % 2]
                    I.xrot[0] += 1
                    P.dma(lambda e, xr=xr, src_rows=src_rows: e.dma_start(out=xr[:], in_=src_rows), writes=[bxr])
                    P.dve(lambda e, xi=xi, ci=ci: e.scalar_tensor_tensor(out=x1[xi][:], in0=x1[xi][:], scalar=stat[:, 21:22], in1=G1[:, ci, :], op0=ALU.mult, op1=ALU.mult),
                          reads=[B_x1[xi], B_stat, B_G1], writes=[B_x1[xi]])
                    P.dve(lambda e, xi=xi, xr=xr: e.tensor_tensor(out=x1[xi][:], in0=x1[xi][:], in1=xr[:], op=ALU.add), reads=[B_x1[xi], bxr], writes=[B_x1[xi]])
                    P.dma(lambda e, xi=xi, tt=tt: e.dma_start(out=x1s[tt * 128:(tt + 1) * 128, :], in_=x1[xi][:]), reads=[B_x1[xi]], writes=[B_x1s[tt]])
                    make_uT(I, None, ci, 2, u2T, tt * 128, B_u2T[tt], src_sb=x1[xi], B_src=B_x1[xi])

                for tt in range(9):
                    if tt < 8:
                        c_front(tt)
                    if tt >= 1:
                        c_back(tt - 1)
                    if tt < 8:
                        c_front_b(tt)
            P.barrier()
            check("C")

            with contextlib.ExitStack() as es:
                acc = T("acc", [128, 8, D], es=es)
                B_acc = [Buf("acc%d" % i) for i in range(8)]
                with contextlib.ExitStack() as es2:
                    hdn = T("hdn", [128, 4, 1024], BF16, es=es2)
                    B_hdn = Buf("hdn")
                    rl = [T("rl%d" % i, [128, 512], es=es2) for i in range(2)]
                    B_rl = [Buf("rl0"), Buf("rl1")]
                    w1 = [T("w1_%d" % i, [128, KT, 512], BF16, es=es2) for i in range(2)]
                    B_w1 = [Buf("w1a"), Buf("w1b")]
                    w2 = [T("w2_%d" % i, [128, 4, D], BF16, es=es2) for i in range(2)]
                    B_w2 = [Buf("w2a"), Buf("w2b")]
                    rli = [0]
                    for fb in range(16):
                        wi = fb % 2
                        P.dma(lambda e, wi=wi, fb=fb: e.dma_start(out=w1[wi][:], in_=w_mlp_in[:, fb * 512:(fb + 1) * 512].rearrange("(kt p) c -> p kt c", p=128)),
                              writes=[B_w1[wi]], eng="pool")
                        P.dma(lambda e, wi=wi, fb=fb: e.dma_start(out=w2[wi][:], in_=w_mlp_out[fb * 512:(fb + 1) * 512, :].rearrange("(ft p) c -> p ft c", p=128)),
                              writes=[B_w2[wi]], eng="pool")
                        for ft in range(4):
                            for half in range(2):
                                pt, pb = (next_tr() if (ft * 2 + half) % 2 == 0 else next_sm())
                                for kt in range(KT):
                                    P.pe(lambda e, kt=kt, ft=ft, half=half, wi=wi, pt=pt: e.matmul(pt[:], lhsT=w1[wi][:, kt, ft * 128:(ft + 1) * 128],
                                                                                                 rhs=u2T[:, kt, half * 512:(half + 1) * 512], start=(kt == 0), stop=(kt == KT - 1)),
                                         reads=B_u2T[4 * half:4 * half + 4] + [B_w1[wi]], writes=[pb])
                                ri = rli[0] % 2
                                rli[0] += 1
                                P.act(lambda e, ri=ri, pt=pt: e.activation(out=rl[ri][:], in_=pt[:], func=AF.Relu), reads=[pb], writes=[B_rl[ri]])
                                P.pool(lambda e, ri=ri, ft=ft, half=half: e.tensor_tensor(out=hdn[:, ft, half * 512:(half + 1) * 512], in0=rl[ri][:], in1=rl[ri][:], op=ALU.mult),
                                       reads=[B_rl[ri]], writes=[B_hdn])
                        for tt in range(8):
                            for cbk in range(4):
                                for ft in range(4):
                                    P.pe(lambda e, ft=ft, cbk=cbk, tt=tt, wi=wi: e.matmul(pbig[:, cbk * 512:(cbk + 1) * 512], lhsT=hdn[:, ft, tt * 128:(tt + 1) * 128],
                                                                                      rhs=w2[wi][:, ft, cbk * 512:(cbk + 1) * 512], start=(ft == 0), stop=(ft == 3)),
                                         reads=[B_hdn, B_w2[wi]], writes=[PB[cbk]])
                                if fb == 0:
                                    P.dve(lambda e, tt=tt, cbk=cbk: e.tensor_copy(out=acc[:, tt, cbk * 512:(cbk + 1) * 512], in_=pbig[:, cbk * 512:(cbk + 1) * 512]),
                                          reads=[PB[cbk]], writes=[B_acc[tt]])
                                else:
                                    P.dve(lambda e, tt=tt, cbk=cbk: e.tensor_tensor(out=acc[:, tt, cbk * 512:(cbk + 1) * 512], in0=pbig[:, cbk * 512:(cbk + 1) * 512],
                                                                                   in1=acc[:, tt, cbk * 512:(cbk + 1) * 512], op=ALU.add),
                                          reads=[PB[cbk], B_acc[tt]], writes=[B_acc[tt]])
                P.barrier()
                with contextlib.ExitStack() as es2:
                    G2 = T("G2", [128, 2, D], es=es2)
                    B_G2 = Buf("G2")
                    xr2 = [T("xr2_%d" % i, [128, D], es=es2) for i in range(2)]
                    B_xr2 = [Buf("xr2a"), Buf("xr2b")]
                    P.dma(lambda e: e.dma_start(out=xr2[0][:], in_=g_post_mlp[0:1, :].to_broadcast([128, D])), writes=[B_xr2[0]])
                    for ci in range(2):
                        P.dma(lambda e, ci=ci: e.dma_start(out=G2[:, ci, :], in_=modscr[ci:ci + 1, 5 * D:6 * D].to_broadcast([128, D])), reads=[B_modscr], writes=[B_G2])
                        P.dve(lambda e, ci=ci: e.tensor_tensor(out=G2[:, ci, :], in0=G2[:, ci, :], in1=xr2[0][:], op=ALU.mult), reads=[B_G2, B_xr2[0]], writes=[B_G2])
                    for tt in range(8):
                        ci = 0 if tt < 4 else 1
                        P.act(lambda e, tt=tt: e.activation(out=sq_junk[:], in_=acc[:, tt, :], func=AF.Square, accum_out=stat[:, 24:25]), reads=[B_acc[tt]], writes=[B_junk, B_stat])
                        rstd_of(stat[:, 24:25], stat[:, 25:26], D)
                        xi = tt % 2
                        P.dma(lambda e, xi=xi, tt=tt: e.dma_start(out=xr2[xi][:], in_=x1s[tt * 128:(tt + 1) * 128, :]), reads=[B_x1s[tt]], writes=[B_xr2[xi]])
                        P.dve(lambda e, tt=tt, ci=ci: e.scalar_tensor_tensor(out=acc[:, tt, :], in0=acc[:, tt, :], scalar=stat[:, 25:26], in1=G2[:, ci, :], op0=ALU.mult, op1=ALU.mult),
                              reads=[B_acc[tt], B_stat, B_G2], writes=[B_acc[tt]])
                        P.dve(lambda e, tt=tt, xi=xi: e.tensor_tensor(out=acc[:, tt, :], in0=acc[:, tt, :], in1=xr2[xi][:], op=ALU.add), reads=[B_acc[tt], B_xr2[xi]], writes=[B_acc[tt]])
                        dst = yp[tt * 128:(tt + 1) * 128, :] if tt < 4 else ys[(tt - 4) * 128:(tt - 3) * 128, :]
                        P.dma(lambda e, tt=tt, dst=dst: e.dma_start(out=dst, in_=acc[:, tt, :]), reads=[B_acc[tt]])
        P.emit()
    return nc, P


def build_debug(stop_after):
    try:
        return build_program(stop_after)
    except _Stop:
        pass


_CACHE = {}


def _prep_inputs(inp):
    f = lambda a: np.ascontiguousarray(np.asarray(a, dtype=np.float32))
    x_prompt = f(inp["x_prompt"])
    x_sample = f(inp["x_sample"])
    shared = {
        "w_mod": f(inp["w_mod"][0]), "b_mod": f(inp["b_mod"][0]).reshape(1, 6 * D),
        "g_pre_mix": f(inp["g_pre_mix"][0]).reshape(16, 128), "g_post_mix": f(inp["g_post_mix"][0]).reshape(1, D),
        "w_in": f(inp["w_in"][0]), "b_igate": f(inp["b_igate"][0]).reshape(1, 8), "b_fgate": f(inp["b_fgate"][0]).reshape(1, 8),
        "conv_w": f(inp["conv_w"][0]).reshape(144, 128), "conv_b": f(inp["conv_b"][0]).reshape(16, 128),
        "dt_bias": f(inp["dt_bias"][0]).reshape(1, 32), "a_log": f(inp["a_log"][0]).reshape(1, 32),
        "d_skip": f(inp["d_skip"][0]).reshape(1, 16), "g_mlstm_norm": f(inp["g_mlstm_norm"][0]).reshape(1, 1024),
        "g_ssd_norm": f(inp["g_ssd_norm"][0]).reshape(1, 1024), "w_out": f(inp["w_out"][0]),
        "g_pre_mlp": f(inp["g_pre_mlp"][0]).reshape(16, 128), "g_post_mlp": f(inp["g_post_mlp"][0]).reshape(1, D),
        "w_mlp_in": f(inp["w_mlp_in"][0]), "w_mlp_out": f(inp["w_mlp_out"][0]),
    }
    c = f(inp["c"])
    c_ctx = f(inp["c_ctx"])
    maps = []
    for r in range(8):
        b, q = r // 4, r % 4
        xs_b = x_sample[b]
        xh = np.zeros((128, D), np.float32)
        if q > 0:
            xh[0:64] = xs_b[512 * q - 64:512 * q]
        if q < 3:
            xh[64:128] = xs_b[512 * q + 512:512 * q + 576]
        fl = np.zeros((1, NFLAG), np.float32)
        fl[0, 0 + 4 * q] = 1.0
        fl[0, 16 + 4 * q + 2] = 1.0
        fl[0, 32 + 4 * q + 1] = 1.0
        fl[0, 48 + 4 * q + 3] = 1.0
        fl[0, 64] = 1.0 if q > 0 else 0.0
        fl[0, 65] = 1.0 if q < 3 else 0.0
        m = dict(shared)
        m.update({
            "xp": np.ascontiguousarray(x_prompt[2 * r:2 * r + 2].reshape(512, D)),
            "xo": np.ascontiguousarray(xs_b[512 * q:512 * q + 512]),
            "xh": xh,
            "xf": np.ascontiguousarray(xs_b),
            "cv": np.ascontiguousarray(np.stack([c_ctx, c[b]], 0).reshape(32, 128)),
            "st_c": f(inp["state_mlstm_c"][b, 0]),
            "st_n": f(inp["state_mlstm_n"][b, 0]),
            "st_m": f(inp["state_mlstm_m"][b, 0]).reshape(1, 8),
            "st_s": f(inp["state_ssd"][b, 0]).reshape(2, 1024, 128),
            "flags": fl,
        })
        maps.append(m)
    return maps


def kernel(**inp):
    if "nc" not in _CACHE:
        _CACHE["nc"] = build_program()[0]
    nc = _CACHE["nc"]
    maps = _prep_inputs(inp)
    res = run_bass_kernel_spmd(nc, maps, core_ids=list(range(8)))
    y_prompt = np.zeros((16, 256, D), np.float32)
    y_sample = np.zeros((2, 2048, D), np.float32)
    new_c = np.zeros((16, 1, 2, 4, 128, 256), np.float32)
    new_n = np.zeros((16, 1, 2, 4, 128), np.float32)
    new_m = np.zeros((16, 1, 2, 4), np.float32)
    new_s = np.zeros((16, 1, 2, 16, 64, 128), np.float32)
    for r in range(8):
        o = res.results[r]
        b, q = r // 4, r % 4
        y_prompt[2 * r:2 * r + 2] = np.asarray(o["yp"]).reshape(2, 256, D)
        y_sample[b, 512 * q:512 * q + 512] = np.asarray(o["ys"])
        new_c[2 * r:2 * r + 2, 0] = np.asarray(o["o_c"])
        new_n[2 * r:2 * r + 2, 0] = np.asarray(o["o_n"])
        new_m[2 * r:2 * r + 2, 0] = np.asarray(o["o_m"])
        new_s[2 * r:2 * r + 2, 0] = np.asarray(o["o_s"]).reshape(2, 2, 16, 64, 128)
    return (y_prompt, y_sample, new_c, new_n, new_m, new_s)
```

```python
import contextlib
import numpy as np
import concourse.bass as bass
import concourse.mybir as mybir
from concourse.bass_utils import run_bass_kernel_spmd

F32 = mybir.dt.float32
BF16 = mybir.dt.bfloat16
AF = mybir.ActivationFunctionType
ALU = mybir.AluOpType
AX = mybir.AxisListType

D = 2048
KT = 16
EPS = 1e-6
NEG = -30000.0
NFLAG = 80
ESZ = 1024 + 4 + 4 + 1024


class Buf:
    __slots__ = ("name", "last_w", "readers", "excl")

    def __init__(self, name="", excl=False):
        self.name = name
        self.last_w = None
        self.readers = []
        self.excl = excl


class Op:
    __slots__ = ("idx", "eng", "fn", "deps", "is_dma", "sig", "sem", "val", "pre")

    def __init__(self, idx, eng, fn, is_dma):
        self.idx = idx
        self.eng = eng
        self.fn = fn
        self.deps = set()
        self.is_dma = is_dma
        self.sig = False
        self.sem = None
        self.val = 0
        self.pre = None


ENGS = ("pe", "act", "dve", "pool", "sp")
N_DMA_SEMS = 48


class Prog:
    def __init__(self, nc):
        self.nc = nc
        self.ops = []
        self.last_on = {}
        self.dma_since_bar = []

    def add(self, eng, fn, reads=(), writes=(), dma=False):
        op = Op(len(self.ops), eng, fn, dma)
        for r in reads:
            if r.last_w is not None:
                op.deps.add(r.last_w)
            if r.excl:
                for rd in r.readers:
                    if rd.eng != eng:
                        op.deps.add(rd)
        for w in writes:
            lw = w.last_w
            if lw is not None and (dma or lw.is_dma or lw.eng != eng):
                op.deps.add(lw)
            last_rd = {}
            for rd in w.readers:
                if rd.is_dma:
                    op.deps.add(rd)
                elif dma or rd.eng != eng:
                    last_rd[rd.eng] = rd
            for rd in last_rd.values():
                op.deps.add(rd)
        for w in writes:
            w.last_w = op
            w.readers = []
        for r in reads:
            r.readers.append(op)
        op.deps.discard(op)
        if not dma:
            self.last_on[eng] = op
        else:
            self.dma_since_bar.append(op)
        self.ops.append(op)
        return op

    def pe(self, fn, reads=(), writes=()):
        return self.add("pe", fn, reads, writes)

    def act(self, fn, reads=(), writes=()):
        return self.add("act", fn, reads, writes)

    def dve(self, fn, reads=(), writes=()):
        return self.add("dve", fn, reads, writes)

    def pool(self, fn, reads=(), writes=()):
        return self.add("pool", fn, reads, writes)

    def dma(self, fn, reads=(), writes=(), eng="sp"):
        return self.add(eng, fn, reads, writes, dma=True)

    def barrier(self):
        prev = [o for o in self.last_on.values()] + list(self.dma_since_bar)
        self.dma_since_bar = []
        for eng in ENGS:
            op = Op(len(self.ops), eng, (lambda e: e.nop()), False)
            op.deps = set(prev)
            self.ops.append(op)
            self.last_on[eng] = op

    def emit(self):
        nc = self.nc
        ops = self.ops
        for op in ops:
            for d in op.deps:
                d.sig = True
        with contextlib.ExitStack() as es:
            esem = {e: es.enter_context(nc.semaphore("s_" + e)) for e in ENGS}
            dsems = [es.enter_context(nc.semaphore("d%d" % i)) for i in range(N_DMA_SEMS)]
            cnt = {e: 0 for e in esem}
            dcnt = [0] * N_DMA_SEMS
            dlast = [None] * N_DMA_SEMS
            rr = 0
            rr_sw = 0
            rr_hw = 0
            N_SW = 16
            for op in ops:
                if op.is_dma:
                    if op.eng == "pool":
                        s = rr_sw % N_SW
                        rr_sw += 1
                    else:
                        s = N_SW + rr_hw % (N_DMA_SEMS - N_SW)
                        rr_hw += 1
                    rr += 1
                    op.pre = dlast[s]
                    dcnt[s] += 16
                    op.sem = dsems[s]
                    op.val = dcnt[s]
                    dlast[s] = op
                elif op.sig:
                    cnt[op.eng] += 1
                    op.sem = esem[op.eng]
                    op.val = cnt[op.eng]
            self.stats = dict(cnt=dict(cnt), n_dma=rr, n_ops=len(ops))
            by_eng = {e: [o for o in ops if o.eng == e] for e in ENGS}
            last_dma = [d for d in dlast if d is not None]
            blk = es.enter_context(nc.Block())

            def run(engname, e):
                seen = {}

                def wait(d):
                    key = id(d.sem)
                    if seen.get(key, 0) >= d.val:
                        return
                    seen[key] = d.val
                    e.wait_ge(d.sem, d.val)

                for op in by_eng[engname]:
                    if op.pre is not None:
                        wait(op.pre)
                    for d in sorted(op.deps, key=lambda o: o.idx):
                        wait(d)
                    ins = op.fn(e)
                    if op.sem is not None:
                        ins.then_inc(op.sem, 16 if op.is_dma else 1)
                if engname == "sp":
                    for d in last_dma:
                        wait(d)

            @blk.tensor
            def _(e):
                run("pe", e)

            @blk.scalar
            def _(e):
                run("act", e)

            @blk.vector
            def _(e):
                run("dve", e)

            @blk.gpsimd
            def _(e):
                run("pool", e)

            @blk.sync
            def _(e):
                run("sp", e)


class NS:
    pass


NSC = 232
SC_A, SC_B, SC_BT, SC_AMX = 0, 8, 16, 24
SC_DT, SC_CS, SC_ECS, SC_WK, SC_ETOT, SC_NCS = 32, 64, 96, 128, 160, 200
R_BIG, R_BFG, R_DTB, R_ALOG, R_DSK = 0, 8, 16, 48, 80
C_GPRE, C_GMLP, C_CW, C_CB = 0, 16, 32, 176


class _Stop(Exception):
    pass


def build_program(stop_after=None):
    nc = bass.Bass("TRN2", target_bir_lowering=False)
    P = Prog(nc)

    def check(stage):
        if stop_after == stage:
            P.emit()
            raise _Stop()

    def din(name, shape):
        return nc.dram_tensor(name, list(shape), F32, kind="ExternalInput").ap()

    def dout(name, shape):
        return nc.dram_tensor(name, list(shape), F32, kind="ExternalOutput").ap()

    def dscr(name, shape, dt=F32):
        return nc.dram_tensor(name, list(shape), dt, kind="Internal").ap()

    xp = din("xp", [512, D])
    xo = din("xo", [512, D])
    xh = din("xh", [128, D])
    xf = din("xf", [2048, D])
    cv = din("cv", [32, 128])
    st_c = din("st_c", [2, 4, 128, 256])
    st_n = din("st_n", [2, 4, 128])
    st_m = din("st_m", [1, 8])
    st_s = din("st_s", [2, 1024, 128])
    flags_d = din("flags", [1, NFLAG])
    w_mod = din("w_mod", [D, 6 * D])
    b_mod = din("b_mod", [1, 6 * D])
    g_pre_mix = din("g_pre_mix", [16, 128])
    g_post_mix = din("g_post_mix", [1, D])
    w_in = din("w_in", [D, 6192])
    b_ig = din("b_igate", [1, 8])
    b_fg = din("b_fgate", [1, 8])
    conv_w = din("conv_w", [144, 128])
    conv_b = din("conv_b", [16, 128])
    dt_bias = din("dt_bias", [1, 32])
    a_log = din("a_log", [1, 32])
    d_skip = din("d_skip", [1, 16])
    g_ml = din("g_mlstm_norm", [1, 1024])
    g_sn = din("g_ssd_norm", [1, 1024])
    w_out = din("w_out", [D, D])
    g_pre_mlp = din("g_pre_mlp", [16, 128])
    g_post_mlp = din("g_post_mlp", [1, D])
    w_mlp_in = din("w_mlp_in", [D, 4 * D])
    w_mlp_out = din("w_mlp_out", [4 * D, D])
    yp = dout("yp", [512, D])
    ys = dout("ys", [512, D])
    o_c = dout("o_c", [2, 2, 4, 128, 256])
    o_n = dout("o_n", [2, 2, 4, 128])
    o_m = dout("o_m", [2, 2, 4])
    o_s = dout("o_s", [2, 2, 1024, 128])
    modscr = dscr("modscr", [2, 6 * D])
    stg = dscr("stg", [16, 128, 3072], BF16)
    x1s = dscr("x1s", [1024, D])
    catTs = dscr("catTs", [8, 128, KT * 128], BF16)
    Es = dscr("Es", [4, 128, ESZ])
    B_modscr = Buf("modscr")
    B_stg = [Buf("stg%d" % i) for i in range(16)]
    B_x1s = [Buf("x1s%d" % i) for i in range(8)]
    B_cats = [Buf("cats%d" % i) for i in range(8)]
    B_Es = [Buf("Es%d" % i) for i in range(4)]

    ES = contextlib.ExitStack()
    with ES:
        tcount = [0]

        def T(name, shape, dt=F32, es=ES):
            tcount[0] += 1
            return es.enter_context(nc.sbuf_tensor("%s_%d" % (name, tcount[0]), list(shape), dt))

        pbig = ES.enter_context(nc.psum_tensor("pbig", [128, 2048], F32))
        PB = [Buf("pb%d" % i, excl=True) for i in range(4)]
        ptr = [ES.enter_context(nc.psum_tensor("ptr%d" % i, [128, 512], F32)) for i in range(2)]
        PT = [Buf("pt%d" % i, excl=True) for i in range(2)]
        psm = [ES.enter_context(nc.psum_tensor("psm%d" % i, [128, 512], F32)) for i in range(2)]
        PS = [Buf("ps%d" % i, excl=True) for i in range(2)]
        rot = {"tr": 0, "sm": 0, "big": 0}

        def next_tr():
            i = rot["tr"] % 2
            rot["tr"] += 1
            return ptr[i], PT[i]

        def next_sm():
            i = rot["sm"] % 2
            rot["sm"] += 1
            return psm[i], PS[i]

        def next_big():
            i = rot["big"] % 4
            rot["big"] += 1
            return pbig[:, i * 512:(i + 1) * 512], PB[i]

        def next_big_m(M):
            i = 2 * M.pair + (M.brot[0] % 2)
            M.brot[0] += 1
            return pbig[:, i * 512:(i + 1) * 512], PB[i]

        B_c = Buf("consts")
        ident = T("ident", [128, 128])
        identb = T("identb", [128, 128], BF16)
        LT = T("LT", [128, 128])
        UT = T("UT", [128, 128])
        NM = [T("negf", [128, 128]), T("negb", [128, 128])]
        MK = [LT, UT]
        ones = T("ones", [128, 128])
        flags = T("flags_sb", [128, NFLAG])
        P.pool(lambda e: e.memset(ident[:], 0.0), writes=[B_c])
        P.pool(lambda e: e.affine_select(out=ident[:], in_=ident[:], pattern=[[-1, 128]], compare_op=ALU.not_equal,
                                         fill=1.0, base=0, channel_multiplier=1), reads=[B_c], writes=[B_c])
        P.pool(lambda e: e.memset(ones[:], 1.0), writes=[B_c])
        P.pool(lambda e: e.memset(LT[:], 1.0), writes=[B_c])
        P.pool(lambda e: e.affine_select(out=LT[:], in_=LT[:], pattern=[[1, 128]], compare_op=ALU.is_ge,
                                         fill=0.0, base=0, channel_multiplier=-1), reads=[B_c], writes=[B_c])
        P.pool(lambda e: e.memset(UT[:], 1.0), writes=[B_c])
        P.pool(lambda e: e.affine_select(out=UT[:], in_=UT[:], pattern=[[-1, 128]], compare_op=ALU.is_ge,
                                         fill=0.0, base=0, channel_multiplier=1), reads=[B_c], writes=[B_c])
        P.dve(lambda e: e.tensor_copy(out=identb[:], in_=ident[:]), reads=[B_c], writes=[B_c])
        for dd in range(2):
            P.dve(lambda e, dd=dd: e.tensor_scalar(out=NM[dd][:], in0=MK[dd][:], scalar1=-1.0, scalar2=-NEG,
                                                   op0=ALU.add, op1=ALU.mult), reads=[B_c], writes=[B_c])
        P.dma(lambda e: e.dma_start(out=flags[:], in_=flags_d[0:1, :].to_broadcast([128, NFLAG])), writes=[B_c])

        rowp = T("rowp", [128, 96])
        for (src, off, n) in ((b_ig, R_BIG, 8), (b_fg, R_BFG, 8), (dt_bias, R_DTB, 32), (a_log, R_ALOG, 32), (d_skip, R_DSK, 16)):
            P.dma(lambda e, src=src, off=off, n=n: e.dma_start(out=rowp[:, off:off + n], in_=src[0:1, :].to_broadcast([128, n])),
                  writes=[B_c])
        arow = T("arow", [128, 32])
        P.act(lambda e: e.activation(out=arow[:], in_=rowp[:, R_ALOG:R_ALOG + 32], func=AF.Exp), reads=[B_c], writes=[B_c])
        P.dve(lambda e: e.tensor_scalar(out=arow[:], in0=arow[:], scalar1=-1.0, scalar2=None, op0=ALU.mult), reads=[B_c], writes=[B_c])

        colp = T("colp", [128, 192])
        ld = T("ldtmp", [128, 128])
        B_ld = Buf("ld")

        def col_load(src, nrows, off, r0=0):
            P.dma(lambda e: e.dma_start(out=ld[0:nrows, :], in_=src[r0:r0 + nrows, :]), writes=[B_ld])
            pt, pb = next_sm()
            P.pe(lambda e: e.transpose(pt[:, 0:nrows], ld[0:nrows, :], ident[0:nrows, 0:nrows]), reads=[B_ld, B_c], writes=[pb])
            P.dve(lambda e: e.tensor_copy(out=colp[:, off:off + nrows], in_=pt[:, 0:nrows]), reads=[pb], writes=[B_c])

        col_load(g_pre_mix, 16, C_GPRE)
        col_load(g_pre_mlp, 16, C_GMLP)
        col_load(conv_w, 128, C_CW)
        col_load(conv_w, 16, C_CW + 128, r0=128)
        col_load(conv_b, 16, C_CB)

        def cw(ct, kk):
            c0 = C_CW + kk * 16 + ct
            return colp[:, c0:c0 + 1]

        def cbias(ct):
            return colp[:, C_CB + ct:C_CB + ct + 1]

        wg = T("wg", [128, KT, 48])
        P.dma(lambda e: e.dma_start(out=wg[:, :, 0:16], in_=w_in[:, 3072:3088].rearrange("(kt p) c -> p kt c", p=128)), writes=[B_c])
        P.dma(lambda e: e.dma_start(out=wg[:, :, 16:48], in_=w_in[:, 6160:6192].rearrange("(kt p) c -> p kt c", p=128)), writes=[B_c])

        wgm = T("wgm", [128, 2, KT, 48])
        gbB = T("gbB", [128, 2, 48])
        B_wgm = Buf("wgm")
        csT = T("csT", [128, 32], BF16)
        B_cv = Buf("cv")
        modT = T("modT", [128, 2, 96])
        B_modT = Buf("modT")
        modc = T("modc", [128, 2, 4, 16])
        B_modc = Buf("modc")
        stat = T("stat", [128, 64])
        B_stat = Buf("stat")
        sq_junk = T("sqjunk", [128, D], BF16)
        B_junk = Buf("junk")
        gtmp = T("gtmp", [128, 128])
        B_gt = Buf("gtmp")
        amx8 = T("amx8", [8, 128])
        B_amx = Buf("amx8")

        def alloc_inproj(es, nbuf=2):
            I = NS()
            I.xt = [T("xt%d" % i, [128, D], es=es) for i in range(2)]
            I.B_xt = [Buf("xt0"), Buf("xt1")]
            I.xrot = [0]
            I.uTf_flat = T("uTf", [128, KT * 128], es=es)
            I.uTf = I.uTf_flat[:].rearrange("p (k t) -> p k t", k=KT)
            I.B_uTf = Buf("uTf")
            I.wbuf = [T("wbuf%d" % i, [128, KT, 512], BF16, es=es) for i in range(nbuf)]
            I.B_w = [Buf("wbuf%d" % i) for i in range(nbuf)]
            I.wrot = [0]
            return I

        def load_wblock(I, src_ap):
            i = I.wrot[0] % len(I.wbuf)
            I.wrot[0] += 1
            ncols = src_ap.shape[2]
            nk = src_ap.shape[1]
            wt, wb = I.wbuf[i], I.B_w[i]
            P.dma(lambda e: e.dma_start(out=wt[:, 0:nk, 0:ncols], in_=src_ap), writes=[wb], eng="pool")
            return wt, wb

        def win_cols(c0, n):
            return w_in[:, c0:c0 + n].rearrange("(kt p) c -> p kt c", p=128)

        def rstd_of(ss_ap, out_ap, n):
            P.dve(lambda e: e.tensor_scalar(out=out_ap, in0=ss_ap, scalar1=1.0 / n, scalar2=EPS, op0=ALU.mult, op1=ALU.add),
                  reads=[B_stat], writes=[B_stat])
            P.act(lambda e: e.activation(out=out_ap, in_=out_ap, func=AF.Sqrt), reads=[B_stat], writes=[B_stat])
            P.dve(lambda e: e.reciprocal(out=out_ap, in_=out_ap), reads=[B_stat], writes=[B_stat])

        def uT_s1(I, src_rows_ap):
            i = I.xrot[0] % 2
            I.xrot[0] += 1
            xt, bx = I.xt[i], I.B_xt[i]
            c0 = 40 + 2 * i
            P.dma(lambda e: e.dma_start(out=xt[:], in_=src_rows_ap), writes=[bx])
            P.act(lambda e: e.activation(out=sq_junk[:], in_=xt[:], func=AF.Square, accum_out=stat[:, c0:c0 + 1]),
                  reads=[bx], writes=[B_junk, B_stat])
            rstd_of(stat[:, c0:c0 + 1], stat[:, c0 + 1:c0 + 2], D)
            P.act(lambda e: e.activation(out=xt[:], in_=xt[:], func=AF.Copy, scale=stat[:, c0 + 1:c0 + 2]), reads=[bx, B_stat], writes=[bx])
            return xt, bx

        def uT_s2(I, h, ci, which, uT, tcol, B_u, want_f32=False):
            xn, bxn = h
            uTf, B_uTf = I.uTf, I.B_uTf
            for g4 in range(4):
                pt, pb = next_tr()
                for j in range(4):
                    kt = g4 * 4 + j
                    P.pe(lambda e, kt=kt, j=j, pt=pt: e.transpose(pt[:, j * 128:(j + 1) * 128], xn[:, kt * 128:(kt + 1) * 128], ident[:]),
                         reads=[bxn, B_c], writes=[pb])
                for j in range(4):
                    kt = g4 * 4 + j
                    a_ap = modc[:, ci, which, kt:kt + 1]
                    s_ap = modc[:, ci, which + 1, kt:kt + 1]
                    P.dve(lambda e, kt=kt, j=j, pt=pt, a_ap=a_ap, s_ap=s_ap: e.tensor_scalar(
                        out=uT[:, kt, tcol:tcol + 128], in0=pt[:, j * 128:(j + 1) * 128], scalar1=a_ap, scalar2=s_ap,
                        op0=ALU.mult, op1=ALU.add), reads=[pb, B_modc], writes=[B_u])
                if want_f32:
                    P.act(lambda e, g4=g4, pt=pt: e.copy(out=uTf[:, g4 * 4:g4 * 4 + 4, :], in_=pt[:, 0:512].rearrange("p (k t) -> p k t", k=4)),
                          reads=[pb], writes=[B_uTf])

        def make_uT(I, src_rows_ap, ci, which, uT, tcol, B_u, want_f32=False, src_sb=None, B_src=None):
            if src_sb is None:
                i = I.xrot[0] % 2
                I.xrot[0] += 1
                xt, bx = I.xt[i], I.B_xt[i]
                P.dma(lambda e: e.dma_start(out=xt[:], in_=src_rows_ap), writes=[bx])
            else:
                xt, bx = src_sb, B_src
            P.act(lambda e: e.activation(out=sq_junk[:], in_=xt[:], func=AF.Square, accum_out=stat[:, 0:1]),
                  reads=[bx], writes=[B_junk, B_stat])
            rstd_of(stat[:, 0:1], stat[:, 1:2], D)
            if src_sb is None:
                xn, bxn = xt, bx
            else:
                i2 = I.xrot[0] % 2
                I.xrot[0] += 1
                xn, bxn = I.xt[i2], I.B_xt[i2]
            P.act(lambda e: e.activation(out=xn[:], in_=xt[:], func=AF.Copy, scale=stat[:, 1:2]), reads=[bx, B_stat], writes=[bxn])
            uTf, B_uTf = I.uTf, I.B_uTf
            for g4 in range(4):
                pt, pb = next_tr()
                for j in range(4):
                    kt = g4 * 4 + j
                    P.pe(lambda e, kt=kt, j=j, pt=pt: e.transpose(pt[:, j * 128:(j + 1) * 128], xn[:, kt * 128:(kt + 1) * 128], ident[:]),
                         reads=[bxn, B_c], writes=[pb])
                for j in range(4):
                    kt = g4 * 4 + j
                    a_ap = modc[:, ci, which, kt:kt + 1]
                    s_ap = modc[:, ci, which + 1, kt:kt + 1]
                    if True:
                        P.dve(lambda e, kt=kt, j=j, pt=pt, a_ap=a_ap, s_ap=s_ap: e.tensor_scalar(
                            out=uT[:, kt, tcol:tcol + 128], in0=pt[:, j * 128:(j + 1) * 128], scalar1=a_ap, scalar2=s_ap,
                            op0=ALU.mult, op1=ALU.add), reads=[pb, B_modc], writes=[B_u])
                    else:
                        P.act(lambda e, kt=kt, j=j, pt=pt, a_ap=a_ap, s_ap=s_ap: e.activation(
                            out=uT[:, kt, tcol:tcol + 128], in_=pt[:, j * 128:(j + 1) * 128], func=AF.Identity, bias=s_ap, scale=a_ap),
                            reads=[pb, B_modc], writes=[B_u])
                if want_f32:
                    P.act(lambda e, g4=g4, pt=pt: e.copy(out=uTf[:, g4 * 4:g4 * 4 + 4, :], in_=pt[:, 0:512].rearrange("p (k t) -> p k t", k=4)),
                          reads=[pb], writes=[B_uTf])

        def gate_preact(I, ci, pgall, c, B_pg):
            uTf, B_uTf = I.uTf, I.B_uTf
            pg, pgb = next_sm()
            for kt in range(KT):
                P.pe(lambda e, kt=kt: e.matmul(pg[:, 0:48], lhsT=uTf[:, kt, :], rhs=wgm[:, ci, kt, :], start=(kt == 0), stop=(kt == KT - 1)),
                     reads=[B_uTf, B_wgm], writes=[pgb])
            P.dve(lambda e: e.tensor_tensor(out=pgall[:, c, :], in0=pg[:, 0:48], in1=gbB[:, ci, :], op=ALU.add),
                  reads=[pgb, B_wgm], writes=[B_pg])

        def prep_scalars_batched(nch, pgall, B_pg, scall, B_sc_list, es, gt_flat, B_g):
            gt = gt_flat[:, 0:nch * 128].rearrange("p (c k) -> p c k", k=128)
            amxb = T("amxb", [128, 128], es=es)
            B_am = Buf("amxb")
            n8 = nch * 8

            def S(off, n):
                return scall[:, 0:nch, off:off + n]
            P.act(lambda e: e.activation(out=gt[:, :, 72:80], in_=pgall[:, 0:nch, 8:16], func=AF.Exp, scale=-1.0), reads=[B_pg], writes=[B_g])
            P.act(lambda e: e.activation(out=gt[:, :, 72:80], in_=gt[:, :, 72:80], func=AF.Ln, bias=1.0, scale=1.0), reads=[B_g], writes=[B_g])
            P.dve(lambda e: e.tensor_scalar(out=gt[:, :, 0:8], in0=gt[:, :, 72:80], scalar1=-1.0, scalar2=None, op0=ALU.mult), reads=[B_g], writes=[B_g])
            P.act(lambda e: e.activation(out=gt[:, :, 40:72], in_=pgall[:, 0:nch, 16:48], func=AF.Exp), reads=[B_pg], writes=[B_g])
            P.act(lambda e: e.activation(out=S(SC_DT, 32), in_=gt[:, :, 40:72], func=AF.Ln, bias=1.0, scale=1.0), reads=[B_g], writes=B_sc_list)
            P.dve(lambda e: e.tensor_tensor(out=gt[:, :, 8:40], in0=S(SC_DT, 32), in1=arow[:].unsqueeze(1).to_broadcast([128, nch, 32]), op=ALU.mult),
                  reads=B_sc_list + [B_c], writes=[B_g])
            p1, p1b = next_sm()
            p2, p2b = next_tr()
            p3, p3b = next_tr()
            p1v = p1[:, 0:nch * 16].rearrange("p (c k) -> p c k", k=16)
            p2v = p2[:, 0:nch * 32].rearrange("p (c k) -> p c k", k=32)
            p3v = p3[:, 0:nch * 32].rearrange("p (c k) -> p c k", k=32)
            P.pe(lambda e: e.matmul(p1v[:, :, 0:4], lhsT=LT[:], rhs=gt[:, :, 0:4], start=True, stop=True), reads=[B_g, B_c], writes=[p1b])
            P.pe(lambda e: e.matmul(p1v[:, :, 4:8], lhsT=UT[:], rhs=gt[:, :, 4:8], start=True, stop=True), reads=[B_g, B_c], writes=[p1b])
            P.pe(lambda e: e.matmul(p1v[:, :, 8:16], lhsT=ones[:], rhs=gt[:, :, 0:8], start=True, stop=True), reads=[B_g, B_c], writes=[p1b])
            P.pe(lambda e: e.matmul(p2v[:, :, 0:16], lhsT=LT[:], rhs=gt[:, :, 8:24], start=True, stop=True), reads=[B_g, B_c], writes=[p2b])
            P.pe(lambda e: e.matmul(p2v[:, :, 16:32], lhsT=UT[:], rhs=gt[:, :, 24:40], start=True, stop=True), reads=[B_g, B_c], writes=[p2b])
            P.pe(lambda e: e.matmul(p3v, lhsT=ones[:], rhs=gt[:, :, 8:40], start=True, stop=True), reads=[B_g, B_c], writes=[p3b])
            P.dve(lambda e: e.tensor_copy(out=S(SC_B, 16), in_=p1v), reads=[p1b], writes=B_sc_list)
            P.dve(lambda e: e.tensor_copy(out=S(SC_CS, 32), in_=p2v), reads=[p2b], writes=B_sc_list)
            P.dve(lambda e: e.tensor_scalar(out=S(SC_NCS, 32), in0=p2v, scalar1=-1.0, scalar2=None, op0=ALU.mult), reads=[p2b], writes=B_sc_list)
            P.dve(lambda e: e.tensor_tensor(out=S(SC_A, 8), in0=pgall[:, 0:nch, 0:8], in1=S(SC_B, 8), op=ALU.subtract),
                  reads=[B_pg] + B_sc_list, writes=B_sc_list)
            P.act(lambda e: e.activation(out=S(SC_ECS, 32), in_=S(SC_CS, 32), func=AF.Exp), reads=B_sc_list, writes=B_sc_list)
            P.act(lambda e: e.activation(out=S(SC_ETOT, 32), in_=p3v, func=AF.Exp), reads=[p3b], writes=B_sc_list)
            P.dve(lambda e: e.tensor_tensor(out=gt[:, :, 40:72], in0=p3v, in1=S(SC_CS, 32), op=ALU.subtract), reads=[p3b] + B_sc_list, writes=[B_g])
            P.act(lambda e: e.activation(out=gt[:, :, 40:72], in_=gt[:, :, 40:72], func=AF.Exp), reads=[B_g], writes=[B_g])
            P.dve(lambda e: e.tensor_tensor(out=S(SC_WK, 32), in0=gt[:, :, 40:72], in1=S(SC_DT, 32), op=ALU.mult), reads=[B_g] + B_sc_list, writes=B_sc_list)
            P.dve(lambda e: e.tensor_copy(out=gt[:, :, 80:88], in_=S(SC_A, 8)), reads=B_sc_list, writes=[B_g])
            P.dve(lambda e: e.tensor_copy(out=amxb[:, 0:n8].rearrange("p (c k) -> p c k", k=8), in_=gt[:, :, 80:88]), reads=[B_g], writes=[B_am])
            pa, pab = next_sm()
            P.pe(lambda e: e.transpose(pa[0:n8, 0:128], amxb[:, 0:n8], ident[:]), reads=[B_am, B_c], writes=[pab])
            P.dve(lambda e: e.tensor_reduce(out=gtmp[0:n8, 120:121], in_=pa[0:n8, 0:128], axis=AX.X, op=ALU.max), reads=[pab], writes=[B_gt])
            P.dve(lambda e: e.tensor_copy(out=amxb[0:n8, :], in_=gtmp[0:n8, 120:121].to_broadcast([n8, 128])), reads=[B_gt, B_am], writes=[B_am])
            pb2, pb2b = next_sm()
            P.pe(lambda e: e.matmul(pb2[:, 0:n8], lhsT=amxb[0:n8, :], rhs=ident[0:n8, 0:n8], start=True, stop=True), reads=[B_am, B_c], writes=[pb2b])
            P.dve(lambda e: e.tensor_copy(out=S(SC_AMX, 8), in_=pb2[:, 0:n8].rearrange("p (c k) -> p c k", k=8)), reads=[pb2b], writes=B_sc_list)

        def conv_apply(center_in, center_out, taps, ct, B_pre, B_acc, acc_flat, dst, B_dst, do_silu=True):
            P.act(lambda e: e.activation(out=center_out, in_=center_in, func=AF.Identity, scale=cw(ct, 4), bias=cbias(ct)),
                  reads=[B_pre, B_c], writes=[B_acc])
            for (ov, iv, kk) in taps:
                P.dve(lambda e, ov=ov, iv=iv, kk=kk: e.scalar_tensor_tensor(out=ov, in0=iv, scalar=cw(ct, kk), in1=ov, op0=ALU.mult, op1=ALU.add),
                      reads=[B_pre, B_acc, B_c], writes=[B_acc])
            if do_silu:
                P.act(lambda e: e.activation(out=dst, in_=acc_flat, func=AF.Silu), reads=[B_acc], writes=[B_dst])

        def grid_taps(pre_flat, acc_flat, nrows):
            pv = pre_flat.rearrange("p (r c) -> p r c", c=64)
            av = acc_flat.rearrange("p (r c) -> p r c", c=64)
            taps = []
            for di in range(3):
                for dj in range(3):
                    if di == 1 and dj == 1:
                        continue
                    if dj == 0:
                        oc, ic = slice(1, 64), slice(0, 63)
                    elif dj == 1:
                        oc, ic = slice(0, 64), slice(0, 64)
                    else:
                        oc, ic = slice(0, 63), slice(1, 64)
                    taps.append((av[:, :, oc], pv[:, di:di + nrows, ic], di * 3 + dj))
            return pv[:, 1:1 + nrows, :], av[:, :, :], taps

        def seq_taps(pre_flat, acc_flat):
            pv = pre_flat.rearrange("p (s t) -> p s t", s=2)
            av = acc_flat.rearrange("p (s t) -> p s t", s=2)
            taps = [(av[:, :, 1:256], pv[:, :, 0:255], 3), (av[:, :, 0:255], pv[:, :, 1:256], 5)]
            return pv, av, taps

        mix_count = [0]

        def alloc_mixer(es):
            M = NS()
            M.pair = mix_count[0] % 2
            mix_count[0] += 1
            M.brot = [0]
            M.chs = T("chs", [128, 32], es=es); M.B_chs = Buf("chs")
            M.wvb = T("wvb", [128, 4], BF16, es=es); M.B_wvb = Buf("wvb")
            M.vp = T("vp", [128, 4, 256], BF16, es=es); M.B_vp = Buf("vp")
            M.Sm = T("Sm", [128, 4, 128], BF16, es=es); M.B_Sm = Buf("Sm")
            M.htmp = T("htmp", [128, 1024], es=es); M.B_ht = Buf("htmp")
            M.argt2 = [T("argt%d" % i, [128, 512], es=es) for i in range(2)]; M.B_arg2 = [Buf("argt0"), Buf("argt1")]
            M.Wt = [T("Wt%d" % i, [128, 4, 128], BF16, es=es) for i in range(2)]
            M.B_Wt = [Buf("Wt0"), Buf("Wt1")]
            M.GTs = T("GTs", [128, 4, 128], es=es); M.B_GT = Buf("GTs")
            M.xw = T("xw", [128, 1024], BF16, es=es); M.B_xw = Buf("xw")
            M.qs = T("qs", [128, 4, 128], BF16, es=es); M.B_qs = Buf("qs")
            return M

        def new_state(name, es, bf=True):
            s = NS()
            s.Cn = T(name + "_Cn", [128, 4, 256], es=es)
            s.Nn = T(name + "_Nn", [128, 4], es=es)
            s.m = T(name + "_m", [128, 4], es=es)
            s.Sf = T(name + "_Sf", [128, 1024], es=es)
            if bf:
                s.Cb = T(name + "_Cb", [128, 4, 256], BF16, es=es)
                s.Nb = T(name + "_Nb", [128, 4], BF16, es=es)
                s.Sb = T(name + "_Sb", [128, 1024], BF16, es=es)
            s.B = Buf(name)
            s.Bm = s.B
            return s

        def state_zero(s, bf=True):
            P.pool(lambda e: e.memset(s.Cn[:], 0.0), writes=[s.B])
            P.pool(lambda e: e.memset(s.Nn[:], 0.0), writes=[s.B])
            P.pool(lambda e: e.memset(s.m[:], 0.0), writes=[s.B])
            P.pool(lambda e: e.memset(s.Sf[:], 0.0), writes=[s.B])
            if bf:
                P.pool(lambda e: e.memset(s.Cb[:], 0.0), writes=[s.B])
                P.pool(lambda e: e.memset(s.Nb[:], 0.0), writes=[s.B])
                P.pool(lambda e: e.memset(s.Sb[:], 0.0), writes=[s.B])

        def state_refresh_bf(s):
            P.act(lambda e: e.copy(out=s.Cb[:], in_=s.Cn[:]), reads=[s.B], writes=[s.B])
            P.act(lambda e: e.copy(out=s.Nb[:], in_=s.Nn[:]), reads=[s.B], writes=[s.B])
            P.act(lambda e: e.copy(out=s.Sb[:], in_=s.Sf[:]), reads=[s.B], writes=[s.B])

        def state_macc(dst, src, fcol):
            f_ap = flags[:, fcol:fcol + 1]
            for a, b_ in ((dst.Cn, src.Cn), (dst.Nn, src.Nn), (dst.m, src.m), (dst.Sf, src.Sf)):
                P.dve(lambda e, a=a, b_=b_: e.scalar_tensor_tensor(out=a[:], in0=b_[:], scalar=f_ap, in1=a[:], op0=ALU.mult, op1=ALU.add),
                      reads=[src.B, dst.B, B_c], writes=[dst.B])

        def state_store_scr(s, i):
            P.dma(lambda e: e.dma_start(out=Es[i, :, 0:1024], in_=s.Cn[:].rearrange("p h e -> p (h e)")), reads=[s.B], writes=[B_Es[i]])
            P.dma(lambda e: e.dma_start(out=Es[i, :, 1024:1028], in_=s.Nn[:]), reads=[s.B], writes=[B_Es[i]])
            P.dma(lambda e: e.dma_start(out=Es[i, :, 1028:1032], in_=s.m[:]), reads=[s.B], writes=[B_Es[i]])
            P.dma(lambda e: e.dma_start(out=Es[i, :, 1032:2056], in_=s.Sf[:]), reads=[s.B], writes=[B_Es[i]])

        def state_load_scr(s, i):
            P.dma(lambda e: e.dma_start(out=s.Cn[:].rearrange("p h e -> p (h e)"), in_=Es[i, :, 0:1024]), reads=[B_Es[i]], writes=[s.B])
            P.dma(lambda e: e.dma_start(out=s.Nn[:], in_=Es[i, :, 1024:1028]), reads=[B_Es[i]], writes=[s.B])
            P.dma(lambda e: e.dma_start(out=s.m[:], in_=Es[i, :, 1028:1032]), reads=[B_Es[i]], writes=[s.B])
            P.dma(lambda e: e.dma_start(out=s.Sf[:], in_=Es[i, :, 1032:2056]), reads=[B_Es[i]], writes=[s.B])
            state_refresh_bf(s)

        def mlstm_chain_scalars(M, sc, B_sc, d, st, qT, B_qk, tsl):
            chs, B_chs = M.chs, M.B_chs
            wvb, B_wvb = M.wvb, M.B_wvb
            o4 = 4 * d
            P.dve(lambda e: e.tensor_tensor(out=chs[:, 0:4], in0=st.m[:], in1=sc[:, SC_AMX + o4:SC_AMX + o4 + 4], op=ALU.max),
                  reads=[st.Bm, B_sc], writes=[B_chs])
            P.dve(lambda e: e.tensor_tensor(out=chs[:, 4:8], in0=sc[:, SC_A + o4:SC_A + o4 + 4], in1=chs[:, 0:4], op=ALU.subtract),
                  reads=[B_sc, B_chs], writes=[B_chs])
            P.dve(lambda e: e.scalar_tensor_tensor(out=chs[:, 8:12], in0=sc[:, SC_B + o4:SC_B + o4 + 4], scalar=-1.0, in1=chs[:, 0:4],
                                                   op0=ALU.mult, op1=ALU.subtract), reads=[B_sc, B_chs], writes=[B_chs])
            P.dve(lambda e: e.tensor_tensor(out=chs[:, 12:16], in0=st.m[:], in1=chs[:, 0:4], op=ALU.subtract),
                  reads=[st.Bm, B_chs], writes=[B_chs])
            P.act(lambda e: e.activation(out=chs[:, 4:16], in_=chs[:, 4:16], func=AF.Exp), reads=[B_chs], writes=[B_chs])
            P.dve(lambda e: e.tensor_copy(out=wvb[:], in_=chs[:, 4:8]), reads=[B_chs], writes=[B_wvb])
            P.dve(lambda e: e.tensor_tensor(out=st.m[:], in0=sc[:, SC_BT + o4:SC_BT + o4 + 4], in1=chs[:, 0:4], op=ALU.add),
                  reads=[B_sc, B_chs, st.Bm], writes=[st.Bm])
            for h in range(4):
                P.dve(lambda e, h=h: e.tensor_scalar(out=M.qs[:, h, :], in0=qT[:, h, tsl], scalar1=chs[:, 12 + h:13 + h], scalar2=None, op0=ALU.mult),
                      reads=[B_qk, B_chs], writes=[M.B_qs])

        def make_vp(M, v_ap, B_v):
            vp, chs = M.vp, M.chs
            for h in range(4):
                P.act(lambda e, h=h: e.activation(out=vp[:, h, :], in_=v_ap[:, h, :], func=AF.Copy, scale=chs[:, 4 + h:5 + h]),
                      reads=[B_v, M.B_chs], writes=[M.B_vp])

        def mlstm_update(M, st, k_tm, B_k):
            vp, wvb, chs = M.vp, M.wvb, M.chs
            pcs = [next_big_m(M), next_big_m(M)]
            for half in range(2):
                po0, pob0 = pcs[half]
                for hh in range(2):
                    h = half * 2 + hh
                    P.pe(lambda e, h=h, hh=hh, po0=po0: e.matmul(po0[:, hh * 256:(hh + 1) * 256], lhsT=k_tm[:, h, :], rhs=vp[:, h, :], start=True, stop=True),
                         reads=[B_k, M.B_vp], writes=[pob0])
            pn, pnb = next_sm()
            for h in range(4):
                P.pe(lambda e, h=h: e.matmul(pn[:, h:h + 1], lhsT=k_tm[:, h, :], rhs=wvb[:, h:h + 1], start=True, stop=True),
                     reads=[B_k, M.B_wvb], writes=[pnb])
            for h in range(4):
                po0, pob0 = pcs[h // 2]
                P.dve(lambda e, h=h, po0=po0: e.scalar_tensor_tensor(out=st.Cn[:, h, :], in0=st.Cn[:, h, :], scalar=chs[:, 12 + h:13 + h],
                                                                    in1=po0[:, (h % 2) * 256:(h % 2 + 1) * 256], op0=ALU.mult, op1=ALU.add),
                      reads=[st.B, M.B_chs, pob0], writes=[st.B])
            P.dve(lambda e: e.tensor_tensor(out=st.Nn[:], in0=st.Nn[:], in1=chs[:, 12:16], op=ALU.mult), reads=[st.B, M.B_chs], writes=[st.B])
            P.dve(lambda e: e.tensor_tensor(out=st.Nn[:], in0=pn[:, 0:4], in1=st.Nn[:], op=ALU.add), reads=[pnb, st.B], writes=[st.B])
            P.act(lambda e: e.copy(out=st.Cb[:], in_=st.Cn[:]), reads=[st.B], writes=[st.B])
            P.act(lambda e: e.copy(out=st.Nb[:], in_=st.Nn[:]), reads=[st.B], writes=[st.B])

        def mlstm_out(M, d, st, qT, kT, B_qk, tsl, hs_ap, B_hs, first):
            chs, B_chs, Sm, vp, wvb, htmp = M.chs, M.B_chs, M.Sm, M.vp, M.wvb, M.htmp
            pS, pSb = next_tr()
            for h in range(4):
                P.pe(lambda e, h=h: e.matmul(pS[:, h * 128:(h + 1) * 128], lhsT=kT[:, h, tsl], rhs=qT[:, h, tsl], start=True, stop=True),
                     reads=[B_qk], writes=[pSb])
            P.dve(lambda e: e.tensor_tensor(out=Sm[:], in0=pS[:, 0:512].rearrange("p (h t) -> p h t", h=4),
                                            in1=MK[d][:].unsqueeze(1).to_broadcast([128, 4, 128]), op=ALU.mult),
                  reads=[pSb, B_c], writes=[M.B_Sm])
            pbs = []
            for half in range(2):
                po0, pob0 = next_big_m(M)
                pbs.append((po0, pob0))
                for hh in range(2):
                    h = half * 2 + hh
                    P.pe(lambda e, h=h, hh=hh, po0=po0: e.matmul(po0[:, hh * 256:(hh + 1) * 256], lhsT=M.qs[:, h, :], rhs=st.Cb[:, h, :], start=True, stop=False),
                         reads=[M.B_qs, st.B], writes=[pob0])
                    P.pe(lambda e, h=h, hh=hh, po0=po0: e.matmul(po0[:, hh * 256:(hh + 1) * 256], lhsT=Sm[:, h, :], rhs=vp[:, h, :], start=False, stop=True),
                         reads=[M.B_Sm, M.B_vp], writes=[pob0])
            pn, pnb = next_sm()
            for h in range(4):
                P.pe(lambda e, h=h: e.matmul(pn[:, h:h + 1], lhsT=M.qs[:, h, :], rhs=st.Nb[:, h:h + 1], start=True, stop=False),
                     reads=[M.B_qs, st.B], writes=[pnb])
                P.pe(lambda e, h=h: e.matmul(pn[:, h:h + 1], lhsT=Sm[:, h, :], rhs=wvb[:, h:h + 1], start=False, stop=True),
                     reads=[M.B_Sm, M.B_wvb], writes=[pnb])
            P.dve(lambda e: e.tensor_copy(out=chs[:, 24:28], in_=pn[:, 0:4]), reads=[pnb], writes=[B_chs])
            P.dve(lambda e: e.scalar_tensor_tensor(out=chs[:, 16:20], in0=chs[:, 24:28], scalar=-1.0, in1=chs[:, 24:28], op0=ALU.mult, op1=ALU.max),
                  reads=[B_chs], writes=[B_chs])
            P.dve(lambda e: e.tensor_tensor(out=chs[:, 16:20], in0=chs[:, 16:20], in1=chs[:, 8:12], op=ALU.max), reads=[B_chs], writes=[B_chs])
            P.dve(lambda e: e.reciprocal(out=chs[:, 20:24], in_=chs[:, 16:20]), reads=[B_chs], writes=[B_chs])
            for half in range(2):
                po0, pob0 = pbs[half]
                dst = hs_ap[:, half * 512:(half + 1) * 512].rearrange("p (h e) -> p h e", h=2)
                rdb = chs[:, 20 + half * 2:22 + half * 2].unsqueeze(2).to_broadcast([128, 2, 256])
                if first:
                    P.dve(lambda e, po0=po0, dst=dst, rdb=rdb: e.tensor_tensor(out=dst, in0=po0.rearrange("p (h e) -> p h e", h=2), in1=rdb, op=ALU.mult),
                          reads=[pob0, B_chs], writes=[B_hs])
                else:
                    tmp = htmp[:, half * 512:(half + 1) * 512].rearrange("p (h e) -> p h e", h=2)
                    P.dve(lambda e, po0=po0, tmp=tmp, rdb=rdb: e.tensor_tensor(out=tmp, in0=po0.rearrange("p (h e) -> p h e", h=2), in1=rdb, op=ALU.mult),
                          reads=[pob0, B_chs], writes=[M.B_ht])
                    P.pool(lambda e, dst=dst, tmp=tmp: e.tensor_tensor(out=dst, in0=dst, in1=tmp, op=ALU.add), reads=[M.B_ht, B_hs], writes=[B_hs])

        def ssd_update(M, sc, B_sc, d, st, x_tm_ap, B_x, Btm, B_B):
            xw = M.xw
            o16 = 16 * d
            P.pool(lambda e: e.tensor_tensor(out=xw[:].rearrange("p (j q) -> p j q", j=16), in0=x_tm_ap.rearrange("p (j q) -> p j q", j=16),
                                             in1=sc[:, SC_WK + o16:SC_WK + o16 + 16].unsqueeze(2).to_broadcast([128, 16, 64]), op=ALU.mult),
                   reads=[B_x, B_sc], writes=[M.B_xw])
            P.pool(lambda e: e.tensor_tensor(out=st.Sf[:].rearrange("p (j q) -> p j q", j=16), in0=st.Sf[:].rearrange("p (j q) -> p j q", j=16),
                                             in1=sc[:, SC_ETOT + o16:SC_ETOT + o16 + 16].unsqueeze(2).to_broadcast([128, 16, 64]), op=ALU.mult),
                   reads=[st.B, B_sc], writes=[st.B])
            for half in range(2):
                po0, pob0 = next_big_m(M)
                for gg in range(2):
                    g = half * 2 + gg
                    P.pe(lambda e, g=g, gg=gg, po0=po0: e.matmul(po0[:, gg * 256:(gg + 1) * 256], lhsT=Btm[:, g, :], rhs=xw[:, g * 256:(g + 1) * 256], start=True, stop=True),
                         reads=[B_B, M.B_xw], writes=[pob0])
                P.dve(lambda e, half=half, po0=po0: e.tensor_tensor(out=st.Sf[:, half * 512:(half + 1) * 512], in0=po0,
                                                                    in1=st.Sf[:, half * 512:(half + 1) * 512], op=ALU.add),
                      reads=[pob0, st.B], writes=[st.B])
            P.act(lambda e: e.copy(out=st.Sb[:], in_=st.Sf[:]), reads=[st.B], writes=[st.B])

        def ssd_gt(M, BT, CT, B_bc, tsl):
            GTs = M.GTs
            pG, pGb = next_tr()
            for g in range(4):
                P.pe(lambda e, g=g: e.matmul(pG[:, g * 128:(g + 1) * 128], lhsT=BT[:, g, tsl], rhs=CT[:, g, tsl], start=True, stop=True),
                     reads=[B_bc], writes=[pGb])
            P.act(lambda e: e.copy(out=GTs[:], in_=pG[:, 0:512].rearrange("p (g t) -> p g t", g=4)), reads=[pGb], writes=[M.B_GT])

        def ssd_out(M, sc, B_sc, d, st, CT, B_bc, tsl, x_tm_ap, B_x, ys_ap, B_ys, first):
            Wt, B_Wt, GTs, htmp = M.Wt, M.B_Wt, M.GTs, M.htmp
            o16 = 16 * d

            def stage0(g):
                argt, B_arg = M.argt2[g % 2], M.B_arg2[g % 2]
                wi = g % 2
                pC, pCb = next_tr()
                for jj in range(4):
                    col = SC_CS + o16 + g * 4 + jj
                    P.pe(lambda e, jj=jj, col=col: e.matmul(pC[:, jj * 128:(jj + 1) * 128], lhsT=sc[:, col:col + 1].to_broadcast([128, 128]),
                                                           rhs=ident[:], start=True, stop=True), reads=[B_sc, B_c], writes=[pCb])
                P.dve(lambda e: e.tensor_tensor(out=argt[:].rearrange("p (j t) -> p j t", j=4), in0=pC[:, 0:512].rearrange("p (j t) -> p j t", j=4),
                                                in1=NM[d][:].unsqueeze(1).to_broadcast([128, 4, 128]), op=ALU.add),
                      reads=[pCb, B_c], writes=[B_arg])
                for jj in range(4):
                    col = SC_NCS + o16 + g * 4 + jj
                    P.act(lambda e, jj=jj, col=col: e.activation(out=argt[:, jj * 128:(jj + 1) * 128], in_=argt[:, jj * 128:(jj + 1) * 128], func=AF.Exp,
                                                                bias=sc[:, col:col + 1], scale=1.0), reads=[B_arg, B_sc], writes=[B_arg])

            def stage0b(g):
                argt, B_arg = M.argt2[g % 2], M.B_arg2[g % 2]
                wi = g % 2
                for jj in range(4):
                    col = SC_DT + o16 + g * 4 + jj
                    P.dve(lambda e, jj=jj, col=col: e.scalar_tensor_tensor(out=Wt[wi][:, jj, :], in0=argt[:, jj * 128:(jj + 1) * 128],
                                                                          scalar=sc[:, col:col + 1], in1=GTs[:, g, :], op0=ALU.mult, op1=ALU.mult),
                          reads=[B_arg, B_sc, M.B_GT], writes=[B_Wt[wi]])

            def stage1(g, po0, pob0):
                wi = g % 2
                for jj in range(4):
                    j = g * 4 + jj
                    c0 = (j % 8) * 64
                    P.pe(lambda e, jj=jj, j=j, c0=c0: e.matmul(po0[:, c0:c0 + 64], lhsT=Wt[wi][:, jj, :], rhs=x_tm_ap[:, j * 64:(j + 1) * 64],
                                                             start=True, stop=True), reads=[B_Wt[wi], B_x], writes=[pob0])

            def finish(half, po0, pob0):
                po2, pob2 = next_big_m(M)
                for gg in range(2):
                    g = half * 2 + gg
                    P.pe(lambda e, g=g, gg=gg: e.matmul(po2[:, gg * 256:(gg + 1) * 256], lhsT=CT[:, g, tsl], rhs=st.Sb[:, g * 256:(g + 1) * 256],
                                                        start=True, stop=True), reads=[B_bc, st.B], writes=[pob2])
                tmp = htmp[:, half * 512:(half + 1) * 512]
                ecs = sc[:, SC_ECS + o16 + half * 8:SC_ECS + o16 + half * 8 + 8]
                P.dve(lambda e: e.tensor_tensor(
                    out=tmp.rearrange("p (j q) -> p j q", j=8), in0=po2.rearrange("p (j q) -> p j q", j=8),
                    in1=ecs.unsqueeze(2).to_broadcast([128, 8, 64]), op=ALU.mult), reads=[pob2, B_sc], writes=[M.B_ht])
                dst = ys_ap[:, half * 512:(half + 1) * 512]
                if first:
                    P.dve(lambda e: e.tensor_tensor(out=dst, in0=po0, in1=tmp, op=ALU.add), reads=[pob0, M.B_ht], writes=[B_ys])
                else:
                    P.dve(lambda e: e.tensor_tensor(out=tmp, in0=po0, in1=tmp, op=ALU.add), reads=[pob0, M.B_ht], writes=[M.B_ht])
                    P.pool(lambda e: e.tensor_tensor(out=dst, in0=dst, in1=tmp, op=ALU.add), reads=[M.B_ht, B_ys], writes=[B_ys])

            pa, pab = next_big_m(M)
            stage0(0)
            yield
            stage0(1)
            stage0b(0)
            yield
            stage1(0, pa, pab)
            stage0(2)
            stage0b(1)
            yield
            stage1(1, pa, pab)
            finish(0, pa, pab)
            yield
            pb_, pbb = next_big_m(M)
            stage0(3)
            stage0b(2)
            yield
            stage1(2, pb_, pbb)
            stage0b(3)
            stage1(3, pb_, pbb)
            finish(1, pb_, pbb)
            yield

        cvt = T("cvt", [32, 128])
        bmr = [T("bmr%d" % i, [2, 512]) for i in range(3)]
        B_bmr = [Buf("bmr%d" % i) for i in range(3)]
        mrow = [T("mrow%d" % i, [2, 512]) for i in range(3)]
        B_mrow = [Buf("mrow%d" % i) for i in range(3)]
        csT3 = csT[:].rearrange("p (c k) -> p c k", c=2)

        def mod_prologue():
            P.dma(lambda e: e.dma_start(out=cvt[:], in_=cv), writes=[B_cv])
            P.act(lambda e: e.activation(out=cvt[:], in_=cvt[:], func=AF.Silu), reads=[B_cv], writes=[B_cv])
            pt, pb = next_sm()
            P.pe(lambda e: e.transpose(pt[:, 0:32], cvt[:], ident[0:32, 0:32]), reads=[B_cv, B_c], writes=[pb])
            P.dve(lambda e: e.tensor_copy(out=csT[:], in_=pt[:, 0:32]), reads=[pb], writes=[B_cv])

        def mod_load(c0, ncols, wt, wb):
            P.dma(lambda e: e.dma_start(out=wt[:, :, 0:ncols], in_=w_mod[:, c0:c0 + ncols].rearrange("(kt p) c -> p kt c", p=128)), writes=[wb], eng="pool")

        mod_cnt = [0]

        def mod_compute(c0, ncols, wt, wb, defer=False, small_psum=False):
            mi = mod_cnt[0] % 3
            mod_cnt[0] += 1
            P.dma(lambda e: e.dma_start(out=bmr[mi][:, 0:ncols], in_=b_mod[0:1, c0:c0 + ncols].to_broadcast([2, ncols])), writes=[B_bmr[mi]])
            po, pob = next_sm() if small_psum else next_big()
            for kt in range(KT):
                P.pe(lambda e, kt=kt: e.matmul(po[0:2, 0:ncols], lhsT=csT3[:, :, kt], rhs=wt[:, kt, 0:ncols], start=(kt == 0), stop=(kt == KT - 1)),
                     reads=[B_cv, wb], writes=[pob])
            P.dve(lambda e: e.tensor_tensor(out=mrow[mi][:, 0:ncols], in0=po[0:2, 0:ncols], in1=bmr[mi][:, 0:ncols], op=ALU.add),
                  reads=[pob, B_bmr[mi]], writes=[B_mrow[mi]])
            def store():
                P.dma(lambda e: e.dma_start(out=modscr[:, c0:c0 + ncols], in_=mrow[mi][:, 0:ncols]),
                      reads=[B_mrow[mi]], writes=[B_modscr])
            if defer:
                return store
            store()
            return None

        def mod_cols(j0, nj):
            for ci in range(2):
                P.dma(lambda e, ci=ci: e.dma_start(out=ld[0:nj, :], in_=modscr[ci:ci + 1, j0 * 128:(j0 + nj) * 128].rearrange("o (j p) -> (o j) p", p=128)),
                      reads=[B_modscr], writes=[B_ld])
                pt, pb = next_sm()
                P.pe(lambda e, pt=pt: e.transpose(pt[:, 0:nj], ld[0:nj, :], ident[0:nj, 0:nj]), reads=[B_ld, B_c], writes=[pb])
                P.dve(lambda e, pt=pt, ci=ci: e.tensor_copy(out=modT[:, ci, j0:j0 + nj], in_=pt[:, 0:nj]), reads=[pb], writes=[B_modT])

        def mod_consts(dst, gcol, sc_off, sh_off):
            for ci in range(2):
                P.dve(lambda e, ci=ci: e.scalar_tensor_tensor(
                    out=modc[:, ci, dst, :], in0=modT[:, ci, sc_off:sc_off + 16], scalar=1.0, in1=colp[:, gcol:gcol + 16],
                    op0=ALU.add, op1=ALU.mult), reads=[B_modT, B_c], writes=[B_modc])
                P.dve(lambda e, ci=ci: e.tensor_copy(out=modc[:, ci, dst + 1, :], in_=modT[:, ci, sh_off:sh_off + 16]),
                      reads=[B_modT], writes=[B_modc])

        def gate_consts():
            for ci in range(2):
                P.dve(lambda e, ci=ci: e.tensor_tensor(out=wgm[:, ci, :, :], in0=wg[:], in1=modc[:, ci, 0, :].unsqueeze(2).to_broadcast([128, KT, 48]), op=ALU.mult),
                      reads=[B_c, B_modc], writes=[B_wgm])
                pg, pgb = next_sm()
                for kt in range(KT):
                    P.pe(lambda e, kt=kt, ci=ci, pg=pg: e.matmul(pg[:, 0:48], lhsT=modc[:, ci, 1, kt:kt + 1].to_broadcast([128, 128]), rhs=wg[:, kt, :],
                                                                 start=(kt == 0), stop=(kt == KT - 1)), reads=[B_modc, B_c], writes=[pgb])
                P.dve(lambda e, ci=ci, pg=pg: e.tensor_tensor(out=gbB[:, ci, :], in0=pg[:, 0:48], in1=rowp[:, 0:48], op=ALU.add),
                      reads=[pgb, B_c], writes=[B_wgm])

        with contextlib.ExitStack() as es1:
            scall = T("scall", [128, 16, NSC], es=es1)
            B_scall = [Buf("scall%d" % i) for i in range(16)]
            with contextlib.ExitStack() as es:
                I = alloc_inproj(es)
                uTs = T("uTs", [128, KT, 2048], BF16, es=es)
                B_uTs = [Buf("uTs%d" % i) for i in range(16)]
                pre2 = [T("pre_s%d" % i, [128, 34 * 64], es=es) for i in range(2)]
                B_pre2 = [Buf("pre_s0"), Buf("pre_s1")]
                cacc = T("cacc_s", [128, 2048], es=es)
                B_cacc = Buf("cacc_s")
                cvo = T("cvo_s", [128, 2048], BF16, es=es)
                B_cvo = Buf("cvo_s")
                ost = [T("ost%d" % i, [128, 512], BF16, es=es) for i in range(2)]
                B_ost = [Buf("ost0"), Buf("ost1")]
                for i_ in range(2):
                    P.pool(lambda e, i_=i_: e.memset(pre2[i_][:], 0.0), writes=[B_pre2[i_]])
                pgall_flat = T("pgall", [128, 2048], es=es)
                pgall = pgall_flat[:, 0:768].rearrange("p (c k) -> p c k", k=48)
                B_pg = Buf("pgall")
                hnd = uT_s1(I, xf[0:128, :])
                mod_prologue()
                for blk_i in range(8):
                    wi_ = I.wrot[0] % 2
                    I.wrot[0] += 1
                    mod_load(blk_i * 512, 512, I.wbuf[wi_], I.B_w[wi_])
                    mod_compute(blk_i * 512, 512, I.wbuf[wi_], I.B_w[wi_])
                mod_cols(0, 32)
                mod_consts(0, C_GPRE, 16, 0)
                gate_consts()
                check("A0")
                for c in range(16):
                    nxt = uT_s1(I, xf[(c + 1) * 128:(c + 2) * 128, :]) if c < 15 else None
                    uT_s2(I, hnd, 1, 0, uTs, c * 128, B_uTs[c], want_f32=True)
                    gate_preact(I, 1, pgall, c, B_pg)
                    hnd = nxt
                check("A1")
                prep_scalars_batched(16, pgall, B_pg, scall, B_scall, es, cacc, B_cacc)
                check("B1a1")
                orot = [0]
                for (c0, dst0) in ((512, 0), (1024, 512), (1536, 1024)):
                    wt, wb = load_wblock(I, win_cols(c0, 512))
                    for c in range(16):
                        po, pob = next_big()
                        for kt in range(KT):
                            P.pe(lambda e, kt=kt, c=c, wt=wt, po=po: e.matmul(po, lhsT=uTs[:, kt, c * 128:(c + 1) * 128], rhs=wt[:, kt, :],
                                                                          start=(kt == 0), stop=(kt == KT - 1)), reads=[B_uTs[c], wb], writes=[pob])
                        oi = orot[0] % 2
                        orot[0] += 1
                        P.act(lambda e, oi=oi, po=po: e.copy(out=ost[oi][:], in_=po), reads=[pob], writes=[B_ost[oi]])
                        P.dma(lambda e, oi=oi, c=c, dst0=dst0: e.dma_start(out=stg[c, :, dst0:dst0 + 512], in_=ost[oi][:]), reads=[B_ost[oi]], writes=[B_stg[c]])
                check("B1a2")
                cacc2 = [cacc, pgall_flat]
                B_cacc2 = [B_cacc, B_pg]
                tap_sets = [grid_taps(pre2[i_][:], cacc2[i_][:, 0:2048], 32) for i_ in range(2)]
                cvo2 = [cvo, I.uTf_flat[:].bitcast(BF16)[:, 0:2048]]
                B_cvo2 = [B_cvo, I.B_uTf]

                def fm_front(ct, cc, wt, wb):
                    pre, B_pre = pre2[ct % 2], B_pre2[ct % 2]
                    cen_in, cen_out, taps = tap_sets[ct % 2]
                    for g in range(4):
                        po, pob = next_big()
                        for kt in range(KT):
                            P.pe(lambda e, kt=kt, g=g, po=po: e.matmul(po, lhsT=wt[:, kt, cc * 128:(cc + 1) * 128],
                                                                      rhs=uTs[:, kt, g * 512:(g + 1) * 512], start=(kt == 0), stop=(kt == KT - 1)),
                                 reads=B_uTs[4 * g:4 * g + 4] + [wb], writes=[pob])
                        P.act(lambda e, g=g, po=po: e.copy(out=pre[:, 64 + 512 * g:64 + 512 * (g + 1)], in_=po), reads=[pob], writes=[B_pre])
                    conv_apply(cen_in, cen_out, taps, ct, B_pre, B_cacc2[ct % 2], cacc2[ct % 2][:, 0:2048], None, None, do_silu=False)

                def fm_back(ct):
                    cv_, bcv = cvo2[ct % 2], B_cvo2[ct % 2]
                    P.act(lambda e: e.activation(out=cv_[:], in_=cacc2[ct % 2][:, 0:2048], func=AF.Silu), reads=[B_cacc2[ct % 2]], writes=[bcv])
                    for c4 in range(4):
                        pt, pb = next_tr()
                        ptb = pt[:].bitcast(BF16)
                        for j in range(4):
                            c = c4 * 4 + j
                            P.pe(lambda e, c=c, j=j, ptb=ptb: e.transpose(ptb[:, j * 128:(j + 1) * 128], cv_[:, c * 128:(c + 1) * 128], identb[:]),
                                 reads=[bcv, B_c], writes=[pb])
                        oi = orot[0] % 2
                        orot[0] += 1
                        P.act(lambda e, oi=oi, ptb=ptb: e.copy(out=ost[oi][:], in_=ptb[:, 0:512]), reads=[pb], writes=[B_ost[oi]])
                        dcol = 1536 + ct * 128
                        P.dma(lambda e, oi=oi, c4=c4, dcol=dcol: e.dma_start(out=stg[c4 * 4:c4 * 4 + 4, :, dcol:dcol + 128].rearrange("c p n -> p c n"),
                                                                            in_=ost[oi][:].rearrange("p (c n) -> p c n", c=4)),
                              reads=[B_ost[oi]], writes=B_stg[c4 * 4:c4 * 4 + 4])

                pending = None
                for blk_i in range(3):
                    wt, wb = load_wblock(I, win_cols(4112 + blk_i * 512, 512))
                    for cc in range(4):
                        ct = blk_i * 4 + cc
                        fm_front(ct, cc, wt, wb)
                        if pending is not None:
                            fm_back(pending)
                        pending = ct
                fm_back(pending)
            P.barrier()
            check("B1a")
            with contextlib.ExitStack() as es:
                Ms = [alloc_mixer(es), alloc_mixer(es)]
                sgt = [[T("sgt%d_%d" % (d, i), [128, 3072], BF16, es=es) for i in range(2)] for d in range(2)]
                B_sgt = [[Buf("sgt"), Buf("sgt")] for d in range(2)]
                stcs = [new_state("stc%d" % d, es, bf=False) for d in range(2)]
                E = [new_state("E%d" % i, es, bf=False) for i in range(4)]
                for s_ in E:
                    state_zero(s_, bf=False)
                ld2 = [T("ld2_%d" % i, [128, 128], es=es) for i in range(2)]
                B_ld2 = [Buf("ld2a"), Buf("ld2b")]

                refall = T("refall", [128, 16, 8], es=es)
                cdall = T("cdall", [128, 16, 8], es=es)
                wvall = T("wvall", [128, 16, 8], es=es)
                wvball = T("wvball", [128, 16, 8], BF16, es=es)
                B_rf = [Buf("rf0"), Buf("rf1")]

                def sp_chain(d):
                    stc = stcs[d]
                    M = Ms[d]
                    d4, d16 = 4 * d, 16 * d
                    order = list(range(16)) if d == 0 else list(range(15, -1, -1))
                    P.dma(lambda e: e.dma_start(out=stc.m[:], in_=st_m[0:1, d4:d4 + 4].to_broadcast([128, 4])), writes=[stc.B])
                    for n_, c in enumerate(order):
                        if (d == 0 and c % 4 == 0) or (d == 1 and c % 4 == 1):
                            fcol = (0 if d == 0 else 32) + c
                            P.dve(lambda e, fcol=fcol: e.scalar_tensor_tensor(out=E[2 * d].m[:], in0=stc.m[:], scalar=flags[:, fcol:fcol + 1], in1=E[2 * d].m[:],
                                                                              op0=ALU.mult, op1=ALU.add), reads=[stc.B, E[2 * d].B, B_c], writes=[E[2 * d].B])
                        if (d == 0 and c % 4 == 2) or (d == 1 and c % 4 == 3):
                            fcol = (16 if d == 0 else 48) + c
                            P.dve(lambda e, fcol=fcol: e.scalar_tensor_tensor(out=E[2 * d + 1].m[:], in0=stc.m[:], scalar=flags[:, fcol:fcol + 1], in1=E[2 * d + 1].m[:],
                                                                              op0=ALU.mult, op1=ALU.add), reads=[stc.B, E[2 * d + 1].B, B_c], writes=[E[2 * d + 1].B])
                        if n_ == 15:
                            break
                        P.dve(lambda e, c=c: e.tensor_tensor(out=refall[:, c, d4:d4 + 4], in0=stc.m[:], in1=scall[:, c, SC_AMX + d4:SC_AMX + d4 + 4], op=ALU.max),
                              reads=[stc.B, B_scall[c]], writes=[B_rf[d]])
                        P.dve(lambda e, c=c: e.tensor_tensor(out=cdall[:, c, d4:d4 + 4], in0=stc.m[:], in1=refall[:, c, d4:d4 + 4], op=ALU.subtract),
                              reads=[stc.B, B_rf[d]], writes=[B_rf[d]])
                        P.dve(lambda e, c=c: e.tensor_tensor(out=stc.m[:], in0=scall[:, c, SC_BT + d4:SC_BT + d4 + 4], in1=refall[:, c, d4:d4 + 4], op=ALU.add),
                              reads=[B_scall[c], B_rf[d]], writes=[stc.B])
                        if n_ % 4 == 3:
                            yield
                    cs_ = slice(0, 15) if d == 0 else slice(1, 16)
                    P.dve(lambda e: e.tensor_tensor(out=wvall[:, cs_, d4:d4 + 4], in0=scall[:, cs_, SC_A + d4:SC_A + d4 + 4], in1=refall[:, cs_, d4:d4 + 4], op=ALU.subtract),
                          reads=B_scall + [B_rf[d]], writes=[B_rf[d]])
                    P.act(lambda e: e.activation(out=wvall[:, cs_, d4:d4 + 4], in_=wvall[:, cs_, d4:d4 + 4], func=AF.Exp), reads=[B_rf[d]], writes=[B_rf[d]])
                    P.act(lambda e: e.activation(out=cdall[:, cs_, d4:d4 + 4], in_=cdall[:, cs_, d4:d4 + 4], func=AF.Exp), reads=[B_rf[d]], writes=[B_rf[d]])
                    P.dve(lambda e: e.tensor_copy(out=wvball[:, cs_, d4:d4 + 4], in_=wvall[:, cs_, d4:d4 + 4]), reads=[B_rf[d]], writes=[B_rf[d]])
                    yield
                    P.dma(lambda e: e.dma_start(out=stc.Cn[:], in_=st_c[d].rearrange("h d e -> d h e")), writes=[stc.B])
                    P.dma(lambda e: e.dma_start(out=ld2[d][0:4, :], in_=st_n[d]), writes=[B_ld2[d]])
                    pt, pb = next_sm()
                    P.pe(lambda e, pt=pt: e.transpose(pt[:, 0:4], ld2[d][0:4, :], ident[0:4, 0:4]), reads=[B_ld2[d], B_c], writes=[pb])
                    P.dve(lambda e, pt=pt: e.tensor_copy(out=stc.Nn[:], in_=pt[:, 0:4]), reads=[pb], writes=[stc.B])
                    for a in range(8):
                        P.dma(lambda e, a=a: e.dma_start(out=ld2[d][:], in_=st_s[d, a * 128:(a + 1) * 128, :]), writes=[B_ld2[d]])
                        pt, pb = next_sm()
                        P.pe(lambda e, pt=pt: e.transpose(pt[:, 0:128], ld2[d][:], ident[:]), reads=[B_ld2[d], B_c], writes=[pb])
                        P.dve(lambda e, pt=pt, a=a: e.tensor_copy(out=stc.Sf[:, a * 128:(a + 1) * 128], in_=pt[:, 0:128]), reads=[pb], writes=[stc.B])
                        if a % 4 == 3:
                            yield
                    for n_, c in enumerate(order):
                        si = n_ % 2
                        sg, bsg = sgt[d][si], B_sgt[d][si]

                        def macc3(Et, fcol):
                            f_ap = flags[:, fcol:fcol + 1]
                            for a_, b_ in ((Et.Cn, stc.Cn), (Et.Nn, stc.Nn), (Et.Sf, stc.Sf)):
                                P.dve(lambda e, a_=a_, b_=b_: e.scalar_tensor_tensor(out=a_[:], in0=b_[:], scalar=f_ap, in1=a_[:], op0=ALU.mult, op1=ALU.add),
                                      reads=[stc.B, Et.B, B_c], writes=[Et.B])
                        if d == 0 and c % 4 == 0:
                            macc3(E[0], c)
                        if d == 0 and c % 4 == 2:
                            macc3(E[1], 16 + c)
                        if d == 1 and c % 4 == 1:
                            macc3(E[2], 32 + c)
                        if d == 1 and c % 4 == 3:
                            macc3(E[3], 48 + c)
                        if n_ == 15:
                            break
                        P.dma(lambda e, sg=sg, c=c: e.dma_start(out=sg[:], in_=stg[c]), reads=[B_stg[c]], writes=[bsg])
                        vp, xw = M.vp, M.xw
                        k_tm = sg[:, 0:512].rearrange("p (h d) -> p h d", h=4)
                        Btm = sg[:, 2560:3072].rearrange("p (g n) -> p g n", g=4)
                        for h in range(4):
                            P.act(lambda e, sg=sg, c=c, h=h: e.activation(out=vp[:, h, :], in_=sg[:, 512 + h * 256:512 + (h + 1) * 256], func=AF.Copy,
                                                                         scale=wvall[:, c, d4 + h:d4 + h + 1]),
                                  reads=[bsg, B_rf[d]], writes=[M.B_vp])
                        P.pool(lambda e, sg=sg, c=c: e.tensor_tensor(out=xw[:].rearrange("p (j q) -> p j q", j=16), in0=sg[:, 1536:2560].rearrange("p (j q) -> p j q", j=16),
                                                                     in1=scall[:, c, SC_WK + d16:SC_WK + d16 + 16].unsqueeze(2).to_broadcast([128, 16, 64]), op=ALU.mult),
                               reads=[bsg, B_scall[c]], writes=[M.B_xw])
                        pcs = [next_big_m(M), next_big_m(M)]
                        for half in range(2):
                            po0, pob0 = pcs[half]
                            for hh in range(2):
                                h = half * 2 + hh
                                P.pe(lambda e, h=h, hh=hh, po0=po0, k_tm=k_tm: e.matmul(po0[:, hh * 256:(hh + 1) * 256], lhsT=k_tm[:, h, :], rhs=vp[:, h, :], start=True, stop=True),
                                     reads=[bsg, M.B_vp], writes=[pob0])
                        pn, pnb = next_sm()
                        for h in range(4):
                            P.pe(lambda e, h=h, c=c, k_tm=k_tm, pn=pn: e.matmul(pn[:, h:h + 1], lhsT=k_tm[:, h, :], rhs=wvball[:, c, d4 + h:d4 + h + 1], start=True, stop=True),
                                 reads=[bsg, B_rf[d]], writes=[pnb])
                        for h in range(4):
                            po0, pob0 = pcs[h // 2]
                            P.dve(lambda e, h=h, c=c, po0=po0: e.scalar_tensor_tensor(out=stc.Cn[:, h, :], in0=stc.Cn[:, h, :], scalar=cdall[:, c, d4 + h:d4 + h + 1],
                                                                                   in1=po0[:, (h % 2) * 256:(h % 2 + 1) * 256], op0=ALU.mult, op1=ALU.add),
                                  reads=[stc.B, B_rf[d], pob0], writes=[stc.B])
                        P.dve(lambda e, c=c: e.tensor_tensor(out=stc.Nn[:], in0=stc.Nn[:], in1=cdall[:, c, d4:d4 + 4], op=ALU.mult), reads=[stc.B, B_rf[d]], writes=[stc.B])
                        P.dve(lambda e, pn=pn: e.tensor_tensor(out=stc.Nn[:], in0=pn[:, 0:4], in1=stc.Nn[:], op=ALU.add), reads=[pnb, stc.B], writes=[stc.B])
                        yield
                        pss = [next_big_m(M), next_big_m(M)]
                        for half in range(2):
                            po0, pob0 = pss[half]
                            for gg in range(2):
                                g = half * 2 + gg
                                P.pe(lambda e, g=g, gg=gg, po0=po0, Btm=Btm: e.matmul(po0[:, gg * 256:(gg + 1) * 256], lhsT=Btm[:, g, :], rhs=xw[:, g * 256:(g + 1) * 256], start=True, stop=True),
                                     reads=[bsg, M.B_xw], writes=[pob0])
                        P.pool(lambda e, c=c: e.tensor_tensor(out=stc.Sf[:].rearrange("p (j q) -> p j q", j=16), in0=stc.Sf[:].rearrange("p (j q) -> p j q", j=16),
                                                              in1=scall[:, c, SC_ETOT + d16:SC_ETOT + d16 + 16].unsqueeze(2).to_broadcast([128, 16, 64]), op=ALU.mult),
                               reads=[stc.B, B_scall[c]], writes=[stc.B])
                        for half in range(2):
                            po0, pob0 = pss[half]
                            P.dve(lambda e, half=half, po0=po0: e.tensor_tensor(out=stc.Sf[:, half * 512:(half + 1) * 512], in0=po0,
                                                                              in1=stc.Sf[:, half * 512:(half + 1) * 512], op=ALU.add),
                                  reads=[pob0, stc.B], writes=[stc.B])
                        yield

                gens = [sp_chain(0), sp_chain(1)]
                nstep = 0
                while gens:
                    for g_ in list(gens):
                        try:
                            next(g_)
                        except StopIteration:
                            gens.remove(g_)
                for i in range(4):
                    state_store_scr(E[i], i)
        P.barrier()
        check("B1b")

        mod_k = [16]
        mod_pend = [None]

        def super_unit(su):
            ci = su
            ntok = 512 if su == 0 else 640
            src = xp if su == 0 else xo
            with contextlib.ExitStack() as esu:
                sc4 = T("sc4", [128, 4, NSC], es=esu)
                B_sc4 = [Buf("sc4_%d" % i) for i in range(4)]
                qT = T("qT", [128, 4, 512], BF16, es=esu)
                kTt = T("kT", [128, 4, 512], BF16, es=esu)
                B_qk = Buf("qk")
                k_tm = T("k_tm", [128, 4, 4, 128], BF16, es=esu)
                B_ktm = [Buf("ktm%d" % i) for i in range(4)]
                v_tm = T("v_tm", [128, 4, 1024], BF16, es=esu)
                B_v = [Buf("v%d" % i) for i in range(4)]
                og = T("og", [128, 4, 1024], BF16, es=esu)
                B_og = [Buf("og%d" % i) for i in range(4)]
                sz = T("sz", [128, 4, 1024], BF16, es=esu)
                B_sz = [Buf("sz%d" % i) for i in range(4)]
                BCT = T("BCT", [128, 8, 512], BF16, es=esu)
                B_bct = Buf("BCT")
                x_tm = T("x_tm", [128, 4, 1024], BF16, es=esu)
                B_xtm = [Buf("xtm%d" % i) for i in range(4)]
                B_tm = T("B_tm", [128, 4, 4, 128], BF16, es=esu)
                B_Btm = [Buf("Btm%d" % i) for i in range(4)]
                gml = T("gml", [128, 1024], es=esu)
                gsn = T("gsn", [128, 1024], es=esu)
                B_gg = Buf("gg")
                P.dma(lambda e: e.dma_start(out=gml[:], in_=g_ml[0:1, :].to_broadcast([128, 1024])), writes=[B_gg])
                P.dma(lambda e: e.dma_start(out=gsn[:], in_=g_sn[0:1, :].to_broadcast([128, 1024])), writes=[B_gg])
                with contextlib.ExitStack() as es:
                    I = alloc_inproj(es, nbuf=3)
                    uT = T("uT", [128, KT, ntok], BF16, es=es)
                    B_uT = [Buf("uT%d" % i) for i in range(ntok // 128)]
                    pre = T("pre_u", [128, 640], es=es)
                    B_pre = Buf("pre_u")
                    cacc = T("cacc_u", [128, 512], es=es)
                    B_cacc = Buf("cacc_u")
                    xTt = [T("xTt%d" % i, [128, 512], BF16, es=es) for i in range(2)]
                    B_xTt = [Buf("xTt0"), Buf("xTt1")]
                    sig = T("sigt", [128, 512], es=es)
                    B_sig = Buf("sigt")
                    pg4 = T("pg4", [128, 4, 48], es=es)
                    B_pg4 = Buf("pg4")
                    hnd = uT_s1(I, src[0:128, :])
                    for tt in range(4):
                        if tt < 3:
                            nxt = uT_s1(I, src[(tt + 1) * 128:(tt + 2) * 128, :])
                        else:
                            nxt = uT_s1(I, xh[:, :]) if su == 1 else None
                        uT_s2(I, hnd, ci, 0, uT, tt * 128, B_uT[tt], want_f32=True)
                        gate_preact(I, ci, pg4, tt, B_pg4)
                        hnd = nxt
                    if su == 1:
                        uT_s2(I, hnd, ci, 0, uT, 512, B_uT[4])
                    prep_scalars_batched(4, pg4, B_pg4, sc4, B_sc4, es, cacc, B_cacc)
                    B_own = B_uT[0:4]
                    for (c0, dstT, scale) in ((0, qT, 128.0 ** -0.5), (512, kTt, 1.0)):
                        wt, wb = load_wblock(I, win_cols(c0, 512))
                        for h in range(4):
                            po, pob = next_big()
                            for kt in range(KT):
                                P.pe(lambda e, kt=kt, h=h, wt=wt, po=po: e.matmul(po, lhsT=wt[:, kt, h * 128:(h + 1) * 128], rhs=uT[:, kt, 0:512],
                                                                              start=(kt == 0), stop=(kt == KT - 1)), reads=B_own + [wb], writes=[pob])
                            P.act(lambda e, h=h, po=po, dstT=dstT, scale=scale: e.activation(out=dstT[:, h, :], in_=po, func=AF.Copy, scale=scale),
                                  reads=[pob], writes=[B_qk])
                    for c in range(4):
                        pt, pb = next_tr()
                        ptb = pt[:].bitcast(BF16)
                        for h in range(4):
                            P.pe(lambda e, c=c, h=h, ptb=ptb: e.transpose(ptb[:, h * 128:(h + 1) * 128], kTt[:, h, c * 128:(c + 1) * 128], identb[:]),
                                 reads=[B_qk, B_c], writes=[pb])
                        P.dve(lambda e, c=c, ptb=ptb: e.tensor_copy(out=k_tm[:, c, :, :], in_=ptb[:, 0:512].rearrange("p (h d) -> p h d", h=4)),
                              reads=[pb], writes=[B_ktm[c]])

                    def tm_block(c0, evac):
                        wt, wb = load_wblock(I, win_cols(c0, 512))
                        for c in range(4):
                            po, pob = next_big()
                            for kt in range(KT):
                                P.pe(lambda e, kt=kt, c=c, wt=wt, po=po: e.matmul(po, lhsT=uT[:, kt, c * 128:(c + 1) * 128], rhs=wt[:, kt, :],
                                                                              start=(kt == 0), stop=(kt == KT - 1)), reads=[B_uT[c], wb], writes=[pob])
                            evac(c, po, pob)

                    for half in range(2):
                        def ev_v(c, po, pob, half=half):
                            P.act(lambda e: e.copy(out=v_tm[:, c, half * 512:(half + 1) * 512], in_=po), reads=[pob], writes=[B_v[c]])
                        tm_block(1024 + half * 512, ev_v)
                    for half in range(2):
                        def ev_o(c, po, pob, half=half):
                            P.act(lambda e: e.activation(out=sig[:], in_=po, func=AF.Sigmoid), reads=[pob], writes=[B_sig])
                            P.dve(lambda e: e.tensor_tensor(out=og[:, c, half * 512:(half + 1) * 512], in0=sig[:], in1=gml[:, half * 512:(half + 1) * 512], op=ALU.mult),
                                  reads=[B_sig, B_gg], writes=[B_og[c]])
                        tm_block(2048 + half * 512, ev_o)
                    for half in range(2):
                        def ev_z(c, po, pob, half=half):
                            P.act(lambda e: e.activation(out=sz[:, c, half * 512:(half + 1) * 512], in_=po, func=AF.Silu), reads=[pob], writes=[B_sz[c]])
                        tm_block(3088 + half * 512, ev_z)
                    if su == 0:
                        cen_in, cen_out, taps = seq_taps(pre[:, 0:512], cacc[:])
                    else:
                        cen_in, cen_out, taps = grid_taps(pre[:], cacc[:], 8)
                    cacc_b = [cacc, T("cacc_u2", [128, 512], es=es)]
                    B_cacc_b = [B_cacc, Buf("cacc_u2")]
                    pre_b = [pre, T("pre_u2", [128, 640], es=es)]
                    B_pre_b = [B_pre, Buf("pre_u2")]
                    if su == 0:
                        tapsets = [seq_taps(pre_b[i_][:, 0:512], cacc_b[i_][:]) for i_ in range(2)]
                    else:
                        tapsets = [grid_taps(pre_b[i_][:], cacc_b[i_][:], 8) for i_ in range(2)]

                    def x_front(ct, cc, wt, wb):
                        pr, bpr = pre_b[ct % 2], B_pre_b[ct % 2]
                        cen_in, cen_out, taps = tapsets[ct % 2]
                        po, pob = next_big()
                        for kt in range(KT):
                            P.pe(lambda e, kt=kt, po=po: e.matmul(po, lhsT=wt[:, kt, cc * 128:(cc + 1) * 128], rhs=uT[:, kt, 0:512],
                                                                  start=(kt == 0), stop=(kt == KT - 1)), reads=B_own + [wb], writes=[pob])
                        if su == 0:
                            P.act(lambda e, po=po: e.copy(out=pr[:, 0:512], in_=po), reads=[pob], writes=[bpr])
                        else:
                            P.act(lambda e, po=po: e.copy(out=pr[:, 64:576], in_=po), reads=[pob], writes=[bpr])
                            po2, pob2 = next_big()
                            for kt in range(KT):
                                P.pe(lambda e, kt=kt, po2=po2: e.matmul(po2[:, 0:128], lhsT=wt[:, kt, cc * 128:(cc + 1) * 128], rhs=uT[:, kt, 512:640],
                                                                        start=(kt == 0), stop=(kt == KT - 1)), reads=[B_uT[4], wb], writes=[pob2])
                            P.dve(lambda e, po2=po2: e.tensor_scalar(out=pr[:, 0:64], in0=po2[:, 0:64], scalar1=flags[:, 64:65], scalar2=None, op0=ALU.mult),
                                  reads=[pob2, B_c], writes=[bpr])
                            P.dve(lambda e, po2=po2: e.tensor_scalar(out=pr[:, 576:640], in0=po2[:, 64:128], scalar1=flags[:, 65:66], scalar2=None, op0=ALU.mult),
                                  reads=[pob2, B_c], writes=[bpr])
                        conv_apply(cen_in, cen_out, taps, ct, bpr, B_cacc_b[ct % 2], cacc_b[ct % 2][:], None, None, do_silu=False)

                    def x_back(ct):
                        if ct < 8:
                            P.act(lambda e: e.activation(out=xTt[ct % 2][:], in_=cacc_b[ct % 2][:], func=AF.Silu), reads=[B_cacc_b[ct % 2]], writes=[B_xTt[ct % 2]])
                        else:
                            P.act(lambda e: e.activation(out=BCT[:, ct - 8, :], in_=cacc_b[ct % 2][:], func=AF.Silu), reads=[B_cacc_b[ct % 2]], writes=[B_bct])
                        if ct < 8:
                            xi = ct % 2
                            pt, pb = next_tr()
                            ptb = pt[:].bitcast(BF16)
                            for c in range(4):
                                P.pe(lambda e, c=c, ptb=ptb: e.transpose(ptb[:, c * 128:(c + 1) * 128], xTt[xi][:, c * 128:(c + 1) * 128], identb[:]),
                                     reads=[B_xTt[xi], B_c], writes=[pb])
                            P.act(lambda e, ptb=ptb: e.copy(out=x_tm[:, :, ct * 128:(ct + 1) * 128], in_=ptb[:, 0:512].rearrange("p (c n) -> p c n", c=4)),
                                  reads=[pb], writes=B_xtm)
                        elif ct < 12:
                            g = ct - 8
                            pt, pb = next_tr()
                            ptb = pt[:].bitcast(BF16)
                            for c in range(4):
                                P.pe(lambda e, c=c, ptb=ptb: e.transpose(ptb[:, c * 128:(c + 1) * 128], BCT[:, g, c * 128:(c + 1) * 128], identb[:]),
                                     reads=[B_bct, B_c], writes=[pb])
                            P.act(lambda e, ptb=ptb: e.copy(out=B_tm[:, :, g, :], in_=ptb[:, 0:512].rearrange("p (c n) -> p c n", c=4)),
                                  reads=[pb], writes=B_Btm)

                    pend = None
                    for blk_i in range(4):
                        wt, wb = load_wblock(I, win_cols(4112 + blk_i * 512, 512))
                        for cc in range(4):
                            ct = blk_i * 4 + cc
                            x_front(ct, cc, wt, wb)
                            if pend is not None:
                                x_back(pend)
                            pend = ct
                    x_back(pend)
                P.barrier()
                with contextlib.ExitStack() as es:
                    M = alloc_mixer(es)
                    M2 = alloc_mixer(es)
                    hsum = T("hsum", [128, 2, 1024], es=es)
                    B_hs = [Buf("hs0"), Buf("hs1")]
                    ysum = T("ysum", [128, 2, 1024], es=es)
                    B_ys = [Buf("ys0"), Buf("ys1")]
                    cat0 = T("cat0", [128, D], es=es)
                    cat = [cat0, cat0]
                    B_cat0 = Buf("cat0")
                    B_cat = [B_cat0, B_cat0]
                    cTo0 = T("cTout0", [128, KT * 128], BF16, es=es)
                    cTout = [cTo0, cTo0]
                    B_cTo0 = Buf("cTout0")
                    B_cTout = [B_cTo0, B_cTo0]
                    wmh = [T("wmh%d" % i, [128, KT, 256], BF16, es=es) for i in range(2)]
                    B_wmh = [Buf("wmh0"), Buf("wmh1")]
                    k_end = 32 if su == 1 else 48
                    if mod_k[0] < k_end:
                        mod_load(mod_k[0] * 256, 256, wmh[mod_k[0] % 2], B_wmh[mod_k[0] % 2])

                    def mod_bg():
                        k = mod_k[0]
                        if k >= k_end:
                            return
                        if k + 1 < k_end:
                            mod_load((k + 1) * 256, 256, wmh[(k + 1) % 2], B_wmh[(k + 1) % 2])
                        if mod_pend[0] is not None:
                            mod_pend[0]()
                        mod_pend[0] = mod_compute(k * 256, 256, wmh[k % 2], B_wmh[k % 2], defer=True, small_psum=True)
                        mod_k[0] += 1
                    sf = new_state("sf", es)
                    sb = new_state("sb", es)
                    nt4 = [T("nt4_%d" % i, [4, 128], es=es) for i in range(2)]
                    B_nt4 = [Buf("nt4a"), Buf("nt4b")]
                    sout = [M.htmp, M2.htmp]
                    B_sout = [M.B_ht, M2.B_ht]
                    htmp, B_ht = M.htmp, M.B_ht
                    def chain(Mx, un, d, st):
                        chunks = (2 * un, 2 * un + 1)
                        order = chunks if d == 0 else chunks[::-1]
                        for n_, c in enumerate(order):
                            lc = c - 2 * un
                            tsl = slice(c * 128, (c + 1) * 128)
                            sc = sc4[:, c, :]
                            first = (n_ == 0) if d == 0 else (n_ == 0)
                            first = (d == 0 and lc == 0) or (d == 1 and lc == 1)
                            mlstm_chain_scalars(Mx, sc, B_sc4[c], d, st, qT, B_qk, tsl)
                            yield
                            make_vp(Mx, v_tm[:, c, :].rearrange("p (h e) -> p h e", h=4), B_v[c])
                            mlstm_out(Mx, d, st, qT, kTt, B_qk, tsl, hsum[:, lc, :], B_hs[lc], first)
                            yield
                            ssd_gt(Mx, BCT[:, 0:4, :], BCT[:, 4:8, :], B_bct, tsl)
                            yield from ssd_out(Mx, sc, B_sc4[c], d, st, BCT[:, 4:8, :], B_bct, tsl, x_tm[:, c, :], B_xtm[c], ysum[:, lc, :], B_ys[lc], first)
                            yield
                            if su == 0 or n_ == 0:
                                mlstm_update(Mx, st, k_tm[:, c, :, :], B_ktm[c])
                                yield
                                ssd_update(Mx, sc, B_sc4[c], d, st, x_tm[:, c, :], B_xtm[c], B_tm[:, c, :, :], B_Btm[c])
                                yield
                        if su == 0:
                            seq = un
                            P.dma(lambda e, seq=seq, d=d, st=st: e.dma_start(out=o_c[seq, d].rearrange("h d e -> d h e"), in_=st.Cn[:]), reads=[st.B])
                            pt, pb = next_sm()
                            P.pe(lambda e, pt=pt, st=st: e.transpose(pt[0:4, 0:128], st.Nn[:], ident[:]), reads=[st.B, B_c], writes=[pb])
                            P.dve(lambda e, pt=pt: e.tensor_copy(out=nt4[d][:], in_=pt[0:4, 0:128]), reads=[pb], writes=[B_nt4[d]])
                            P.dma(lambda e, seq=seq, d=d: e.dma_start(out=o_n[seq, d], in_=nt4[d][:]), reads=[B_nt4[d]])
                            P.dma(lambda e, seq=seq, d=d, st=st: e.dma_start(out=o_m[seq, d:d + 1, :], in_=st.m[0:1, :]), reads=[st.B])
                            for a in range(8):
                                pt, pb = next_sm()
                                P.pe(lambda e, a=a, pt=pt, st=st: e.transpose(pt[:, 0:128], st.Sf[:, a * 128:(a + 1) * 128], ident[:]), reads=[st.B, B_c], writes=[pb])
                                P.act(lambda e, a=a, pt=pt: e.copy(out=sout[d][:, a * 128:(a + 1) * 128], in_=pt[:, 0:128]), reads=[pb], writes=[B_sout[d]])
                            P.dma(lambda e, seq=seq, d=d: e.dma_start(out=o_s[seq, d].rearrange("(a p) n -> p a n", p=128),
                                                                     in_=sout[d][:].rearrange("p (a n) -> p a n", a=8)), reads=[B_sout[d]])

                    for un in range(2):
                        chunks = (2 * un, 2 * un + 1)
                        if su == 0:
                            state_zero(sf)
                            state_zero(sb)
                        else:
                            state_load_scr(sf, 0 + un)
                            state_load_scr(sb, 2 + un)
                        gens = [chain(M, un, 0, sf), chain(M2, un, 1, sb)]
                        nstep = 0
                        while gens:
                            for g_ in list(gens):
                                try:
                                    next(g_)
                                except StopIteration:
                                    gens.remove(g_)
                                nstep += 1
                                if nstep % 5 == 0:
                                    mod_bg()
                        for c in chunks:
                            lc = c - 2 * un
                            tt = su * 4 + c
                            hs = hsum[:, lc, :]
                            ct_, B_ct = cat[lc], B_cat[lc]
                            for h in range(4):
                                P.act(lambda e, h=h, hs=hs: e.activation(out=sq_junk[:, 0:256], in_=hs[:, h * 256:(h + 1) * 256], func=AF.Square, accum_out=stat[:, 8 + h:9 + h]),
                                      reads=[B_hs[lc]], writes=[B_junk, B_stat])
                            rstd_of(stat[:, 8:12], stat[:, 12:16], 256)
                            P.dve(lambda e, hs=hs, ct_=ct_: e.tensor_tensor(out=ct_[:, 0:1024].rearrange("p (h e) -> p h e", h=4), in0=hs.rearrange("p (h e) -> p h e", h=4),
                                                                           in1=stat[:, 12:16].unsqueeze(2).to_broadcast([128, 4, 256]), op=ALU.mult),
                                  reads=[B_hs[lc], B_stat], writes=[B_ct])
                            P.dve(lambda e, c=c, ct_=ct_: e.tensor_tensor(out=ct_[:, 0:1024], in0=ct_[:, 0:1024], in1=og[:, c, :], op=ALU.mult), reads=[B_ct, B_og[c]], writes=[B_ct])
                            ysl = ysum[:, lc, :]
                            P.dve(lambda e, c=c: e.tensor_tensor(out=htmp[:].rearrange("p (j q) -> p j q", j=16), in0=x_tm[:, c, :].rearrange("p (j q) -> p j q", j=16),
                                                                 in1=rowp[:, R_DSK:R_DSK + 16].unsqueeze(2).to_broadcast([128, 16, 64]), op=ALU.mult),
                                  reads=[B_xtm[c], B_c], writes=[B_ht])
                            P.dve(lambda e, ysl=ysl: e.tensor_tensor(out=ysl, in0=ysl, in1=htmp[:], op=ALU.add), reads=[B_ys[lc], B_ht], writes=[B_ys[lc]])
                            P.dve(lambda e, ysl=ysl, c=c: e.tensor_tensor(out=ysl, in0=ysl, in1=sz[:, c, :], op=ALU.mult), reads=[B_ys[lc], B_sz[c]], writes=[B_ys[lc]])
                            P.act(lambda e, ysl=ysl: e.activation(out=sq_junk[:, 0:1024], in_=ysl, func=AF.Square, accum_out=stat[:, 16:17]),
                                  reads=[B_ys[lc]], writes=[B_junk, B_stat])
                            rstd_of(stat[:, 16:17], stat[:, 17:18], 1024)
                            P.dve(lambda e, ysl=ysl, ct_=ct_: e.scalar_tensor_tensor(out=ct_[:, 1024:2048], in0=ysl, scalar=stat[:, 17:18], in1=gsn[:], op0=ALU.mult, op1=ALU.mult),
                                  reads=[B_ys[lc], B_stat, B_gg], writes=[B_ct])
                            cTo, B_cTo = cTout[lc], B_cTout[lc]
                            for g4 in range(4):
                                pt, pb = next_tr()
                                for j in range(4):
                                    kt = g4 * 4 + j
                                    P.pe(lambda e, kt=kt, j=j, pt=pt, ct_=ct_: e.transpose(pt[:, j * 128:(j + 1) * 128], ct_[:, kt * 128:(kt + 1) * 128], ident[:]),
                                         reads=[B_ct, B_c], writes=[pb])
                                P.act(lambda e, g4=g4, pt=pt, cTo=cTo: e.copy(out=cTo[:, g4 * 512:(g4 + 1) * 512], in_=pt[:, 0:512]), reads=[pb], writes=[B_cTo])
                            P.dma(lambda e, tt=tt, cTo=cTo: e.dma_start(out=catTs[tt], in_=cTo[:]), reads=[B_cTo], writes=[B_cats[tt]])
                    while mod_k[0] < k_end:
                        mod_bg()
                    if mod_pend[0] is not None:
                        mod_pend[0]()
                        mod_pend[0] = None
            P.barrier()

        super_unit(1)
        check("SU1")
        super_unit(0)
        check("SU0")
        mod_cols(32, 64)
        mod_consts(2, C_GMLP, 64, 48)

        with contextlib.ExitStack() as escd:
            u2T = T("u2T", [128, KT, 1024], BF16, es=escd)
            B_u2T = [Buf("u2T%d" % i) for i in range(8)]
            with contextlib.ExitStack() as es:
                I = NS()
                I.xt = [T("xt%d" % i, [128, D], es=es) for i in range(2)]
                I.B_xt = [Buf("xt0"), Buf("xt1")]
                I.xrot = [0]
                I.uTf, I.B_uTf = None, None
                wo = T("wo", [128, KT, D], BF16, es=es)
                B_wo = Buf("wo")
                G1 = T("G1", [128, 2, D], es=es)
                B_G1 = Buf("G1")
                x1 = [T("x1_%d" % i, [128, D], es=es) for i in range(2)]
                B_x1 = [Buf("x1a"), Buf("x1b")]
                cTt = [T("cTt%d" % i, [128, KT, 128], BF16, es=es) for i in range(2)]
                B_cTt = [Buf("cTt0"), Buf("cTt1")]
                B_wo4 = [Buf("wo%d" % i) for i in range(4)]
                for cbk in range(4):
                    P.dma(lambda e, cbk=cbk: e.dma_start(out=wo[:, :, cbk * 512:(cbk + 1) * 512],
                                                         in_=w_out[:, cbk * 512:(cbk + 1) * 512].rearrange("(kt p) c -> p kt c", p=128)), writes=[B_wo4[cbk]], eng="pool")
                P.dma(lambda e: e.dma_start(out=x1[0][:], in_=g_post_mix[0:1, :].to_broadcast([128, D])), writes=[B_x1[0]])
                for ci in range(2):
                    P.dma(lambda e, ci=ci: e.dma_start(out=G1[:, ci, :], in_=modscr[ci:ci + 1, 2 * D:3 * D].to_broadcast([128, D])), reads=[B_modscr], writes=[B_G1])
                    P.dve(lambda e, ci=ci: e.tensor_tensor(out=G1[:, ci, :], in0=G1[:, ci, :], in1=x1[0][:], op=ALU.mult), reads=[B_G1, B_x1[0]], writes=[B_G1])
                def c_front(tt):
                    ci = 0 if tt < 4 else 1
                    li = tt % 2
                    P.dma(lambda e, li=li, tt=tt: e.dma_start(out=cTt[li][:].rearrange("p k t -> p (k t)"), in_=catTs[tt]), reads=[B_cats[tt]], writes=[B_cTt[li]])
                    for cbk in range(4):
                        for kt in range(KT):
                            P.pe(lambda e, kt=kt, cbk=cbk, li=li: e.matmul(pbig[:, cbk * 512:(cbk + 1) * 512], lhsT=cTt[li][:, kt, :],
                                                                           rhs=wo[:, kt, cbk * 512:(cbk + 1) * 512], start=(kt == 0), stop=(kt == KT - 1)),
                                 reads=[B_cTt[li], B_wo4[cbk]], writes=[PB[cbk]])

                def c_front_b(tt):
                    for cbk in range(4):
                        P.dve(lambda e, cbk=cbk, tt=tt: e.tensor_copy(out=x1[tt % 2][:, cbk * 512:(cbk + 1) * 512], in_=pbig[:, cbk * 512:(cbk + 1) * 512]),
                              reads=[PB[cbk]], writes=[B_x1[tt % 2]])

                def c_back(tt):
                    ci = 0 if tt < 4 else 1
                    xi = tt % 2
                    P.act(lambda e, xi=xi: e.activation(out=sq_junk[:], in_=x1[xi][:], func=AF.Square, accum_out=stat[:, 20:21]), reads=[B_x1[xi]], writes=[B_junk, B_stat])
                    rstd_of(stat[:, 20:21], stat[:, 21:22], D)
                    src_rows = xp[tt * 128:(tt + 1) * 128, :] if tt < 4 else xo[(tt - 4) * 128:(tt - 3) * 128, :]
                    xr, bxr = I.xt[I.xrot[0] % 2], I.B_xt[I.xrot[0] % 2]
                    I.xrot[0] += 1
                    P.dma(lambda e, xr=xr, src_rows=src_rows: e.dma_start(out=xr[:], in_=src_rows), writes=[bxr])
                    P.dve(lambda e, xi=xi, ci=ci: e.scalar_tensor_tensor(out=x1[xi][:], in0=x1[xi][:], scalar=stat[:, 21:22], in1=G1[:, ci, :], op0=ALU.mult, op1=ALU.mult),
                          reads=[B_x1[xi], B_stat, B_G1], writes=[B_x1[xi]])
                    P.dve(lambda e, xi=xi, xr=xr: e.tensor_tensor(out=x1[xi][:], in0=x1[xi][:], in1=xr[:], op=ALU.add), reads=[B_x1[xi], bxr], writes=[B_x1[xi]])
                    P.dma(lambda e, xi=xi, tt=tt: e.dma_start(out=x1s[tt * 128:(tt + 1) * 128, :], in_=x1[xi][:]), reads=[B_x1[xi]], writes=[B_x1s[tt]])
                    make_uT(I, None, ci, 2, u2T, tt * 128, B_u2T[tt], src_sb=x1[xi], B_src=B_x1[xi])

                for tt in range(9):
                    if tt < 8:
                        c_front(tt)
                    if tt >= 1:
                        c_back(tt - 1)
                    if tt < 8:
                        c_front_b(tt)
            P.barrier()
            check("C")

            with contextlib.ExitStack() as es:
                acc = T("acc", [128, 8, D], es=es)
                B_acc = [Buf("acc%d" % i) for i in range(8)]
                with contextlib.ExitStack() as es2:
                    hdn = T("hdn", [128, 4, 1024], BF16, es=es2)
                    B_hdn = Buf("hdn")
                    rl = [T("rl%d" % i, [128, 512], es=es2) for i in range(2)]
                    B_rl = [Buf("rl0"), Buf("rl1")]
                    w1 = [T("w1_%d" % i, [128, KT, 512], BF16, es=es2) for i in range(2)]
                    B_w1 = [Buf("w1a"), Buf("w1b")]
                    w2 = [T("w2_%d" % i, [128, 4, D], BF16, es=es2) for i in range(2)]
                    B_w2 = [Buf("w2a"), Buf("w2b")]
                    rli = [0]
                    for fb in range(16):
                        wi = fb % 2
                        P.dma(lambda e, wi=wi, fb=fb: e.dma_start(out=w1[wi][:], in_=w_mlp_in[:, fb * 512:(fb + 1) * 512].rearrange("(kt p) c -> p kt c", p=128)),
                              writes=[B_w1[wi]], eng="pool")
                        P.dma(lambda e, wi=wi, fb=fb: e.dma_start(out=w2[wi][:], in_=w_mlp_out[fb * 512:(fb + 1) * 512, :].rearrange("(ft p) c -> p ft c", p=128)),
                              writes=[B_w2[wi]], eng="pool")
                        for ft in range(4):
                            for half in range(2):
                                pt, pb = (next_tr() if (ft * 2 + half) % 2 == 0 else next_sm())
                                for kt in range(KT):
                                    P.pe(lambda e, kt=kt, ft=ft, half=half, wi=wi, pt=pt: e.matmul(pt[:], lhsT=w1[wi][:, kt, ft * 128:(ft + 1) * 128],
                                                                                                 rhs=u2T[:, kt, half * 512:(half + 1) * 512], start=(kt == 0), stop=(kt == KT - 1)),
                                         reads=B_u2T[4 * half:4 * half + 4] + [B_w1[wi]], writes=[pb])
                                ri = rli[0] % 2
                                rli[0] += 1
                                P.act(lambda e, ri=ri, pt=pt: e.activation(out=rl[ri][:], in_=pt[:], func=AF.Relu), reads=[pb], writes=[B_rl[ri]])
                                P.pool(lambda e, ri=ri, ft=ft, half=half: e.tensor_tensor(out=hdn[:, ft, half * 512:(half + 1) * 512], in0=rl[ri][:], in1=rl[ri][:], op=ALU.mult),
                                       reads=[B_rl[ri]], writes=[B_hdn])
                        for tt in range(8):
                            for cbk in range(4):
                                for ft in range(4):
                                    P.pe(lambda e, ft=ft, cbk=cbk, tt=tt, wi=wi: e.matmul(pbig[:, cbk * 512:(cbk + 1) * 512], lhsT=hdn[:, ft, tt * 128:(tt + 1) * 128],
                                                                                      rhs=w2[wi][:, ft, cbk * 512:(cbk + 1) * 512], start=(ft == 0), stop=(ft == 3)),
                                         reads=[B_hdn, B_w2[wi]], writes=[PB[cbk]])
                                if fb == 0:
                                    P.dve(lambda e, tt=tt, cbk=cbk: e.tensor_copy(out=acc[:, tt, cbk * 512:(cbk + 1) * 512], in_=pbig[:, cbk * 512:(cbk + 1) * 512]),
                                          reads=[PB[cbk]], writes=[B_acc[tt]])
                                else:
                                    P.dve(lambda e, tt=tt, cbk=cbk: e.tensor_tensor(out=acc[:, tt, cbk * 512:(cbk + 1) * 512], in0=pbig[:, cbk * 512:(cbk + 1) * 512],
                                                                                   in1=acc[:, tt, cbk * 512:(cbk + 1) * 512], op=ALU.add),
                                          reads=[PB[cbk], B_acc[tt]], writes=[B_acc[tt]])
                P.barrier()
                with contextlib.ExitStack() as es2:
                    G2 = T("G2", [128, 2, D], es=es2)
                    B_G2 = Buf("G2")
                    xr2 = [T("xr2_%d" % i, [128, D], es=es2) for i in range(2)]
                    B_xr2 = [Buf("xr2a"), Buf("xr2b")]
                    P.dma(lambda e: e.dma_start(out=xr2[0][:], in_=g_post_mlp[0:1, :].to_broadcast([128, D])), writes=[B_xr2[0]])
                    for ci in range(2):
                        P.dma(lambda e, ci=ci: e.dma_start(out=G2[:, ci, :], in_=modscr[ci:ci + 1, 5 * D:6 * D].to_broadcast([128, D])), reads=[B_modscr], writes=[B_G2])
                        P.dve(lambda e, ci=ci: e.tensor_tensor(out=G2[:, ci, :], in0=G2[:, ci, :], in1=xr2[0][:], op=ALU.mult), reads=[B_G2, B_xr2[0]], writes=[B_G2])
                    for tt in range(8):
                        ci = 0 if tt < 4 else 1
                        P.act(lambda e, tt=tt: e.activation(out=sq_junk[:], in_=acc[:, tt, :], func=AF.Square, accum_out=stat[:, 24:25]), reads=[B_acc[tt]], writes=[B_junk, B_stat])
                        rstd_of(stat[:, 24:25], stat[:, 25:26], D)
                        xi = tt % 2
                        P.dma(lambda e, xi=xi, tt=tt: e.dma_start(out=xr2[xi][:], in_=x1s[tt * 128:(tt + 1) * 128, :]), reads=[B_x1s[tt]], writes=[B_xr2[xi]])
                        P.dve(lambda e, tt=tt, ci=ci: e.scalar_tensor_tensor(out=acc[:, tt, :], in0=acc[:, tt, :], scalar=stat[:, 25:26], in1=G2[:, ci, :], op0=ALU.mult, op1=ALU.mult),
                              reads=[B_acc[tt], B_stat, B_G2], writes=[B_acc[tt]])
                        P.dve(lambda e, tt=tt, xi=xi: e.tensor_tensor(out=acc[:, tt, :], in0=acc[:, tt, :], in1=xr2[xi][:], op=ALU.add), reads=[B_acc[tt], B_xr2[xi]], writes=[B_acc[tt]])
                        dst = yp[tt * 128:(tt + 1) * 128, :] if tt < 4 else ys[(tt - 4) * 128:(tt - 3) * 128, :]
                        P.dma(lambda e, tt=tt, dst=dst: e.dma_start(out=dst, in_=acc[:, tt, :]), reads=[B_acc[tt]])
        P.emit()
    return nc, P


def build_debug(stop_after):
    try:
        return build_program(stop_after)
    except _Stop:
        pass


_CACHE = {}


def _prep_inputs(inp):
    f = lambda a: np.ascontiguousarray(np.asarray(a, dtype=np.float32))
    x_prompt = f(inp["x_prompt"])
    x_sample = f(inp["x_sample"])
    shared = {
        "w_mod": f(inp["w_mod"][0]), "b_mod": f(inp["b_mod"][0]).reshape(1, 6 * D),
        "g_pre_mix": f(inp["g_pre_mix"][0]).reshape(16, 128), "g_post_mix": f(inp["g_post_mix"][0]).reshape(1, D),
        "w_in": f(inp["w_in"][0]), "b_igate": f(inp["b_igate"][0]).reshape(1, 8), "b_fgate": f(inp["b_fgate"][0]).reshape(1, 8),
        "conv_w": f(inp["conv_w"][0]).reshape(144, 128), "conv_b": f(inp["conv_b"][0]).reshape(16, 128),
        "dt_bias": f(inp["dt_bias"][0]).reshape(1, 32), "a_log": f(inp["a_log"][0]).reshape(1, 32),
        "d_skip": f(inp["d_skip"][0]).reshape(1, 16), "g_mlstm_norm": f(inp["g_mlstm_norm"][0]).reshape(1, 1024),
        "g_ssd_norm": f(inp["g_ssd_norm"][0]).reshape(1, 1024), "w_out": f(inp["w_out"][0]),
        "g_pre_mlp": f(inp["g_pre_mlp"][0]).reshape(16, 128), "g_post_mlp": f(inp["g_post_mlp"][0]).reshape(1, D),
        "w_mlp_in": f(inp["w_mlp_in"][0]), "w_mlp_out": f(inp["w_mlp_out"][0]),
    }
    c = f(inp["c"])
    c_ctx = f(inp["c_ctx"])
    maps = []
    for r in range(8):
        b, q = r // 4, r % 4
        xs_b = x_sample[b]
        xh = np.zeros((128, D), np.float32)
        if q > 0:
            xh[0:64] = xs_b[512 * q - 64:512 * q]
        if q < 3:
            xh[64:128] = xs_b[512 * q + 512:512 * q + 576]
        fl = np.zeros((1, NFLAG), np.float32)
        fl[0, 0 + 4 * q] = 1.0
        fl[0, 16 + 4 * q + 2] = 1.0
        fl[0, 32 + 4 * q + 1] = 1.0
        fl[0, 48 + 4 * q + 3] = 1.0
        fl[0, 64] = 1.0 if q > 0 else 0.0
        fl[0, 65] = 1.0 if q < 3 else 0.0
        m = dict(shared)
        m.update({
            "xp": np.ascontiguousarray(x_prompt[2 * r:2 * r + 2].reshape(512, D)),
            "xo": np.ascontiguousarray(xs_b[512 * q:512 * q + 512]),
            "xh": xh,
            "xf": np.ascontiguousarray(xs_b),
            "cv": np.ascontiguousarray(np.stack([c_ctx, c[b]], 0).reshape(32, 128)),
            "st_c": f(inp["state_mlstm_c"][b, 0]),
            "st_n": f(inp["state_mlstm_n"][b, 0]),
            "st_m": f(inp["state_mlstm_m"][b, 0]).reshape(1, 8),
            "st_s": f(inp["state_ssd"][b, 0]).reshape(2, 1024, 128),
            "flags": fl,
        })
        maps.append(m)
    return maps


def kernel(**inp):
    if "nc" not in _CACHE:
        _CACHE["nc"] = build_program()[0]
    nc = _CACHE["nc"]
    maps = _prep_inputs(inp)
    res = run_bass_kernel_spmd(nc, maps, core_ids=list(range(8)))
    y_prompt = np.zeros((16, 256, D), np.float32)
    y_sample = np.zeros((2, 2048, D), np.float32)
    new_c = np.zeros((16, 1, 2, 4, 128, 256), np.float32)
    new_n = np.zeros((16, 1, 2, 4, 128), np.float32)
    new_m = np.zeros((16, 1, 2, 4), np.float32)
    new_s = np.zeros((16, 1, 2, 16, 64, 128), np.float32)
    for r in range(8):
        o = res.results[r]
        b, q = r // 4, r % 4
        y_prompt[2 * r:2 * r + 2] = np.asarray(o["yp"]).reshape(2, 256, D)
        y_sample[b, 512 * q:512 * q + 512] = np.asarray(o["ys"])
        new_c[2 * r:2 * r + 2, 0] = np.asarray(o["o_c"])
        new_n[2 * r:2 * r + 2, 0] = np.asarray(o["o_n"])
        new_m[2 * r:2 * r + 2, 0] = np.asarray(o["o_m"])
        new_s[2 * r:2 * r + 2, 0] = np.asarray(o["o_s"]).reshape(2, 2, 16, 64, 128)
    return (y_prompt, y_sample, new_c, new_n, new_m, new_s)
```

```python
import contextlib
import numpy as np
import concourse.bass as bass
import concourse.mybir as mybir
from concourse.bass_utils import run_bass_kernel_spmd

F32 = mybir.dt.float32
BF16 = mybir.dt.bfloat16
AF = mybir.ActivationFunctionType
ALU = mybir.AluOpType
AX = mybir.AxisListType

D = 2048
KT = 16
EPS = 1e-6
NEG = -30000.0
NFLAG = 80
ESZ = 1024 + 4 + 4 + 1024


class Buf:
    __slots__ = ("name", "last_w", "readers", "excl")

    def __init__(self, name="", excl=False):
        self.name = name
        self.last_w = None
        self.readers = []
        self.excl = excl


class Op:
    __slots__ = ("idx", "eng", "fn", "deps", "is_dma", "sig", "sem", "val", "pre")

    def __init__(self, idx, eng, fn, is_dma):
        self.idx = idx
        self.eng = eng
        self.fn = fn
        self.deps = set()
        self.is_dma = is_dma
        self.sig = False
        self.sem = None
        self.val = 0
        self.pre = None


ENGS = ("pe", "act", "dve", "pool", "sp")
N_DMA_SEMS = 48


class Prog:
    def __init__(self, nc):
        self.nc = nc
        self.ops = []
        self.last_on = {}
        self.dma_since_bar = []

    def add(self, eng, fn, reads=(), writes=(), dma=False):
        op = Op(len(self.ops), eng, fn, dma)
        for r in reads:
            if r.last_w is not None:
                op.deps.add(r.last_w)
            if r.excl:
                for rd in r.readers:
                    if rd.eng != eng:
                        op.deps.add(rd)
        for w in writes:
            lw = w.last_w
            if lw is not None and (dma or lw.is_dma or lw.eng != eng):
                op.deps.add(lw)
            last_rd = {}
            for rd in w.readers:
                if rd.is_dma:
                    op.deps.add(rd)
                elif dma or rd.eng != eng:
                    last_rd[rd.eng] = rd
            for rd in last_rd.values():
                op.deps.add(rd)
        for w in writes:
            w.last_w = op
            w.readers = []
        for r in reads:
            r.readers.append(op)
        op.deps.discard(op)
        if not dma:
            self.last_on[eng] = op
        else:
            self.dma_since_bar.append(op)
        self.ops.append(op)
        return op

    def pe(self, fn, reads=(), writes=()):
        return self.add("pe", fn, reads, writes)

    def act(self, fn, reads=(), writes=()):
        return self.add("act", fn, reads, writes)

    def dve(self, fn, reads=(), writes=()):
        return self.add("dve", fn, reads, writes)

    def pool(self, fn, reads=(), writes=()):
        return self.add("pool", fn, reads, writes)

    def dma(self, fn, reads=(), writes=(), eng="sp"):
        return self.add(eng, fn, reads, writes, dma=True)

    def barrier(self):
        prev = [o for o in self.last_on.values()] + list(self.dma_since_bar)
        self.dma_since_bar = []
        for eng in ENGS:
            op = Op(len(self.ops), eng, (lambda e: e.nop()), False)
            op.deps = set(prev)
            self.ops.append(op)
            self.last_on[eng] = op

    def emit(self):
        nc = self.nc
        ops = self.ops
        for op in ops:
            for d in op.deps:
                d.sig = True
        with contextlib.ExitStack() as es:
            esem = {e: es.enter_context(nc.semaphore("s_" + e)) for e in ENGS}
            dsems = [es.enter_context(nc.semaphore("d%d" % i)) for i in range(N_DMA_SEMS)]
            cnt = {e: 0 for e in esem}
            dcnt = [0] * N_DMA_SEMS
            dlast = [None] * N_DMA_SEMS
            rr = 0
            rr_sw = 0
            rr_hw = 0
            N_SW = 16
            for op in ops:
                if op.is_dma:
                    if op.eng == "pool":
                        s = rr_sw % N_SW
                        rr_sw += 1
                    else:
                        s = N_SW + rr_hw % (N_DMA_SEMS - N_SW)
                        rr_hw += 1
                    rr += 1
                    op.pre = dlast[s]
                    dcnt[s] += 16
                    op.sem = dsems[s]
                    op.val = dcnt[s]
                    dlast[s] = op
                elif op.sig:
                    cnt[op.eng] += 1
                    op.sem = esem[op.eng]
                    op.val = cnt[op.eng]
            self.stats = dict(cnt=dict(cnt), n_dma=rr, n_ops=len(ops))
            by_eng = {e: [o for o in ops if o.eng == e] for e in ENGS}
            last_dma = [d for d in dlast if d is not None]
            blk = es.enter_context(nc.Block())

            def run(engname, e):
                seen = {}

                def wait(d):
                    key = id(d.sem)
                    if seen.get(key, 0) >= d.val:
                        return
                    seen[key] = d.val
                    e.wait_ge(d.sem, d.val)

                for op in by_eng[engname]:
                    if op.pre is not None:
                        wait(op.pre)
                    for d in sorted(op.deps, key=lambda o: o.idx):
                        wait(d)
                    ins = op.fn(e)
                    if op.sem is not None:
                        ins.then_inc(op.sem, 16 if op.is_dma else 1)
                if engname == "sp":
                    for d in last_dma:
                        wait(d)

            @blk.tensor
            def _(e):
                run("pe", e)

            @blk.scalar
            def _(e):
                run("act", e)

            @blk.vector
            def _(e):
                run("dve", e)

            @blk.gpsimd
            def _(e):
                run("pool", e)

            @blk.sync
            def _(e):
                run("sp", e)


class NS:
    pass


NSC = 232
SC_A, SC_B, SC_BT, SC_AMX = 0, 8, 16, 24
SC_DT, SC_CS, SC_ECS, SC_WK, SC_ETOT, SC_NCS = 32, 64, 96, 128, 160, 200
R_BIG, R_BFG, R_DTB, R_ALOG, R_DSK = 0, 8, 16, 48, 80
C_GPRE, C_GMLP, C_CW, C_CB = 0, 16, 32, 176


class _Stop(Exception):
    pass


def build_program(stop_after=None):
    nc = bass.Bass("TRN2", target_bir_lowering=False)
    P = Prog(nc)

    def check(stage):
        if stop_after == stage:
            P.emit()
            raise _Stop()

    def din(name, shape):
        return nc.dram_tensor(name, list(shape), F32, kind="ExternalInput").ap()

    def dout(name, shape):
        return nc.dram_tensor(name, list(shape), F32, kind="ExternalOutput").ap()

    def dscr(name, shape, dt=F32):
        return nc.dram_tensor(name, list(shape), dt, kind="Internal").ap()

    xp = din("xp", [512, D])
    xo = din("xo", [512, D])
    xh = din("xh", [128, D])
    xf = din("xf", [2048, D])
    cv = din("cv", [32, 128])
    st_c = din("st_c", [2, 4, 128, 256])
    st_n = din("st_n", [2, 4, 128])
    st_m = din("st_m", [1, 8])
    st_s = din("st_s", [2, 1024, 128])
    flags_d = din("flags", [1, NFLAG])
    w_mod = din("w_mod", [D, 6 * D])
    b_mod = din("b_mod", [1, 6 * D])
    g_pre_mix = din("g_pre_mix", [16, 128])
    g_post_mix = din("g_post_mix", [1, D])
    w_in = din("w_in", [D, 6192])
    b_ig = din("b_igate", [1, 8])
    b_fg = din("b_fgate", [1, 8])
    conv_w = din("conv_w", [144, 128])
    conv_b = din("conv_b", [16, 128])
    dt_bias = din("dt_bias", [1, 32])
    a_log = din("a_log", [1, 32])
    d_skip = din("d_skip", [1, 16])
    g_ml = din("g_mlstm_norm", [1, 1024])
    g_sn = din("g_ssd_norm", [1, 1024])
    w_out = din("w_out", [D, D])
    g_pre_mlp = din("g_pre_mlp", [16, 128])
    g_post_mlp = din("g_post_mlp", [1, D])
    w_mlp_in = din("w_mlp_in", [D, 4 * D])
    w_mlp_out = din("w_mlp_out", [4 * D, D])
    yp = dout("yp", [512, D])
    ys = dout("ys", [512, D])
    o_c = dout("o_c", [2, 2, 4, 128, 256])
    o_n = dout("o_n", [2, 2, 4, 128])
    o_m = dout("o_m", [2, 2, 4])
    o_s = dout("o_s", [2, 2, 1024, 128])
    modscr = dscr("modscr", [2, 6 * D])
    stg = dscr("stg", [16, 128, 3072], BF16)
    x1s = dscr("x1s", [1024, D])
    catTs = dscr("catTs", [8, 128, KT * 128], BF16)
    Es = dscr("Es", [4, 128, ESZ])
    B_modscr = Buf("modscr")
    B_stg = [Buf("stg%d" % i) for i in range(16)]
    B_x1s = [Buf("x1s%d" % i) for i in range(8)]
    B_cats = [Buf("cats%d" % i) for i in range(8)]
    B_Es = [Buf("Es%d" % i) for i in range(4)]

    ES = contextlib.ExitStack()
    with ES:
        tcount = [0]

        def T(name, shape, dt=F32, es=ES):
            tcount[0] += 1
            return es.enter_context(nc.sbuf_tensor("%s_%d" % (name, tcount[0]), list(shape), dt))

        pbig = ES.enter_context(nc.psum_tensor("pbig", [128, 2048], F32))
        PB = [Buf("pb%d" % i, excl=True) for i in range(4)]
        ptr = [ES.enter_context(nc.psum_tensor("ptr%d" % i, [128, 512], F32)) for i in range(2)]
        PT = [Buf("pt%d" % i, excl=True) for i in range(2)]
        psm = [ES.enter_context(nc.psum_tensor("psm%d" % i, [128, 512], F32)) for i in range(2)]
        PS = [Buf("ps%d" % i, excl=True) for i in range(2)]
        rot = {"tr": 0, "sm": 0, "big": 0}

        def next_tr():
            i = rot["tr"] % 2
            rot["tr"] += 1
            return ptr[i], PT[i]

        def next_sm():
            i = rot["sm"] % 2
            rot["sm"] += 1
            return psm[i], PS[i]

        def next_big():
            i = rot["big"] % 4
            rot["big"] += 1
            return pbig[:, i * 512:(i + 1) * 512], PB[i]

        def next_big_m(M):
            i = 2 * M.pair + (M.brot[0] % 2)
            M.brot[0] += 1
            return pbig[:, i * 512:(i + 1) * 512], PB[i]

        B_c = Buf("consts")
        ident = T("ident", [128, 128])
        identb = T("identb", [128, 128], BF16)
        LT = T("LT", [128, 128])
        UT = T("UT", [128, 128])
        NM = [T("negf", [128, 128]), T("negb", [128, 128])]
        MK = [LT, UT]
        ones = T("ones", [128, 128])
        flags = T("flags_sb", [128, NFLAG])
        P.pool(lambda e: e.memset(ident[:], 0.0), writes=[B_c])
        P.pool(lambda e: e.affine_select(out=ident[:], in_=ident[:], pattern=[[-1, 128]], compare_op=ALU.not_equal,
                                         fill=1.0, base=0, channel_multiplier=1), reads=[B_c], writes=[B_c])
        P.pool(lambda e: e.memset(ones[:], 1.0), writes=[B_c])
        P.pool(lambda e: e.memset(LT[:], 1.0), writes=[B_c])
        P.pool(lambda e: e.affine_select(out=LT[:], in_=LT[:], pattern=[[1, 128]], compare_op=ALU.is_ge,
                                         fill=0.0, base=0, channel_multiplier=-1), reads=[B_c], writes=[B_c])
        P.pool(lambda e: e.memset(UT[:], 1.0), writes=[B_c])
        P.pool(lambda e: e.affine_select(out=UT[:], in_=UT[:], pattern=[[-1, 128]], compare_op=ALU.is_ge,
                                         fill=0.0, base=0, channel_multiplier=1), reads=[B_c], writes=[B_c])
        P.dve(lambda e: e.tensor_copy(out=identb[:], in_=ident[:]), reads=[B_c], writes=[B_c])
        for dd in range(2):
            P.dve(lambda e, dd=dd: e.tensor_scalar(out=NM[dd][:], in0=MK[dd][:], scalar1=-1.0, scalar2=-NEG,
                                                   op0=ALU.add, op1=ALU.mult), reads=[B_c], writes=[B_c])
        P.dma(lambda e: e.dma_start(out=flags[:], in_=flags_d[0:1, :].to_broadcast([128, NFLAG])), writes=[B_c])

        rowp = T("rowp", [128, 96])
        for (src, off, n) in ((b_ig, R_BIG, 8), (b_fg, R_BFG, 8), (dt_bias, R_DTB, 32), (a_log, R_ALOG, 32), (d_skip, R_DSK, 16)):
            P.dma(lambda e, src=src, off=off, n=n: e.dma_start(out=rowp[:, off:off + n], in_=src[0:1, :].to_broadcast([128, n])),
                  writes=[B_c])
        arow = T("arow", [128, 32])
        P.act(lambda e: e.activation(out=arow[:], in_=rowp[:, R_ALOG:R_ALOG + 32], func=AF.Exp), reads=[B_c], writes=[B_c])
        P.dve(lambda e: e.tensor_scalar(out=arow[:], in0=arow[:], scalar1=-1.0, scalar2=None, op0=ALU.mult), reads=[B_c], writes=[B_c])

        colp = T("colp", [128, 192])
        ld = T("ldtmp", [128, 128])
        B_ld = Buf("ld")

        def col_load(src, nrows, off, r0=0):
            P.dma(lambda e: e.dma_start(out=ld[0:nrows, :], in_=src[r0:r0 + nrows, :]), writes=[B_ld])
            pt, pb = next_sm()
            P.pe(lambda e: e.transpose(pt[:, 0:nrows], ld[0:nrows, :], ident[0:nrows, 0:nrows]), reads=[B_ld, B_c], writes=[pb])
            P.dve(lambda e: e.tensor_copy(out=colp[:, off:off + nrows], in_=pt[:, 0:nrows]), reads=[pb], writes=[B_c])

        col_load(g_pre_mix, 16, C_GPRE)
        col_load(g_pre_mlp, 16, C_GMLP)
        col_load(conv_w, 128, C_CW)
        col_load(conv_w, 16, C_CW + 128, r0=128)
        col_load(conv_b, 16, C_CB)

        def cw(ct, kk):
            c0 = C_CW + kk * 16 + ct
            return colp[:, c0:c0 + 1]

        def cbias(ct):
            return colp[:, C_CB + ct:C_CB + ct + 1]

        wg = T("wg", [128, KT, 48])
        P.dma(lambda e: e.dma_start(out=wg[:, :, 0:16], in_=w_in[:, 3072:3088].rearrange("(kt p) c -> p kt c", p=128)), writes=[B_c])
        P.dma(lambda e: e.dma_start(out=wg[:, :, 16:48], in_=w_in[:, 6160:6192].rearrange("(kt p) c -> p kt c", p=128)), writes=[B_c])

        wgm = T("wgm", [128, 2, KT, 48])
        gbB = T("gbB", [128, 2, 48])
        B_wgm = Buf("wgm")
        csT = T("csT", [128, 32], BF16)
        B_cv = Buf("cv")
        modT = T("modT", [128, 2, 96])
        B_modT = Buf("modT")
        modc = T("modc", [128, 2, 4, 16])
        B_modc = Buf("modc")
        stat = T("stat", [128, 64])
        B_stat = Buf("stat")
        sq_junk = T("sqjunk", [128, D], BF16)
        B_junk = Buf("junk")
        gtmp = T("gtmp", [128, 128])
        B_gt = Buf("gtmp")
        amx8 = T("amx8", [8, 128])
        B_amx = Buf("amx8")

        def alloc_inproj(es, nbuf=2):
            I = NS()
            I.xt = [T("xt%d" % i, [128, D], es=es) for i in range(2)]
            I.B_xt = [Buf("xt0"), Buf("xt1")]
            I.xrot = [0]
            I.uTf_flat = T("uTf", [128, KT * 128], es=es)
            I.uTf = I.uTf_flat[:].rearrange("p (k t) -> p k t", k=KT)
            I.B_uTf = Buf("uTf")
            I.wbuf = [T("wbuf%d" % i, [128, KT, 512], BF16, es=es) for i in range(nbuf)]
            I.B_w = [Buf("wbuf%d" % i) for i in range(nbuf)]
            I.wrot = [0]
            return I

        def load_wblock(I, src_ap):
            i = I.wrot[0] % len(I.wbuf)
            I.wrot[0] += 1
            ncols = src_ap.shape[2]
            nk = src_ap.shape[1]
            wt, wb = I.wbuf[i], I.B_w[i]
            P.dma(lambda e: e.dma_start(out=wt[:, 0:nk, 0:ncols], in_=src_ap), writes=[wb], eng="pool")
            return wt, wb

        def win_cols(c0, n):
            return w_in[:, c0:c0 + n].rearrange("(kt p) c -> p kt c", p=128)

        def rstd_of(ss_ap, out_ap, n):
            P.dve(lambda e: e.tensor_scalar(out=out_ap, in0=ss_ap, scalar1=1.0 / n, scalar2=EPS, op0=ALU.mult, op1=ALU.add),
                  reads=[B_stat], writes=[B_stat])
            P.act(lambda e: e.activation(out=out_ap, in_=out_ap, func=AF.Sqrt), reads=[B_stat], writes=[B_stat])
            P.dve(lambda e: e.reciprocal(out=out_ap, in_=out_ap), reads=[B_stat], writes=[B_stat])

        def uT_s1(I, src_rows_ap):
            i = I.xrot[0] % 2
            I.xrot[0] += 1
            xt, bx = I.xt[i], I.B_xt[i]
            c0 = 40 + 2 * i
            P.dma(lambda e: e.dma_start(out=xt[:], in_=src_rows_ap), writes=[bx])
            P.act(lambda e: e.activation(out=sq_junk[:], in_=xt[:], func=AF.Square, accum_out=stat[:, c0:c0 + 1]),
                  reads=[bx], writes=[B_junk, B_stat])
            rstd_of(stat[:, c0:c0 + 1], stat[:, c0 + 1:c0 + 2], D)
            P.act(lambda e: e.activation(out=xt[:], in_=xt[:], func=AF.Copy, scale=stat[:, c0 + 1:c0 + 2]), reads=[bx, B_stat], writes=[bx])
            return xt, bx

        def uT_s2(I, h, ci, which, uT, tcol, B_u, want_f32=False):
            xn, bxn = h
            uTf, B_uTf = I.uTf, I.B_uTf
            for g4 in range(4):
                pt, pb = next_tr()
                for j in range(4):
                    kt = g4 * 4 + j
                    P.pe(lambda e, kt=kt, j=j, pt=pt: e.transpose(pt[:, j * 128:(j + 1) * 128], xn[:, kt * 128:(kt + 1) * 128], ident[:]),
                         reads=[bxn, B_c], writes=[pb])
                for j in range(4):
                    kt = g4 * 4 + j
                    a_ap = modc[:, ci, which, kt:kt + 1]
                    s_ap = modc[:, ci, which + 1, kt:kt + 1]
                    P.dve(lambda e, kt=kt, j=j, pt=pt, a_ap=a_ap, s_ap=s_ap: e.tensor_scalar(
                        out=uT[:, kt, tcol:tcol + 128], in0=pt[:, j * 128:(j + 1) * 128], scalar1=a_ap, scalar2=s_ap,
                        op0=ALU.mult, op1=ALU.add), reads=[pb, B_modc], writes=[B_u])
                if want_f32:
                    P.act(lambda e, g4=g4, pt=pt: e.copy(out=uTf[:, g4 * 4:g4 * 4 + 4, :], in_=pt[:, 0:512].rearrange("p (k t) -> p k t", k=4)),
                          reads=[pb], writes=[B_uTf])

        def make_uT(I, src_rows_ap, ci, which, uT, tcol, B_u, want_f32=False, src_sb=None, B_src=None):
            if src_sb is None:
                i = I.xrot[0] % 2
                I.xrot[0] += 1
                xt, bx = I.xt[i], I.B_xt[i]
                P.dma(lambda e: e.dma_start(out=xt[:], in_=src_rows_ap), writes=[bx])
            else:
                xt, bx = src_sb, B_src
            P.act(lambda e: e.activation(out=sq_junk[:], in_=xt[:], func=AF.Square, accum_out=stat[:, 0:1]),
                  reads=[bx], writes=[B_junk, B_stat])
            rstd_of(stat[:, 0:1], stat[:, 1:2], D)
            if src_sb is None:
                xn, bxn = xt, bx
            else:
                i2 = I.xrot[0] % 2
                I.xrot[0] += 1
                xn, bxn = I.xt[i2], I.B_xt[i2]
            P.act(lambda e: e.activation(out=xn[:], in_=xt[:], func=AF.Copy, scale=stat[:, 1:2]), reads=[bx, B_stat], writes=[bxn])
            uTf, B_uTf = I.uTf, I.B_uTf
            for g4 in range(4):
                pt, pb = next_tr()
                for j in range(4):
                    kt = g4 * 4 + j
                    P.pe(lambda e, kt=kt, j=j, pt=pt: e.transpose(pt[:, j * 128:(j + 1) * 128], xn[:, kt * 128:(kt + 1) * 128], ident[:]),
                         reads=[bxn, B_c], writes=[pb])
                for j in range(4):
                    kt = g4 * 4 + j
                    a_ap = modc[:, ci, which, kt:kt + 1]
                    s_ap = modc[:, ci, which + 1, kt:kt + 1]
                    if True:
                        P.dve(lambda e, kt=kt, j=j, pt=pt, a_ap=a_ap, s_ap=s_ap: e.tensor_scalar(
                            out=uT[:, kt, tcol:tcol + 128], in0=pt[:, j * 128:(j + 1) * 128], scalar1=a_ap, scalar2=s_ap,
                            op0=ALU.mult, op1=ALU.add), reads=[pb, B_modc], writes=[B_u])
                    else:
                        P.act(lambda e, kt=kt, j=j, pt=pt, a_ap=a_ap, s_ap=s_ap: e.activation(
                            out=uT[:, kt, tcol:tcol + 128], in_=pt[:, j * 128:(j + 1) * 128], func=AF.Identity, bias=s_ap, scale=a_ap),
                            reads=[pb, B_modc], writes=[B_u])
                if want_f32:
                    P.act(lambda e, g4=g4, pt=pt: e.copy(out=uTf[:, g4 * 4:g4 * 4 + 4, :], in_=pt[:, 0:512].rearrange("p (k t) -> p k t", k=4)),
                          reads=[pb], writes=[B_uTf])

        def gate_preact(I, ci, pgall, c, B_pg):
            uTf, B_uTf = I.uTf, I.B_uTf
            pg, pgb = next_sm()
            for kt in range(KT):
                P.pe(lambda e, kt=kt: e.matmul(pg[:, 0:48], lhsT=uTf[:, kt, :], rhs=wgm[:, ci, kt, :], start=(kt == 0), stop=(kt == KT - 1)),
                     reads=[B_uTf, B_wgm], writes=[pgb])
            P.dve(lambda e: e.tensor_tensor(out=pgall[:, c, :], in0=pg[:, 0:48], in1=gbB[:, ci, :], op=ALU.add),
                  reads=[pgb, B_wgm], writes=[B_pg])

        def prep_scalars_batched(nch, pgall, B_pg, scall, B_sc_list, es, gt_flat, B_g):
            gt = gt_flat[:, 0:nch * 128].rearrange("p (c k) -> p c k", k=128)
            amxb = T("amxb", [128, 128], es=es)
            B_am = Buf("amxb")
            n8 = nch * 8

            def S(off, n):
                return scall[:, 0:nch, off:off + n]
            P.act(lambda e: e.activation(out=gt[:, :, 72:80], in_=pgall[:, 0:nch, 8:16], func=AF.Exp, scale=-1.0), reads=[B_pg], writes=[B_g])
            P.act(lambda e: e.activation(out=gt[:, :, 72:80], in_=gt[:, :, 72:80], func=AF.Ln, bias=1.0, scale=1.0), reads=[B_g], writes=[B_g])
            P.dve(lambda e: e.tensor_scalar(out=gt[:, :, 0:8], in0=gt[:, :, 72:80], scalar1=-1.0, scalar2=None, op0=ALU.mult), reads=[B_g], writes=[B_g])
            P.act(lambda e: e.activation(out=gt[:, :, 40:72], in_=pgall[:, 0:nch, 16:48], func=AF.Exp), reads=[B_pg], writes=[B_g])
            P.act(lambda e: e.activation(out=S(SC_DT, 32), in_=gt[:, :, 40:72], func=AF.Ln, bias=1.0, scale=1.0), reads=[B_g], writes=B_sc_list)
            P.dve(lambda e: e.tensor_tensor(out=gt[:, :, 8:40], in0=S(SC_DT, 32), in1=arow[:].unsqueeze(1).to_broadcast([128, nch, 32]), op=ALU.mult),
                  reads=B_sc_list + [B_c], writes=[B_g])
            p1, p1b = next_sm()
            p2, p2b = next_tr()
            p3, p3b = next_tr()
            p1v = p1[:, 0:nch * 16].rearrange("p (c k) -> p c k", k=16)
            p2v = p2[:, 0:nch * 32].rearrange("p (c k) -> p c k", k=32)
            p3v = p3[:, 0:nch * 32].rearrange("p (c k) -> p c k", k=32)
            P.pe(lambda e: e.matmul(p1v[:, :, 0:4], lhsT=LT[:], rhs=gt[:, :, 0:4], start=True, stop=True), reads=[B_g, B_c], writes=[p1b])
            P.pe(lambda e: e.matmul(p1v[:, :, 4:8], lhsT=UT[:], rhs=gt[:, :, 4:8], start=True, stop=True), reads=[B_g, B_c], writes=[p1b])
            P.pe(lambda e: e.matmul(p1v[:, :, 8:16], lhsT=ones[:], rhs=gt[:, :, 0:8], start=True, stop=True), reads=[B_g, B_c], writes=[p1b])
            P.pe(lambda e: e.matmul(p2v[:, :, 0:16], lhsT=LT[:], rhs=gt[:, :, 8:24], start=True, stop=True), reads=[B_g, B_c], writes=[p2b])
            P.pe(lambda e: e.matmul(p2v[:, :, 16:32], lhsT=UT[:], rhs=gt[:, :, 24:40], start=True, stop=True), reads=[B_g, B_c], writes=[p2b])
            P.pe(lambda e: e.matmul(p3v, lhsT=ones[:], rhs=gt[:, :, 8:40], start=True, stop=True), reads=[B_g, B_c], writes=[p3b])
            P.dve(lambda e: e.tensor_copy(out=S(SC_B, 16), in_=p1v), reads=[p1b], writes=B_sc_list)
            P.dve(lambda e: e.tensor_copy(out=S(SC_CS, 32), in_=p2v), reads=[p2b], writes=B_sc_list)
            P.dve(lambda e: e.tensor_scalar(out=S(SC_NCS, 32), in0=p2v, scalar1=-1.0, scalar2=None, op0=ALU.mult), reads=[p2b], writes=B_sc_list)
            P.dve(lambda e: e.tensor_tensor(out=S(SC_A, 8), in0=pgall[:, 0:nch, 0:8], in1=S(SC_B, 8), op=ALU.subtract),
                  reads=[B_pg] + B_sc_list, writes=B_sc_list)
            P.act(lambda e: e.activation(out=S(SC_ECS, 32), in_=S(SC_CS, 32), func=AF.Exp), reads=B_sc_list, writes=B_sc_list)
            P.act(lambda e: e.activation(out=S(SC_ETOT, 32), in_=p3v, func=AF.Exp), reads=[p3b], writes=B_sc_list)
            P.dve(lambda e: e.tensor_tensor(out=gt[:, :, 40:72], in0=p3v, in1=S(SC_CS, 32), op=ALU.subtract), reads=[p3b] + B_sc_list, writes=[B_g])
            P.act(lambda e: e.activation(out=gt[:, :, 40:72], in_=gt[:, :, 40:72], func=AF.Exp), reads=[B_g], writes=[B_g])
            P.dve(lambda e: e.tensor_tensor(out=S(SC_WK, 32), in0=gt[:, :, 40:72], in1=S(SC_DT, 32), op=ALU.mult), reads=[B_g] + B_sc_list, writes=B_sc_list)
            P.dve(lambda e: e.tensor_copy(out=gt[:, :, 80:88], in_=S(SC_A, 8)), reads=B_sc_list, writes=[B_g])
            P.dve(lambda e: e.tensor_copy(out=amxb[:, 0:n8].rearrange("p (c k) -> p c k", k=8), in_=gt[:, :, 80:88]), reads=[B_g], writes=[B_am])
            pa, pab = next_sm()
            P.pe(lambda e: e.transpose(pa[0:n8, 0:128], amxb[:, 0:n8], ident[:]), reads=[B_am, B_c], writes=[pab])
            P.dve(lambda e: e.tensor_reduce(out=gtmp[0:n8, 120:121], in_=pa[0:n8, 0:128], axis=AX.X, op=ALU.max), reads=[pab], writes=[B_gt])
            P.dve(lambda e: e.tensor_copy(out=amxb[0:n8, :], in_=gtmp[0:n8, 120:121].to_broadcast([n8, 128])), reads=[B_gt, B_am], writes=[B_am])
            pb2, pb2b = next_sm()
            P.pe(lambda e: e.matmul(pb2[:, 0:n8], lhsT=amxb[0:n8, :], rhs=ident[0:n8, 0:n8], start=True, stop=True), reads=[B_am, B_c], writes=[pb2b])
            P.dve(lambda e: e.tensor_copy(out=S(SC_AMX, 8), in_=pb2[:, 0:n8].rearrange("p (c k) -> p c k", k=8)), reads=[pb2b], writes=B_sc_list)

        def conv_apply(center_in, center_out, taps, ct, B_pre, B_acc, acc_flat, dst, B_dst, do_silu=True):
            P.act(lambda e: e.activation(out=center_out, in_=center_in, func=AF.Identity, scale=cw(ct, 4), bias=cbias(ct)),
                  reads=[B_pre, B_c], writes=[B_acc])
            for (ov, iv, kk) in taps:
                P.dve(lambda e, ov=ov, iv=iv, kk=kk: e.scalar_tensor_tensor(out=ov, in0=iv, scalar=cw(ct, kk), in1=ov, op0=ALU.mult, op1=ALU.add),
                      reads=[B_pre, B_acc, B_c], writes=[B_acc])
            if do_silu:
                P.act(lambda e: e.activation(out=dst, in_=acc_flat, func=AF.Silu), reads=[B_acc], writes=[B_dst])

        def grid_taps(pre_flat, acc_flat, nrows):
            pv = pre_flat.rearrange("p (r c) -> p r c", c=64)
            av = acc_flat.rearrange("p (r c) -> p r c", c=64)
            taps = []
            for di in range(3):
                for dj in range(3):
                    if di == 1 and dj == 1:
                        continue
                    if dj == 0:
                        oc, ic = slice(1, 64), slice(0, 63)
                    elif dj == 1:
                        oc, ic = slice(0, 64), slice(0, 64)
                    else:
                        oc, ic = slice(0, 63), slice(1, 64)
                    taps.append((av[:, :, oc], pv[:, di:di + nrows, ic], di * 3 + dj))
            return pv[:, 1:1 + nrows, :], av[:, :, :], taps

        def seq_taps(pre_flat, acc_flat):
            pv = pre_flat.rearrange("p (s t) -> p s t", s=2)
            av = acc_flat.rearrange("p (s t) -> p s t", s=2)
            taps = [(av[:, :, 1:256], pv[:, :, 0:255], 3), (av[:, :, 0:255], pv[:, :, 1:256], 5)]
            return pv, av, taps

        mix_count = [0]

        def alloc_mixer(es):
            M = NS()
            M.pair = mix_count[0] % 2
            mix_count[0] += 1
            M.brot = [0]
            M.chs = T("chs", [128, 32], es=es); M.B_chs = Buf("chs")
            M.wvb = T("wvb", [128, 4], BF16, es=es); M.B_wvb = Buf("wvb")
            M.vp = T("vp", [128, 4, 256], BF16, es=es); M.B_vp = Buf("vp")
            M.Sm = T("Sm", [128, 4, 128], BF16, es=es); M.B_Sm = Buf("Sm")
            M.htmp = T("htmp", [128, 1024], es=es); M.B_ht = Buf("htmp")
            M.argt2 = [T("argt%d" % i, [128, 512], es=es) for i in range(2)]; M.B_arg2 = [Buf("argt0"), Buf("argt1")]
            M.Wt = [T("Wt%d" % i, [128, 4, 128], BF16, es=es) for i in range(2)]
            M.B_Wt = [Buf("Wt0"), Buf("Wt1")]
            M.GTs = T("GTs", [128, 4, 128], es=es); M.B_GT = Buf("GTs")
            M.xw = T("xw", [128, 1024], BF16, es=es); M.B_xw = Buf("xw")
            M.qs = T("qs", [128, 4, 128], BF16, es=es); M.B_qs = Buf("qs")
            return M

        def new_state(name, es, bf=True):
            s = NS()
            s.Cn = T(name + "_Cn", [128, 4, 256], es=es)
            s.Nn = T(name + "_Nn", [128, 4], es=es)
            s.m = T(name + "_m", [128, 4], es=es)
            s.Sf = T(name + "_Sf", [128, 1024], es=es)
            if bf:
                s.Cb = T(name + "_Cb", [128, 4, 256], BF16, es=es)
                s.Nb = T(name + "_Nb", [128, 4], BF16, es=es)
                s.Sb = T(name + "_Sb", [128, 1024], BF16, es=es)
            s.B = Buf(name)
            s.Bm = s.B
            return s

        def state_zero(s, bf=True):
            P.pool(lambda e: e.memset(s.Cn[:], 0.0), writes=[s.B])
            P.pool(lambda e: e.memset(s.Nn[:], 0.0), writes=[s.B])
            P.pool(lambda e: e.memset(s.m[:], 0.0), writes=[s.B])
            P.pool(lambda e: e.memset(s.Sf[:], 0.0), writes=[s.B])
            if bf:
                P.pool(lambda e: e.memset(s.Cb[:], 0.0), writes=[s.B])
                P.pool(lambda e: e.memset(s.Nb[:], 0.0), writes=[s.B])
                P.pool(lambda e: e.memset(s.Sb[:], 0.0), writes=[s.B])

        def state_refresh_bf(s):
            P.act(lambda e: e.copy(out=s.Cb[:], in_=s.Cn[:]), reads=[s.B], writes=[s.B])
            P.act(lambda e: e.copy(out=s.Nb[:], in_=s.Nn[:]), reads=[s.B], writes=[s.B])
            P.act(lambda e: e.copy(out=s.Sb[:], in_=s.Sf[:]), reads=[s.B], writes=[s.B])

        def state_macc(dst, src, fcol):
            f_ap = flags[:, fcol:fcol + 1]
            for a, b_ in ((dst.Cn, src.Cn), (dst.Nn, src.Nn), (dst.m, src.m), (dst.Sf, src.Sf)):
                P.dve(lambda e, a=a, b_=b_: e.scalar_tensor_tensor(out=a[:], in0=b_[:], scalar=f_ap, in1=a[:], op0=ALU.mult, op1=ALU.add),
                      reads=[src.B, dst.B, B_c], writes=[dst.B])

        def state_store_scr(s, i):
            P.dma(lambda e: e.dma_start(out=Es[i, :, 0:1024], in_=s.Cn[:].rearrange("p h e -> p (h e)")), reads=[s.B], writes=[B_Es[i]])
            P.dma(lambda e: e.dma_start(out=Es[i, :, 1024:1028], in_=s.Nn[:]), reads=[s.B], writes=[B_Es[i]])
            P.dma(lambda e: e.dma_start(out=Es[i, :, 1028:1032], in_=s.m[:]), reads=[s.B], writes=[B_Es[i]])
            P.dma(lambda e: e.dma_start(out=Es[i, :, 1032:2056], in_=s.Sf[:]), reads=[s.B], writes=[B_Es[i]])

        def state_load_scr(s, i):
            P.dma(lambda e: e.dma_start(out=s.Cn[:].rearrange("p h e -> p (h e)"), in_=Es[i, :, 0:1024]), reads=[B_Es[i]], writes=[s.B])
            P.dma(lambda e: e.dma_start(out=s.Nn[:], in_=Es[i, :, 1024:1028]), reads=[B_Es[i]], writes=[s.B])
            P.dma(lambda e: e.dma_start(out=s.m[:], in_=Es[i, :, 1028:1032]), reads=[B_Es[i]], writes=[s.B])
            P.dma(lambda e: e.dma_start(out=s.Sf[:], in_=Es[i, :, 1032:2056]), reads=[B_Es[i]], writes=[s.B])
            state_refresh_bf(s)

        def mlstm_chain_scalars(M, sc, B_sc, d, st, qT, B_qk, tsl):
            chs, B_chs = M.chs, M.B_chs
            wvb, B_wvb = M.wvb, M.B_wvb
            o4 = 4 * d
            P.dve(lambda e: e.tensor_tensor(out=chs[:, 0:4], in0=st.m[:], in1=sc[:, SC_AMX + o4:SC_AMX + o4 + 4], op=ALU.max),
                  reads=[st.Bm, B_sc], writes=[B_chs])
            P.dve(lambda e: e.tensor_tensor(out=chs[:, 4:8], in0=sc[:, SC_A + o4:SC_A + o4 + 4], in1=chs[:, 0:4], op=ALU.subtract),
                  reads=[B_sc, B_chs], writes=[B_chs])
            P.dve(lambda e: e.scalar_tensor_tensor(out=chs[:, 8:12], in0=sc[:, SC_B + o4:SC_B + o4 + 4], scalar=-1.0, in1=chs[:, 0:4],
                                                   op0=ALU.mult, op1=ALU.subtract), reads=[B_sc, B_chs], writes=[B_chs])
            P.dve(lambda e: e.tensor_tensor(out=chs[:, 12:16], in0=st.m[:], in1=chs[:, 0:4], op=ALU.subtract),
                  reads=[st.Bm, B_chs], writes=[B_chs])
            P.act(lambda e: e.activation(out=chs[:, 4:16], in_=chs[:, 4:16], func=AF.Exp), reads=[B_chs], writes=[B_chs])
            P.dve(lambda e: e.tensor_copy(out=wvb[:], in_=chs[:, 4:8]), reads=[B_chs], writes=[B_wvb])
            P.dve(lambda e: e.tensor_tensor(out=st.m[:], in0=sc[:, SC_BT + o4:SC_BT + o4 + 4], in1=chs[:, 0:4], op=ALU.add),
                  reads=[B_sc, B_chs, st.Bm], writes=[st.Bm])
            for h in range(4):
                P.dve(lambda e, h=h: e.tensor_scalar(out=M.qs[:, h, :], in0=qT[:, h, tsl], scalar1=chs[:, 12 + h:13 + h], scalar2=None, op0=ALU.mult),
                      reads=[B_qk, B_chs], writes=[M.B_qs])

        def make_vp(M, v_ap, B_v):
            vp, chs = M.vp, M.chs
            for h in range(4):
                P.act(lambda e, h=h: e.activation(out=vp[:, h, :], in_=v_ap[:, h, :], func=AF.Copy, scale=chs[:, 4 + h:5 + h]),
                      reads=[B_v, M.B_chs], writes=[M.B_vp])

        def mlstm_update(M, st, k_tm, B_k):
            vp, wvb, chs = M.vp, M.wvb, M.chs
            pcs = [next_big_m(M), next_big_m(M)]
            for half in range(2):
                po0, pob0 = pcs[half]
                for hh in range(2):
                    h = half * 2 + hh
                    P.pe(lambda e, h=h, hh=hh, po0=po0: e.matmul(po0[:, hh * 256:(hh + 1) * 256], lhsT=k_tm[:, h, :], rhs=vp[:, h, :], start=True, stop=True),
                         reads=[B_k, M.B_vp], writes=[pob0])
            pn, pnb = next_sm()
            for h in range(4):
                P.pe(lambda e, h=h: e.matmul(pn[:, h:h + 1], lhsT=k_tm[:, h, :], rhs=wvb[:, h:h + 1], start=True, stop=True),
                     reads=[B_k, M.B_wvb], writes=[pnb])
            for h in range(4):
                po0, pob0 = pcs[h // 2]
                P.dve(lambda e, h=h, po0=po0: e.scalar_tensor_tensor(out=st.Cn[:, h, :], in0=st.Cn[:, h, :], scalar=chs[:, 12 + h:13 + h],
                                                                    in1=po0[:, (h % 2) * 256:(h % 2 + 1) * 256], op0=ALU.mult, op1=ALU.add),
                      reads=[st.B, M.B_chs, pob0], writes=[st.B])
            P.dve(lambda e: e.tensor_tensor(out=st.Nn[:], in0=st.Nn[:], in1=chs[:, 12:16], op=ALU.mult), reads=[st.B, M.B_chs], writes=[st.B])
            P.dve(lambda e: e.tensor_tensor(out=st.Nn[:], in0=pn[:, 0:4], in1=st.Nn[:], op=ALU.add), reads=[pnb, st.B], writes=[st.B])
            P.act(lambda e: e.copy(out=st.Cb[:], in_=st.Cn[:]), reads=[st.B], writes=[st.B])
            P.act(lambda e: e.copy(out=st.Nb[:], in_=st.Nn[:]), reads=[st.B], writes=[st.B])

        def mlstm_out(M, d, st, qT, kT, B_qk, tsl, hs_ap, B_hs, first):
            chs, B_chs, Sm, vp, wvb, htmp = M.chs, M.B_chs, M.Sm, M.vp, M.wvb, M.htmp
            pS, pSb = next_tr()
            for h in range(4):
                P.pe(lambda e, h=h: e.matmul(pS[:, h * 128:(h + 1) * 128], lhsT=kT[:, h, tsl], rhs=qT[:, h, tsl], start=True, stop=True),
                     reads=[B_qk], writes=[pSb])
            P.dve(lambda e: e.tensor_tensor(out=Sm[:], in0=pS[:, 0:512].rearrange("p (h t) -> p h t", h=4),
                                            in1=MK[d][:].unsqueeze(1).to_broadcast([128, 4, 128]), op=ALU.mult),
                  reads=[pSb, B_c], writes=[M.B_Sm])
            pbs = []
            for half in range(2):
                po0, pob0 = next_big_m(M)
                pbs.append((po0, pob0))
                for hh in range(2):
                    h = half * 2 + hh
                    P.pe(lambda e, h=h, hh=hh, po0=po0: e.matmul(po0[:, hh * 256:(hh + 1) * 256], lhsT=M.qs[:, h, :], rhs=st.Cb[:, h, :], start=True, stop=False),
                         reads=[M.B_qs, st.B], writes=[pob0])
                    P.pe(lambda e, h=h, hh=hh, po0=po0: e.matmul(po0[:, hh * 256:(hh + 1) * 256], lhsT=Sm[:, h, :], rhs=vp[:, h, :], start=False, stop=True),
                         reads=[M.B_Sm, M.B_vp], writes=[pob0])
            pn, pnb = next_sm()
            for h in range(4):
                P.pe(lambda e, h=h: e.matmul(pn[:, h:h + 1], lhsT=M.qs[:, h, :], rhs=st.Nb[:, h:h + 1], start=True, stop=False),
                     reads=[M.B_qs, st.B], writes=[pnb])
                P.pe(lambda e, h=h: e.matmul(pn[:, h:h + 1], lhsT=Sm[:, h, :], rhs=wvb[:, h:h + 1], start=False, stop=True),
                     reads=[M.B_Sm, M.B_wvb], writes=[pnb])
            P.dve(lambda e: e.tensor_copy(out=chs[:, 24:28], in_=pn[:, 0:4]), reads=[pnb], writes=[B_chs])
            P.dve(lambda e: e.scalar_tensor_tensor(out=chs[:, 16:20], in0=chs[:, 24:28], scalar=-1.0, in1=chs[:, 24:28], op0=ALU.mult, op1=ALU.max),
                  reads=[B_chs], writes=[B_chs])
            P.dve(lambda e: e.tensor_tensor(out=chs[:, 16:20], in0=chs[:, 16:20], in1=chs[:, 8:12], op=ALU.max), reads=[B_chs], writes=[B_chs])
            P.dve(lambda e: e.reciprocal(out=chs[:, 20:24], in_=chs[:, 16:20]), reads=[B_chs], writes=[B_chs])
            for half in range(2):
                po0, pob0 = pbs[half]
                dst = hs_ap[:, half * 512:(half + 1) * 512].rearrange("p (h e) -> p h e", h=2)
                rdb = chs[:, 20 + half * 2:22 + half * 2].unsqueeze(2).to_broadcast([128, 2, 256])
                if first:
                    P.dve(lambda e, po0=po0, dst=dst, rdb=rdb: e.tensor_tensor(out=dst, in0=po0.rearrange("p (h e) -> p h e", h=2), in1=rdb, op=ALU.mult),
                          reads=[pob0, B_chs], writes=[B_hs])
                else:
                    tmp = htmp[:, half * 512:(half + 1) * 512].rearrange("p (h e) -> p h e", h=2)
                    P.dve(lambda e, po0=po0, tmp=tmp, rdb=rdb: e.tensor_tensor(out=tmp, in0=po0.rearrange("p (h e) -> p h e", h=2), in1=rdb, op=ALU.mult),
                          reads=[pob0, B_chs], writes=[M.B_ht])
                    P.pool(lambda e, dst=dst, tmp=tmp: e.tensor_tensor(out=dst, in0=dst, in1=tmp, op=ALU.add), reads=[M.B_ht, B_hs], writes=[B_hs])

        def ssd_update(M, sc, B_sc, d, st, x_tm_ap, B_x, Btm, B_B):
            xw = M.xw
            o16 = 16 * d
            P.pool(lambda e: e.tensor_tensor(out=xw[:].rearrange("p (j q) -> p j q", j=16), in0=x_tm_ap.rearrange("p (j q) -> p j q", j=16),
                                             in1=sc[:, SC_WK + o16:SC_WK + o16 + 16].unsqueeze(2).to_broadcast([128, 16, 64]), op=ALU.mult),
                   reads=[B_x, B_sc], writes=[M.B_xw])
            P.pool(lambda e: e.tensor_tensor(out=st.Sf[:].rearrange("p (j q) -> p j q", j=16), in0=st.Sf[:].rearrange("p (j q) -> p j q", j=16),
                                             in1=sc[:, SC_ETOT + o16:SC_ETOT + o16 + 16].unsqueeze(2).to_broadcast([128, 16, 64]), op=ALU.mult),
                   reads=[st.B, B_sc], writes=[st.B])
            for half in range(2):
                po0, pob0 = next_big_m(M)
                for gg in range(2):
                    g = half * 2 + gg
                    P.pe(lambda e, g=g, gg=gg, po0=po0: e.matmul(po0[:, gg * 256:(gg + 1) * 256], lhsT=Btm[:, g, :], rhs=xw[:, g * 256:(g + 1) * 256], start=True, stop=True),
                         reads=[B_B, M.B_xw], writes=[pob0])
                P.dve(lambda e, half=half, po0=po0: e.tensor_tensor(out=st.Sf[:, half * 512:(half + 1) * 512], in0=po0,
                                                                    in1=st.Sf[:, half * 512:(half + 1) * 512], op=ALU.add),
                      reads=[pob0, st.B], writes=[st.B])
            P.act(lambda e: e.copy(out=st.Sb[:], in_=st.Sf[:]), reads=[st.B], writes=[st.B])

        def ssd_gt(M, BT, CT, B_bc, tsl):
            GTs = M.GTs
            pG, pGb = next_tr()
            for g in range(4):
                P.pe(lambda e, g=g: e.matmul(pG[:, g * 128:(g + 1) * 128], lhsT=BT[:, g, tsl], rhs=CT[:, g, tsl], start=True, stop=True),
                     reads=[B_bc], writes=[pGb])
            P.act(lambda e: e.copy(out=GTs[:], in_=pG[:, 0:512].rearrange("p (g t) -> p g t", g=4)), reads=[pGb], writes=[M.B_GT])

        def ssd_out(M, sc, B_sc, d, st, CT, B_bc, tsl, x_tm_ap, B_x, ys_ap, B_ys, first):
            Wt, B_Wt, GTs, htmp = M.Wt, M.B_Wt, M.GTs, M.htmp
            o16 = 16 * d

            def stage0(g):
                argt, B_arg = M.argt2[g % 2], M.B_arg2[g % 2]
                wi = g % 2
                pC, pCb = next_tr()
                for jj in range(4):
                    col = SC_CS + o16 + g * 4 + jj
                    P.pe(lambda e, jj=jj, col=col: e.matmul(pC[:, jj * 128:(jj + 1) * 128], lhsT=sc[:, col:col + 1].to_broadcast([128, 128]),
                                                           rhs=ident[:], start=True, stop=True), reads=[B_sc, B_c], writes=[pCb])
                P.dve(lambda e: e.tensor_tensor(out=argt[:].rearrange("p (j t) -> p j t", j=4), in0=pC[:, 0:512].rearrange("p (j t) -> p j t", j=4),
                                                in1=NM[d][:].unsqueeze(1).to_broadcast([128, 4, 128]), op=ALU.add),
                      reads=[pCb, B_c], writes=[B_arg])
                for jj in range(4):
                    col = SC_NCS + o16 + g * 4 + jj
                    P.act(lambda e, jj=jj, col=col: e.activation(out=argt[:, jj * 128:(jj + 1) * 128], in_=argt[:, jj * 128:(jj + 1) * 128], func=AF.Exp,
                                                                bias=sc[:, col:col + 1], scale=1.0), reads=[B_arg, B_sc], writes=[B_arg])

            def stage0b(g):
                argt, B_arg = M.argt2[g % 2], M.B_arg2[g % 2]
                wi = g % 2
                for jj in range(4):
                    col = SC_DT + o16 + g * 4 + jj
                    P.dve(lambda e, jj=jj, col=col: e.scalar_tensor_tensor(out=Wt[wi][:, jj, :], in0=argt[:, jj * 128:(jj + 1) * 128],
                                                                          scalar=sc[:, col:col + 1], in1=GTs[:, g, :], op0=ALU.mult, op1=ALU.mult),
                          reads=[B_arg, B_sc, M.B_GT], writes=[B_Wt[wi]])

            def stage1(g, po0, pob0):
                wi = g % 2
                for jj in range(4):
                    j = g * 4 + jj
                    c0 = (j % 8) * 64
                    P.pe(lambda e, jj=jj, j=j, c0=c0: e.matmul(po0[:, c0:c0 + 64], lhsT=Wt[wi][:, jj, :], rhs=x_tm_ap[:, j * 64:(j + 1) * 64],
                                                             start=True, stop=True), reads=[B_Wt[wi], B_x], writes=[pob0])

            def finish(half, po0, pob0):
                po2, pob2 = next_big_m(M)
                for gg in range(2):
                    g = half * 2 + gg
                    P.pe(lambda e, g=g, gg=gg: e.matmul(po2[:, gg * 256:(gg + 1) * 256], lhsT=CT[:, g, tsl], rhs=st.Sb[:, g * 256:(g + 1) * 256],
                                                        start=True, stop=True), reads=[B_bc, st.B], writes=[pob2])
                tmp = htmp[:, half * 512:(half + 1) * 512]
                ecs = sc[:, SC_ECS + o16 + half * 8:SC_ECS + o16 + half * 8 + 8]
                P.dve(lambda e: e.tensor_tensor(
                    out=tmp.rearrange("p (j q) -> p j q", j=8), in0=po2.rearrange("p (j q) -> p j q", j=8),
                    in1=ecs.unsqueeze(2).to_broadcast([128, 8, 64]), op=ALU.mult), reads=[pob2, B_sc], writes=[M.B_ht])
                dst = ys_ap[:, half * 512:(half + 1) * 512]
                if first:
                    P.dve(lambda e: e.tensor_tensor(out=dst, in0=po0, in1=tmp, op=ALU.add), reads=[pob0, M.B_ht], writes=[B_ys])
                else:
                    P.dve(lambda e: e.tensor_tensor(out=tmp, in0=po0, in1=tmp, op=ALU.add), reads=[pob0, M.B_ht], writes=[M.B_ht])
                    P.pool(lambda e: e.tensor_tensor(out=dst, in0=dst, in1=tmp, op=ALU.add), reads=[M.B_ht, B_ys], writes=[B_ys])

            pa, pab = next_big_m(M)
            stage0(0)
            yield
            stage0(1)
            stage0b(0)
            yield
            stage1(0, pa, pab)
            stage0(2)
            stage0b(1)
            yield
            stage1(1, pa, pab)
            finish(0, pa, pab)
            yield
            pb_, pbb = next_big_m(M)
            stage0(3)
            stage0b(2)
            yield
            stage1(2, pb_, pbb)
            stage0b(3)
            stage1(3, pb_, pbb)
            finish(1, pb_, pbb)
            yield

        cvt = T("cvt", [32, 128])
        bmr = [T("bmr%d" % i, [2, 512]) for i in range(3)]
        B_bmr = [Buf("bmr%d" % i) for i in range(3)]
        mrow = [T("mrow%d" % i, [2, 512]) for i in range(3)]
        B_mrow = [Buf("mrow%d" % i) for i in range(3)]
        csT3 = csT[:].rearrange("p (c k) -> p c k", c=2)

        def mod_prologue():
            P.dma(lambda e: e.dma_start(out=cvt[:], in_=cv), writes=[B_cv])
            P.act(lambda e: e.activation(out=cvt[:], in_=cvt[:], func=AF.Silu), reads=[B_cv], writes=[B_cv])
            pt, pb = next_sm()
            P.pe(lambda e: e.transpose(pt[:, 0:32], cvt[:], ident[0:32, 0:32]), reads=[B_cv, B_c], writes=[pb])
            P.dve(lambda e: e.tensor_copy(out=csT[:], in_=pt[:, 0:32]), reads=[pb], writes=[B_cv])

        def mod_load(c0, ncols, wt, wb):
            P.dma(lambda e: e.dma_start(out=wt[:, :, 0:ncols], in_=w_mod[:, c0:c0 + ncols].rearrange("(kt p) c -> p kt c", p=128)), writes=[wb], eng="pool")

        mod_cnt = [0]

        def mod_compute(c0, ncols, wt, wb, defer=False, small_psum=False):
            mi = mod_cnt[0] % 3
            mod_cnt[0] += 1
            P.dma(lambda e: e.dma_start(out=bmr[mi][:, 0:ncols], in_=b_mod[0:1, c0:c0 + ncols].to_broadcast([2, ncols])), writes=[B_bmr[mi]])
            po, pob = next_sm() if small_psum else next_big()
            for kt in range(KT):
                P.pe(lambda e, kt=kt: e.matmul(po[0:2, 0:ncols], lhsT=csT3[:, :, kt], rhs=wt[:, kt, 0:ncols], start=(kt == 0), stop=(kt == KT - 1)),
                     reads=[B_cv, wb], writes=[pob])
            P.dve(lambda e: e.tensor_tensor(out=mrow[mi][:, 0:ncols], in0=po[0:2, 0:ncols], in1=bmr[mi][:, 0:ncols], op=ALU.add),
                  reads=[pob, B_bmr[mi]], writes=[B_mrow[mi]])
            def store():
                P.dma(lambda e: e.dma_start(out=modscr[:, c0:c0 + ncols], in_=mrow[mi][:, 0:ncols]),
                      reads=[B_mrow[mi]], writes=[B_modscr])
            if defer:
                return store
            store()
            return None

        def mod_cols(j0, nj):
            for ci in range(2):
                P.dma(lambda e, ci=ci: e.dma_start(out=ld[0:nj, :], in_=modscr[ci:ci + 1, j0 * 128:(j0 + nj) * 128].rearrange("o (j p) -> (o j) p", p=128)),
                      reads=[B_modscr], writes=[B_ld])
                pt, pb = next_sm()
                P.pe(lambda e, pt=pt: e.transpose(pt[:, 0:nj], ld[0:nj, :], ident[0:nj, 0:nj]), reads=[B_ld, B_c], writes=[pb])
                P.dve(lambda e, pt=pt, ci=ci: e.tensor_copy(out=modT[:, ci, j0:j0 + nj], in_=pt[:, 0:nj]), reads=[pb], writes=[B_modT])

        def mod_consts(dst, gcol, sc_off, sh_off):
            for ci in range(2):
                P.dve(lambda e, ci=ci: e.scalar_tensor_tensor(
                    out=modc[:, ci, dst, :], in0=modT[:, ci, sc_off:sc_off + 16], scalar=1.0, in1=colp[:, gcol:gcol + 16],
                    op0=ALU.add, op1=ALU.mult), reads=[B_modT, B_c], writes=[B_modc])
                P.dve(lambda e, ci=ci: e.tensor_copy(out=modc[:, ci, dst + 1, :], in_=modT[:, ci, sh_off:sh_off + 16]),
                      reads=[B_modT], writes=[B_modc])

        def gate_consts():
            for ci in range(2):
                P.dve(lambda e, ci=ci: e.tensor_tensor(out=wgm[:, ci, :, :], in0=wg[:], in1=modc[:, ci, 0, :].unsqueeze(2).to_broadcast([128, KT, 48]), op=ALU.mult),
                      reads=[B_c, B_modc], writes=[B_wgm])
                pg, pgb = next_sm()
                for kt in range(KT):
                    P.pe(lambda e, kt=kt, ci=ci, pg=pg: e.matmul(pg[:, 0:48], lhsT=modc[:, ci, 1, kt:kt + 1].to_broadcast([128, 128]), rhs=wg[:, kt, :],
                                                                 start=(kt == 0), stop=(kt == KT - 1)), reads=[B_modc, B_c], writes=[pgb])
                P.dve(lambda e, ci=ci, pg=pg: e.tensor_tensor(out=gbB[:, ci, :], in0=pg[:, 0:48], in1=rowp[:, 0:48], op=ALU.add),
                      reads=[pgb, B_c], writes=[B_wgm])

        with contextlib.ExitStack() as es1:
            scall = T("scall", [128, 16, NSC], es=es1)
            B_scall = [Buf("scall%d" % i) for i in range(16)]
            with contextlib.ExitStack() as es:
                I = alloc_inproj(es)
                uTs = T("uTs", [128, KT, 2048], BF16, es=es)
                B_uTs = [Buf("uTs%d" % i) for i in range(16)]
                pre2 = [T("pre_s%d" % i, [128, 34 * 64], es=es) for i in range(2)]
                B_pre2 = [Buf("pre_s0"), Buf("pre_s1")]
                cacc = T("cacc_s", [128, 2048], es=es)
                B_cacc = Buf("cacc_s")
                cvo = T("cvo_s", [128, 2048], BF16, es=es)
                B_cvo = Buf("cvo_s")
                ost = [T("ost%d" % i, [128, 512], BF16, es=es) for i in range(2)]
                B_ost = [Buf("ost0"), Buf("ost1")]
                for i_ in range(2):
                    P.pool(lambda e, i_=i_: e.memset(pre2[i_][:], 0.0), writes=[B_pre2[i_]])
                pgall_flat = T("pgall", [128, 2048], es=es)
                pgall = pgall_flat[:, 0:768].rearrange("p (c k) -> p c k", k=48)
                B_pg = Buf("pgall")
                mod_prologue()
                for blk_i in range(8):
                    wi_ = I.wrot[0] % 2
                    I.wrot[0] += 1
                    mod_load(blk_i * 512, 512, I.wbuf[wi_], I.B_w[wi_])
                    mod_compute(blk_i * 512, 512, I.wbuf[wi_], I.B_w[wi_])
                mod_cols(0, 32)
                mod_consts(0, C_GPRE, 16, 0)
                gate_consts()
                check("A0")
                hnd = uT_s1(I, xf[0:128, :])
                for c in range(16):
                    nxt = uT_s1(I, xf[(c + 1) * 128:(c + 2) * 128, :]) if c < 15 else None
                    uT_s2(I, hnd, 1, 0, uTs, c * 128, B_uTs[c], want_f32=True)
                    gate_preact(I, 1, pgall, c, B_pg)
                    hnd = nxt
                check("A1")
                prep_scalars_batched(16, pgall, B_pg, scall, B_scall, es, cacc, B_cacc)
                check("B1a1")
                orot = [0]
                for (c0, dst0) in ((512, 0), (1024, 512), (1536, 1024)):
                    wt, wb = load_wblock(I, win_cols(c0, 512))
                    for c in range(16):
                        po, pob = next_big()
                        for kt in range(KT):
                            P.pe(lambda e, kt=kt, c=c, wt=wt, po=po: e.matmul(po, lhsT=uTs[:, kt, c * 128:(c + 1) * 128], rhs=wt[:, kt, :],
                                                                          start=(kt == 0), stop=(kt == KT - 1)), reads=[B_uTs[c], wb], writes=[pob])
                        oi = orot[0] % 2
                        orot[0] += 1
                        P.act(lambda e, oi=oi, po=po: e.copy(out=ost[oi][:], in_=po), reads=[pob], writes=[B_ost[oi]])
                        P.dma(lambda e, oi=oi, c=c, dst0=dst0: e.dma_start(out=stg[c, :, dst0:dst0 + 512], in_=ost[oi][:]), reads=[B_ost[oi]], writes=[B_stg[c]])
                check("B1a2")
                cacc2 = [cacc, pgall_flat]
                B_cacc2 = [B_cacc, B_pg]
                tap_sets = [grid_taps(pre2[i_][:], cacc2[i_][:, 0:2048], 32) for i_ in range(2)]
                cvo2 = [cvo, I.uTf_flat[:].bitcast(BF16)[:, 0:2048]]
                B_cvo2 = [B_cvo, I.B_uTf]

                def fm_front(ct, cc, wt, wb):
                    pre, B_pre = pre2[ct % 2], B_pre2[ct % 2]
                    cen_in, cen_out, taps = tap_sets[ct % 2]
                    for g in range(4):
                        po, pob = next_big()
                        for kt in range(KT):
                            P.pe(lambda e, kt=kt, g=g, po=po: e.matmul(po, lhsT=wt[:, kt, cc * 128:(cc + 1) * 128],
                                                                      rhs=uTs[:, kt, g * 512:(g + 1) * 512], start=(kt == 0), stop=(kt == KT - 1)),
                                 reads=B_uTs[4 * g:4 * g + 4] + [wb], writes=[pob])
                        P.act(lambda e, g=g, po=po: e.copy(out=pre[:, 64 + 512 * g:64 + 512 * (g + 1)], in_=po), reads=[pob], writes=[B_pre])
                    conv_apply(cen_in, cen_out, taps, ct, B_pre, B_cacc2[ct % 2], cacc2[ct % 2][:, 0:2048], None, None, do_silu=False)

                def fm_back(ct):
                    cv_, bcv = cvo2[ct % 2], B_cvo2[ct % 2]
                    P.act(lambda e: e.activation(out=cv_[:], in_=cacc2[ct % 2][:, 0:2048], func=AF.Silu), reads=[B_cacc2[ct % 2]], writes=[bcv])
                    for c4 in range(4):
                        pt, pb = next_tr()
                        ptb = pt[:].bitcast(BF16)
                        for j in range(4):
                            c = c4 * 4 + j
                            P.pe(lambda e, c=c, j=j, ptb=ptb: e.transpose(ptb[:, j * 128:(j + 1) * 128], cv_[:, c * 128:(c + 1) * 128], identb[:]),
                                 reads=[bcv, B_c], writes=[pb])
                        oi = orot[0] % 2
                        orot[0] += 1
                        P.act(lambda e, oi=oi, ptb=ptb: e.copy(out=ost[oi][:], in_=ptb[:, 0:512]), reads=[pb], writes=[B_ost[oi]])
                        dcol = 1536 + ct * 128
                        P.dma(lambda e, oi=oi, c4=c4, dcol=dcol: e.dma_start(out=stg[c4 * 4:c4 * 4 + 4, :, dcol:dcol + 128].rearrange("c p n -> p c n"),
                                                                            in_=ost[oi][:].rearrange("p (c n) -> p c n", c=4)),
                              reads=[B_ost[oi]], writes=B_stg[c4 * 4:c4 * 4 + 4])

                pending = None
                for blk_i in range(3):
                    wt, wb = load_wblock(I, win_cols(4112 + blk_i * 512, 512))
                    for cc in range(4):
                        ct = blk_i * 4 + cc
                        fm_front(ct, cc, wt, wb)
                        if pending is not None:
                            fm_back(pending)
                        pending = ct
                fm_back(pending)
            P.barrier()
            check("B1a")
            with contextlib.ExitStack() as es:
                Ms = [alloc_mixer(es), alloc_mixer(es)]
                sgt = [[T("sgt%d_%d" % (d, i), [128, 3072], BF16, es=es) for i in range(2)] for d in range(2)]
                B_sgt = [[Buf("sgt"), Buf("sgt")] for d in range(2)]
                stcs = [new_state("stc%d" % d, es, bf=False) for d in range(2)]
                E = [new_state("E%d" % i, es, bf=False) for i in range(4)]
                for s_ in E:
                    state_zero(s_, bf=False)
                ld2 = [T("ld2_%d" % i, [128, 128], es=es) for i in range(2)]
                B_ld2 = [Buf("ld2a"), Buf("ld2b")]

                refall = T("refall", [128, 16, 8], es=es)
                cdall = T("cdall", [128, 16, 8], es=es)
                wvall = T("wvall", [128, 16, 8], es=es)
                wvball = T("wvball", [128, 16, 8], BF16, es=es)
                B_rf = [Buf("rf0"), Buf("rf1")]

                def sp_chain(d):
                    stc = stcs[d]
                    M = Ms[d]
                    d4, d16 = 4 * d, 16 * d
                    order = list(range(16)) if d == 0 else list(range(15, -1, -1))
                    P.dma(lambda e: e.dma_start(out=stc.m[:], in_=st_m[0:1, d4:d4 + 4].to_broadcast([128, 4])), writes=[stc.B])
                    for n_, c in enumerate(order):
                        if (d == 0 and c % 4 == 0) or (d == 1 and c % 4 == 1):
                            fcol = (0 if d == 0 else 32) + c
                            P.dve(lambda e, fcol=fcol: e.scalar_tensor_tensor(out=E[2 * d].m[:], in0=stc.m[:], scalar=flags[:, fcol:fcol + 1], in1=E[2 * d].m[:],
                                                                              op0=ALU.mult, op1=ALU.add), reads=[stc.B, E[2 * d].B, B_c], writes=[E[2 * d].B])
                        if (d == 0 and c % 4 == 2) or (d == 1 and c % 4 == 3):
                            fcol = (16 if d == 0 else 48) + c
                            P.dve(lambda e, fcol=fcol: e.scalar_tensor_tensor(out=E[2 * d + 1].m[:], in0=stc.m[:], scalar=flags[:, fcol:fcol + 1], in1=E[2 * d + 1].m[:],
                                                                              op0=ALU.mult, op1=ALU.add), reads=[stc.B, E[2 * d + 1].B, B_c], writes=[E[2 * d + 1].B])
                        if n_ == 15:
                            break
                        P.dve(lambda e, c=c: e.tensor_tensor(out=refall[:, c, d4:d4 + 4], in0=stc.m[:], in1=scall[:, c, SC_AMX + d4:SC_AMX + d4 + 4], op=ALU.max),
                              reads=[stc.B, B_scall[c]], writes=[B_rf[d]])
                        P.dve(lambda e, c=c: e.tensor_tensor(out=cdall[:, c, d4:d4 + 4], in0=stc.m[:], in1=refall[:, c, d4:d4 + 4], op=ALU.subtract),
                              reads=[stc.B, B_rf[d]], writes=[B_rf[d]])
                        P.dve(lambda e, c=c: e.tensor_tensor(out=stc.m[:], in0=scall[:, c, SC_BT + d4:SC_BT + d4 + 4], in1=refall[:, c, d4:d4 + 4], op=ALU.add),
                              reads=[B_scall[c], B_rf[d]], writes=[stc.B])
                        if n_ % 4 == 3:
                            yield
                    cs_ = slice(0, 15) if d == 0 else slice(1, 16)
                    P.dve(lambda e: e.tensor_tensor(out=wvall[:, cs_, d4:d4 + 4], in0=scall[:, cs_, SC_A + d4:SC_A + d4 + 4], in1=refall[:, cs_, d4:d4 + 4], op=ALU.subtract),
                          reads=B_scall + [B_rf[d]], writes=[B_rf[d]])
                    P.act(lambda e: e.activation(out=wvall[:, cs_, d4:d4 + 4], in_=wvall[:, cs_, d4:d4 + 4], func=AF.Exp), reads=[B_rf[d]], writes=[B_rf[d]])
                    P.act(lambda e: e.activation(out=cdall[:, cs_, d4:d4 + 4], in_=cdall[:, cs_, d4:d4 + 4], func=AF.Exp), reads=[B_rf[d]], writes=[B_rf[d]])
                    P.dve(lambda e: e.tensor_copy(out=wvball[:, cs_, d4:d4 + 4], in_=wvall[:, cs_, d4:d4 + 4]), reads=[B_rf[d]], writes=[B_rf[d]])
                    yield
                    P.dma(lambda e: e.dma_start(out=stc.Cn[:], in_=st_c[d].rearrange("h d e -> d h e")), writes=[stc.B])
                    P.dma(lambda e: e.dma_start(out=ld2[d][0:4, :], in_=st_n[d]), writes=[B_ld2[d]])
                    pt, pb = next_sm()
                    P.pe(lambda e, pt=pt: e.transpose(pt[:, 0:4], ld2[d][0:4, :], ident[0:4, 0:4]), reads=[B_ld2[d], B_c], writes=[pb])
                    P.dve(lambda e, pt=pt: e.tensor_copy(out=stc.Nn[:], in_=pt[:, 0:4]), reads=[pb], writes=[stc.B])
                    for a in range(8):
                        P.dma(lambda e, a=a: e.dma_start(out=ld2[d][:], in_=st_s[d, a * 128:(a + 1) * 128, :]), writes=[B_ld2[d]])
                        pt, pb = next_sm()
                        P.pe(lambda e, pt=pt: e.transpose(pt[:, 0:128], ld2[d][:], ident[:]), reads=[B_ld2[d], B_c], writes=[pb])
                        P.dve(lambda e, pt=pt, a=a: e.tensor_copy(out=stc.Sf[:, a * 128:(a + 1) * 128], in_=pt[:, 0:128]), reads=[pb], writes=[stc.B])
                        if a % 4 == 3:
                            yield
                    for n_, c in enumerate(order):
                        si = n_ % 2
                        sg, bsg = sgt[d][si], B_sgt[d][si]

                        def macc3(Et, fcol):
                            f_ap = flags[:, fcol:fcol + 1]
                            for a_, b_ in ((Et.Cn, stc.Cn), (Et.Nn, stc.Nn), (Et.Sf, stc.Sf)):
                                P.dve(lambda e, a_=a_, b_=b_: e.scalar_tensor_tensor(out=a_[:], in0=b_[:], scalar=f_ap, in1=a_[:], op0=ALU.mult, op1=ALU.add),
                                      reads=[stc.B, Et.B, B_c], writes=[Et.B])
                        if d == 0 and c % 4 == 0:
                            macc3(E[0], c)
                        if d == 0 and c % 4 == 2:
                            macc3(E[1], 16 + c)
                        if d == 1 and c % 4 == 1:
                            macc3(E[2], 32 + c)
                        if d == 1 and c % 4 == 3:
                            macc3(E[3], 48 + c)
                        if n_ == 15:
                            break
                        P.dma(lambda e, sg=sg, c=c: e.dma_start(out=sg[:], in_=stg[c]), reads=[B_stg[c]], writes=[bsg])
                        vp, xw = M.vp, M.xw
                        k_tm = sg[:, 0:512].rearrange("p (h d) -> p h d", h=4)
                        Btm = sg[:, 2560:3072].rearrange("p (g n) -> p g n", g=4)
                        for h in range(4):
                            P.act(lambda e, sg=sg, c=c, h=h: e.activation(out=vp[:, h, :], in_=sg[:, 512 + h * 256:512 + (h + 1) * 256], func=AF.Copy,
                                                                         scale=wvall[:, c, d4 + h:d4 + h + 1]),
                                  reads=[bsg, B_rf[d]], writes=[M.B_vp])
                        P.pool(lambda e, sg=sg, c=c: e.tensor_tensor(out=xw[:].rearrange("p (j q) -> p j q", j=16), in0=sg[:, 1536:2560].rearrange("p (j q) -> p j q", j=16),
                                                                     in1=scall[:, c, SC_WK + d16:SC_WK + d16 + 16].unsqueeze(2).to_broadcast([128, 16, 64]), op=ALU.mult),
                               reads=[bsg, B_scall[c]], writes=[M.B_xw])
                        pcs = [next_big_m(M), next_big_m(M)]
                        for half in range(2):
                            po0, pob0 = pcs[half]
                            for hh in range(2):
                                h = half * 2 + hh
                                P.pe(lambda e, h=h, hh=hh, po0=po0, k_tm=k_tm: e.matmul(po0[:, hh * 256:(hh + 1) * 256], lhsT=k_tm[:, h, :], rhs=vp[:, h, :], start=True, stop=True),
                                     reads=[bsg, M.B_vp], writes=[pob0])
                        pn, pnb = next_sm()
                        for h in range(4):
                            P.pe(lambda e, h=h, c=c, k_tm=k_tm, pn=pn: e.matmul(pn[:, h:h + 1], lhsT=k_tm[:, h, :], rhs=wvball[:, c, d4 + h:d4 + h + 1], start=True, stop=True),
                                 reads=[bsg, B_rf[d]], writes=[pnb])
                        for h in range(4):
                            po0, pob0 = pcs[h // 2]
                            P.dve(lambda e, h=h, c=c, po0=po0: e.scalar_tensor_tensor(out=stc.Cn[:, h, :], in0=stc.Cn[:, h, :], scalar=cdall[:, c, d4 + h:d4 + h + 1],
                                                                                   in1=po0[:, (h % 2) * 256:(h % 2 + 1) * 256], op0=ALU.mult, op1=ALU.add),
                                  reads=[stc.B, B_rf[d], pob0], writes=[stc.B])
                        P.dve(lambda e, c=c: e.tensor_tensor(out=stc.Nn[:], in0=stc.Nn[:], in1=cdall[:, c, d4:d4 + 4], op=ALU.mult), reads=[stc.B, B_rf[d]], writes=[stc.B])
                        P.dve(lambda e, pn=pn: e.tensor_tensor(out=stc.Nn[:], in0=pn[:, 0:4], in1=stc.Nn[:], op=ALU.add), reads=[pnb, stc.B], writes=[stc.B])
                        yield
                        pss = [next_big_m(M), next_big_m(M)]
                        for half in range(2):
                            po0, pob0 = pss[half]
                            for gg in range(2):
                                g = half * 2 + gg
                                P.pe(lambda e, g=g, gg=gg, po0=po0, Btm=Btm: e.matmul(po0[:, gg * 256:(gg + 1) * 256], lhsT=Btm[:, g, :], rhs=xw[:, g * 256:(g + 1) * 256], start=True, stop=True),
                                     reads=[bsg, M.B_xw], writes=[pob0])
                        P.pool(lambda e, c=c: e.tensor_tensor(out=stc.Sf[:].rearrange("p (j q) -> p j q", j=16), in0=stc.Sf[:].rearrange("p (j q) -> p j q", j=16),
                                                              in1=scall[:, c, SC_ETOT + d16:SC_ETOT + d16 + 16].unsqueeze(2).to_broadcast([128, 16, 64]), op=ALU.mult),
                               reads=[stc.B, B_scall[c]], writes=[stc.B])
                        for half in range(2):
                            po0, pob0 = pss[half]
                            P.dve(lambda e, half=half, po0=po0: e.tensor_tensor(out=stc.Sf[:, half * 512:(half + 1) * 512], in0=po0,
                                                                              in1=stc.Sf[:, half * 512:(half + 1) * 512], op=ALU.add),
                                  reads=[pob0, stc.B], writes=[stc.B])
                        yield

                gens = [sp_chain(0), sp_chain(1)]
                nstep = 0
                while gens:
                    for g_ in list(gens):
                        try:
                            next(g_)
                        except StopIteration:
                            gens.remove(g_)
                for i in range(4):
                    state_store_scr(E[i], i)
        P.barrier()
        check("B1b")

        mod_k = [16]
        mod_pend = [None]

        def super_unit(su):
            ci = su
            ntok = 512 if su == 0 else 640
            src = xp if su == 0 else xo
            with contextlib.ExitStack() as esu:
                sc4 = T("sc4", [128, 4, NSC], es=esu)
                B_sc4 = [Buf("sc4_%d" % i) for i in range(4)]
                qT = T("qT", [128, 4, 512], BF16, es=esu)
                kTt = T("kT", [128, 4, 512], BF16, es=esu)
                B_qk = Buf("qk")
                k_tm = T("k_tm", [128, 4, 4, 128], BF16, es=esu)
                B_ktm = [Buf("ktm%d" % i) for i in range(4)]
                v_tm = T("v_tm", [128, 4, 1024], BF16, es=esu)
                B_v = [Buf("v%d" % i) for i in range(4)]
                og = T("og", [128, 4, 1024], BF16, es=esu)
                B_og = [Buf("og%d" % i) for i in range(4)]
                sz = T("sz", [128, 4, 1024], BF16, es=esu)
                B_sz = [Buf("sz%d" % i) for i in range(4)]
                BCT = T("BCT", [128, 8, 512], BF16, es=esu)
                B_bct = Buf("BCT")
                x_tm = T("x_tm", [128, 4, 1024], BF16, es=esu)
                B_xtm = [Buf("xtm%d" % i) for i in range(4)]
                B_tm = T("B_tm", [128, 4, 4, 128], BF16, es=esu)
                B_Btm = [Buf("Btm%d" % i) for i in range(4)]
                gml = T("gml", [128, 1024], es=esu)
                gsn = T("gsn", [128, 1024], es=esu)
                B_gg = Buf("gg")
                P.dma(lambda e: e.dma_start(out=gml[:], in_=g_ml[0:1, :].to_broadcast([128, 1024])), writes=[B_gg])
                P.dma(lambda e: e.dma_start(out=gsn[:], in_=g_sn[0:1, :].to_broadcast([128, 1024])), writes=[B_gg])
                with contextlib.ExitStack() as es:
                    I = alloc_inproj(es, nbuf=3)
                    uT = T("uT", [128, KT, ntok], BF16, es=es)
                    B_uT = [Buf("uT%d" % i) for i in range(ntok // 128)]
                    pre = T("pre_u", [128, 640], es=es)
                    B_pre = Buf("pre_u")
                    cacc = T("cacc_u", [128, 512], es=es)
                    B_cacc = Buf("cacc_u")
                    xTt = [T("xTt%d" % i, [128, 512], BF16, es=es) for i in range(2)]
                    B_xTt = [Buf("xTt0"), Buf("xTt1")]
                    sig = T("sigt", [128, 512], es=es)
                    B_sig = Buf("sigt")
                    pg4 = T("pg4", [128, 4, 48], es=es)
                    B_pg4 = Buf("pg4")
                    hnd = uT_s1(I, src[0:128, :])
                    for tt in range(4):
                        if tt < 3:
                            nxt = uT_s1(I, src[(tt + 1) * 128:(tt + 2) * 128, :])
                        else:
                            nxt = uT_s1(I, xh[:, :]) if su == 1 else None
                        uT_s2(I, hnd, ci, 0, uT, tt * 128, B_uT[tt], want_f32=True)
                        gate_preact(I, ci, pg4, tt, B_pg4)
                        hnd = nxt
                    if su == 1:
                        uT_s2(I, hnd, ci, 0, uT, 512, B_uT[4])
                    prep_scalars_batched(4, pg4, B_pg4, sc4, B_sc4, es, cacc, B_cacc)
                    B_own = B_uT[0:4]
                    for (c0, dstT, scale) in ((0, qT, 128.0 ** -0.5), (512, kTt, 1.0)):
                        wt, wb = load_wblock(I, win_cols(c0, 512))
                        for h in range(4):
                            po, pob = next_big()
                            for kt in range(KT):
                                P.pe(lambda e, kt=kt, h=h, wt=wt, po=po: e.matmul(po, lhsT=wt[:, kt, h * 128:(h + 1) * 128], rhs=uT[:, kt, 0:512],
                                                                              start=(kt == 0), stop=(kt == KT - 1)), reads=B_own + [wb], writes=[pob])
                            P.act(lambda e, h=h, po=po, dstT=dstT, scale=scale: e.activation(out=dstT[:, h, :], in_=po, func=AF.Copy, scale=scale),
                                  reads=[pob], writes=[B_qk])
                    for c in range(4):
                        pt, pb = next_tr()
                        ptb = pt[:].bitcast(BF16)
                        for h in range(4):
                            P.pe(lambda e, c=c, h=h, ptb=ptb: e.transpose(ptb[:, h * 128:(h + 1) * 128], kTt[:, h, c * 128:(c + 1) * 128], identb[:]),
                                 reads=[B_qk, B_c], writes=[pb])
                        P.dve(lambda e, c=c, ptb=ptb: e.tensor_copy(out=k_tm[:, c, :, :], in_=ptb[:, 0:512].rearrange("p (h d) -> p h d", h=4)),
                              reads=[pb], writes=[B_ktm[c]])

                    def tm_block(c0, evac):
                        wt, wb = load_wblock(I, win_cols(c0, 512))
                        for c in range(4):
                            po, pob = next_big()
                            for kt in range(KT):
                                P.pe(lambda e, kt=kt, c=c, wt=wt, po=po: e.matmul(po, lhsT=uT[:, kt, c * 128:(c + 1) * 128], rhs=wt[:, kt, :],
                                                                              start=(kt == 0), stop=(kt == KT - 1)), reads=[B_uT[c], wb], writes=[pob])
                            evac(c, po, pob)

                    for half in range(2):
                        def ev_v(c, po, pob, half=half):
                            P.act(lambda e: e.copy(out=v_tm[:, c, half * 512:(half + 1) * 512], in_=po), reads=[pob], writes=[B_v[c]])
                        tm_block(1024 + half * 512, ev_v)
                    for half in range(2):
                        def ev_o(c, po, pob, half=half):
                            P.act(lambda e: e.activation(out=sig[:], in_=po, func=AF.Sigmoid), reads=[pob], writes=[B_sig])
                            P.dve(lambda e: e.tensor_tensor(out=og[:, c, half * 512:(half + 1) * 512], in0=sig[:], in1=gml[:, half * 512:(half + 1) * 512], op=ALU.mult),
                                  reads=[B_sig, B_gg], writes=[B_og[c]])
                        tm_block(2048 + half * 512, ev_o)
                    for half in range(2):
                        def ev_z(c, po, pob, half=half):
                            P.act(lambda e: e.activation(out=sz[:, c, half * 512:(half + 1) * 512], in_=po, func=AF.Silu), reads=[pob], writes=[B_sz[c]])
                        tm_block(3088 + half * 512, ev_z)
                    if su == 0:
                        cen_in, cen_out, taps = seq_taps(pre[:, 0:512], cacc[:])
                    else:
                        cen_in, cen_out, taps = grid_taps(pre[:], cacc[:], 8)
                    cacc_b = [cacc, T("cacc_u2", [128, 512], es=es)]
                    B_cacc_b = [B_cacc, Buf("cacc_u2")]
                    pre_b = [pre, T("pre_u2", [128, 640], es=es)]
                    B_pre_b = [B_pre, Buf("pre_u2")]
                    if su == 0:
                        tapsets = [seq_taps(pre_b[i_][:, 0:512], cacc_b[i_][:]) for i_ in range(2)]
                    else:
                        tapsets = [grid_taps(pre_b[i_][:], cacc_b[i_][:], 8) for i_ in range(2)]

                    def x_front(ct, cc, wt, wb):
                        pr, bpr = pre_b[ct % 2], B_pre_b[ct % 2]
                        cen_in, cen_out, taps = tapsets[ct % 2]
                        po, pob = next_big()
                        for kt in range(KT):
                            P.pe(lambda e, kt=kt, po=po: e.matmul(po, lhsT=wt[:, kt, cc * 128:(cc + 1) * 128], rhs=uT[:, kt, 0:512],
                                                                  start=(kt == 0), stop=(kt == KT - 1)), reads=B_own + [wb], writes=[pob])
                        if su == 0:
                            P.act(lambda e, po=po: e.copy(out=pr[:, 0:512], in_=po), reads=[pob], writes=[bpr])
                        else:
                            P.act(lambda e, po=po: e.copy(out=pr[:, 64:576], in_=po), reads=[pob], writes=[bpr])
                            po2, pob2 = next_big()
                            for kt in range(KT):
                                P.pe(lambda e, kt=kt, po2=po2: e.matmul(po2[:, 0:128], lhsT=wt[:, kt, cc * 128:(cc + 1) * 128], rhs=uT[:, kt, 512:640],
                                                                        start=(kt == 0), stop=(kt == KT - 1)), reads=[B_uT[4], wb], writes=[pob2])
                            P.dve(lambda e, po2=po2: e.tensor_scalar(out=pr[:, 0:64], in0=po2[:, 0:64], scalar1=flags[:, 64:65], scalar2=None, op0=ALU.mult),
                                  reads=[pob2, B_c], writes=[bpr])
                            P.dve(lambda e, po2=po2: e.tensor_scalar(out=pr[:, 576:640], in0=po2[:, 64:128], scalar1=flags[:, 65:66], scalar2=None, op0=ALU.mult),
                                  reads=[pob2, B_c], writes=[bpr])
                        conv_apply(cen_in, cen_out, taps, ct, bpr, B_cacc_b[ct % 2], cacc_b[ct % 2][:], None, None, do_silu=False)

                    def x_back(ct):
                        if ct < 8:
                            P.act(lambda e: e.activation(out=xTt[ct % 2][:], in_=cacc_b[ct % 2][:], func=AF.Silu), reads=[B_cacc_b[ct % 2]], writes=[B_xTt[ct % 2]])
                        else:
                            P.act(lambda e: e.activation(out=BCT[:, ct - 8, :], in_=cacc_b[ct % 2][:], func=AF.Silu), reads=[B_cacc_b[ct % 2]], writes=[B_bct])
                        if ct < 8:
                            xi = ct % 2
                            pt, pb = next_tr()
                            ptb = pt[:].bitcast(BF16)
                            for c in range(4):
                                P.pe(lambda e, c=c, ptb=ptb: e.transpose(ptb[:, c * 128:(c + 1) * 128], xTt[xi][:, c * 128:(c + 1) * 128], identb[:]),
                                     reads=[B_xTt[xi], B_c], writes=[pb])
                            P.act(lambda e, ptb=ptb: e.copy(out=x_tm[:, :, ct * 128:(ct + 1) * 128], in_=ptb[:, 0:512].rearrange("p (c n) -> p c n", c=4)),
                                  reads=[pb], writes=B_xtm)
                        elif ct < 12:
                            g = ct - 8
                            pt, pb = next_tr()
                            ptb = pt[:].bitcast(BF16)
                            for c in range(4):
                                P.pe(lambda e, c=c, ptb=ptb: e.transpose(ptb[:, c * 128:(c + 1) * 128], BCT[:, g, c * 128:(c + 1) * 128], identb[:]),
                                     reads=[B_bct, B_c], writes=[pb])
                            P.act(lambda e, ptb=ptb: e.copy(out=B_tm[:, :, g, :], in_=ptb[:, 0:512].rearrange("p (c n) -> p c n", c=4)),
                                  reads=[pb], writes=B_Btm)

                    pend = None
                    for blk_i in range(4):
                        wt, wb = load_wblock(I, win_cols(4112 + blk_i * 512, 512))
                        for cc in range(4):
                            ct = blk_i * 4 + cc
                            x_front(ct, cc, wt, wb)
                            if pend is not None:
                                x_back(pend)
                            pend = ct
                    x_back(pend)
                P.barrier()
                with contextlib.ExitStack() as es:
                    M = alloc_mixer(es)
                    M2 = alloc_mixer(es)
                    hsum = T("hsum", [128, 2, 1024], es=es)
                    B_hs = [Buf("hs0"), Buf("hs1")]
                    ysum = T("ysum", [128, 2, 1024], es=es)
                    B_ys = [Buf("ys0"), Buf("ys1")]
                    cat0 = T("cat0", [128, D], es=es)
                    cat = [cat0, cat0]
                    B_cat0 = Buf("cat0")
                    B_cat = [B_cat0, B_cat0]
                    cTo0 = T("cTout0", [128, KT * 128], BF16, es=es)
                    cTout = [cTo0, cTo0]
                    B_cTo0 = Buf("cTout0")
                    B_cTout = [B_cTo0, B_cTo0]
                    wmh = [T("wmh%d" % i, [128, KT, 256], BF16, es=es) for i in range(2)]
                    B_wmh = [Buf("wmh0"), Buf("wmh1")]
                    k_end = 28 if su == 1 else 40
                    if mod_k[0] < k_end:
                        mod_load(mod_k[0] * 256, 256, wmh[mod_k[0] % 2], B_wmh[mod_k[0] % 2])

                    def mod_bg():
                        k = mod_k[0]
                        if k >= k_end:
                            return
                        if k + 1 < k_end:
                            mod_load((k + 1) * 256, 256, wmh[(k + 1) % 2], B_wmh[(k + 1) % 2])
                        if mod_pend[0] is not None:
                            mod_pend[0]()
                        mod_pend[0] = mod_compute(k * 256, 256, wmh[k % 2], B_wmh[k % 2], defer=True, small_psum=True)
                        mod_k[0] += 1
                    sf = new_state("sf", es)
                    sb = new_state("sb", es)
                    nt4 = [T("nt4_%d" % i, [4, 128], es=es) for i in range(2)]
                    B_nt4 = [Buf("nt4a"), Buf("nt4b")]
                    sout = [M.htmp, M2.htmp]
                    B_sout = [M.B_ht, M2.B_ht]
                    htmp, B_ht = M.htmp, M.B_ht
                    def chain(Mx, un, d, st):
                        chunks = (2 * un, 2 * un + 1)
                        order = chunks if d == 0 else chunks[::-1]
                        for n_, c in enumerate(order):
                            lc = c - 2 * un
                            tsl = slice(c * 128, (c + 1) * 128)
                            sc = sc4[:, c, :]
                            first = (n_ == 0) if d == 0 else (n_ == 0)
                            first = (d == 0 and lc == 0) or (d == 1 and lc == 1)
                            mlstm_chain_scalars(Mx, sc, B_sc4[c], d, st, qT, B_qk, tsl)
                            yield
                            make_vp(Mx, v_tm[:, c, :].rearrange("p (h e) -> p h e", h=4), B_v[c])
                            mlstm_out(Mx, d, st, qT, kTt, B_qk, tsl, hsum[:, lc, :], B_hs[lc], first)
                            yield
                            ssd_gt(Mx, BCT[:, 0:4, :], BCT[:, 4:8, :], B_bct, tsl)
                            yield from ssd_out(Mx, sc, B_sc4[c], d, st, BCT[:, 4:8, :], B_bct, tsl, x_tm[:, c, :], B_xtm[c], ysum[:, lc, :], B_ys[lc], first)
                            yield
                            if su == 0 or n_ == 0:
                                mlstm_update(Mx, st, k_tm[:, c, :, :], B_ktm[c])
                                yield
                                ssd_update(Mx, sc, B_sc4[c], d, st, x_tm[:, c, :], B_xtm[c], B_tm[:, c, :, :], B_Btm[c])
                                yield
                        if su == 0:
                            seq = un
                            P.dma(lambda e, seq=seq, d=d, st=st: e.dma_start(out=o_c[seq, d].rearrange("h d e -> d h e"), in_=st.Cn[:]), reads=[st.B])
                            pt, pb = next_sm()
                            P.pe(lambda e, pt=pt, st=st: e.transpose(pt[0:4, 0:128], st.Nn[:], ident[:]), reads=[st.B, B_c], writes=[pb])
                            P.dve(lambda e, pt=pt: e.tensor_copy(out=nt4[d][:], in_=pt[0:4, 0:128]), reads=[pb], writes=[B_nt4[d]])
                            P.dma(lambda e, seq=seq, d=d: e.dma_start(out=o_n[seq, d], in_=nt4[d][:]), reads=[B_nt4[d]])
                            P.dma(lambda e, seq=seq, d=d, st=st: e.dma_start(out=o_m[seq, d:d + 1, :], in_=st.m[0:1, :]), reads=[st.B])
                            for a in range(8):
                                pt, pb = next_sm()
                                P.pe(lambda e, a=a, pt=pt, st=st: e.transpose(pt[:, 0:128], st.Sf[:, a * 128:(a + 1) * 128], ident[:]), reads=[st.B, B_c], writes=[pb])
                                P.act(lambda e, a=a, pt=pt: e.copy(out=sout[d][:, a * 128:(a + 1) * 128], in_=pt[:, 0:128]), reads=[pb], writes=[B_sout[d]])
                            P.dma(lambda e, seq=seq, d=d: e.dma_start(out=o_s[seq, d].rearrange("(a p) n -> p a n", p=128),
                                                                     in_=sout[d][:].rearrange("p (a n) -> p a n", a=8)), reads=[B_sout[d]])

                    for un in range(2):
                        chunks = (2 * un, 2 * un + 1)
                        if su == 0:
                            state_zero(sf)
                            state_zero(sb)
                        else:
                            state_load_scr(sf, 0 + un)
                            state_load_scr(sb, 2 + un)
                        gens = [chain(M, un, 0, sf), chain(M2, un, 1, sb)]
                        nstep = 0
                        while gens:
                            for g_ in list(gens):
                                try:
                                    next(g_)
                                except StopIteration:
                                    gens.remove(g_)
                                nstep += 1
                                if nstep % 6 == 0:
                                    mod_bg()
                        for c in chunks:
                            lc = c - 2 * un
                            tt = su * 4 + c
                            hs = hsum[:, lc, :]
                            ct_, B_ct = cat[lc], B_cat[lc]
                            for h in range(4):
                                P.act(lambda e, h=h, hs=hs: e.activation(out=sq_junk[:, 0:256], in_=hs[:, h * 256:(h + 1) * 256], func=AF.Square, accum_out=stat[:, 8 + h:9 + h]),
                                      reads=[B_hs[lc]], writes=[B_junk, B_stat])
                            rstd_of(stat[:, 8:12], stat[:, 12:16], 256)
                            P.dve(lambda e, hs=hs, ct_=ct_: e.tensor_tensor(out=ct_[:, 0:1024].rearrange("p (h e) -> p h e", h=4), in0=hs.rearrange("p (h e) -> p h e", h=4),
                                                                           in1=stat[:, 12:16].unsqueeze(2).to_broadcast([128, 4, 256]), op=ALU.mult),
                                  reads=[B_hs[lc], B_stat], writes=[B_ct])
                            P.dve(lambda e, c=c, ct_=ct_: e.tensor_tensor(out=ct_[:, 0:1024], in0=ct_[:, 0:1024], in1=og[:, c, :], op=ALU.mult), reads=[B_ct, B_og[c]], writes=[B_ct])
                            ysl = ysum[:, lc, :]
                            P.dve(lambda e, c=c: e.tensor_tensor(out=htmp[:].rearrange("p (j q) -> p j q", j=16), in0=x_tm[:, c, :].rearrange("p (j q) -> p j q", j=16),
                                                                 in1=rowp[:, R_DSK:R_DSK + 16].unsqueeze(2).to_broadcast([128, 16, 64]), op=ALU.mult),
                                  reads=[B_xtm[c], B_c], writes=[B_ht])
                            P.dve(lambda e, ysl=ysl: e.tensor_tensor(out=ysl, in0=ysl, in1=htmp[:], op=ALU.add), reads=[B_ys[lc], B_ht], writes=[B_ys[lc]])
                            P.dve(lambda e, ysl=ysl, c=c: e.tensor_tensor(out=ysl, in0=ysl, in1=sz[:, c, :], op=ALU.mult), reads=[B_ys[lc], B_sz[c]], writes=[B_ys[lc]])
                            P.act(lambda e, ysl=ysl: e.activation(out=sq_junk[:, 0:1024], in_=ysl, func=AF.Square, accum_out=stat[:, 16:17]),
                                  reads=[B_ys[lc]], writes=[B_junk, B_stat])
                            rstd_of(stat[:, 16:17], stat[:, 17:18], 1024)
                            P.dve(lambda e, ysl=ysl, ct_=ct_: e.scalar_tensor_tensor(out=ct_[:, 1024:2048], in0=ysl, scalar=stat[:, 17:18], in1=gsn[:], op0=ALU.mult, op1=ALU.mult),
                                  reads=[B_ys[lc], B_stat, B_gg], writes=[B_ct])
                            cTo, B_cTo = cTout[lc], B_cTout[lc]
                            for g4 in range(4):
                                pt, pb = next_tr()
                                for j in range(4):
                                    kt = g4 * 4 + j
                                    P.pe(lambda e, kt=kt, j=j, pt=pt, ct_=ct_: e.transpose(pt[:, j * 128:(j + 1) * 128], ct_[:, kt * 128:(kt + 1) * 128], ident[:]),
                                         reads=[B_ct, B_c], writes=[pb])
                                P.act(lambda e, g4=g4, pt=pt, cTo=cTo: e.copy(out=cTo[:, g4 * 512:(g4 + 1) * 512], in_=pt[:, 0:512]), reads=[pb], writes=[B_cTo])
                            P.dma(lambda e, tt=tt, cTo=cTo: e.dma_start(out=catTs[tt], in_=cTo[:]), reads=[B_cTo], writes=[B_cats[tt]])
                    while mod_k[0] < k_end:
                        mod_bg()
                    if mod_pend[0] is not None:
                        mod_pend[0]()
                        mod_pend[0] = None
            P.barrier()

        super_unit(1)
        check("SU1")
        super_unit(0)
        check("SU0")
        mod_cols(32, 48)
        mod_consts(2, C_GMLP, 64, 48)

        with contextlib.ExitStack() as escd:
            u2T = T("u2T", [128, KT, 1024], BF16, es=escd)
            B_u2T = [Buf("u2T%d" % i) for i in range(8)]
            with contextlib.ExitStack() as es:
                I = NS()
                I.xt = [T("xt%d" % i, [128, D], es=es) for i in range(2)]
                I.B_xt = [Buf("xt0"), Buf("xt1")]
                I.xrot = [0]
                I.uTf, I.B_uTf = None, None
                wo = T("wo", [128, KT, D], BF16, es=es)
                B_wo = Buf("wo")
                G1 = T("G1", [128, 2, D], es=es)
                B_G1 = Buf("G1")
                x1 = [T("x1_%d" % i, [128, D], es=es) for i in range(2)]
                B_x1 = [Buf("x1a"), Buf("x1b")]
                cTt = [T("cTt%d" % i, [128, KT, 128], BF16, es=es) for i in range(2)]
                B_cTt = [Buf("cTt0"), Buf("cTt1")]
                B_wo4 = [Buf("wo%d" % i) for i in range(4)]
                wmc = [T("wmc%d" % i, [128, KT, 256], BF16, es=es) for i in range(2)]
                B_wmc = [Buf("wmc0"), Buf("wmc1")]

                def mod_bg_c():
                    k = mod_k[0]
                    if k >= 48:
                        return
                    if k + 1 < 48:
                        mod_load((k + 1) * 256, 256, wmc[(k + 1) % 2], B_wmc[(k + 1) % 2])
                    if mod_pend[0] is not None:
                        mod_pend[0]()
                    mod_pend[0] = mod_compute(k * 256, 256, wmc[k % 2], B_wmc[k % 2], defer=True, small_psum=True)
                    mod_k[0] += 1
                for cbk in range(4):
                    P.dma(lambda e, cbk=cbk: e.dma_start(out=wo[:, :, cbk * 512:(cbk + 1) * 512],
                                                         in_=w_out[:, cbk * 512:(cbk + 1) * 512].rearrange("(kt p) c -> p kt c", p=128)), writes=[B_wo4[cbk]], eng="pool")
                mod_load(mod_k[0] * 256, 256, wmc[mod_k[0] % 2], B_wmc[mod_k[0] % 2])
                P.dma(lambda e: e.dma_start(out=x1[0][:], in_=g_post_mix[0:1, :].to_broadcast([128, D])), writes=[B_x1[0]])
                for ci in range(2):
                    P.dma(lambda e, ci=ci: e.dma_start(out=G1[:, ci, :], in_=modscr[ci:ci + 1, 2 * D:3 * D].to_broadcast([128, D])), reads=[B_modscr], writes=[B_G1])
                    P.dve(lambda e, ci=ci: e.tensor_tensor(out=G1[:, ci, :], in0=G1[:, ci, :], in1=x1[0][:], op=ALU.mult), reads=[B_G1, B_x1[0]], writes=[B_G1])
                def c_front(tt):
                    ci = 0 if tt < 4 else 1
                    li = tt % 2
                    P.dma(lambda e, li=li, tt=tt: e.dma_start(out=cTt[li][:].rearrange("p k t -> p (k t)"), in_=catTs[tt]), reads=[B_cats[tt]], writes=[B_cTt[li]])
                    for cbk in range(4):
                        for kt in range(KT):
                            P.pe(lambda e, kt=kt, cbk=cbk, li=li: e.matmul(pbig[:, cbk * 512:(cbk + 1) * 512], lhsT=cTt[li][:, kt, :],
                                                                           rhs=wo[:, kt, cbk * 512:(cbk + 1) * 512], start=(kt == 0), stop=(kt == KT - 1)),
                                 reads=[B_cTt[li], B_wo4[cbk]], writes=[PB[cbk]])

                def c_front_b(tt):
                    for cbk in range(4):
                        P.dve(lambda e, cbk=cbk, tt=tt: e.tensor_copy(out=x1[tt % 2][:, cbk * 512:(cbk + 1) * 512], in_=pbig[:, cbk * 512:(cbk + 1) * 512]),
                              reads=[PB[cbk]], writes=[B_x1[tt % 2]])

                def c_back(tt):
                    ci = 0 if tt < 4 else 1
                    xi = tt % 2
                    P.act(lambda e, xi=xi: e.activation(out=sq_junk[:], in_=x1[xi][:], func=AF.Square, accum_out=stat[:, 20:21]), reads=[B_x1[xi]], writes=[B_junk, B_stat])
                    rstd_of(stat[:, 20:21], stat[:, 21:22], D)
                    src_rows = xp[tt * 128:(tt + 1) * 128, :] if tt < 4 else xo[(tt - 4) * 128:(tt - 3) * 128, :]
                    xr, bxr = I.xt[I.xrot[0] % 2], I.B_xt[I.xrot[0] % 2]
                    I.xrot[0] += 1
                    P.dma(lambda e, xr=xr, src_rows=src_rows: e.dma_start(out=xr[:], in_=src_rows), writes=[bxr])
                    P.dve(lambda e, xi=xi, ci=ci: e.scalar_tensor_tensor(out=x1[xi][:], in0=x1[xi][:], scalar=stat[:, 21:22], in1=G1[:, ci, :], op0=ALU.mult, op1=ALU.mult),
                          reads=[B_x1[xi], B_stat, B_G1], writes=[B_x1[xi]])
                    P.dve(lambda e, xi=xi, xr=xr: e.tensor_tensor(out=x1[xi][:], in0=x1[xi][:], in1=xr[:], op=ALU.add), reads=[B_x1[xi], bxr], writes=[B_x1[xi]])
                    P.dma(lambda e, xi=xi, tt=tt: e.dma_start(out=x1s[tt * 128:(tt + 1) * 128, :], in_=x1[xi][:]), reads=[B_x1[xi]], writes=[B_x1s[tt]])
                    make_uT(I, None, ci, 2, u2T, tt * 128, B_u2T[tt], src_sb=x1[xi], B_src=B_x1[xi])

                for tt in range(9):
                    if tt < 8:
                        c_front(tt)
                    if tt >= 1:
                        c_back(tt - 1)
                    if tt < 8:
                        c_front_b(tt)
                        mod_bg_c()
                if mod_pend[0] is not None:
                    mod_pend[0]()
                    mod_pend[0] = None
            P.barrier()
            check("C")

            with contextlib.ExitStack() as es:
                acc = T("acc", [128, 8, D], es=es)
                B_acc = [Buf("acc%d" % i) for i in range(8)]
                with contextlib.ExitStack() as es2:
                    hdn = T("hdn", [128, 4, 1024], BF16, es=es2)
                    B_hdn = Buf("hdn")
                    rl = [T("rl%d" % i, [128, 512], es=es2) for i in range(2)]
                    B_rl = [Buf("rl0"), Buf("rl1")]
                    w1 = [T("w1_%d" % i, [128, KT, 512], BF16, es=es2) for i in range(2)]
                    B_w1 = [Buf("w1a"), Buf("w1b")]
                    w2 = [T("w2_%d" % i, [128, 4, D], BF16, es=es2) for i in range(2)]
                    B_w2 = [Buf("w2a"), Buf("w2b")]
                    rli = [0]
                    for fb in range(16):
                        wi = fb % 2
                        P.dma(lambda e, wi=wi, fb=fb: e.dma_start(out=w1[wi][:], in_=w_mlp_in[:, fb * 512:(fb + 1) * 512].rearrange("(kt p) c -> p kt c", p=128)),
                              writes=[B_w1[wi]], eng="pool")
                        P.dma(lambda e, wi=wi, fb=fb: e.dma_start(out=w2[wi][:], in_=w_mlp_out[fb * 512:(fb + 1) * 512, :].rearrange("(ft p) c -> p ft c", p=128)),
                              writes=[B_w2[wi]], eng="pool")
                        for ft in range(4):
                            for half in range(2):
                                pt, pb = (next_tr() if (ft * 2 + half) % 2 == 0 else next_sm())
                                for kt in range(KT):
                                    P.pe(lambda e, kt=kt, ft=ft, half=half, wi=wi, pt=pt: e.matmul(pt[:], lhsT=w1[wi][:, kt, ft * 128:(ft + 1) * 128],
                                                                                                 rhs=u2T[:, kt, half * 512:(half + 1) * 512], start=(kt == 0), stop=(kt == KT - 1)),
                                         reads=B_u2T[4 * half:4 * half + 4] + [B_w1[wi]], writes=[pb])
                                ri = rli[0] % 2
                                rli[0] += 1
                                P.act(lambda e, ri=ri, pt=pt: e.activation(out=rl[ri][:], in_=pt[:], func=AF.Relu), reads=[pb], writes=[B_rl[ri]])
                                P.pool(lambda e, ri=ri, ft=ft, half=half: e.tensor_tensor(out=hdn[:, ft, half * 512:(half + 1) * 512], in0=rl[ri][:], in1=rl[ri][:], op=ALU.mult),
                                       reads=[B_rl[ri]], writes=[B_hdn])
                        for tt in range(8):
                            for cbk in range(4):
                                for ft in range(4):
                                    P.pe(lambda e, ft=ft, cbk=cbk, tt=tt, wi=wi: e.matmul(pbig[:, cbk * 512:(cbk + 1) * 512], lhsT=hdn[:, ft, tt * 128:(tt + 1) * 128],
                                                                                      rhs=w2[wi][:, ft, cbk * 512:(cbk + 1) * 512], start=(ft == 0), stop=(ft == 3)),
                                         reads=[B_hdn, B_w2[wi]], writes=[PB[cbk]])
                                if fb == 0:
                                    P.dve(lambda e, tt=tt, cbk=cbk: e.tensor_copy(out=acc[:, tt, cbk * 512:(cbk + 1) * 512], in_=pbig[:, cbk * 512:(cbk + 1) * 512]),
                                          reads=[PB[cbk]], writes=[B_acc[tt]])
                                else:
                                    P.dve(lambda e, tt=tt, cbk=cbk: e.tensor_tensor(out=acc[:, tt, cbk * 512:(cbk + 1) * 512], in0=pbig[:, cbk * 512:(cbk + 1) * 512],
                                                                                   in1=acc[:, tt, cbk * 512:(cbk + 1) * 512], op=ALU.add),
                                          reads=[PB[cbk], B_acc[tt]], writes=[B_acc[tt]])
                P.barrier()
                with contextlib.ExitStack() as es2:
                    G2 = T("G2", [128, 2, D], es=es2)
                    B_G2 = Buf("G2")
                    xr2 = [T("xr2_%d" % i, [128, D], es=es2) for i in range(2)]
                    B_xr2 = [Buf("xr2a"), Buf("xr2b")]
                    P.dma(lambda e: e.dma_start(out=xr2[0][:], in_=g_post_mlp[0:1, :].to_broadcast([128, D])), writes=[B_xr2[0]])
                    for ci in range(2):
                        P.dma(lambda e, ci=ci: e.dma_start(out=G2[:, ci, :], in_=modscr[ci:ci + 1, 5 * D:6 * D].to_broadcast([128, D])), reads=[B_modscr], writes=[B_G2])
                        P.dve(lambda e, ci=ci: e.tensor_tensor(out=G2[:, ci, :], in0=G2[:, ci, :], in1=xr2[0][:], op=ALU.mult), reads=[B_G2, B_xr2[0]], writes=[B_G2])
                    for tt in range(8):
                        ci = 0 if tt < 4 else 1
                        P.act(lambda e, tt=tt: e.activation(out=sq_junk[:], in_=acc[:, tt, :], func=AF.Square, accum_out=stat[:, 24:25]), reads=[B_acc[tt]], writes=[B_junk, B_stat])
                        rstd_of(stat[:, 24:25], stat[:, 25:26], D)
                        xi = tt % 2
                        P.dma(lambda e, xi=xi, tt=tt: e.dma_start(out=xr2[xi][:], in_=x1s[tt * 128:(tt + 1) * 128, :]), reads=[B_x1s[tt]], writes=[B_xr2[xi]])
                        P.dve(lambda e, tt=tt, ci=ci: e.scalar_tensor_tensor(out=acc[:, tt, :], in0=acc[:, tt, :], scalar=stat[:, 25:26], in1=G2[:, ci, :], op0=ALU.mult, op1=ALU.mult),
                              reads=[B_acc[tt], B_stat, B_G2], writes=[B_acc[tt]])
                        P.dve(lambda e, tt=tt, xi=xi: e.tensor_tensor(out=acc[:, tt, :], in0=acc[:, tt, :], in1=xr2[xi][:], op=ALU.add), reads=[B_acc[tt], B_xr2[xi]], writes=[B_acc[tt]])
                        dst = yp[tt * 128:(tt + 1) * 128, :] if tt < 4 else ys[(tt - 4) * 128:(tt - 3) * 128, :]
                        P.dma(lambda e, tt=tt, dst=dst: e.dma_start(out=dst, in_=acc[:, tt, :]), reads=[B_acc[tt]])
        P.emit()
    return nc, P


def build_debug(stop_after):
    try:
        return build_program(stop_after)
    except _Stop:
        pass


_CACHE = {}


def _prep_inputs(inp):
    f = lambda a: np.ascontiguousarray(np.asarray(a, dtype=np.float32))
    x_prompt = f(inp["x_prompt"])
    x_sample = f(inp["x_sample"])
    shared = {
        "w_mod": f(inp["w_mod"][0]), "b_mod": f(inp["b_mod"][0]).reshape(1, 6 * D),
        "g_pre_mix": f(inp["g_pre_mix"][0]).reshape(16, 128), "g_post_mix": f(inp["g_post_mix"][0]).reshape(1, D),
        "w_in": f(inp["w_in"][0]), "b_igate": f(inp["b_igate"][0]).reshape(1, 8), "b_fgate": f(inp["b_fgate"][0]).reshape(1, 8),
        "conv_w": f(inp["conv_w"][0]).reshape(144, 128), "conv_b": f(inp["conv_b"][0]).reshape(16, 128),
        "dt_bias": f(inp["dt_bias"][0]).reshape(1, 32), "a_log": f(inp["a_log"][0]).reshape(1, 32),
        "d_skip": f(inp["d_skip"][0]).reshape(1, 16), "g_mlstm_norm": f(inp["g_mlstm_norm"][0]).reshape(1, 1024),
        "g_ssd_norm": f(inp["g_ssd_norm"][0]).reshape(1, 1024), "w_out": f(inp["w_out"][0]),
        "g_pre_mlp": f(inp["g_pre_mlp"][0]).reshape(16, 128), "g_post_mlp": f(inp["g_post_mlp"][0]).reshape(1, D),
        "w_mlp_in": f(inp["w_mlp_in"][0]), "w_mlp_out": f(inp["w_mlp_out"][0]),
    }
    c = f(inp["c"])
    c_ctx = f(inp["c_ctx"])
    maps = []
    for r in range(8):
        b, q = r // 4, r % 4
        xs_b = x_sample[b]
        xh = np.zeros((128, D), np.float32)
        if q > 0:
            xh[0:64] = xs_b[512 * q - 64:512 * q]
        if q < 3:
            xh[64:128] = xs_b[512 * q + 512:512 * q + 576]
        fl = np.zeros((1, NFLAG), np.float32)
        fl[0, 0 + 4 * q] = 1.0
        fl[0, 16 + 4 * q + 2] = 1.0
        fl[0, 32 + 4 * q + 1] = 1.0
        fl[0, 48 + 4 * q + 3] = 1.0
        fl[0, 64] = 1.0 if q > 0 else 0.0
        fl[0, 65] = 1.0 if q < 3 else 0.0
        m = dict(shared)
        m.update({
            "xp": np.ascontiguousarray(x_prompt[2 * r:2 * r + 2].reshape(512, D)),
            "xo": np.ascontiguousarray(xs_b[512 * q:512 * q + 512]),
            "xh": xh,
            "xf": np.ascontiguousarray(xs_b),
            "cv": np.ascontiguousarray(np.stack([c_ctx, c[b]], 0).reshape(32, 128)),
            "st_c": f(inp["state_mlstm_c"][b, 0]),
            "st_n": f(inp["state_mlstm_n"][b, 0]),
            "st_m": f(inp["state_mlstm_m"][b, 0]).reshape(1, 8),
            "st_s": f(inp["state_ssd"][b, 0]).reshape(2, 1024, 128),
            "flags": fl,
        })
        maps.append(m)
    return maps


def kernel(**inp):
    if "nc" not in _CACHE:
        _CACHE["nc"] = build_program()[0]
    nc = _CACHE["nc"]
    maps = _prep_inputs(inp)
    res = run_bass_kernel_spmd(nc, maps, core_ids=list(range(8)))
    y_prompt = np.zeros((16, 256, D), np.float32)
    y_sample = np.zeros((2, 2048, D), np.float32)
    new_c = np.zeros((16, 1, 2, 4, 128, 256), np.float32)
    new_n = np.zeros((16, 1, 2, 4, 128), np.float32)
    new_m = np.zeros((16, 1, 2, 4), np.float32)
    new_s = np.zeros((16, 1, 2, 16, 64, 128), np.float32)
    for r in range(8):
        o = res.results[r]
        b, q = r // 4, r % 4
        y_prompt[2 * r:2 * r + 2] = np.asarray(o["yp"]).reshape(2, 256, D)
        y_sample[b, 512 * q:512 * q + 512] = np.asarray(o["ys"])
        new_c[2 * r:2 * r + 2, 0] = np.asarray(o["o_c"])
        new_n[2 * r:2 * r + 2, 0] = np.asarray(o["o_n"])
        new_m[2 * r:2 * r + 2, 0] = np.asarray(o["o_m"])
        new_s[2 * r:2 * r + 2, 0] = np.asarray(o["o_s"]).reshape(2, 2, 16, 64, 128)
    return (y_prompt, y_sample, new_c, new_n, new_m, new_s)
```
